# Optimizing a Trainium2 kernel written in Bass

```python
import math
import jax
import jax.numpy as jnp
from jax import lax
import numpy as np

D_MODEL = 1024
BATCH = 8
SEQ = 2048
DEPTH = 2
DEC_BATCH = 32
DEC_SEQ = 1
PAST_LEN = 16384
PAGE_SIZE = 128

GROUP_WIDTH = 256
N_MIX_GROUPS = 4
MIX_WIDTH = GROUP_WIDTH * N_MIX_GROUPS
HEAD_DIM = 64

A_HEADS = GROUP_WIDTH // HEAD_DIM
MOBA_BLOCK = 256
MOBA_TOPK = 3
MOBA_QBLK = 64
ROPE_THETA = 10000.0

LRU_WIDTH = GROUP_WIDTH
LRU_BLOCKS = 4
LRU_BDIM = LRU_WIDTH // LRU_BLOCKS
LRU_C = 8.0
CONV_W = 4

SSD_HEADS = 4
SSD_HEADDIM = GROUP_WIDTH // SSD_HEADS
SSD_GROUPS = 2
SSD_STATE = 64
SSD_CHUNK = 128
SSD_CONV_CH = GROUP_WIDTH + 2 * SSD_GROUPS * SSD_STATE

RWKV_HEADS = 4
RWKV_HEADDIM = GROUP_WIDTH // RWKV_HEADS
RWKV_W_RANK = 32
RWKV_A_RANK = 32
RWKV_G_RANK = 64
RWKV_COLS = 3 * GROUP_WIDTH + RWKV_W_RANK + RWKV_A_RANK + RWKV_G_RANK
RWKV_LN_EPS = 64e-5

A_OFF = 0
A_COLS = 3 * GROUP_WIDTH
B_OFF = A_OFF + A_COLS
B_COLS = 2 * LRU_WIDTH
C_OFF = B_OFF + B_COLS
C_COLS = GROUP_WIDTH + SSD_CONV_CH + SSD_HEADS
D_OFF = C_OFF + C_COLS
IN_WIDTH = D_OFF + RWKV_COLS

MEM_LEN = 256
X_HEADS = 4
X_HEADDIM = D_MODEL // X_HEADS

PEER_HEADS = 8
PEER_NKEYS = 128
PEER_EXPERTS = PEER_NKEYS * PEER_NKEYS
PEER_QDIM = 256
PEER_HALF = PEER_QDIM // 2
PEER_TOPK = 16
PEER_TOKBLK = 128

NORM_EPS = 1e-6
NEG_INF = -1e30

kernel_name = 'hybrid_moba_lru_ssd_rwkv_peer_step'


def rmsnorm(x, g):
    xf = x.astype(jnp.float32)
    y = xf * lax.rsqrt(jnp.mean(xf * xf, axis=-1, keepdims=True) + NORM_EPS)
    return (y * g.astype(jnp.float32)).astype(x.dtype)


def rope(x, pos):
    half = x.shape[-1] // 2
    freq = 1.0 / (ROPE_THETA ** (jnp.arange(half, dtype=jnp.float32) / half))
    ang = pos.astype(jnp.float32)[:, None] * freq[None, :]
    cos = jnp.cos(ang)[None, :, None, :]
    sin = jnp.sin(ang)[None, :, None, :]
    xf = x.astype(jnp.float32)
    x1, x2 = xf[..., :half], xf[..., half:]
    return jnp.concatenate([x1 * cos - x2 * sin, x2 * cos + x1 * sin], axis=-1).astype(x.dtype)


def causal_conv(x, buf, w, b):
    T = x.shape[1]
    xp = jnp.concatenate([buf.astype(x.dtype), x], axis=1)
    y = b + xp[:, 0:T] * w[0]
    for j in range(1, CONV_W):
        y = y + xp[:, j:j + T] * w[j]
    return y, xp[:, -(CONV_W - 1):]


def linear_scan(a, b, h0):
    b = b.at[:, 0].add(a[:, 0] * h0)
    def comb(l, r):
        al, bl = l
        ar, br = r
        return al * ar, ar * bl + br
    _, h = lax.associative_scan(comb, (a, b), axis=1)
    return h


def moba_attend(q, k_all, v_all, q_pos0):
    bsz, T, H, hd = q.shape
    L = k_all.shape[1]
    nb = -(-L // MOBA_BLOCK)
    pad = nb * MOBA_BLOCK - L
    def to_blocks(t):
        t = jnp.pad(t, ((0, 0), (0, pad), (0, 0), (0, 0)))
        return t.reshape(bsz, nb, MOBA_BLOCK, H, hd).transpose(0, 3, 1, 2, 4)
    kb = to_blocks(k_all)
    vb = to_blocks(v_all)
    kmean = jnp.mean(kb.astype(jnp.float32), axis=3)
    n_sel = min(MOBA_TOPK, nb)
    qblk = math.gcd(T, MOBA_QBLK)
    nq = T // qblk
    qs = q.reshape(bsz, nq, qblk, H, hd).transpose(1, 0, 3, 2, 4)
    pos = (q_pos0 + jnp.arange(T, dtype=jnp.int32)).reshape(nq, qblk)
    bi = jnp.arange(bsz)[:, None, None, None]
    hi = jnp.arange(H)[None, :, None, None]
    offs = jnp.arange(MOBA_BLOCK, dtype=jnp.int32)
    scale = hd ** -0.5

    def one_block(args):
        qb, pb = args
        cur = pb // MOBA_BLOCK
        gate = jnp.einsum('bhqd,bhnd->bhqn', qb.astype(jnp.float32), kmean)
        past = jnp.arange(nb, dtype=jnp.int32)[None, :] < cur[:, None]
        gate = jnp.where(past, gate, NEG_INF)
        _, top = lax.top_k(gate, n_sel)
        own = jnp.broadcast_to(cur[:, None], (bsz, H, qblk, 1)).astype(top.dtype)
        idx = jnp.concatenate([top, own], axis=-1)
        ok = jnp.concatenate([top < cur[:, None], jnp.ones(own.shape, dtype=bool)], axis=-1)
        kg = kb[bi, hi, idx]
        vg = vb[bi, hi, idx]
        s = jnp.einsum('bhqd,bhqnkd->bhqnk', qb, kg).astype(jnp.float32) * scale
        kpos = idx[..., None] * MOBA_BLOCK + offs
        mask = ok[..., None] & (kpos <= pb[:, None, None])
        s = jnp.where(mask, s, NEG_INF)
        p = jax.nn.softmax(s.reshape(bsz, H, qblk, -1), axis=-1).reshape(s.shape)
        return jnp.einsum('bhqnk,bhqnkd->bhqd', p.astype(vg.dtype), vg)

    out = lax.map(one_block, (qs, pos))
    return out.transpose(1, 0, 3, 2, 4).reshape(bsz, T, H * hd)


def rglru_mixer(u, gate, conv_buf, h0, conv_w, conv_b, wa, ba, wx, bx, lam):
    bsz, T, W = u.shape
    xc, new_buf = causal_conv(u, conv_buf, conv_w, conv_b)
    xh = xc.reshape(bsz, T, LRU_BLOCKS, LRU_BDIM)
    r = jax.nn.sigmoid(jnp.einsum('btnd,nde->btne', xh, wa).reshape(bsz, T, W) + ba).astype(jnp.float32)
    i = jax.nn.sigmoid(jnp.einsum('btnd,nde->btne', xh, wx).reshape(bsz, T, W) + bx).astype(jnp.float32)
    log_a = -LRU_C * r * jax.nn.softplus(-lam.astype(jnp.float32))
    a = jnp.exp(log_a)
    b = jnp.sqrt(-jnp.expm1(2.0 * log_a)) * (i * xc.astype(jnp.float32))
    h = linear_scan(a, b, h0.astype(jnp.float32))
    y = h * jax.nn.gelu(gate.astype(jnp.float32))
    return y.astype(u.dtype), new_buf, h[:, -1]


def ssd_chunked(xdt, dA, Bm, Cm, s0):
    bsz, T, H, P = xdt.shape
    N = Bm.shape[-1]
    Q = SSD_CHUNK
    nc = T // Q
    xc = xdt.reshape(bsz, nc, Q, H, P)
    Bc = Bm.reshape(bsz, nc, Q, H, N)
    Cc = Cm.reshape(bsz, nc, Q, H, N)
    cs = jnp.cumsum(dA.reshape(bsz, nc, Q, H), axis=2)
    seg = cs[:, :, :, None, :] - cs[:, :, None, :, :]
    causal = jnp.tril(jnp.ones((Q, Q), dtype=bool))[None, None, :, :, None]
    Lm = jnp.exp(jnp.where(causal, seg, -jnp.inf))
    y_diag = jnp.einsum('bcihn,bcjhn,bcijh,bcjhp->bcihp', Cc, Bc, Lm, xc)
    decay_to_end = jnp.exp(cs[:, :, -1:, :] - cs)
    chunk_state = jnp.einsum('bcjhn,bcjh,bcjhp->bchpn', Bc, decay_to_end, xc)
    chunk_decay = jnp.exp(cs[:, :, -1, :])
    def step(s, inp):
        cst, cd = inp
        return s * cd[:, :, None, None] + cst, s
    s_final, s_starts = lax.scan(step, s0, (jnp.moveaxis(chunk_state, 1, 0), jnp.moveaxis(chunk_decay, 1, 0)))
    s_starts = jnp.moveaxis(s_starts, 0, 1)
    y_off = jnp.einsum('bcihn,bchpn,bcih->bcihp', Cc, s_starts, jnp.exp(cs))
    return (y_diag + y_off).reshape(bsz, T, H, P), s_final


def ssd_recurrent(xdt, dA, Bm, Cm, s0):
    def step(s, inp):
        xt, dat, bt, ct = inp
        s = s * jnp.exp(dat)[:, :, None, None] + jnp.einsum('bhp,bhn->bhpn', xt, bt)
        return s, jnp.einsum('bhpn,bhn->bhp', s, ct)
    tm = lambda t: jnp.moveaxis(t, 1, 0)
    s, ys = lax.scan(step, s0, (tm(xdt), tm(dA), tm(Bm), tm(Cm)))
    return jnp.moveaxis(ys, 0, 1), s


def ssd_mixer(z, xbc, dt_raw, conv_buf, s0, conv_w, conv_b, dt_bias, a_log, d_skip, norm_w, prompt):
    bsz, T, _ = z.shape
    G = GROUP_WIDTH
    GN = SSD_GROUPS * SSD_STATE
    xbc, new_buf = causal_conv(xbc, conv_buf, conv_w, conv_b)
    xbc_f = jax.nn.silu(xbc).astype(jnp.float32)
    x = xbc_f[..., :G].reshape(bsz, T, SSD_HEADS, SSD_HEADDIM)
    rep = SSD_HEADS // SSD_GROUPS
    Bm = jnp.repeat(xbc_f[..., G:G + GN].reshape(bsz, T, SSD_GROUPS, SSD_STATE), rep, axis=2)
    Cm = jnp.repeat(xbc_f[..., G + GN:].reshape(bsz, T, SSD_GROUPS, SSD_STATE), rep, axis=2)
    dt = jax.nn.softplus(dt_raw.astype(jnp.float32) + dt_bias.astype(jnp.float32))
    dA = dt * (-jnp.exp(a_log.astype(jnp.float32)))
    xdt = x * dt[..., None]
    s0 = s0.astype(jnp.float32)
    if prompt:
        y, s = ssd_chunked(xdt, dA, Bm, Cm, s0)
    else:
        y, s = ssd_recurrent(xdt, dA, Bm, Cm, s0)
    y = y + d_skip.astype(jnp.float32)[:, None] * x
    y = y.reshape(bsz, T, G) * jax.nn.silu(z.astype(jnp.float32))
    return rmsnorm(y, norm_w).astype(z.dtype), new_buf, s


def rwkv_mixer(cur, shift_buf, s0, mu, w0, w_up, a0, a_up, g_up, k_k, k_a, r_k, ln_w, ln_b):
    bsz, T, _ = cur.shape
    G = GROUP_WIDTH
    prev = jnp.concatenate([shift_buf[:, None].astype(cur.dtype), cur[:, :-1]], axis=1)
    m = (cur + (prev - cur) * mu).astype(jnp.float32)
    r, k, v = m[..., :G], m[..., G:2 * G], m[..., 2 * G:3 * G]
    o = 3 * G
    wd = m[..., o:o + RWKV_W_RANK]
    o += RWKV_W_RANK
    ad = m[..., o:o + RWKV_A_RANK]
    o += RWKV_A_RANK
    gd = m[..., o:o + RWKV_G_RANK]
    w = -jax.nn.softplus(-(w0 + jnp.tanh(wd) @ w_up)) - 0.5
    decay = jnp.exp(-jnp.exp(w))
    a = jax.nn.sigmoid(a0 + ad @ a_up)
    g = jax.nn.sigmoid(gd) @ g_up
    hs = lambda t: t.reshape(bsz, T, RWKV_HEADS, RWKV_HEADDIM)
    kk = hs(k * k_k)
    kk = kk / jnp.maximum(jnp.sqrt(jnp.sum(kk * kk, axis=-1, keepdims=True)), 1e-12)
    k = hs(k * (1.0 + (a - 1.0) * k_a))
    r, v, decay, a = hs(r), hs(v), hs(decay), hs(a)
    def step(S, inp):
        rt, kt, vt, dt, kkt, at = inp
        sa = jnp.einsum('bhvk,bhk->bhv', S, -kkt)
        S = S * dt[:, :, None, :] + sa[..., None] * (kkt * at)[:, :, None, :] + vt[..., None] * kt[:, :, None, :]
        return S, jnp.einsum('bhvk,bhk->bhv', S, rt)
    tm = lambda t: jnp.moveaxis(t, 1, 0)
    S, ys = lax.scan(step, s0.astype(jnp.float32), (tm(r), tm(k), tm(v), tm(decay), tm(kk), tm(a)))
    y = jnp.moveaxis(ys, 0, 1)
    mean = jnp.mean(y, axis=-1, keepdims=True)
    var = jnp.mean((y - mean) ** 2, axis=-1, keepdims=True)
    y = ((y - mean) * lax.rsqrt(var + RWKV_LN_EPS)).reshape(bsz, T, G) * ln_w + ln_b
    bonus = jnp.sum(r * k * r_k, axis=-1, keepdims=True) * v
    y = (y + bonus.reshape(bsz, T, G)) * g
    return y.astype(cur.dtype), cur[:, -1], S


def memory_kv(mem, wk, wv):
    bsz, M, _ = mem.shape
    k = (mem @ wk).reshape(bsz, M, X_HEADS, X_HEADDIM)
    v = (mem @ wv).reshape(bsz, M, X_HEADS, X_HEADDIM)
    return k, v


def cross_attn(h, mk, mv, wq, wo):
    bsz, T, _ = h.shape
    q = (h @ wq).reshape(bsz, T, X_HEADS, X_HEADDIM)
    s = jnp.einsum('bthd,bmhd->bhtm', q, mk.astype(q.dtype)).astype(jnp.float32) * (X_HEADDIM ** -0.5)
    p = jax.nn.softmax(s, axis=-1)
    o = jnp.einsum('bhtm,bmhd->bthd', p.astype(q.dtype), mv.astype(q.dtype)).reshape(bsz, T, D_MODEL)
    return o @ wo


def peer_ffn(h, wq, subkeys, u_tab, v_tab):
    bsz, T, D = h.shape
    n = bsz * T
    nblk = -(-n // PEER_TOKBLK)
    hp = jnp.pad(h.reshape(n, D), ((0, nblk * PEER_TOKBLK - n), (0, 0))).reshape(nblk, PEER_TOKBLK, D)
    sk = subkeys.astype(jnp.float32)
    def one(hb):
        q = (hb @ wq).astype(jnp.float32).reshape(PEER_TOKBLK, PEER_HEADS, 2, PEER_HALF)
        s = jnp.einsum('thcd,hckd->thck', q, sk)
        s1, i1 = lax.top_k(s[:, :, 0], PEER_TOPK)
        s2, i2 = lax.top_k(s[:, :, 1], PEER_TOPK)
        cand = (s1[..., :, None] + s2[..., None, :]).reshape(PEER_TOKBLK, PEER_HEADS, PEER_TOPK * PEER_TOPK)
        cidx = (i1[..., :, None] * PEER_NKEYS + i2[..., None, :]).reshape(PEER_TOKBLK, PEER_HEADS, PEER_TOPK * PEER_TOPK)
        top_s, pick = lax.top_k(cand, PEER_TOPK)
        e = jnp.take_along_axis(cidx, pick, axis=-1)
        g = jax.nn.softmax(top_s, axis=-1)
        act = jax.nn.gelu(jnp.einsum('td,thkd->thk', hb, u_tab[e]).astype(jnp.float32))
        return jnp.einsum('thk,thkd->td', (g * act).astype(hb.dtype), v_tab[e])
    out = lax.map(one, hp)
    return out.reshape(-1, D)[:n].reshape(bsz, T, D)


def trunk_layer(x, lp, st, pos0, prompt):
    bsz, T, _ = x.shape
    G = GROUP_WIDTH
    h = rmsnorm(x, lp['norm_mix'])
    proj = h @ lp['w_in']
    pos = pos0 + jnp.arange(T, dtype=jnp.int32)
    pa = proj[..., A_OFF:A_OFF + A_COLS]
    heads = lambda t: t.reshape(bsz, T, A_HEADS, HEAD_DIM)
    q = rope(heads(pa[..., :G]), pos)
    k = rope(heads(pa[..., G:2 * G]), pos)
    v = heads(pa[..., 2 * G:])
    if prompt:
        k_all, v_all = k, v
    else:
        k_all = jnp.concatenate([st['k_past'].astype(k.dtype), k], axis=1)
        v_all = jnp.concatenate([st['v_past'].astype(v.dtype), v], axis=1)
    ya = moba_attend(q, k_all, v_all, pos0)
    pb = proj[..., B_OFF:B_OFF + B_COLS]
    yb, lru_conv, lru_h = rglru_mixer(pb[..., :LRU_WIDTH], pb[..., LRU_WIDTH:], st['lru_conv'], st['lru_h'],
                                      lp['lru_conv_w'], lp['lru_conv_b'], lp['lru_wa'], lp['lru_ba'],
                                      lp['lru_wx'], lp['lru_bx'], lp['lru_lambda'])
    pc = proj[..., C_OFF:C_OFF + C_COLS]
    yc, ssd_conv, ssd_s = ssd_mixer(pc[..., :G], pc[..., G:G + SSD_CONV_CH], pc[..., G + SSD_CONV_CH:],
                                    st['ssd_conv'], st['ssd'], lp['ssd_conv_w'], lp['ssd_conv_b'],
                                    lp['ssd_dt_bias'], lp['ssd_a_log'], lp['ssd_d'], lp['ssd_norm'], prompt)
    yd, rwkv_shift, rwkv_s = rwkv_mixer(proj[..., D_OFF:D_OFF + RWKV_COLS], st['rwkv_shift'], st['rwkv'],
                                        lp['rwkv_mu'], lp['rwkv_w0'], lp['rwkv_w_up'], lp['rwkv_a0'],
                                        lp['rwkv_a_up'], lp['rwkv_g_up'], lp['rwkv_k_k'], lp['rwkv_k_a'],
                                        lp['rwkv_r_k'], lp['rwkv_ln_w'], lp['rwkv_ln_b'])
    y_mix = jnp.concatenate([ya.astype(x.dtype), yb.astype(x.dtype), yc.astype(x.dtype), yd.astype(x.dtype)], axis=-1)
    x = x + (y_mix @ lp['w_out']).astype(x.dtype)
    x = x + cross_attn(rmsnorm(x, lp['norm_x']), st['mem_k'], st['mem_v'], lp['x_wq'], lp['x_wo']).astype(x.dtype)
    x = x + peer_ffn(rmsnorm(x, lp['norm_ffn']), lp['peer_wq'], lp['peer_subkeys'], lp['peer_u'], lp['peer_v']).astype(x.dtype)
    new = {'k': k, 'v': v, 'lru_h': lru_h, 'lru_conv': lru_conv, 'ssd': ssd_s, 'ssd_conv': ssd_conv,
           'rwkv': rwkv_s, 'rwkv_shift': rwkv_shift}
    return x, new


def setup_inputs(seed: int = 0) -> dict:
    key = jax.random.key(seed)
    ks = jax.random.split(key, 96)
    kit = iter([ks[i] for i in range(96)])
    f32 = jnp.float32
    def nrm(shape, scale):
        return jax.random.normal(next(kit), shape, f32) * scale
    def gain(shape):
        return 1.0 + 0.02 * jax.random.normal(next(kit), shape, f32)
    def unif(shape, lo, hi):
        return jax.random.uniform(next(kit), shape, f32, lo, hi)
    n_pages = PAST_LEN // PAGE_SIZE
    n_used = DEC_BATCH * n_pages
    n_phys = n_used + max(1, n_used // 4)
    page_table = jax.random.permutation(next(kit), n_phys)[:n_used].reshape(DEC_BATCH, n_pages).astype(jnp.int32)
    sig = unif((DEPTH, LRU_WIDTH), 0.9, 0.999)
    dt0 = jnp.exp(unif((DEPTH, SSD_HEADS), math.log(1e-3), math.log(1e-1)))
    return {
        'x_prompt': nrm((BATCH, SEQ, D_MODEL), 1.0),
        'x_sample': nrm((DEC_BATCH, DEC_SEQ, D_MODEL), 1.0),
        'mem_prompt': nrm((BATCH, MEM_LEN, D_MODEL), 1.0),
        'cache_moba_k': nrm((DEPTH, n_phys, PAGE_SIZE, A_HEADS, HEAD_DIM), 1.0),
        'cache_moba_v': nrm((DEPTH, n_phys, PAGE_SIZE, A_HEADS, HEAD_DIM), 1.0),
        'page_table': page_table,
        'state_lru_h': nrm((DEPTH, DEC_BATCH, LRU_WIDTH), 0.5),
        'state_lru_conv': nrm((DEPTH, DEC_BATCH, CONV_W - 1, LRU_WIDTH), 1.0),
        'state_ssd': nrm((DEPTH, DEC_BATCH, SSD_HEADS, SSD_HEADDIM, SSD_STATE), 0.1),
        'state_ssd_conv': nrm((DEPTH, DEC_BATCH, CONV_W - 1, SSD_CONV_CH), 1.0),
        'state_rwkv': nrm((DEPTH, DEC_BATCH, RWKV_HEADS, RWKV_HEADDIM, RWKV_HEADDIM), 0.1),
        'state_rwkv_shift': nrm((DEPTH, DEC_BATCH, RWKV_COLS), 1.0),
        'cache_mem_k': nrm((DEPTH, DEC_BATCH, MEM_LEN, X_HEADS, X_HEADDIM), 1.0),
        'cache_mem_v': nrm((DEPTH, DEC_BATCH, MEM_LEN, X_HEADS, X_HEADDIM), 1.0),
        'norm_mix': gain((DEPTH, D_MODEL)),
        'w_in': nrm((DEPTH, D_MODEL, IN_WIDTH), D_MODEL ** -0.5),
        'w_out': nrm((DEPTH, MIX_WIDTH, D_MODEL), MIX_WIDTH ** -0.5),
        'lru_conv_w': nrm((DEPTH, CONV_W, LRU_WIDTH), 0.5),
        'lru_conv_b': nrm((DEPTH, LRU_WIDTH), 0.02),
        'lru_wa': nrm((DEPTH, LRU_BLOCKS, LRU_BDIM, LRU_BDIM), LRU_BDIM ** -0.5),
        'lru_ba': nrm((DEPTH, LRU_WIDTH), 0.1),
        'lru_wx': nrm((DEPTH, LRU_BLOCKS, LRU_BDIM, LRU_BDIM), LRU_BDIM ** -0.5),
        'lru_bx': nrm((DEPTH, LRU_WIDTH), 0.1),
        'lru_lambda': jnp.log(sig) - jnp.log1p(-sig),
        'ssd_conv_w': nrm((DEPTH, CONV_W, SSD_CONV_CH), 0.5),
        'ssd_conv_b': nrm((DEPTH, SSD_CONV_CH), 0.02),
        'ssd_dt_bias': dt0 + jnp.log(-jnp.expm1(-dt0)),
        'ssd_a_log': jnp.log(unif((DEPTH, SSD_HEADS), 1.0, 16.0)),
        'ssd_d': gain((DEPTH, SSD_HEADS)),
        'ssd_norm': gain((DEPTH, GROUP_WIDTH)),
        'rwkv_mu': unif((DEPTH, RWKV_COLS), 0.0, 1.0),
        'rwkv_w0': unif((DEPTH, GROUP_WIDTH), -4.0, -0.5),
        'rwkv_w_up': nrm((DEPTH, RWKV_W_RANK, GROUP_WIDTH), 0.1),
        'rwkv_a0': nrm((DEPTH, GROUP_WIDTH), 0.1),
        'rwkv_a_up': nrm((DEPTH, RWKV_A_RANK, GROUP_WIDTH), 0.1),
        'rwkv_g_up': nrm((DEPTH, RWKV_G_RANK, GROUP_WIDTH), RWKV_G_RANK ** -0.5),
        'rwkv_k_k': 0.85 + nrm((DEPTH, GROUP_WIDTH), 0.05),
        'rwkv_k_a': 1.0 + nrm((DEPTH, GROUP_WIDTH), 0.05),
        'rwkv_r_k': nrm((DEPTH, RWKV_HEADS, RWKV_HEADDIM), 0.1),
        'rwkv_ln_w': gain((DEPTH, GROUP_WIDTH)),
        'rwkv_ln_b': nrm((DEPTH, GROUP_WIDTH), 0.02),
        'norm_x': gain((DEPTH, D_MODEL)),
        'x_wq': nrm((DEPTH, D_MODEL, D_MODEL), D_MODEL ** -0.5),
        'x_wk': nrm((DEPTH, D_MODEL, D_MODEL), D_MODEL ** -0.5),
        'x_wv': nrm((DEPTH, D_MODEL, D_MODEL), D_MODEL ** -0.5),
        'x_wo': nrm((DEPTH, D_MODEL, D_MODEL), D_MODEL ** -0.5),
        'norm_ffn': gain((DEPTH, D_MODEL)),
        'peer_wq': nrm((DEPTH, D_MODEL, PEER_HEADS * PEER_QDIM), D_MODEL ** -0.5),
        'peer_subkeys': nrm((DEPTH, PEER_HEADS, 2, PEER_NKEYS, PEER_HALF), PEER_HALF ** -0.5),
        'peer_u': nrm((DEPTH, PEER_EXPERTS, D_MODEL), D_MODEL ** -0.5),
        'peer_v': nrm((DEPTH, PEER_EXPERTS, D_MODEL), 0.1),
        'final_norm': gain((D_MODEL,)),
    }


def reference(x_prompt, x_sample, mem_prompt, cache_moba_k, cache_moba_v, page_table,
              state_lru_h, state_lru_conv, state_ssd, state_ssd_conv, state_rwkv, state_rwkv_shift,
              cache_mem_k, cache_mem_v,
              norm_mix, w_in, w_out,
              lru_conv_w, lru_conv_b, lru_wa, lru_ba, lru_wx, lru_bx, lru_lambda,
              ssd_conv_w, ssd_conv_b, ssd_dt_bias, ssd_a_log, ssd_d, ssd_norm,
              rwkv_mu, rwkv_w0, rwkv_w_up, rwkv_a0, rwkv_a_up, rwkv_g_up, rwkv_k_k, rwkv_k_a,
              rwkv_r_k, rwkv_ln_w, rwkv_ln_b,
              norm_x, x_wq, x_wk, x_wv, x_wo,
              norm_ffn, peer_wq, peer_subkeys, peer_u, peer_v,
              final_norm):
    n_pages = PAST_LEN // PAGE_SIZE
    bp = x_prompt.shape[0]
    bd = x_sample.shape[0]
    dt = x_prompt.dtype
    f32 = jnp.float32
    yp, ys = x_prompt, x_sample
    p_new = {n: [] for n in ('k', 'v', 'lru_h', 'lru_conv', 'ssd', 'ssd_conv', 'rwkv', 'rwkv_shift', 'mem_k', 'mem_v')}
    s_new = {n: [] for n in ('k', 'v', 'lru_h', 'lru_conv', 'ssd', 'ssd_conv', 'rwkv', 'rwkv_shift')}
    for l in range(DEPTH):
        lp = {
            'norm_mix': norm_mix[l], 'w_in': w_in[l], 'w_out': w_out[l],
            'lru_conv_w': lru_conv_w[l], 'lru_conv_b': lru_conv_b[l], 'lru_wa': lru_wa[l], 'lru_ba': lru_ba[l],
            'lru_wx': lru_wx[l], 'lru_bx': lru_bx[l], 'lru_lambda': lru_lambda[l],
            'ssd_conv_w': ssd_conv_w[l], 'ssd_conv_b': ssd_conv_b[l], 'ssd_dt_bias': ssd_dt_bias[l],
            'ssd_a_log': ssd_a_log[l], 'ssd_d': ssd_d[l], 'ssd_norm': ssd_norm[l],
            'rwkv_mu': rwkv_mu[l], 'rwkv_w0': rwkv_w0[l], 'rwkv_w_up': rwkv_w_up[l], 'rwkv_a0': rwkv_a0[l],
            'rwkv_a_up': rwkv_a_up[l], 'rwkv_g_up': rwkv_g_up[l], 'rwkv_k_k': rwkv_k_k[l], 'rwkv_k_a': rwkv_k_a[l],
            'rwkv_r_k': rwkv_r_k[l], 'rwkv_ln_w': rwkv_ln_w[l], 'rwkv_ln_b': rwkv_ln_b[l],
            'norm_x': norm_x[l], 'x_wq': x_wq[l], 'x_wo': x_wo[l],
            'norm_ffn': norm_ffn[l], 'peer_wq': peer_wq[l], 'peer_subkeys': peer_subkeys[l],
            'peer_u': peer_u[l], 'peer_v': peer_v[l],
        }
        mk, mv = memory_kv(mem_prompt, x_wk[l], x_wv[l])
        st_p = {
            'k_past': None, 'v_past': None,
            'lru_conv': jnp.zeros((bp, CONV_W - 1, LRU_WIDTH), dt),
            'lru_h': jnp.zeros((bp, LRU_WIDTH), f32),
            'ssd_conv': jnp.zeros((bp, CONV_W - 1, SSD_CONV_CH), dt),
            'ssd': jnp.zeros((bp, SSD_HEADS, SSD_HEADDIM, SSD_STATE), f32),
            'rwkv_shift': jnp.zeros((bp, RWKV_COLS), dt),
            'rwkv': jnp.zeros((bp, RWKV_HEADS, RWKV_HEADDIM, RWKV_HEADDIM), f32),
            'mem_k': mk, 'mem_v': mv,
        }
        yp, npl = trunk_layer(yp, lp, st_p, 0, True)
        for n in s_new:
            p_new[n].append(npl[n])
        p_new['mem_k'].append(mk)
        p_new['mem_v'].append(mv)
        k_past = cache_moba_k[l][page_table].reshape(bd, n_pages * PAGE_SIZE, A_HEADS, HEAD_DIM)
        v_past = cache_moba_v[l][page_table].reshape(bd, n_pages * PAGE_SIZE, A_HEADS, HEAD_DIM)
        st_s = {
            'k_past': k_past, 'v_past': v_past,
            'lru_conv': state_lru_conv[l], 'lru_h': state_lru_h[l],
            'ssd_conv': state_ssd_conv[l], 'ssd': state_ssd[l],
            'rwkv_shift': state_rwkv_shift[l], 'rwkv': state_rwkv[l],
            'mem_k': cache_mem_k[l], 'mem_v': cache_mem_v[l],
        }
        ys, nsl = trunk_layer(ys, lp, st_s, PAST_LEN, False)
        for n in s_new:
            s_new[n].append(nsl[n])
    y_prompt = rmsnorm(yp, final_norm)
    y_sample = rmsnorm(ys, final_norm)
    return (y_prompt, y_sample,
            jnp.stack(p_new['k']), jnp.stack(p_new['v']), jnp.stack(p_new['lru_h']), jnp.stack(p_new['lru_conv']),
            jnp.stack(p_new['ssd']), jnp.stack(p_new['ssd_conv']), jnp.stack(p_new['rwkv']), jnp.stack(p_new['rwkv_shift']),
            jnp.stack(p_new['mem_k']), jnp.stack(p_new['mem_v']),
            jnp.stack(s_new['k']), jnp.stack(s_new['v']), jnp.stack(s_new['lru_h']), jnp.stack(s_new['lru_conv']),
            jnp.stack(s_new['ssd']), jnp.stack(s_new['ssd_conv']), jnp.stack(s_new['rwkv']), jnp.stack(s_new['rwkv_shift']))
```

```python
import numpy as np
import ml_dtypes
import concourse.bass as bass
import concourse.mybir as mybir
from concourse.bass_utils import run_bass_kernel_spmd

F32 = mybir.dt.float32
BF16 = mybir.dt.bfloat16
I32 = mybir.dt.int32
U32 = mybir.dt.uint32
AF = mybir.ActivationFunctionType
ALU = mybir.AluOpType
AX = mybir.AxisListType

NCORES = 8
D = 1024
T = 2048
NT = 16
TB = 512
NTB = 4
NS = 4
DEPTH = 2
INW = 2948
NEG = -30000.0


class Buf:
    def __init__(self, name, t=None):
        self.name = name
        self.t = t
        self.w = None
        self.rs = {}
        self.dsem = None
        self.dcnt = 0
        self.excl = False

    def __getitem__(self, k):
        return self.t[k]


class Prog:
    def __init__(self):
        self.nc = bass.Bass("TRN2", target_bir_lowering=False)
        nc = self.nc
        self.E = {'pe': nc.tensor, 'dve': nc.vector, 'act': nc.scalar, 'pool': nc.gpsimd, 'sp': nc.sync}
        self.sem = {e: nc.alloc_semaphore("sem_" + e) for e in self.E}
        self.cnt = {e: 0 for e in self.E}
        self.seen = {e: {} for e in self.E}
        self.out_events = []
        self.nsem = 5
        self.ninst = 0
        self.dbufs = []
        self.nops = 0
        self.stop_at = None
        self.oplog = None
        self.arenas = {}
        self.dsems = {}

    def sb(self, name, shape, dt=F32):
        return Buf(name, self.nc.alloc_sbuf_tensor(name, list(shape), dt))

    def arena(self, name, nbytes):
        a = self.nc.alloc_sbuf_tensor(name, [128, nbytes // 4], F32)
        self.arenas[name] = [a, 0, nbytes]

    def carve(self, an, name, shape, dt=F32):
        a = self.arenas[an]
        esz = 4 if dt in (F32, I32, U32) else 2
        n = int(np.prod(shape[1:]))
        nb = (n * esz + 3) // 4 * 4
        assert a[1] + nb <= a[2], "arena %s overflow carving %s (%d + %d > %d)" % (an, name, a[1], nb, a[2])
        ap = a[0][0:shape[0], a[1] // 4:(a[1] + nb) // 4]
        if esz == 2:
            ap = ap.bitcast(dt)[:, 0:n]
        elif dt != F32:
            ap = ap.bitcast(dt)
        if len(shape) == 3:
            ap = ap.rearrange("p (a b) -> p a b", a=shape[1])
        elif len(shape) == 4:
            ap = ap.rearrange("p (a b c) -> p a b c", a=shape[1], b=shape[2])
        a[1] += nb
        return Buf(name, ap)

    def reset(self, an):
        self.barrier()
        self.arenas[an][1] = 0

    def barrier(self):
        for e in self.E:
            for e2 in self.E:
                if e2 != e and e2 != 'sp' and self.cnt[e2] > 0:
                    self._wait(e, (self.sem[e2], self.cnt[e2], 'bar'))
            for b in self.dbufs:
                self._wait(e, (b.dsem, b.dcnt, 'dma'))

    def ps(self, name, shape, dt=F32):
        b = Buf(name, self.nc.alloc_psum_tensor(name, list(shape), dt))
        b.excl = True
        return b

    def din(self, name, shape, dt=F32):
        return self.nc.dram_tensor(name, list(shape), dt, kind="ExternalInput").ap()

    def dout(self, name, shape, dt=F32):
        return self.nc.dram_tensor(name, list(shape), dt, kind="ExternalOutput").ap()

    def _wait(self, eng, ev):
        if ev is None:
            return
        sem, val, src = ev
        if eng == 'pe' and src == 'pe':
            return
        k = id(sem)
        if self.seen[eng].get(k, 0) >= val:
            return
        self.E[eng].wait_ge(sem, val)
        self.seen[eng][k] = val
        self.ninst += 1

    def _deps(self, eng, r, w):
        for b in r:
            self._wait(eng, b.w)
            if b.excl:
                for ev in list(b.rs.values()):
                    self._wait(eng, ev)
        for b in w:
            self._wait(eng, b.w)
            for ev in list(b.rs.values()):
                self._wait(eng, ev)

    def _commit(self, ev, r, w):
        for b in r:
            b.rs[id(ev[0])] = ev
        for b in w:
            b.w = ev
            b.rs = {}

    def op(self, eng, fn, r=(), w=()):
        self.nops += 1
        if self.oplog is not None:
            import sys as _s
            f = _s._getframe(1)
            while f.f_code.co_name not in ('build',) and f.f_back is not None:
                f = f.f_back
            self.oplog.append((self.nops, eng, f.f_lineno))
        if self.stop_at is not None and self.nops > self.stop_at:
            return
        self._deps(eng, r, w)
        ins = fn(self.E[eng])
        if isinstance(ins, (list, tuple)):
            self.ninst += len(ins)
            ins = ins[-1]
        else:
            self.ninst += 1
        self.cnt[eng] += 1
        ins.then_inc(self.sem[eng], 1)
        ev = (self.sem[eng], self.cnt[eng], eng)
        self._commit(ev, r, w)

    def dma(self, q, pairs, r=(), w=(), sembuf=None, is_out=False, **kw):
        self.nops += 1
        if self.stop_at is not None and self.nops > self.stop_at:
            return
        self._deps(q, r, w)
        sb0 = sembuf if sembuf is not None else (w[0] if len(w) else r[0])
        if sb0.name not in self.dsems:
            self.dsems[sb0.name] = Buf("ds_" + sb0.name)
            self.dsems[sb0.name].dsem = self.nc.alloc_semaphore("ds_" + sb0.name)
            self.nsem += 1
            self.dbufs.append(self.dsems[sb0.name])
        sbuf = self.dsems[sb0.name]
        for (o, i) in pairs:
            self.E[q].dma_start(out=o, in_=i, **kw).then_inc(sbuf.dsem, 16)
            sbuf.dcnt += 16
            self.ninst += 1
        ev = (sbuf.dsem, sbuf.dcnt, 'dma')
        self._commit(ev, r, w)
        if is_out:
            self.out_events.append(ev)

    def dma_custom(self, q, fn, r=(), w=(), sembuf=None, is_out=False):
        self.nops += 1
        if self.stop_at is not None and self.nops > self.stop_at:
            return
        self._deps(q, r, w)
        sb0 = sembuf if sembuf is not None else (w[0] if len(w) else r[0])
        if sb0.name not in self.dsems:
            self.dsems[sb0.name] = Buf("ds_" + sb0.name)
            self.dsems[sb0.name].dsem = self.nc.alloc_semaphore("ds_" + sb0.name)
            self.nsem += 1
            self.dbufs.append(self.dsems[sb0.name])
        sbuf = self.dsems[sb0.name]
        fn(self.E[q]).then_inc(sbuf.dsem, 16)
        sbuf.dcnt += 16
        self.ninst += 1
        ev = (sbuf.dsem, sbuf.dcnt, 'dma')
        self._commit(ev, r, w)
        if is_out:
            self.out_events.append(ev)

    def finish(self):
        last = {}
        for ev in self.out_events:
            k = id(ev[0])
            if k not in last or last[k][1] < ev[1]:
                last[k] = ev
        for ev in last.values():
            self.E['sp'].wait_ge(ev[0], ev[1])
        for b in self.dbufs:
            self.E['sp'].wait_ge(b.dsem, b.dcnt)
        for e in self.E:
            if e != 'sp' and self.cnt[e] > 0:
                self.E['sp'].wait_ge(self.sem[e], self.cnt[e])

    def mm(self, groups, r=(), w=()):
        def fn(e):
            return [e.matmul(o, lhsT=l, rhs=rr, start=st, stop=sp) for (o, l, rr, st, sp) in groups]
        self.op('pe', fn, r, w)

    def tr(self, items, ident, r=(), w=()):
        def fn(e):
            return [e.transpose(o, i, ident[0:i.shape[0], 0:i.shape[0]]) for (o, i) in items]
        self.op('pe', fn, r, w)

    def act(self, out, in_, func, r=(), w=(), **kw):
        self.op('act', lambda e: e.activation(out=out, in_=in_, func=func, **kw), r, w)

    def tt(self, out, in0, in1, op, r=(), w=(), eng='dve'):
        self.op(eng, lambda e: e.tensor_tensor(out=out, in0=in0, in1=in1, op=op), r, w)

    def ts(self, out, in0, s1, op0, s2=None, op1=None, r=(), w=(), eng='dve', **kw):
        if op1 is None:
            self.op(eng, lambda e: e.tensor_scalar(out=out, in0=in0, scalar1=s1, scalar2=None, op0=op0, **kw), r, w)
        else:
            self.op(eng, lambda e: e.tensor_scalar(out=out, in0=in0, scalar1=s1, scalar2=s2, op0=op0, op1=op1, **kw), r, w)

    def stt(self, out, in0, scalar, in1, op0, op1, r=(), w=()):
        self.op('dve', lambda e: e.scalar_tensor_tensor(out=out, in0=in0, scalar=scalar, in1=in1, op0=op0, op1=op1), r, w)

    def cp(self, out, in_, r=(), w=(), eng='dve'):
        if eng == 'act':
            self.op('act', lambda e: e.copy(out=out, in_=in_), r, w)
        else:
            self.op(eng, lambda e: e.tensor_copy(out=out, in_=in_), r, w)

    def memset(self, ap, val, w=(), eng='dve'):
        self.op(eng, lambda e: e.memset(ap, val), (), w)


PM_ROWS = {}
def _pm_layout():
    names = ['lru_cw0', 'lru_cw1', 'lru_cw2', 'lru_cw3', 'lru_cb', 'lru_ba', 'lru_bx', 'lru_lam',
             'ssd_cw0a', 'ssd_cw1a', 'ssd_cw2a', 'ssd_cw3a', 'ssd_cba', 'ssd_cw0b', 'ssd_cw1b', 'ssd_cw2b', 'ssd_cw3b', 'ssd_cbb',
             'ssd_norm', 'rw_w0', 'rw_a0', 'rw_kk', 'rw_ka', 'rw_lnw', 'rw_lnb', 'rw_rk', 'rw_mu_r', 'rw_mu_k', 'rw_mu_v', 'rw_mu_x']
    for i, n in enumerate(names):
        PM_ROWS[n] = i
_pm_layout()
NPM = 32


def cand_alias(orig, view):
    class _Shared(Buf):
        pass
    view.__class__ = _AliasBuf
    view._o = orig
    return orig


class _AliasBuf(Buf):
    @property
    def w(self):
        return self._o.w
    @w.setter
    def w(self, v):
        if '_o' in self.__dict__:
            self._o.w = v
    @property
    def rs(self):
        return self._o.rs
    @rs.setter
    def rs(self, v):
        if '_o' in self.__dict__:
            self._o.rs = v


def build(dev=None):
    dev = dev or {}
    stage = dev.get('stage', 'all')
    P = Prog()
    P.stop_at = dev.get('stop_at')
    P.oplog = [] if dev.get('oplog') else None
    nc = P.nc
    I = {}
    def inp(name, shape, dt=F32):
        I[name] = P.din(name, shape, dt)
    inp('xp', [T, D]); inp('xs', [NS, D]); inp('memp', [256, D])
    inp('pt', [NS, 128], I32)
    inp('st_lru_h', [DEPTH, NS, 256]); inp('st_lru_conv', [DEPTH, NS, 3, 256])
    inp('st_ssd', [DEPTH, NS, 4, 64, 64]); inp('st_ssd_conv', [DEPTH, NS, 3, 512])
    inp('st_rwkv', [DEPTH, NS, 4, 64, 64]); inp('st_rwkv_shift', [DEPTH, NS, 896])
    inp('cmk', [DEPTH, NS, 256, D]); inp('cmv', [DEPTH, NS, 256, D])
    if dev.get('with_kv', True):
        inp('ck', [DEPTH * 5120 * 128, 256]); inp('cv', [DEPTH * 5120 * 128, 256])
    wshapes = {'norm_mix': [DEPTH, D], 'w_in': [DEPTH, D, INW], 'w_out': [DEPTH, D, D],
               'lru_conv_w': [DEPTH, 4, 256], 'lru_conv_b': [DEPTH, 256], 'lru_wa': [DEPTH, 4, 64, 64], 'lru_ba': [DEPTH, 256],
               'lru_wx': [DEPTH, 4, 64, 64], 'lru_bx': [DEPTH, 256], 'lru_lambda': [DEPTH, 256],
               'ssd_conv_w': [DEPTH, 4, 512], 'ssd_conv_b': [DEPTH, 512], 'ssd_dt_bias': [DEPTH, 4], 'ssd_a_log': [DEPTH, 4],
               'ssd_d': [DEPTH, 4], 'ssd_norm': [DEPTH, 256],
               'rwkv_mu': [DEPTH, 896], 'rwkv_w0': [DEPTH, 256], 'rwkv_w_up': [DEPTH, 32, 256], 'rwkv_a0': [DEPTH, 256],
               'rwkv_a_up': [DEPTH, 32, 256], 'rwkv_g_up': [DEPTH, 64, 256], 'rwkv_k_k': [DEPTH, 256], 'rwkv_k_a': [DEPTH, 256],
               'rwkv_r_k': [DEPTH, 256], 'rwkv_ln_w': [DEPTH, 256], 'rwkv_ln_b': [DEPTH, 256],
               'norm_x': [DEPTH, D], 'x_wq': [DEPTH, D, D], 'x_wk': [DEPTH, D, D], 'x_wv': [DEPTH, D, D], 'x_wo': [DEPTH, D, D],
               'norm_ffn': [DEPTH, D], 'peer_wq': [DEPTH, D, 2048], 'peer_subkeys': [DEPTH, 16, 128, 128],
               'peer_u': [DEPTH, 16384, D], 'peer_v': [DEPTH, 16384, D], 'final_norm': [1, D]}
    for k, s in wshapes.items():
        inp(k, s)
    inp('c_ident', [128, 128]); inp('c_rope_p', [T, 64]); inp('c_rope_s', [NS, 64])
    inp('c_causal', [128, 128]); inp('c_tri', [128, 128]); inp('c_negmask', [128, 128])
    inp('c_past', [128, 64]); inp('c_pm', [128, 64]); inp('c_iota', [128, 128]); inp('c_bd16', [128, 128])
    inp('c_pidx', [128, 1]); inp('c_zsel', [128, 127]); inp('c_dmask4', [4, 256]); inp('c_dmask4x', [4, D])
    O = {}
    def outp(name, shape):
        O[name] = P.dout(name, shape)
    outp('y_p', [T, D]); outp('y_s', [NS, D]); outp('k_p', [DEPTH, T, 256]); outp('v_p', [DEPTH, T, 256])
    outp('lru_h_p', [DEPTH, 256]); outp('lru_conv_p', [DEPTH, 3, 256]); outp('ssd_p', [DEPTH, 4, 64, 64])
    outp('ssd_conv_p', [DEPTH, 3, 512]); outp('rwkv_p', [DEPTH, 4, 64, 64]); outp('rwkv_shift_p', [DEPTH, 896])
    outp('mem_k_p', [DEPTH, 256, D]); outp('mem_v_p', [DEPTH, 256, D])
    outp('k_s', [DEPTH, NS, 256]); outp('v_s', [DEPTH, NS, 256]); outp('lru_h_s', [DEPTH, NS, 256])
    outp('lru_conv_s', [DEPTH, NS, 3, 256]); outp('ssd_s', [DEPTH, NS, 4, 64, 64]); outp('ssd_conv_s', [DEPTH, NS, 3, 512])
    outp('rwkv_s', [DEPTH, NS, 4, 64, 64]); outp('rwkv_shift_s', [DEPTH, NS, 896])
    dbg_specs = dev.get('dbg', {})
    DBG = {n: P.dout('dbg_' + n, s[0], s[1]) for n, s in dbg_specs.items()}

    Wd = nc.dram_tensor('Wd_scratch', [NT + 1, 128, 128 * 128], BF16, kind='Internal').ap()
    Wdb = Buf('Wdb')
    xres = P.sb('xres', [128, NT, D])
    bank = [P.ps('bank%d' % i, [128, 512]) for i in range(8)]
    identf = P.sb('identf', [128, 128]); identb = P.sb('identb', [128, 128], BF16)
    onesf = P.sb('onesf', [128, 128])
    causal = P.sb('causal', [128, 128], BF16)
    tri = P.sb('tri', [128, 128]); negmask = P.sb('negmask', [128, 128])
    cpast = P.sb('cpast', [128, 8, 8]); cpm = P.sb('cpm', [128, 8, 8])
    rope = P.sb('rope', [128, NT, 64])
    gbc = P.sb('gbc', [128, D])
    PM = P.sb('PM', [NPM, 256]); PT = P.sb('PT', [128, 2, NPM])
    small = P.sb('small', [128, 64])
    epsb = P.sb('epsb', [128, 4])
    BDones = P.sb('BDones', [128, 128]); hmask = P.sb('hmask', [128, 2])
    xsp = P.sb('xsp', [128, D]); hTs = P.sb('hTs', [128, 8, 4], BF16); ymTs = P.sb('ymTs', [128, 8, 4], BF16)
    ropes = P.sb('ropes', [4, 64]); PIf = P.sb('PIf', [128, 512])
    zsel = P.sb('zsel', [128, 127]); dmask4 = P.sb('dmask4', [4, 256]); pidx = P.sb('pidx', [128, 1])
    WITH_S = dev.get('with_s', dev.get('with_kv', True))
    WITH_KV = dev.get('with_kv', True)
    P.arena('AL', 78 * 1024)
    P.arena('AS', 44 * 1024)

    def dbg(name, dst_ap, src_ap, r):
        P.dma('sp', [(dst_ap, src_ap)], r=r, sembuf=Buf('dbg_' + name), is_out=True)

    P.dma('sp', [(identf.t[:], I['c_ident'][:, :])], w=[identf])
    P.cp(identb.t[:], identf.t[:], r=[identf], w=[identb])
    P.memset(onesf.t[:], 1.0, w=[onesf])
    P.dma('pool', [(causal.t[:], I['c_causal'][:, :])], w=[causal])
    P.dma('sp', [(tri.t[:], I['c_tri'][:, :]), (negmask.t[:], I['c_negmask'][:, :])], w=[tri, negmask], sembuf=tri)
    P.dma('sp', [(cpast.t[:].rearrange("p a b -> p (a b)"), I['c_past'][:, :]), (cpm.t[:].rearrange("p a b -> p (a b)"), I['c_pm'][:, :])],
          w=[cpast, cpm], sembuf=cpast)
    P.dma('sp', [(rope.t[:], I['c_rope_p'].rearrange("(n p) c -> p n c", p=128))], w=[rope])
    P.dma('sp', [(xres.t[:, i, :], I['xp'][i * 128:(i + 1) * 128, :]) for i in range(NT)], w=[xres])
    P.memset(xsp.t[:], 0.0, w=[xsp])
    P.memset(ymTs.t[:].rearrange("p a b -> p (a b)"), 0.0, w=[ymTs])
    P.dma('sp', [(xsp.t[4:128, :], I['xp'][4:128, :])], w=[xsp])
    P.dma('sp', [(xsp.t[0:4, :], I['xs'][:, :]), (ropes.t[:, :], I['c_rope_s'][:, :]), (zsel.t[:, :], I['c_zsel'][:, :]),
                 (dmask4.t[:, :], I['c_dmask4'][:, :]), (pidx.t[:, :], I['c_pidx'][:, :])], w=[xsp, ropes, zsel, dmask4, pidx], sembuf=xsp)
    PIu = P.carve('AS', 'PIu', [128, 512], U32)
    P.dma('sp', [(PIu.t[:, :].bitcast(I32), I['pt'].rearrange("s g -> (s g)").unsqueeze(0).to_broadcast([128, 512]))], w=[PIu])
    P.cp(PIf.t[:, :], PIu.t[:, :].bitcast(I32), r=[PIu], w=[PIf])
    P.ts(PIf.t[:, :], PIf.t[:, :], 128.0, ALU.mult, pidx.t[:, 0:1], ALU.add, r=[PIf, pidx], w=[PIf])
    P.memset(epsb.t[:, 0:1], 1e-6, w=[epsb])
    P.memset(epsb.t[:, 1:2], 64e-5, w=[epsb])
    P.memset(BDones.t[:], 0.0, w=[BDones]); P.memset(hmask.t[:], 0.0, w=[hmask])
    for h2 in range(2):
        pr = slice(h2 * 64, (h2 + 1) * 64)
        P.memset(BDones.t[pr, h2 * 64:(h2 + 1) * 64], 1.0, w=[BDones])
        P.memset(hmask.t[pr, h2:h2 + 1], 1.0, w=[hmask])

    def rmsnorm_tile(xt_ap, xbuf, hbuf, npart=128, width=D, gain=None, gbuf_=None, eps_col=0):
        gain = gbc.t[0:npart, 0:width] if gain is None else gain
        gbuf_ = gbc if gbuf_ is None else gbuf_
        P.act(hbuf.t[0:npart, 0:width], xt_ap, AF.Square, r=[xbuf], w=[hbuf, small], accum_out=small.t[0:npart, 0:1])
        P.act(small.t[0:npart, 1:2], small.t[0:npart, 0:1], AF.Sqrt, r=[small, epsb], w=[small], scale=1.0 / width, bias=epsb.t[0:npart, eps_col:eps_col + 1])
        P.op('dve', lambda e: e.reciprocal(out=small.t[0:npart, 2:3], in_=small.t[0:npart, 1:2]), r=[small], w=[small])
        P.stt(hbuf.t[0:npart, 0:width], xt_ap, small.t[0:npart, 2:3], gain, ALU.mult, ALU.mult, r=[xbuf, small, gbuf_], w=[hbuf])

    if stage == 'const':
        P.finish()
        return P, I, O, DBG

    for l in range(DEPTH):
        P.reset('AL'); P.reset('AS')
        hT = P.carve('AL', 'hT', [128, 8, TB], BF16)
        ymT = P.carve('AL', 'ymT', [128, 8, TB], BF16)
        WA = P.carve('AL', 'WA', [128, 8, D], BF16)
        KT = P.carve('AL', 'KT', [128, 2, T], BF16)
        Vaug = P.carve('AL', 'Vaug', [128, NT, 4, 65], BF16)
        kmT = P.carve('AL', 'kmT', [128, 2, 2, 8]); kmacc = P.carve('AL', 'kmacc', [128, 2])
        ubuf = [P.carve('AL', 'ubuf%d' % i, [128, 3 + TB]) for i in range(2)]
        hstate = P.carve('AL', 'hstate', [128, 2])
        BDa = [P.carve('AL', 'BDa%d' % i, [128, 128]) for i in range(2)]
        BDx = [P.carve('AL', 'BDx%d' % i, [128, 128]) for i in range(2)]
        lruc = P.carve('AL', 'lruc', [128, 2, 4])
        cbuf = [P.carve('AL', 'cbuf%d' % i, [128, 3 + TB]) for i in range(4)]
        hp4 = P.carve('AL', 'hp4', [128, 12])
        sT = P.carve('AL', 'sT', [128, 4, 64])
        nwbc = P.carve('AL', 'nwbc', [128, 256])
        cb7 = [P.carve('AL', 'cb7_%d' % i, [128, 129]) for i in range(7)]
        Wlow = P.carve('AL', 'Wlow', [128, 256])
        ST = P.carve('AL', 'ST', [128, 2, 64])
        rwc = P.carve('AL', 'rwc', [128, 8])

        def load_w(src, c0, c1):
            for kc in range(8):
                P.dma('pool', [(WA.t[:, kc, 0:c1 - c0], src[l, kc * 128:(kc + 1) * 128, c0:c1])], w=[WA])

        P.memset(Vaug.t[:].rearrange("p a b c -> p (a b c)"), 1.0, w=[Vaug])
        P.memset(ymT.t[:].rearrange("p a b -> p (a b)"), 0.0, w=[ymT])
        P.dma('sp', [(gbc.t[:], I['norm_mix'][l:l + 1, :].to_broadcast([128, D]))], w=[gbc])
        P.memset(PM.t[:], 0.0, w=[PM])
        rows = []
        def prow(name, src):
            rows.append((PM.t[PM_ROWS[name]:PM_ROWS[name] + 1, 0:src.shape[-1]], src))
        for j in range(4):
            prow('lru_cw%d' % j, I['lru_conv_w'][l, j:j + 1, :])
            prow('ssd_cw%da' % j, I['ssd_conv_w'][l, j:j + 1, 0:256])
            prow('ssd_cw%db' % j, I['ssd_conv_w'][l, j:j + 1, 256:512])
        prow('lru_cb', I['lru_conv_b'][l:l + 1, :]); prow('lru_ba', I['lru_ba'][l:l + 1, :]); prow('lru_bx', I['lru_bx'][l:l + 1, :])
        prow('lru_lam', I['lru_lambda'][l:l + 1, :])
        prow('ssd_cba', I['ssd_conv_b'][l:l + 1, 0:256]); prow('ssd_cbb', I['ssd_conv_b'][l:l + 1, 256:512])
        prow('ssd_norm', I['ssd_norm'][l:l + 1, :])
        prow('rw_w0', I['rwkv_w0'][l:l + 1, :]); prow('rw_a0', I['rwkv_a0'][l:l + 1, :]); prow('rw_kk', I['rwkv_k_k'][l:l + 1, :])
        prow('rw_ka', I['rwkv_k_a'][l:l + 1, :]); prow('rw_lnw', I['rwkv_ln_w'][l:l + 1, :]); prow('rw_lnb', I['rwkv_ln_b'][l:l + 1, :])
        prow('rw_rk', I['rwkv_r_k'][l:l + 1, :])
        prow('rw_mu_r', I['rwkv_mu'][l:l + 1, 0:256]); prow('rw_mu_k', I['rwkv_mu'][l:l + 1, 256:512]); prow('rw_mu_v', I['rwkv_mu'][l:l + 1, 512:768])
        prow('rw_mu_x', I['rwkv_mu'][l:l + 1, 768:896])
        P.dma('sp', rows, w=[PM])
        P.tr([(bank[7].t[:, c * NPM:(c + 1) * NPM], PM.t[0:NPM, c * 128:(c + 1) * 128]) for c in range(2)], identf.t, r=[PM, identf], w=[bank[7]])
        P.cp(PT.t[:].rearrange("p c r -> p (c r)"), bank[7].t[:, 0:2 * NPM], r=[bank[7]], w=[PT])
        def pcol(name, fc):
            return PT.t[:, fc, PM_ROWS[name]:PM_ROWS[name] + 1]
        for fc in range(2):
            P.memset(BDa[fc].t[:], 0.0, w=[BDa[fc]]); P.memset(BDx[fc].t[:], 0.0, w=[BDx[fc]])
            P.dma('sp', [(BDa[fc].t[b * 64:(b + 1) * 64, b * 64:(b + 1) * 64], I['lru_wa'][l, 2 * fc + b, :, :]) for b in range(2)], w=[BDa[fc]])
            P.dma('sp', [(BDx[fc].t[b * 64:(b + 1) * 64, b * 64:(b + 1) * 64], I['lru_wx'][l, 2 * fc + b, :, :]) for b in range(2)], w=[BDx[fc]])
            P.act(lruc.t[:, fc, 2:3], pcol('lru_lam', fc), AF.Exp, r=[PT], w=[lruc], scale=-1.0)
            P.act(lruc.t[:, fc, 3:4], lruc.t[:, fc, 2:3], AF.Ln, r=[lruc, onesf], w=[lruc], bias=onesf.t[:, 0:1])
            P.ts(lruc.t[:, fc, 0:1], lruc.t[:, fc, 3:4], -8.0, ALU.mult, r=[lruc], w=[lruc])
            P.ts(lruc.t[:, fc, 1:2], lruc.t[:, fc, 3:4], -16.0, ALU.mult, r=[lruc], w=[lruc])
            P.memset(ubuf[fc].t[:], 0.0, w=[ubuf[fc]])
        P.memset(hstate.t[:], 0.0, w=[hstate])
        P.memset(kmT.t[:].rearrange("p a b c -> p (a b c)"), 0.0, w=[kmT])
        P.dma('sp', [(hp4.t[:, 0:4], I['ssd_dt_bias'][l:l + 1, :].to_broadcast([128, 4])),
                     (hp4.t[:, 4:8], I['ssd_a_log'][l:l + 1, :].to_broadcast([128, 4])),
                     (hp4.t[:, 8:12], I['ssd_d'][l:l + 1, :].to_broadcast([128, 4])),
                     (nwbc.t[:, :], I['ssd_norm'][l:l + 1, :].to_broadcast([128, 256]))], w=[hp4, nwbc], sembuf=hp4)
        P.act(hp4.t[:, 4:8], hp4.t[:, 4:8], AF.Exp, r=[hp4], w=[hp4])
        P.ts(hp4.t[:, 4:8], hp4.t[:, 4:8], -1.0, ALU.mult, r=[hp4], w=[hp4])
        for c4 in range(4):
            P.memset(cbuf[c4].t[:], 0.0, w=[cbuf[c4]])
        P.memset(sT.t[:].rearrange("p a b -> p (a b)"), 0.0, w=[sT])
        P.dma('sp', [(Wlow.t[0:32, :], I['rwkv_w_up'][l, :, :]), (Wlow.t[32:64, :], I['rwkv_a_up'][l, :, :]), (Wlow.t[64:128, :], I['rwkv_g_up'][l, :, :])], w=[Wlow])
        for c7 in range(7):
            P.memset(cb7[c7].t[:], 0.0, w=[cb7[c7]])
        P.memset(ST.t[:].rearrange("p a b -> p (a b)"), 0.0, w=[ST])
        for fc in range(2):
            P.ts(rwc.t[:, fc:fc + 1], pcol('rw_ka', fc), -1.0, ALU.mult, 1.0, ALU.add, r=[PT], w=[rwc])


        def samp_norm(dst):
            hbs = P.carve('AS', 'hbs', [128, D], BF16)
            rmsnorm_tile(xsp.t[0:4, :], xsp, hbs, npart=4)
            psT_ = bank[6].t[:].bitcast(BF16)
            P.tr([(psT_[:, kc * 4:(kc + 1) * 4], hbs.t[0:4, kc * 128:(kc + 1) * 128]) for kc in range(8)], identb.t, r=[hbs, identb], w=[bank[6]])
            P.cp(dst.t[:].rearrange("p k s -> p (k s)"), psT_[:, 0:32], r=[bank[6]], w=[dst])

        def sproj(bk, o0, Wb, c0, n):
            P.mm([(bk.t[0:4, o0:o0 + n], hTs.t[:, kc, :], Wb.t[:, kc, c0:c0 + n], kc == 0, kc == 7) for kc in range(8)], r=[hTs, Wb], w=[bk])

        def put_y(ybuf, yap, chunk0, nch=2):
            P.tr([(bank[6].t[:, c * 4:(c + 1) * 4], yap[0:4, c * 128:(c + 1) * 128]) for c in range(nch)], identf.t, r=[ybuf, identf], w=[bank[6]])
            P.cp(ymTs.t[:, chunk0:chunk0 + nch, :], bank[6].t[:, 0:4 * nch].rearrange("p (c s) -> p c s", c=nch), r=[bank[6]], w=[ymTs])

        def brow(dst, o0, src2d, n):
            return (dst.t[0:4, o0:o0 + n], src2d.to_broadcast([4, n]))

        def samp_A():
            qks = P.carve('AS', 'qks', [4, 512]); vs1 = P.carve('AS', 'vs1', [4, 257]); st1 = P.carve('AS', 'st1', [4, 8, 32]); st2 = P.carve('AS', 'st2', [4, 8, 32])
            en = P.carve('AS', 'en', [4, 16]); enm = P.carve('AS', 'enm', [4, 4])
            qbc = P.carve('AS', 'qbc', [128, 256]); prod = P.carve('AS', 'prod', [128, 256])
            Kpg = [P.carve('AS', 'Kpg%d' % k, [128, 256]) for k in range(3)]
            Vpg = [P.carve('AS', 'Vpg%d' % k, [128, 257]) for k in range(2)]
            SC = P.carve('AS', 'SC', [128, 128, 4]); g64 = P.carve('AS', 'g64', [64, 260]); g4 = P.carve('AS', 'g4', [4, 64 + 8 + 64])
            rhb = P.carve('AS', 'rhb', [4, 64, 4]); ob = P.carve('AS', 'ob', [4, 260])
            PIl = P.carve('AS', 'PIl', [128, 512], U32); PIt = P.carve('AS', 'PIt', [128, 512])
            P.ts(PIt.t[:, :], PIf.t[:, :], float(l * 5120 * 128), ALU.add, r=[PIf], w=[PIt])
            P.cp(PIl.t[:, :], PIt.t[:, :], r=[PIt], w=[PIl])
            sproj(bank[0], 0, WA, 0, 512); sproj(bank[1], 0, WA, 512, 256)
            qk = bank[0].t[0:4, :].rearrange("p (c d) -> p c d", c=8)
            cosb = ropes.t[:, 0:32].unsqueeze(1).to_broadcast([4, 8, 32]); sinb = ropes.t[:, 32:64].unsqueeze(1).to_broadcast([4, 8, 32])
            qo = qks.t[:].rearrange("p (c d) -> p c d", c=8)
            P.tt(st1.t[:], qk[:, :, 0:32], cosb, ALU.mult, r=[bank[0], ropes], w=[st1]); P.tt(st2.t[:], qk[:, :, 32:64], sinb, ALU.mult, r=[bank[0], ropes], w=[st2])
            P.tt(qo[:, :, 0:32], st1.t[:], st2.t[:], ALU.subtract, r=[st1, st2], w=[qks])
            P.tt(st1.t[:], qk[:, :, 32:64], cosb, ALU.mult, r=[bank[0], ropes], w=[st1]); P.tt(st2.t[:], qk[:, :, 0:32], sinb, ALU.mult, r=[bank[0], ropes], w=[st2])
            P.tt(qo[:, :, 32:64], st1.t[:], st2.t[:], ALU.add, r=[st1, st2], w=[qks])
            P.memset(vs1.t[:, 256:257], 1.0, w=[vs1])
            P.cp(vs1.t[:, 0:256], bank[1].t[0:4, 0:256], r=[bank[1]], w=[vs1])
            P.dma('sp', [(O['k_s'][l, :, :], qks.t[:, 256:512]), (O['v_s'][l, :, :], vs1.t[:, 0:256])], r=[qks, vs1], sembuf=qks, is_out=True)
            P.tt(st1.t[:].rearrange("p a b -> p (a b)"), qks.t[:, 0:256], qks.t[:, 256:512], ALU.mult, r=[qks], w=[st1])
            P.op('dve', lambda e: e.tensor_reduce(out=en.t[:, 0:4], in_=st1.t[:].rearrange("p a b -> p (a b)").rearrange("p (h d) -> p h d", h=4), op=ALU.add, axis=AX.X), r=[st1], w=[en])
            P.act(en.t[:, 4:8], en.t[:, 0:4], AF.Exp, r=[en], w=[en], scale=0.125)
            for k in range(2):
                P.memset(Vpg[k].t[:, 256:257], 1.0, w=[Vpg[k]])
            for s_ in range(4):
                P.mm([(bank[2].t[:, 0:256], identf.t[0:4, s_:s_ + 1].to_broadcast([4, 128]), qks.t[0:4, 0:256], True, True)], r=[identf, qks], w=[bank[2]])
                P.cp(qbc.t[:, :], bank[2].t[:, 0:256], r=[bank[2]], w=[qbc])
                for pg in range(128):
                    kp_ = Kpg[pg % 3]
                    col = s_ * 128 + pg
                    P.dma_custom('pool', lambda e: e.indirect_dma_start(out=kp_.t[:, :], out_offset=None, in_=I['ck'][:, :],
                                                                        in_offset=bass.IndirectOffsetOnAxis(ap=PIl.t[:, col:col + 1], axis=0)), r=[PIl], w=[kp_])
                    n_ = pg // 2
                    P.mm([(bank[3].t[0:64, 0:256], zsel.t[:, 63 - n_:127 - n_], kp_.t[:, :], pg == 0, pg == 127)], r=[zsel, kp_], w=[bank[3]])
                    P.tt(prod.t[:, :], kp_.t[:, :], qbc.t[:, :], ALU.mult, r=[kp_, qbc], w=[prod])
                    P.op('dve', lambda e: e.tensor_reduce(out=SC.t[:, pg, :], in_=prod.t[:, :].rearrange("p (h d) -> p h d", h=4), op=ALU.add, axis=AX.X), r=[prod], w=[SC])
                P.tt(g64.t[:, 0:256], bank[3].t[0:64, 0:256], qbc.t[0:64, :], ALU.mult, r=[bank[3], qbc], w=[g64])
                P.op('dve', lambda e: e.tensor_reduce(out=g64.t[:, 256:260], in_=g64.t[:, 0:256].rearrange("p (h d) -> p h d", h=4), op=ALU.add, axis=AX.X), r=[g64], w=[g64])
                P.tr([(bank[2].t[0:4, 256:320], g64.t[0:64, 256:260])], identf.t, r=[g64, identf], w=[bank[2]])
                P.cp(g4.t[:, 0:64], bank[2].t[0:4, 256:320], r=[bank[2]], w=[g4])
                P.op('dve', lambda e: e.max(out=g4.t[:, 64:72], in_=g4.t[:, 0:64]), r=[g4], w=[g4])
                P.ts(g4.t[:, 72:136], g4.t[:, 0:64], g4.t[:, 66:67], ALU.is_ge, -1.0, ALU.add, r=[g4], w=[g4])
                P.ts(g4.t[:, 72:136], g4.t[:, 72:136], 30000.0, ALU.mult, r=[g4], w=[g4])
                P.tt(rhb.t[:], g4.t[:, 72:136].unsqueeze(2).to_broadcast([4, 64, 4]), identf.t[0:4, 0:4].unsqueeze(1).to_broadcast([4, 64, 4]), ALU.mult, r=[g4, identf], w=[rhb])
                P.mm([(bank[2].t[:, 0:256], onesf.t[0:4, 0:128], rhb.t[:].rearrange("p a b -> p (a b)"), True, True)], r=[onesf, rhb], w=[bank[2]])
                scv = SC.t[:].rearrange("p (n two) h -> p n two h", two=2)
                for two in range(2):
                    P.tt(scv[:, :, two, :], scv[:, :, two, :], bank[2].t[:, 0:256].rearrange("p (n h) -> p n h", h=4), ALU.add, r=[SC, bank[2]], w=[SC])
                P.act(SC.t[:].rearrange("p a b -> p (a b)"), SC.t[:].rearrange("p a b -> p (a b)"), AF.Exp, r=[SC], w=[SC], scale=0.125)
                for pg in range(128):
                    vp_ = Vpg[pg % 2]
                    col = s_ * 128 + pg
                    P.dma_custom('pool', lambda e: e.indirect_dma_start(out=vp_.t[:, 0:256], out_offset=None, in_=I['cv'][:, :],
                                                                        in_offset=bass.IndirectOffsetOnAxis(ap=PIl.t[:, col:col + 1], axis=0)), r=[PIl], w=[vp_])
                    P.mm([(bank[4].t[0:4, 0:257], SC.t[:, pg, :], vp_.t[:, 0:257], pg == 0, False)], r=[SC, vp_], w=[bank[4]])
                P.ts(enm.t[:, :], en.t[:, 4:8], identf.t[0:4, s_:s_ + 1], ALU.mult, r=[en, identf], w=[enm])
                P.mm([(bank[4].t[0:4, 0:257], enm.t[0:4, 0:4], vs1.t[0:4, 0:257], False, True)], r=[enm, vs1], w=[bank[4]])
                P.op('dve', lambda e: e.reciprocal(out=ob.t[:, 256:257], in_=bank[4].t[0:4, 256:257]), r=[bank[4]], w=[ob])
                P.stt(ob.t[:, 0:256], bank[4].t[0:4, 0:256], ob.t[:, 256:257], dmask4.t[:, :], ALU.mult, ALU.mult, r=[bank[4], ob, dmask4], w=[ob])
                P.mm([(bank[5].t[:, s_ * 2 + c:s_ * 2 + c + 1], ob.t[0:4, c * 128:(c + 1) * 128], onesf.t[0:4, 0:1], True, True) for c in range(2)],
                     r=[ob, onesf], w=[bank[5]])
            P.cp(ymTs.t[:, 0:2, :], bank[5].t[:, 0:8].rearrange("p (s c) -> p c s", c=2), r=[bank[5]], w=[ymTs])

        def samp_B():
            pb = P.carve('AS', 'pbB', [4, 2048]); buf = P.carve('AS', 'bufB', [4, 4, 256]); xc = P.carve('AS', 'xcB', [4, 256])
            xcT = P.carve('AS', 'xcTB', [128, 2, 4]); w1 = P.carve('AS', 'w1B', [4, 512]); w2 = P.carve('AS', 'w2B', [4, 512]); h0 = P.carve('AS', 'h0B', [4, 256])
            P.dma('sp', [brow(pb, 0, I['lru_conv_w'][l:l + 1, :, :].rearrange("o j f -> o (j f)"), 1024), brow(pb, 1024, I['lru_conv_b'][l:l + 1, :], 256),
                         brow(pb, 1280, I['lru_ba'][l:l + 1, :], 256), brow(pb, 1536, I['lru_bx'][l:l + 1, :], 256), brow(pb, 1792, I['lru_lambda'][l:l + 1, :], 256),
                         (buf.t[:, 0:3, :], I['st_lru_conv'][l, :, :, :]), (h0.t[:, :], I['st_lru_h'][l, :, :])], w=[pb, buf, h0], sembuf=pb)
            sproj(bank[0], 0, WA, 0, 512)
            P.cp(buf.t[:, 3, :], bank[0].t[0:4, 0:256], r=[bank[0]], w=[buf])
            P.tt(xc.t[:, :], buf.t[:, 0, :], pb.t[:, 0:256], ALU.mult, r=[buf, pb], w=[xc])
            for j in range(1, 4):
                P.tt(w1.t[:, 0:256], buf.t[:, j, :], pb.t[:, j * 256:(j + 1) * 256], ALU.mult, r=[buf, pb], w=[w1])
                P.tt(xc.t[:, :], xc.t[:, :], w1.t[:, 0:256], ALU.add, r=[xc, w1], w=[xc])
            P.tt(xc.t[:, :], xc.t[:, :], pb.t[:, 1024:1280], ALU.add, r=[xc, pb], w=[xc])
            P.tr([(bank[2].t[:, c * 4:(c + 1) * 4], xc.t[0:4, c * 128:(c + 1) * 128]) for c in range(2)], identf.t, r=[xc, identf], w=[bank[2]])
            P.cp(xcT.t[:].rearrange("p c s -> p (c s)"), bank[2].t[:, 0:8], r=[bank[2]], w=[xcT])
            P.mm([(bank[3].t[0:4, fc * 128:(fc + 1) * 128], xcT.t[:, fc, :], BDa[fc].t[:, :], True, True) for fc in range(2)] +
                 [(bank[3].t[0:4, 256 + fc * 128:256 + (fc + 1) * 128], xcT.t[:, fc, :], BDx[fc].t[:, :], True, True) for fc in range(2)],
                 r=[xcT] + BDa + BDx, w=[bank[3]])
            P.tt(w1.t[:, :], bank[3].t[0:4, 0:512], pb.t[:, 1280:1792], ALU.add, r=[bank[3], pb], w=[w1])
            P.act(w1.t[:, :], w1.t[:, :], AF.Sigmoid, r=[w1], w=[w1])
            P.act(w2.t[:, 0:256], pb.t[:, 1792:2048], AF.Exp, r=[pb], w=[w2], scale=-1.0)
            P.act(w2.t[:, 0:256], w2.t[:, 0:256], AF.Ln, r=[w2, onesf], w=[w2], bias=onesf.t[0:4, 0:1])
            P.tt(w2.t[:, 0:256], w2.t[:, 0:256], w1.t[:, 0:256], ALU.mult, r=[w2, w1], w=[w2])
            P.act(w2.t[:, 256:512], w2.t[:, 0:256], AF.Exp, r=[w2], w=[w2], scale=-16.0)
            P.act(w2.t[:, 0:256], w2.t[:, 0:256], AF.Exp, r=[w2], w=[w2], scale=-8.0)
            P.ts(w2.t[:, 256:512], w2.t[:, 256:512], 0.99999994, ALU.min, r=[w2], w=[w2])
            P.act(w2.t[:, 256:512], w2.t[:, 256:512], AF.Sqrt, r=[w2, onesf], w=[w2], scale=-1.0, bias=onesf.t[0:4, 0:1])
            P.tt(w1.t[:, 256:512], w1.t[:, 256:512], xc.t[:, :], ALU.mult, r=[w1, xc], w=[w1])
            P.tt(w2.t[:, 256:512], w2.t[:, 256:512], w1.t[:, 256:512], ALU.mult, r=[w2, w1], w=[w2])
            P.tt(h0.t[:, :], h0.t[:, :], w2.t[:, 0:256], ALU.mult, r=[h0, w2], w=[h0])
            P.tt(h0.t[:, :], h0.t[:, :], w2.t[:, 256:512], ALU.add, r=[h0, w2], w=[h0])
            P.act(w1.t[:, 0:256], bank[0].t[0:4, 256:512], AF.Gelu_apprx_tanh, r=[bank[0]], w=[w1])
            P.tt(w1.t[:, 0:256], w1.t[:, 0:256], h0.t[:, :], ALU.mult, r=[w1, h0], w=[w1])
            put_y(w1, w1.t, 2)
            P.dma('sp', [(O['lru_h_s'][l, :, :], h0.t[:, :]), (O['lru_conv_s'][l, :, :, :], buf.t[:, 1:4, :])], r=[h0, buf], sembuf=h0, is_out=True)

        def samp_C():
            pb = P.carve('AS', 'pbC', [4, 2836]); cbs = P.carve('AS', 'cbsC', [4, 4, 512]); xb = P.carve('AS', 'xbC', [4, 512]); w1 = P.carve('AS', 'w1C', [4, 512])
            dd = P.carve('AS', 'ddC', [4, 16]); xdt = P.carve('AS', 'xdtC', [4, 256]); yy = P.carve('AS', 'yyC', [4, 256]); ycb = P.carve('AS', 'ycbC', [4, 256])
            Sh = [P.carve('AS', 'ShC%d' % k, [4, 16, 64]) for k in range(1)]; Sw = P.carve('AS', 'SwC', [4, 16, 64])
            P.dma('sp', [brow(pb, 0, I['ssd_conv_w'][l:l + 1, :, :].rearrange("o j f -> o (j f)"), 2048), brow(pb, 2048, I['ssd_conv_b'][l:l + 1, :], 512),
                         brow(pb, 2560, I['ssd_dt_bias'][l:l + 1, :], 4), brow(pb, 2564, I['ssd_a_log'][l:l + 1, :], 4), brow(pb, 2568, I['ssd_d'][l:l + 1, :], 4),
                         brow(pb, 2580, I['ssd_norm'][l:l + 1, :], 256), (cbs.t[:, 0:3, :], I['st_ssd_conv'][l, :, :, :])], w=[pb, cbs], sembuf=pb)
            sproj(bank[0], 0, WA, 0, 256); sproj(bank[0], 256, WA, 768, 4); sproj(bank[1], 0, WA, 256, 512)
            P.cp(cbs.t[:, 3, :], bank[1].t[0:4, 0:512], r=[bank[1]], w=[cbs])
            P.tt(xb.t[:, :], cbs.t[:, 0, :], pb.t[:, 0:512], ALU.mult, r=[cbs, pb], w=[xb])
            for j in range(1, 4):
                P.tt(w1.t[:, :], cbs.t[:, j, :], pb.t[:, j * 512:(j + 1) * 512], ALU.mult, r=[cbs, pb], w=[w1])
                P.tt(xb.t[:, :], xb.t[:, :], w1.t[:, :], ALU.add, r=[xb, w1], w=[xb])
            P.tt(xb.t[:, :], xb.t[:, :], pb.t[:, 2048:2560], ALU.add, r=[xb, pb], w=[xb])
            P.act(xb.t[:, :], xb.t[:, :], AF.Silu, r=[xb], w=[xb])
            P.tt(dd.t[:, 0:4], bank[0].t[0:4, 256:260], pb.t[:, 2560:2564], ALU.add, r=[bank[0], pb], w=[dd])
            P.act(dd.t[:, 0:4], dd.t[:, 0:4], AF.Exp, r=[dd], w=[dd])
            P.act(dd.t[:, 0:4], dd.t[:, 0:4], AF.Ln, r=[dd, onesf], w=[dd], bias=onesf.t[0:4, 0:1])
            P.act(dd.t[:, 4:8], pb.t[:, 2564:2568], AF.Exp, r=[pb], w=[dd])
            P.tt(dd.t[:, 4:8], dd.t[:, 4:8], dd.t[:, 0:4], ALU.mult, r=[dd], w=[dd])
            P.act(dd.t[:, 4:8], dd.t[:, 4:8], AF.Exp, r=[dd], w=[dd], scale=-1.0)
            P.tt(xdt.t[:].rearrange("p (h d) -> p h d", h=4), xb.t[:, 0:256].rearrange("p (h d) -> p h d", h=4), dd.t[:, 0:4].unsqueeze(2).to_broadcast([4, 4, 64]),
                 ALU.mult, r=[xb, dd], w=[xdt])
            for h8 in range(16):
                h, ph = h8 // 4, h8 % 4
                g = h // 2
                sh = Sh[0]
                ps_ = slice(ph * 16, (ph + 1) * 16)
                hp_ = slice(h * 64 + ph * 16, h * 64 + (ph + 1) * 16)
                P.dma('sp', [(sh.t[:, :, :], I['st_ssd'][l, :, h, ps_, :])], w=[sh])
                Bg = xb.t[:, 256 + g * 64:256 + (g + 1) * 64]; Cg = xb.t[:, 384 + g * 64:384 + (g + 1) * 64]
                P.tt(Sw.t[:], xdt.t[:, hp_].unsqueeze(2).to_broadcast([4, 16, 64]), Bg.unsqueeze(1).to_broadcast([4, 16, 64]), ALU.mult, r=[xdt, xb], w=[Sw])
                P.stt(sh.t[:].rearrange("p a b -> p (a b)"), sh.t[:].rearrange("p a b -> p (a b)"), dd.t[:, 4 + h:5 + h], Sw.t[:].rearrange("p a b -> p (a b)"), ALU.mult, ALU.add,
                      r=[sh, dd, Sw], w=[sh])
                P.dma('sp', [(O['ssd_s'][l, :, h, ps_, :], sh.t[:, :, :])], r=[sh], sembuf=sh, is_out=True)
                P.tt(Sw.t[:], sh.t[:], Cg.unsqueeze(1).to_broadcast([4, 16, 64]), ALU.mult, r=[sh, xb], w=[Sw])
                P.op('dve', lambda e: e.tensor_reduce(out=yy.t[:, hp_], in_=Sw.t[:], op=ALU.add, axis=AX.X), r=[Sw], w=[yy])
            P.tt(w1.t[:, 0:256].rearrange("p (h d) -> p h d", h=4), xb.t[:, 0:256].rearrange("p (h d) -> p h d", h=4), pb.t[:, 2568:2572].unsqueeze(2).to_broadcast([4, 4, 64]),
                 ALU.mult, r=[xb, pb], w=[w1])
            P.tt(yy.t[:, :], yy.t[:, :], w1.t[:, 0:256], ALU.add, r=[yy, w1], w=[yy])
            P.act(w1.t[:, 0:256], bank[0].t[0:4, 0:256], AF.Silu, r=[bank[0]], w=[w1])
            P.tt(yy.t[:, :], yy.t[:, :], w1.t[:, 0:256], ALU.mult, r=[yy, w1], w=[yy])
            rmsnorm_tile(yy.t[:, :], yy, ycb, npart=4, width=256, gain=pb.t[:, 2580:2836], gbuf_=pb)
            put_y(ycb, ycb.t, 4)
            P.dma('sp', [(O['ssd_conv_s'][l, :, :, :], cbs.t[:, 1:4, :])], r=[cbs], sembuf=cbs, is_out=True)

        def samp_D():
            pb = P.carve('AS', 'pbD', [4, 2688]); cur = P.carve('AS', 'curD', [4, 896]); mm_ = P.carve('AS', 'mD', [4, 896]); prev = P.carve('AS', 'prevD', [4, 896])
            xTs = P.carve('AS', 'xTsD', [128, 4]); w1 = P.carve('AS', 'w1D', [4, 1024]); w2 = P.carve('AS', 'w2D', [4, 1024]); sm = P.carve('AS', 'smD', [4, 32])
            Sh = [P.carve('AS', 'ShD%d' % k, [4, 16, 64]) for k in range(1)]; Sw = P.carve('AS', 'SwD', [4, 16, 64]); yy = P.carve('AS', 'yyD', [4, 256]); sa = P.carve('AS', 'saD', [4, 16])
            names = ['rwkv_w0', 'rwkv_a0', 'rwkv_k_k', 'rwkv_k_a', 'rwkv_ln_w', 'rwkv_ln_b', 'rwkv_r_k']
            P.dma('sp', [brow(pb, 0, I['rwkv_mu'][l:l + 1, :], 896)] + [brow(pb, 896 + 256 * q, I[nm][l:l + 1, :], 256) for q, nm in enumerate(names)] +
                  [(prev.t[:, :], I['st_rwkv_shift'][l, :, :])], w=[pb, prev], sembuf=pb)
            PW0, PA0, PKK, PKA, PLW, PLB, PRK = [896 + 256 * q for q in range(7)]
            sproj(bank[0], 0, WA, 0, 512); sproj(bank[1], 0, WA, 512, 384)
            P.cp(cur.t[:, 0:512], bank[0].t[0:4, 0:512], r=[bank[0]], w=[cur]); P.cp(cur.t[:, 512:896], bank[1].t[0:4, 0:384], r=[bank[1]], w=[cur])
            P.dma('sp', [(O['rwkv_shift_s'][l, :, :], cur.t[:, :])], r=[cur], sembuf=cur, is_out=True)
            P.tt(mm_.t[:, :], prev.t[:, :], cur.t[:, :], ALU.subtract, r=[prev, cur], w=[mm_])
            P.tt(mm_.t[:, :], mm_.t[:, :], pb.t[:, 0:896], ALU.mult, r=[mm_, pb], w=[mm_])
            P.tt(mm_.t[:, :], mm_.t[:, :], cur.t[:, :], ALU.add, r=[mm_, cur], w=[mm_])
            R_, K_, V_ = mm_.t[:, 0:256], mm_.t[:, 256:512], mm_.t[:, 512:768]
            P.tr([(bank[2].t[:, 0:4], mm_.t[0:4, 768:896])], identf.t, r=[mm_, identf], w=[bank[2]])
            P.cp(xTs.t[:, :], bank[2].t[:, 0:4], r=[bank[2]], w=[xTs])
            xT3 = P.carve('AS', 'xT3D', [128, 3, 4])
            P.memset(xT3.t[:].rearrange("p a b -> p (a b)"), 0.0, w=[xT3])
            P.act(xT3.t[0:32, 0, :], xTs.t[0:32, :], AF.Tanh, r=[xTs], w=[xT3])
            P.cp(xT3.t[32:64, 1, :], xTs.t[32:64, :], r=[xTs], w=[xT3])
            P.act(xT3.t[64:128, 2, :], xTs.t[64:128, :], AF.Sigmoid, r=[xTs], w=[xT3])
            P.mm([(bank[3].t[0:4, 0:256], xT3.t[:, 0, :], Wlow.t[:, :], True, True)], r=[xT3, Wlow], w=[bank[3]])
            P.mm([(bank[3].t[0:4, 256:512], xT3.t[:, 1, :], Wlow.t[:, :], True, True)], r=[xT3, Wlow], w=[bank[3]])
            P.mm([(bank[4].t[0:4, 0:256], xT3.t[:, 2, :], Wlow.t[:, :], True, True)], r=[xT3, Wlow], w=[bank[4]])
            Dd, Aa, Gg, KKn = w1.t[:, 0:256], w1.t[:, 256:512], w1.t[:, 512:768], w1.t[:, 768:1024]
            KP, Bb, T1, T2 = w2.t[:, 0:256], w2.t[:, 256:512], w2.t[:, 512:768], w2.t[:, 768:1024]
            P.tt(w1.t[:, 0:512], bank[3].t[0:4, 0:512], pb.t[:, PW0:PW0 + 512], ALU.add, r=[bank[3], pb], w=[w1])
            P.act(w1.t[:, 0:512], w1.t[:, 0:512], AF.Sigmoid, r=[w1], w=[w1])
            P.act(Dd, Dd, AF.Exp, r=[w1], w=[w1], scale=-0.6065306597126334)
            P.cp(Gg, bank[4].t[0:4, 0:256], r=[bank[4]], w=[w1])
            P.tt(KKn, K_, pb.t[:, PKK:PKK + 256], ALU.mult, r=[mm_, pb], w=[w1])
            P.tt(T1, KKn, KKn, ALU.mult, r=[w1], w=[w2])
            P.op('dve', lambda e: e.tensor_reduce(out=sm.t[:, 0:4], in_=T1.rearrange("p (h d) -> p h d", h=4), op=ALU.add, axis=AX.X), r=[w2], w=[sm])
            P.act(sm.t[:, 0:4], sm.t[:, 0:4], AF.Sqrt, r=[sm], w=[sm]); P.ts(sm.t[:, 0:4], sm.t[:, 0:4], 1e-12, ALU.max, r=[sm], w=[sm])
            P.op('dve', lambda e: e.reciprocal(out=sm.t[:, 0:4], in_=sm.t[:, 0:4]), r=[sm], w=[sm])
            P.tt(KKn.rearrange("p (h d) -> p h d", h=4), KKn.rearrange("p (h d) -> p h d", h=4), sm.t[:, 0:4].unsqueeze(2).to_broadcast([4, 4, 64]), ALU.mult, r=[w1, sm], w=[w1])
            P.ts(T1, Aa, -1.0, ALU.add, r=[w1], w=[w2]); P.tt(T1, T1, pb.t[:, PKA:PKA + 256], ALU.mult, r=[w2, pb], w=[w2]); P.ts(T1, T1, 1.0, ALU.add, r=[w2], w=[w2])
            P.tt(KP, K_, T1, ALU.mult, r=[mm_, w2], w=[w2])
            P.tt(Bb, KKn, Aa, ALU.mult, r=[w1], w=[w2])
            for h8 in range(16):
                h, ph = h8 // 4, h8 % 4
                hs_ = slice(h * 64, (h + 1) * 64)
                vs_ = slice(h * 64 + ph * 16, h * 64 + (ph + 1) * 16)
                ps_ = slice(ph * 16, (ph + 1) * 16)
                sh = Sh[0]
                P.dma('sp', [(sh.t[:, :, :], I['st_rwkv'][l, :, h, ps_, :])], w=[sh])
                bk = lambda ap: ap.unsqueeze(1).to_broadcast([4, 16, 64])
                bv = lambda ap: ap.unsqueeze(2).to_broadcast([4, 16, 64])
                P.tt(Sw.t[:], sh.t[:], bk(KKn[:, hs_]), ALU.mult, r=[sh, w1], w=[Sw])
                P.op('dve', lambda e: e.tensor_reduce(out=sa.t[:, :], in_=Sw.t[:], op=ALU.add, axis=AX.X), r=[Sw], w=[sa])
                P.ts(sa.t[:, :], sa.t[:, :], -1.0, ALU.mult, r=[sa], w=[sa])
                P.tt(sh.t[:], sh.t[:], bk(Dd[:, hs_]), ALU.mult, r=[sh, w1], w=[sh])
                P.tt(Sw.t[:], bv(sa.t[:, :]), bk(Bb[:, hs_]), ALU.mult, r=[sa, w2], w=[Sw])
                P.tt(sh.t[:], sh.t[:], Sw.t[:], ALU.add, r=[sh, Sw], w=[sh])
                P.tt(Sw.t[:], bv(V_[:, vs_]), bk(KP[:, hs_]), ALU.mult, r=[mm_, w2], w=[Sw])
                P.tt(sh.t[:], sh.t[:], Sw.t[:], ALU.add, r=[sh, Sw], w=[sh])
                P.dma('sp', [(O['rwkv_s'][l, :, h, ps_, :], sh.t[:, :, :])], r=[sh], sembuf=sh, is_out=True)
                P.tt(Sw.t[:], sh.t[:], bk(R_[:, hs_]), ALU.mult, r=[sh, mm_], w=[Sw])
                P.op('dve', lambda e: e.tensor_reduce(out=yy.t[:, vs_], in_=Sw.t[:], op=ALU.add, axis=AX.X), r=[Sw], w=[yy])
            y3 = yy.t[:, :].rearrange("p (h d) -> p h d", h=4)
            P.op('dve', lambda e: e.tensor_reduce(out=sm.t[:, 4:8], in_=y3, op=ALU.add, axis=AX.X), r=[yy], w=[sm])
            P.ts(sm.t[:, 4:8], sm.t[:, 4:8], 1.0 / 64, ALU.mult, r=[sm], w=[sm])
            P.tt(y3, y3, sm.t[:, 4:8].unsqueeze(2).to_broadcast([4, 4, 64]), ALU.subtract, r=[yy, sm], w=[yy])
            P.tt(T1, yy.t[:, :], yy.t[:, :], ALU.mult, r=[yy], w=[w2])
            P.op('dve', lambda e: e.tensor_reduce(out=sm.t[:, 8:12], in_=T1.rearrange("p (h d) -> p h d", h=4), op=ALU.add, axis=AX.X), r=[w2], w=[sm])
            P.act(sm.t[:, 8:12], sm.t[:, 8:12], AF.Sqrt, r=[sm, epsb], w=[sm], scale=1.0 / 64, bias=epsb.t[0:4, 1:2])
            P.op('dve', lambda e: e.reciprocal(out=sm.t[:, 8:12], in_=sm.t[:, 8:12]), r=[sm], w=[sm])
            P.tt(y3, y3, sm.t[:, 8:12].unsqueeze(2).to_broadcast([4, 4, 64]), ALU.mult, r=[yy, sm], w=[yy])
            P.tt(yy.t[:, :], yy.t[:, :], pb.t[:, PLW:PLW + 256], ALU.mult, r=[yy, pb], w=[yy]); P.tt(yy.t[:, :], yy.t[:, :], pb.t[:, PLB:PLB + 256], ALU.add, r=[yy, pb], w=[yy])
            P.tt(T1, R_, KP, ALU.mult, r=[mm_, w2], w=[w2]); P.tt(T1, T1, pb.t[:, PRK:PRK + 256], ALU.mult, r=[w2, pb], w=[w2])
            P.op('dve', lambda e: e.tensor_reduce(out=sm.t[:, 12:16], in_=T1.rearrange("p (h d) -> p h d", h=4), op=ALU.add, axis=AX.X), r=[w2], w=[sm])
            P.tt(T2.rearrange("p (h d) -> p h d", h=4), V_.rearrange("p (h d) -> p h d", h=4), sm.t[:, 12:16].unsqueeze(2).to_broadcast([4, 4, 64]), ALU.mult, r=[mm_, sm], w=[w2])
            P.tt(yy.t[:, :], yy.t[:, :], T2, ALU.add, r=[yy, w2], w=[yy])
            P.tt(yy.t[:, :], yy.t[:, :], Gg, ALU.mult, r=[yy, w1], w=[yy])
            put_y(yy, yy.t, 6)

        def samp_W():
            for half in range(2):
                hs_ = slice(half * 512, (half + 1) * 512)
                P.mm([(bank[half].t[0:4, :], ymTs.t[:, kc, :], WA.t[:, kc, hs_], kc == 0, kc == 7) for kc in range(8)], r=[ymTs, WA], w=[bank[half]])
                P.tt(xsp.t[0:4, hs_], xsp.t[0:4, hs_], bank[half].t[0:4, :], ALU.add, r=[xsp, bank[half]], w=[xsp])
        if stage == 'init':
            P.finish()
            return P, I, O, DBG
        for tb in range(NTB):
            P.reset('AS')
            hb = [P.carve('AS', 'hb%d' % i, [128, D], BF16) for i in range(2)]
            if tb == 0 and WITH_S:
                samp_norm(hTs)
            for tt in range(4):
                i = tb * 4 + tt
                hbuf = hb[i % 2]
                rmsnorm_tile(xres.t[:, i, :], xres, hbuf)
                psT = bank[6].t[:].bitcast(BF16)
                P.tr([(psT[:, kc * 128:(kc + 1) * 128], hbuf.t[:, kc * 128:(kc + 1) * 128]) for kc in range(8)], identb.t,
                     r=[hbuf, identb], w=[bank[6]])
                P.cp(hT.t[:, :, tt * 128:(tt + 1) * 128], psT.rearrange("p (k t) -> p k t", k=8), r=[bank[6]], w=[hT], eng='act')
            if stage == 'norm':
                P.finish()
                return P, I, O, DBG
            P.reset('AS')
            QT32 = P.carve('AS', 'QT32', [128, 2, TB]); QTm = P.carve('AS', 'QTm', [128, 2, 2, TB], BF16)
            biasT = P.carve('AS', 'biasT', [128, TB], BF16)
            qkr = [P.carve('AS', 'qkr%d' % i, [128, 512]) for i in range(2)]
            vst = [P.carve('AS', 'vst%d' % i, [128, 256]) for i in range(2)]
            rtmp = [P.carve('AS', 'rtmp%d' % i, [128, 8, 32]) for i in range(2)]
            gbuf = P.carve('AS', 'gbuf', [128, 4, 8]); m8 = P.carve('AS', 'm8', [128, 4, 8]); selb = P.carve('AS', 'selb', [128, 4, 8])
            biasq = P.carve('AS', 'biasq', [128, 32])
            PTb = [P.carve('AS', 'PTb%d' % i, [128, 512], BF16) for i in range(2)]
            yatok = P.carve('AS', 'yatok', [128, 256], BF16)
            P.memset(QTm.t[:].rearrange("p a b c -> p (a b c)"), 0.0, w=[QTm])
            P.memset(biasT.t[:], 0.0, w=[biasT])
            load_w(I['w_in'], 0, 768)
            for tt in range(4):
                i = tb * 4 + tt
                par = i % 2
                tok = slice(tt * 128, (tt + 1) * 128)
                P.mm([(bank[0].t[:, 0:512], hT.t[:, kc, tok], WA.t[:, kc, 0:512], kc == 0, kc == 7) for kc in range(8)],
                     r=[hT, WA], w=[bank[0]])
                P.mm([(bank[1].t[:, 0:256], hT.t[:, kc, tok], WA.t[:, kc, 512:768], kc == 0, kc == 7) for kc in range(8)],
                     r=[hT, WA], w=[bank[1]])
                qk = bank[0].t[:].rearrange("p (c d) -> p c d", c=8)
                cosb = rope.t[:, i, 0:32].unsqueeze(1).to_broadcast([128, 8, 32])
                sinb = rope.t[:, i, 32:64].unsqueeze(1).to_broadcast([128, 8, 32])
                qo = qkr[par].t[:].rearrange("p (c d) -> p c d", c=8)
                t1, t2 = rtmp[0], rtmp[1]
                P.tt(t1.t[:], qk[:, :, 0:32], cosb, ALU.mult, r=[bank[0], rope], w=[t1])
                P.tt(t2.t[:], qk[:, :, 32:64], sinb, ALU.mult, r=[bank[0], rope], w=[t2])
                P.tt(qo[:, :, 0:32], t1.t[:], t2.t[:], ALU.subtract, r=[t1, t2], w=[qkr[par]])
                P.tt(t1.t[:], qk[:, :, 32:64], cosb, ALU.mult, r=[bank[0], rope], w=[t1])
                P.tt(t2.t[:], qk[:, :, 0:32], sinb, ALU.mult, r=[bank[0], rope], w=[t2])
                P.tt(qo[:, :, 32:64], t1.t[:], t2.t[:], ALU.add, r=[t1, t2], w=[qkr[par]])
                P.cp(vst[par].t[:], bank[1].t[:, 0:256], r=[bank[1]], w=[vst[par]], eng='act')
                P.cp(Vaug.t[:, i, :, 0:64], bank[1].t[:, 0:256].rearrange("p (h d) -> p h d", h=4), r=[bank[1]], w=[Vaug])
                P.dma('sp', [(O['k_p'][l, i * 128:(i + 1) * 128, :], qkr[par].t[:, 256:512])], r=[qkr[par]], sembuf=qkr[par], is_out=True)
                P.dma('sp', [(O['v_p'][l, i * 128:(i + 1) * 128, :], vst[par].t[:])], r=[vst[par]], sembuf=vst[par], is_out=True)
                if stage == 'rope':
                    P.finish()
                    return P, I, O, DBG
                qkb = PTb[0]
                P.cp(qkb.t[:, :], qkr[par].t[:, :], r=[qkr[par]], w=[qkb], eng='act')
                psb = bank[2].t[:].bitcast(BF16)
                P.tr([(psb[:, c * 128:(c + 1) * 128], qkb.t[:, c * 128:(c + 1) * 128]) for c in range(4)], identb.t, r=[qkb, identb], w=[bank[2]])
                for h2 in range(2):
                    pr = slice(h2 * 64, (h2 + 1) * 64)
                    P.cp(QTm.t[pr, :, h2, tok], psb[pr, 0:256].rearrange("p (c t) -> p c t", c=2), r=[bank[2]], w=[QTm])
                P.cp(KT.t[:, :, i * 128:(i + 1) * 128], psb[:, 256:512].rearrange("p (c t) -> p c t", c=2), r=[bank[2]], w=[KT], eng='act')
                P.tr([(bank[3].t[:, c * 128:(c + 1) * 128], qkr[par].t[:, c * 128:(c + 1) * 128]) for c in range(4)], identf.t,
                     r=[qkr[par], identf], w=[bank[3]])
                P.cp(QT32.t[:, :, tok], bank[3].t[:, 0:256].rearrange("p (c t) -> p c t", c=2), r=[bank[3]], w=[QT32], eng='act')
                n = i // 2
                if i % 2 == 0:
                    P.op('dve', lambda e: e.tensor_reduce(out=kmacc.t[:, :], in_=bank[3].t[:, 256:512].rearrange("p (c t) -> p c t", c=2),
                                                          op=ALU.add, axis=AX.X), r=[bank[3]], w=[kmacc])
                else:
                    P.op('dve', lambda e: e.tensor_reduce(out=small.t[:, 8:10], in_=bank[3].t[:, 256:512].rearrange("p (c t) -> p c t", c=2),
                                                          op=ALU.add, axis=AX.X), r=[bank[3]], w=[small])
                    for h2 in range(2):
                        pr = slice(h2 * 64, (h2 + 1) * 64)
                        P.tt(kmT.t[pr, :, h2, n], kmacc.t[pr, :], small.t[pr, 8:10], ALU.add, r=[kmacc, small], w=[kmT])
                if stage == 'qkT':
                    P.finish()
                    return P, I, O, DBG
                cur = i // 2
                P.mm([(bank[1].t[:, 256 + (2 * hp + h2) * 8:256 + (2 * hp + h2 + 1) * 8], QT32.t[:, hp, tok], kmT.t[:, hp, h2, :], True, True)
                      for hp in range(2) for h2 in range(2)], r=[QT32, kmT], w=[bank[1]])
                P.tt(gbuf.t[:], bank[1].t[:, 256:288].rearrange("p (h n) -> p h n", h=4), cpast.t[:, cur, :].unsqueeze(1).to_broadcast([128, 4, 8]),
                     ALU.add, r=[bank[1], cpast], w=[gbuf])
                for h in range(4):
                    P.op('dve', lambda e: e.max(out=m8.t[:, h, :], in_=gbuf.t[:, h, :]), r=[gbuf], w=[m8])
                P.tt(selb.t[:], gbuf.t[:], m8.t[:, :, 2:3].to_broadcast([128, 4, 8]), ALU.is_ge, r=[gbuf, m8], w=[selb])
                P.stt(biasq.t[:].rearrange("p (h n) -> p h n", h=4), selb.t[:], -1.0, cpm.t[:, cur, :].unsqueeze(1).to_broadcast([128, 4, 8]),
                      ALU.add, ALU.mult, r=[selb, cpm], w=[biasq])
                P.tr([(bank[1].t[0:32, 384:512], biasq.t[:, 0:32])], identf.t, r=[biasq, identf], w=[bank[1]])
                P.cp(biasT.t[0:32, tok], bank[1].t[0:32, 384:512], r=[bank[1]], w=[biasT])
            gcount = 0
            for tt in range(4):
                if stage in ('proj',):
                    break
                i = tb * 4 + tt
                cur = i // 2
                tok = slice(tt * 128, (tt + 1) * 128)
                for h in range(4):
                    keys = list(range(i + 1))
                    ngr = (len(keys) + 3) // 4
                    for g in range(ngr):
                        js = keys[g * 4:(g + 1) * 4]
                        ptb = PTb[gcount % 2]
                        gcount += 1
                        groups = []
                        for s_, j in enumerate(js):
                            o = bank[4].t[:, s_ * 128:(s_ + 1) * 128]
                            hasb = (j // 2) < cur
                            hasc = (j == i)
                            groups.append((o, KT.t[:, h // 2, j * 128:(j + 1) * 128], QTm.t[:, h // 2, h % 2, tok], True, not (hasb or hasc)))
                            if hasb:
                                c = h * 8 + j // 2
                                groups.append((o, identb.t[:, c:c + 1].to_broadcast([128, 128]), biasT.t[:, tok], False, True))
                            if hasc:
                                groups.append((o, identb.t[:, :], causal.t[:, :], False, True))
                        P.mm(groups, r=[KT, QTm, biasT, identb, causal], w=[bank[4]])
                        n = len(js) * 128
                        P.act(ptb.t[:, 0:n], bank[4].t[:, 0:n], AF.Exp, r=[bank[4]], w=[ptb], scale=0.125)
                        P.mm([(bank[5].t[:, 0:65], ptb.t[:, s_ * 128:(s_ + 1) * 128], Vaug.t[:, j, h, :], (g == 0 and s_ == 0), (j == i))
                              for s_, j in enumerate(js)], r=[ptb, Vaug], w=[bank[5]])
                    P.op('dve', lambda e: e.reciprocal(out=small.t[:, 4:5], in_=bank[5].t[:, 64:65]), r=[bank[5]], w=[small])
                    P.ts(yatok.t[:, h * 64:(h + 1) * 64], bank[5].t[:, 0:64], small.t[:, 4:5], ALU.mult, r=[bank[5], small], w=[yatok])
                psT = bank[6].t[:].bitcast(BF16)
                P.tr([(psT[:, c * 128:(c + 1) * 128], yatok.t[:, c * 128:(c + 1) * 128]) for c in range(2)], identb.t, r=[yatok, identb], w=[bank[6]])
                P.cp(ymT.t[:, 0:2, tok], psT[:, 0:256].rearrange("p (c t) -> p c t", c=2), r=[bank[6]], w=[ymT], eng='act')
            if tb == NTB - 1 and WITH_S and WITH_KV and stage not in ('proj',):
                samp_A()
            if stage not in ('proj', 'A'):
                P.reset('AS')
                S = [P.carve('AS', 'S%d' % i, [128, TB]) for i in range(8)]
                load_w(I['w_in'], 768, 1280)
            for fc in range(2):
                if stage in ('proj', 'A'):
                    break
                ub = ubuf[fc]
                if tb > 0:
                    P.cp(ub.t[:, 0:3], ub.t[:, TB:TB + 3], r=[ub], w=[ub])
                P.mm([(bank[0].t[:, :], WA.t[:, kc, fc * 128:(fc + 1) * 128], hT.t[:, kc, :], kc == 0, kc == 7) for kc in range(8)],
                     r=[WA, hT], w=[bank[0]])
                P.mm([(bank[1].t[:, :], WA.t[:, kc, 256 + fc * 128:256 + (fc + 1) * 128], hT.t[:, kc, :], kc == 0, kc == 7) for kc in range(8)],
                     r=[WA, hT], w=[bank[1]])
                P.cp(ub.t[:, 3:3 + TB], bank[0].t[:, :], r=[bank[0]], w=[ub], eng='act')
                xc, rr, ii, aa, a2, hs, gg = S[0], S[1], S[2], S[3], S[4], S[5], S[6]
                P.ts(xc.t[:], ub.t[:, 0:TB], pcol('lru_cw0', fc), ALU.mult, pcol('lru_cb', fc), ALU.add, r=[ub, PT], w=[xc])
                for j in range(1, 4):
                    P.stt(xc.t[:], ub.t[:, j:j + TB], pcol('lru_cw%d' % j, fc), xc.t[:], ALU.mult, ALU.add, r=[ub, PT, xc], w=[xc])
                P.mm([(bank[2].t[:, :], BDa[fc].t[:, :], xc.t[:, :], True, True)], r=[BDa[fc], xc], w=[bank[2]])
                P.mm([(bank[3].t[:, :], BDx[fc].t[:, :], xc.t[:, :], True, True)], r=[BDx[fc], xc], w=[bank[3]])
                P.act(rr.t[:], bank[2].t[:], AF.Sigmoid, r=[bank[2], PT], w=[rr], bias=pcol('lru_ba', fc))
                P.act(ii.t[:], bank[3].t[:], AF.Sigmoid, r=[bank[3], PT], w=[ii], bias=pcol('lru_bx', fc))
                P.act(aa.t[:], rr.t[:], AF.Exp, r=[rr, lruc], w=[aa], scale=lruc.t[:, fc, 0:1])
                P.act(a2.t[:], rr.t[:], AF.Exp, r=[rr, lruc], w=[a2], scale=lruc.t[:, fc, 1:2])
                P.ts(a2.t[:], a2.t[:], 0.99999994, ALU.min, r=[a2], w=[a2])
                P.act(a2.t[:], a2.t[:], AF.Sqrt, r=[a2, onesf], w=[a2], scale=-1.0, bias=onesf.t[:, 0:1])
                P.tt(ii.t[:], ii.t[:], xc.t[:], ALU.mult, r=[ii, xc], w=[ii])
                P.tt(a2.t[:], a2.t[:], ii.t[:], ALU.mult, r=[a2, ii], w=[a2])
                P.op('dve', lambda e: e.tensor_tensor_scan(out=hs.t[:], data0=aa.t[:], data1=a2.t[:], initial=hstate.t[:, fc:fc + 1],
                                                           op0=ALU.mult, op1=ALU.add), r=[aa, a2, hstate], w=[hs])
                P.cp(hstate.t[:, fc:fc + 1], hs.t[:, TB - 1:TB], r=[hs], w=[hstate])
                P.act(gg.t[:], bank[1].t[:], AF.Gelu_apprx_tanh, r=[bank[1]], w=[gg])
                P.tt(ymT.t[:, 2 + fc, :], hs.t[:], gg.t[:], ALU.mult, r=[hs, gg], w=[ymT])
                if tb == NTB - 1:
                    P.dma('sp', [(O['lru_conv_p'][l, :, fc * 128:(fc + 1) * 128].rearrange("j p -> p j"), ub.t[:, TB:TB + 3])],
                          r=[ub], sembuf=ub, is_out=True, allow_slow_non_contiguous=True)
            if tb == NTB - 1 and stage not in ('proj', 'A') and WITH_S:
                samp_B()
            if tb == NTB - 1 and stage not in ('proj', 'A'):
                P.dma('sp', [(O['lru_h_p'][l, :].rearrange("(c p) -> p c", p=128), hstate.t[:, :])], r=[hstate], sembuf=hstate, is_out=True,
                      allow_slow_non_contiguous=True)
            if stage not in ('proj', 'A', 'B'):
                P.reset('AS')
                S = [P.carve('AS', 'S%d' % i, [128, TB]) for i in range(6)]
                Q = [P.carve('AS', 'Q%d' % i, [128, 256]) for i in range(8)]
                Qb = [P.carve('AS', 'Qb%d' % i, [128, 256], BF16) for i in range(7)]
                load_w(I['w_in'], 1280, 2052)
                for c4 in range(4):
                    cbf = cbuf[c4]
                    if tb > 0:
                        P.cp(cbf.t[:, 0:3], cbf.t[:, TB:TB + 3], r=[cbf], w=[cbf])
                    P.mm([(bank[0].t[:, :], WA.t[:, kc, 256 + c4 * 128:256 + (c4 + 1) * 128], hT.t[:, kc, :], kc == 0, kc == 7) for kc in range(8)],
                         r=[WA, hT], w=[bank[0]])
                    P.cp(cbf.t[:, 3:3 + TB], bank[0].t[:, :], r=[bank[0]], w=[cbf], eng='act')
                    ab = 'a' if c4 < 2 else 'b'
                    fc = c4 % 2
                    xo = S[c4]
                    P.ts(xo.t[:], cbf.t[:, 0:TB], pcol('ssd_cw0' + ab, fc), ALU.mult, pcol('ssd_cb' + ab, fc), ALU.add, r=[cbf, PT], w=[xo])
                    for j in range(1, 4):
                        P.stt(xo.t[:], cbf.t[:, j:j + TB], pcol('ssd_cw%d%s' % (j, ab), fc), xo.t[:], ALU.mult, ALU.add, r=[cbf, PT, xo], w=[xo])
                    P.act(xo.t[:], xo.t[:], AF.Silu, r=[xo], w=[xo])
                    if tb == NTB - 1:
                        P.dma('sp', [(O['ssd_conv_p'][l, :, c4 * 128:(c4 + 1) * 128].rearrange("j p -> p j"), cbf.t[:, TB:TB + 3])],
                              r=[cbf], sembuf=cbf, is_out=True, allow_slow_non_contiguous=True)
                for g in range(2):
                    P.memset(S[4 + g].t[:], 0.0, w=[S[4 + g]])
                    pr = slice(g * 64, (g + 1) * 64)
                    P.cp(S[4 + g].t[pr, :], S[3].t[pr, :], r=[S[3]], w=[S[4 + g]])
                for tt in range(4):
                    tok = slice(tt * 128, (tt + 1) * 128)
                    xtok, dtt, xdt, yv, sz = Q[0], Q[1], Q[2], Q[5], Q[6]
                    Btb, xdtd, xdtb, sTb, MTb, CpTb, ycb = Qb[0], Qb[1], Qb[2], Qb[3], Qb[4], Qb[5], Qb[6]
                    P.mm([(bank[1].t[:, 0:256], hT.t[:, kc, tok], WA.t[:, kc, 0:256], kc == 0, kc == 7) for kc in range(8)], r=[hT, WA], w=[bank[1]])
                    P.mm([(bank[1].t[:, 256:260], hT.t[:, kc, tok], WA.t[:, kc, 768:772], kc == 0, kc == 7) for kc in range(8)], r=[hT, WA], w=[bank[1]])
                    P.tr([(bank[2].t[:, c * 128:(c + 1) * 128], S[c].t[:, tok]) for c in range(3)], identf.t, r=[S[0], S[1], S[2], identf], w=[bank[2]])
                    P.cp(xtok.t[:, :], bank[2].t[:, 0:256], r=[bank[2]], w=[xtok], eng='act')
                    P.cp(Btb.t[:, 0:128], bank[2].t[:, 256:384], r=[bank[2]], w=[Btb])
                    P.tt(dtt.t[:, 0:4], bank[1].t[:, 256:260], hp4.t[:, 0:4], ALU.add, r=[bank[1], hp4], w=[dtt])
                    P.act(dtt.t[:, 0:4], dtt.t[:, 0:4], AF.Exp, r=[dtt], w=[dtt])
                    P.act(dtt.t[:, 0:4], dtt.t[:, 0:4], AF.Ln, r=[dtt, onesf], w=[dtt], bias=onesf.t[:, 0:1])
                    P.tt(dtt.t[:, 4:8], dtt.t[:, 0:4], hp4.t[:, 4:8], ALU.mult, r=[dtt, hp4], w=[dtt])
                    P.tt(xdt.t[:].rearrange("p (h d) -> p h d", h=4), xtok.t[:].rearrange("p (h d) -> p h d", h=4),
                         dtt.t[:, 0:4].unsqueeze(2).to_broadcast([128, 4, 64]), ALU.mult, r=[xtok, dtt], w=[xdt])
                    P.mm([(bank[3].t[:, 0:4], tri.t[:, :], dtt.t[:, 4:8], True, True)], r=[tri, dtt], w=[bank[3]])
                    P.mm([(bank[3].t[:, 4:8], onesf.t[:, :], dtt.t[:, 4:8], True, True)], r=[onesf, dtt], w=[bank[3]])
                    P.cp(dtt.t[:, 8:16], bank[3].t[:, 0:8], r=[bank[3]], w=[dtt])
                    P.tt(dtt.t[:, 16:20], dtt.t[:, 12:16], dtt.t[:, 8:12], ALU.subtract, r=[dtt], w=[dtt])
                    P.act(dtt.t[:, 16:20], dtt.t[:, 16:20], AF.Exp, r=[dtt], w=[dtt])
                    P.act(dtt.t[:, 20:24], dtt.t[:, 12:16], AF.Exp, r=[dtt], w=[dtt])
                    P.mm([(bank[4].t[:, h * 128:(h + 1) * 128], dtt.t[:, 4 + h:5 + h].to_broadcast([128, 128]), tri.t[:, :], True, True) for h in range(4)],
                         r=[dtt, tri], w=[bank[4]])
                    P.mm([(bank[5].t[:, g * 128:(g + 1) * 128], S[2].t[:, tok], S[4 + g].t[:, tok], True, True) for g in range(2)],
                         r=[S[2], S[4], S[5]], w=[bank[5]])
                    P.tt(xdtd.t[:].rearrange("p (h d) -> p h d", h=4), xdt.t[:].rearrange("p (h d) -> p h d", h=4),
                         dtt.t[:, 16:20].unsqueeze(2).to_broadcast([128, 4, 64]), ALU.mult, r=[xdt, dtt], w=[xdtd])
                    P.cp(xdtb.t[:, :], xdt.t[:, :], r=[xdt], w=[xdtb], eng='act')
                    P.cp(sTb.t[:, :], sT.t[:].rearrange("p h d -> p (h d)"), r=[sT], w=[sTb])
                    for h in range(4):
                        g = h // 2
                        Dm = Q[3 + (h % 2)]
                        hs_ = slice((h % 2) * 128, (h % 2 + 1) * 128)
                        P.stt(Dm.t[:, 0:128], bank[4].t[:, h * 128:(h + 1) * 128], dtt.t[:, 8 + h:9 + h], negmask.t[:, :], ALU.subtract, ALU.add,
                              r=[bank[4], dtt, negmask], w=[Dm])
                        P.act(Dm.t[:, 0:128], Dm.t[:, 0:128], AF.Exp, r=[Dm], w=[Dm])
                        P.tt(MTb.t[:, hs_], Dm.t[:, 0:128], bank[5].t[:, g * 128:(g + 1) * 128], ALU.mult, r=[Dm, bank[5]], w=[MTb])
                        P.act(Dm.t[:, 128:256], bank[4].t[:, h * 128:(h + 1) * 128], AF.Exp, r=[bank[4]], w=[Dm])
                        P.tt(CpTb.t[:, hs_], S[4 + g].t[:, tok], Dm.t[:, 128:256], ALU.mult, r=[S[4 + g], Dm], w=[CpTb])
                        P.mm([(bank[6].t[:, h * 64:(h + 1) * 64], MTb.t[:, hs_], xdtb.t[:, h * 64:(h + 1) * 64], True, False),
                              (bank[6].t[:, h * 64:(h + 1) * 64], CpTb.t[:, hs_], sTb.t[:, h * 64:(h + 1) * 64], False, True)],
                             r=[MTb, CpTb, xdtb, sTb], w=[bank[6]])
                    P.mm([(bank[7].t[:, h * 64:(h + 1) * 64], Btb.t[:, 0:128], xdtd.t[:, h * 64:(h + 1) * 64], True, True) for h in range(4)],
                         r=[Btb, xdtd], w=[bank[7]])
                    for h in range(4):
                        pr = slice((h // 2) * 64, (h // 2 + 1) * 64)
                        P.stt(sT.t[pr, h, :], sT.t[pr, h, :], dtt.t[pr, 20 + h:21 + h], bank[7].t[pr, h * 64:(h + 1) * 64], ALU.mult, ALU.add,
                              r=[sT, dtt, bank[7]], w=[sT])
                    P.tt(yv.t[:].rearrange("p (h d) -> p h d", h=4), xtok.t[:].rearrange("p (h d) -> p h d", h=4),
                         hp4.t[:, 8:12].unsqueeze(2).to_broadcast([128, 4, 64]), ALU.mult, r=[xtok, hp4], w=[yv])
                    P.tt(yv.t[:, :], yv.t[:, :], bank[6].t[:, 0:256], ALU.add, r=[yv, bank[6]], w=[yv])
                    P.act(sz.t[:, :], bank[1].t[:, 0:256], AF.Silu, r=[bank[1]], w=[sz])
                    P.tt(yv.t[:, :], yv.t[:, :], sz.t[:, :], ALU.mult, r=[yv, sz], w=[yv])
                    rmsnorm_tile(yv.t[:, :], yv, ycb, width=256, gain=nwbc.t[:, :], gbuf_=nwbc)
                    psT = bank[3].t[:].bitcast(BF16)
                    P.tr([(psT[:, c * 128:(c + 1) * 128], ycb.t[:, c * 128:(c + 1) * 128]) for c in range(2)], identb.t, r=[ycb, identb], w=[bank[3]])
                    P.cp(ymT.t[:, 4:6, tok], psT[:, 0:256].rearrange("p (c t) -> p c t", c=2), r=[bank[3]], w=[ymT], eng='act')
                if tb == NTB - 1:
                    P.tr([(bank[2].t[:, c * 128:(c + 1) * 128], sT.t[:].rearrange("p h d -> p (h d)")[:, c * 128:(c + 1) * 128]) for c in range(2)],
                         identf.t, r=[sT, identf], w=[bank[2]])
                    P.cp(Q[7].t[:, :], bank[2].t[:, 0:256], r=[bank[2]], w=[Q[7]])
                    P.dma('sp', [(O['ssd_p'][l, h, :, :], Q[7].t[(h % 2) * 64:(h % 2 + 1) * 64, (h // 2) * 192:(h // 2) * 192 + 64]) for h in range(4)],
                          r=[Q[7]], sembuf=Q[7], is_out=True)
                    if WITH_S:
                        P.reset('AS')
                        samp_C()
            if stage not in ('proj', 'A', 'B', 'C'):
                P.reset('AS')
                load_w(I['w_in'], 2052, 2948)
                RW = [P.carve('AS', 'RW%d' % i, [128, 2, 128]) for i in range(16)]
                Wst = [P.carve('AS', 'Wst%d' % i, [128, 2, 64]) for i in range(2)]
                tmpst = [P.carve('AS', 'tmpst%d' % i, [128, 2, 64]) for i in range(2)]
                vtok = P.carve('AS', 'vtok', [128, 256], BF16)
                xT = P.carve('AS', 'xT', [128, 128])
                fl = lambda b: b.t[:].rearrange("p a b -> p (a b)")
                rT, kT_, vT, dl, dT, aT, gT, kk, kp, bT, nk0, nk1, rm0, rm1, t1, t2 = RW
                for sb in range(4):
                    tok = slice(sb * 128, (sb + 1) * 128)
                    for c7 in range(7):
                        cbx = cb7[c7]
                        P.mm([(bank[0].t[:, 0:128], WA.t[:, kc, c7 * 128:(c7 + 1) * 128], hT.t[:, kc, tok], kc == 0, kc == 7) for kc in range(8)],
                             r=[WA, hT], w=[bank[0]])
                        P.cp(cbx.t[:, 1:129], bank[0].t[:, 0:128], r=[bank[0]], w=[cbx], eng='act')
                        if c7 < 6:
                            dbuf = (rT, kT_, vT)[c7 // 2]
                            dst = dbuf.t[:, c7 % 2, :]
                            mu = pcol(('rw_mu_r', 'rw_mu_k', 'rw_mu_v')[c7 // 2], c7 % 2)
                        else:
                            dbuf = xT
                            dst = xT.t[:, :]
                            mu = pcol('rw_mu_x', 0)
                        P.tt(dl.t[:, 0, :], cbx.t[:, 0:128], cbx.t[:, 1:129], ALU.subtract, r=[cbx], w=[dl])
                        P.stt(dst, dl.t[:, 0, :], mu, cbx.t[:, 1:129], ALU.mult, ALU.add, r=[dl, PT, cbx], w=[dbuf])
                        P.cp(cbx.t[:, 0:1], cbx.t[:, 128:129], r=[cbx], w=[cbx])
                    P.act(xT.t[0:32, :], xT.t[0:32, :], AF.Tanh, r=[xT], w=[xT])
                    P.act(xT.t[64:128, :], xT.t[64:128, :], AF.Sigmoid, r=[xT], w=[xT])
                    for hp in range(2):
                        cols = slice(hp * 128, (hp + 1) * 128)
                        P.mm([(bank[1].t[:, 0:128], Wlow.t[0:32, cols], xT.t[0:32, :], True, True)], r=[Wlow, xT], w=[bank[1]])
                        P.act(dT.t[:, hp, :], bank[1].t[:, 0:128], AF.Sigmoid, r=[bank[1], PT], w=[dT], bias=pcol('rw_w0', hp))
                        P.act(dT.t[:, hp, :], dT.t[:, hp, :], AF.Exp, r=[dT], w=[dT], scale=-0.6065306597126334)
                        P.mm([(bank[1].t[:, 128:256], Wlow.t[32:64, cols], xT.t[32:64, :], True, True)], r=[Wlow, xT], w=[bank[1]])
                        P.act(aT.t[:, hp, :], bank[1].t[:, 128:256], AF.Sigmoid, r=[bank[1], PT], w=[aT], bias=pcol('rw_a0', hp))
                        P.mm([(bank[1].t[:, 256:384], Wlow.t[64:128, cols], xT.t[64:128, :], True, True)], r=[Wlow, xT], w=[bank[1]])
                        P.cp(gT.t[:, hp, :], bank[1].t[:, 256:384], r=[bank[1]], w=[gT], eng='act')
                        P.ts(kk.t[:, hp, :], kT_.t[:, hp, :], pcol('rw_kk', hp), ALU.mult, r=[kT_, PT], w=[kk])
                    P.tt(fl(t1), fl(kk), fl(kk), ALU.mult, r=[kk], w=[t1])
                    P.mm([(bank[2].t[:, 0:256], BDones.t[:, :], fl(t1), True, True)], r=[BDones, t1], w=[bank[2]])
                    P.act(fl(t2), bank[2].t[:, 0:256], AF.Sqrt, r=[bank[2]], w=[t2])
                    P.ts(fl(t2), fl(t2), 1e-12, ALU.max, r=[t2], w=[t2])
                    P.op('dve', lambda e: e.reciprocal(out=fl(t2), in_=fl(t2)), r=[t2], w=[t2])
                    P.tt(fl(kk), fl(kk), fl(t2), ALU.mult, r=[kk, t2], w=[kk])
                    P.ts(fl(nk0), fl(kk), hmask.t[:, 0:1], ALU.mult, -1.0, ALU.mult, r=[kk, hmask], w=[nk0])
                    P.ts(fl(nk1), fl(kk), hmask.t[:, 1:2], ALU.mult, -1.0, ALU.mult, r=[kk, hmask], w=[nk1])
                    for hp in range(2):
                        P.ts(t1.t[:, hp, :], aT.t[:, hp, :], pcol('rw_ka', hp), ALU.mult, rwc.t[:, hp:hp + 1], ALU.add, r=[aT, PT, rwc], w=[t1])
                    P.tt(fl(kp), fl(kT_), fl(t1), ALU.mult, r=[kT_, t1], w=[kp])
                    P.tt(fl(bT), fl(kk), fl(aT), ALU.mult, r=[kk, aT], w=[bT])
                    P.ts(fl(rm0), fl(rT), hmask.t[:, 0:1], ALU.mult, r=[rT, hmask], w=[rm0])
                    P.ts(fl(rm1), fl(rT), hmask.t[:, 1:2], ALU.mult, r=[rT, hmask], w=[rm1])
                    P.tr([(bank[2].t[:, 256 + hp * 128:256 + (hp + 1) * 128], vT.t[:, hp, :]) for hp in range(2)], identf.t, r=[vT, identf], w=[bank[2]])
                    P.cp(vtok.t[:, :], bank[2].t[:, 256:512], r=[bank[2]], w=[vtok])
                    for hp in range(2):
                        P.stt(t1.t[:, hp, :], rT.t[:, hp, :], pcol('rw_rk', hp), kp.t[:, hp, :], ALU.mult, ALU.mult, r=[rT, PT, kp], w=[t1])
                    P.mm([(bank[3].t[:, 0:256], BDones.t[:, :], fl(t1), True, True)], r=[BDones, t1], w=[bank[3]])
                    P.tt(fl(t2), bank[3].t[:, 0:256], fl(vT), ALU.mult, r=[bank[3], vT], w=[t2])
                    vtv = vtok.t[:, :].rearrange("p (hp h2 v) -> p hp h2 v", hp=2, h2=2)
                    nks = (nk0, nk1)
                    rms = (rm0, rm1)
                    Yb = bank[4]
                    for t in range(128):
                        ws = Wst[t % 2]
                        tm = tmpst[t % 2]
                        P.mm([(bank[5].t[h2 * 64:(h2 + 1) * 64, 0:128].rearrange("p (a b) -> p a b", a=2), identb.t[:, t:t + 1].to_broadcast([128, 64]),
                               vtv[:, :, h2, :], True, True) for h2 in range(2)], r=[identb, vtok], w=[bank[5]])
                        P.mm([(bank[6].t[h2 * 64:(h2 + 1) * 64, hp * 64:(hp + 1) * 64], nks[h2].t[:, hp, t:t + 1].to_broadcast([128, 64]), ST.t[:, hp, :], True, True)
                              for hp in range(2) for h2 in range(2)], r=[nk0, nk1, ST], w=[bank[6]])
                        for hp in range(2):
                            P.act(ws.t[:, hp, :], bank[5].t[:, hp * 64:(hp + 1) * 64], AF.Copy, r=[bank[5], kp], w=[ws], scale=kp.t[:, hp, t:t + 1])
                        for hp in range(2):
                            P.stt(tm.t[:, hp, :], bank[6].t[:, hp * 64:(hp + 1) * 64], bT.t[:, hp, t:t + 1], ws.t[:, hp, :], ALU.mult, ALU.add,
                                  r=[bank[6], bT, ws], w=[tm])
                        for hp in range(2):
                            P.stt(ST.t[:, hp, :], ST.t[:, hp, :], dT.t[:, hp, t:t + 1], tm.t[:, hp, :], ALU.mult, ALU.add, r=[ST, dT, tm], w=[ST])
                        P.mm([(Yb.t[h2 * 64:(h2 + 1) * 64, hp * 128 + t:hp * 128 + t + 1], ST.t[:, hp, :], rms[h2].t[:, hp, t:t + 1], True, True)
                              for hp in range(2) for h2 in range(2)], r=[ST, rm0, rm1], w=[Yb])
                    cen, sq = kk, kp
                    P.cp(fl(t1), Yb.t[:, 0:256], r=[Yb], w=[t1], eng='act')
                    P.mm([(bank[3].t[:, 0:256], BDones.t[:, :], fl(t1), True, True)], r=[BDones, t1], w=[bank[3]])
                    P.stt(fl(cen), bank[3].t[:, 0:256], -1.0 / 64, fl(t1), ALU.mult, ALU.add, r=[bank[3], t1], w=[cen])
                    P.tt(fl(sq), fl(cen), fl(cen), ALU.mult, r=[cen], w=[sq])
                    P.mm([(bank[3].t[:, 256:512], BDones.t[:, :], fl(sq), True, True)], r=[BDones, sq], w=[bank[3]])
                    P.act(fl(sq), bank[3].t[:, 256:512], AF.Sqrt, r=[bank[3], epsb], w=[sq], scale=1.0 / 64, bias=epsb.t[:, 1:2])
                    P.op('dve', lambda e: e.reciprocal(out=fl(sq), in_=fl(sq)), r=[sq], w=[sq])
                    P.tt(fl(cen), fl(cen), fl(sq), ALU.mult, r=[cen, sq], w=[cen])
                    for hp in range(2):
                        P.ts(cen.t[:, hp, :], cen.t[:, hp, :], pcol('rw_lnw', hp), ALU.mult, pcol('rw_lnb', hp), ALU.add, r=[cen, PT], w=[cen])
                    P.tt(fl(cen), fl(cen), fl(t2), ALU.add, r=[cen, t2], w=[cen])
                    P.tt(ymT.t[:, 6:8, tok], cen.t[:, :, :], gT.t[:, :, :], ALU.mult, r=[cen, gT], w=[ymT])
                if tb == NTB - 1:
                    P.dma('sp', [(O['rwkv_shift_p'][l, c7 * 128:(c7 + 1) * 128].rearrange("(p o) -> p o", o=1), cb7[c7].t[:, 128:129]) for c7 in range(7)],
                          r=cb7, sembuf=cb7[0], is_out=True, allow_slow_non_contiguous=True)
                    P.tr([(bank[2].t[:, 0:128], ST.t[:].rearrange("p a b -> p (a b)"))], identf.t, r=[ST, identf], w=[bank[2]])
                    P.cp(fl(t1)[:, 0:128], bank[2].t[:, 0:128], r=[bank[2]], w=[t1])
                    P.dma('sp', [(O['rwkv_p'][l, h, :, :], fl(t1)[(h // 2) * 64:(h // 2 + 1) * 64, (h % 2) * 64:(h % 2 + 1) * 64]) for h in range(4)],
                          r=[t1], sembuf=t1, is_out=True)
                    if WITH_S:
                        P.reset('AS')
                        samp_D()
            if stage in ('all', 'W', 'X', 'PEER'):
                P.reset('AS')
                load_w(I['w_out'], 0, 1024)
                for tt in range(4):
                    i = tb * 4 + tt
                    tok = slice(tt * 128, (tt + 1) * 128)
                    for half in range(2):
                        hs_ = slice(half * 512, (half + 1) * 512)
                        P.mm([(bank[half].t[:, :], ymT.t[:, kc, tok], WA.t[:, kc, hs_], kc == 0, kc == 7) for kc in range(8)], r=[ymT, WA], w=[bank[half]])
                        P.tt(xres.t[:, i, hs_], xres.t[:, i, hs_], bank[half].t[:, :], ALU.add, r=[xres, bank[half]], w=[xres])
                if tb == NTB - 1 and WITH_S:
                    samp_W()
            if 'ymTs' in DBG and l == 0 and tb == NTB - 1:
                dbg('ymTs', DBG['ymTs'], ymTs.t[:, :, :], [ymTs])
            if 'ymT' in DBG and l == 0:
                dbg('ymT', DBG['ymT'][:, :, tb * TB:(tb + 1) * TB], ymT.t[:, :, :], [ymT])
        if 'x1' in DBG and l == 0:
            dbg('x1', DBG['x1'].rearrange("(n p) c -> p n c", p=128), xres.t[:, :, :], [xres])
        if stage in ('all', 'X', 'PEER'):
            P.reset('AL'); P.reset('AS')
            hT = P.carve('AL', 'hT', [128, 8, TB], BF16)
            qT = P.carve('AL', 'qT', [128, 8, TB], BF16)
            oT = P.carve('AL', 'oT', [128, 8, TB], BF16)
            W1 = P.carve('AL', 'WA', [128, 8, D], BF16)
            W2 = P.carve('AL', 'WB', [128, 8, D], BF16)
            memT = P.carve('AL', 'memT', [128, 8, 256], BF16)
            KTm = P.carve('AL', 'KTm', [128, 8, 256], BF16)
            Vx = P.carve('AL', 'Vx', [128, 2, D], BF16)
            onesb = P.carve('AL', 'onesb', [128, 128], BF16)
            memf = P.carve('AS', 'memf', [128, 2, D])
            memb = P.carve('AS', 'memb', [128, 2, D], BF16)
            kst = [P.carve('AS', 'kst%d' % i, [128, D]) for i in range(2)]
            P.memset(onesb.t[:], 1.0, w=[onesb])
            def load2(Wb, src):
                for kc in range(8):
                    P.dma('pool', [(Wb.t[:, kc, :], src[l, kc * 128:(kc + 1) * 128, :])], w=[Wb])
            load2(W1, I['x_wk']); load2(W2, I['x_wv'])
            P.dma('sp', [(memf.t[:, mt, :], I['memp'][mt * 128:(mt + 1) * 128, :]) for mt in range(2)], w=[memf])
            P.cp(memb.t[:].rearrange("p a b -> p (a b)"), memf.t[:].rearrange("p a b -> p (a b)"), r=[memf], w=[memb], eng='act')
            for mt in range(2):
                psT = bank[6].t[:].bitcast(BF16)
                P.tr([(psT[:, kc * 128:(kc + 1) * 128], memb.t[:, mt, kc * 128:(kc + 1) * 128]) for kc in range(8)], identb.t, r=[memb, identb], w=[bank[6]])
                P.cp(memT.t[:, :, mt * 128:(mt + 1) * 128], psT.rearrange("p (k t) -> p k t", k=8), r=[bank[6]], w=[memT])
            for (Wb, oname, isv) in ((W1, 'mem_k_p', False), (W2, 'mem_v_p', True)):
                for mt in range(2):
                    stg = kst[mt]
                    for half in range(2):
                        hs_ = slice(half * 512, (half + 1) * 512)
                        P.mm([(bank[half].t[:, :], memT.t[:, kc, mt * 128:(mt + 1) * 128], Wb.t[:, kc, hs_], kc == 0, kc == 7) for kc in range(8)],
                             r=[memT, Wb], w=[bank[half]])
                        P.cp(stg.t[:, hs_], bank[half].t[:, :], r=[bank[half]], w=[stg], eng='act')
                        if isv:
                            P.cp(Vx.t[:, mt, hs_], bank[half].t[:, :], r=[bank[half]], w=[Vx])
                    P.dma('sp', [(O[oname][l, mt * 128:(mt + 1) * 128, :], stg.t[:, :])], r=[stg], sembuf=stg, is_out=True)
            for dc in range(8):
                P.mm([(bank[2].t[:, 0:256], W1.t[:, kc, dc * 128:(dc + 1) * 128], memT.t[:, kc, :], kc == 0, kc == 7) for kc in range(8)],
                     r=[W1, memT], w=[bank[2]])
                P.cp(KTm.t[:, dc, :], bank[2].t[:, 0:256], r=[bank[2]], w=[KTm])
            load2(W1, I['x_wq']); load2(W2, I['x_wo'])
            P.dma('sp', [(gbc.t[:], I['norm_x'][l:l + 1, :].to_broadcast([128, D]))], w=[gbc])
            for tb in range(NTB):
                P.reset('AS')
                hb = [P.carve('AS', 'hb%d' % i, [128, D], BF16) for i in range(2)]
                PTx = P.carve('AS', 'PTx', [128, 2, TB], BF16)
                rec = P.carve('AS', 'rec', [128, TB])
                for tt in range(4):
                    i = tb * 4 + tt
                    hbuf = hb[i % 2]
                    rmsnorm_tile(xres.t[:, i, :], xres, hbuf)
                    psT = bank[6].t[:].bitcast(BF16)
                    P.tr([(psT[:, kc * 128:(kc + 1) * 128], hbuf.t[:, kc * 128:(kc + 1) * 128]) for kc in range(8)], identb.t,
                         r=[hbuf, identb], w=[bank[6]])
                    P.cp(hT.t[:, :, tt * 128:(tt + 1) * 128], psT.rearrange("p (k t) -> p k t", k=8), r=[bank[6]], w=[hT], eng='act')
                for dc in range(8):
                    P.mm([(bank[0].t[:, :], W1.t[:, kc, dc * 128:(dc + 1) * 128], hT.t[:, kc, :], kc == 0, kc == 7) for kc in range(8)],
                         r=[W1, hT], w=[bank[0]])
                    P.cp(qT.t[:, dc, :], bank[0].t[:, :], r=[bank[0]], w=[qT], eng=('act' if dc % 2 else 'dve'))
                for h in range(4):
                    for mt in range(2):
                        P.mm([(bank[1 + mt].t[:, :], KTm.t[:, 2 * h + c, mt * 128:(mt + 1) * 128], qT.t[:, 2 * h + c, :], c == 0, c == 1) for c in range(2)],
                             r=[KTm, qT], w=[bank[1 + mt]])
                        P.act(PTx.t[:, mt, :], bank[1 + mt].t[:, :], AF.Exp, r=[bank[1 + mt]], w=[PTx], scale=1.0 / 16)
                    P.mm([(bank[3].t[:, :], onesb.t[:, :], PTx.t[:, mt, :], mt == 0, mt == 1) for mt in range(2)], r=[onesb, PTx], w=[bank[3]])
                    P.op('dve', lambda e: e.reciprocal(out=rec.t[:, :], in_=bank[3].t[:, :]), r=[bank[3]], w=[rec])
                    for c in range(2):
                        P.mm([(bank[4 + c].t[:, :], Vx.t[:, mt, h * 256 + c * 128:h * 256 + (c + 1) * 128], PTx.t[:, mt, :], mt == 0, mt == 1) for mt in range(2)],
                             r=[Vx, PTx], w=[bank[4 + c]])
                        P.tt(oT.t[:, 2 * h + c, :], bank[4 + c].t[:, :], rec.t[:, :], ALU.mult, r=[bank[4 + c], rec], w=[oT])
                for tt in range(4):
                    i = tb * 4 + tt
                    tok = slice(tt * 128, (tt + 1) * 128)
                    for half in range(2):
                        hs_ = slice(half * 512, (half + 1) * 512)
                        P.mm([(bank[half].t[:, :], oT.t[:, kc, tok], W2.t[:, kc, hs_], kc == 0, kc == 7) for kc in range(8)], r=[oT, W2], w=[bank[half]])
                        P.tt(xres.t[:, i, hs_], xres.t[:, i, hs_], bank[half].t[:, :], ALU.add, r=[xres, bank[half]], w=[xres])
            if WITH_S:
                P.reset('AS')
                samp_norm(hTs)
                qS = P.carve('AS', 'qS', [4, D]); oTs = P.carve('AS', 'oTs', [128, 8, 4], BF16)
                Kc = P.carve('AS', 'Kc', [128, 2, D]); Vc = P.carve('AS', 'Vc', [128, 2, D])
                prodx = P.carve('AS', 'prodx', [128, D]); scx = P.carve('AS', 'scx', [128, 2, 4]); obx = P.carve('AS', 'obx', [4, D + 4])
                dmx = P.carve('AS', 'dmx', [4, D])
                P.dma('sp', [(dmx.t[:, :], I['c_dmask4x'][:, :])], w=[dmx])
                for half in range(2):
                    hs_ = slice(half * 512, (half + 1) * 512)
                    P.mm([(bank[half].t[0:4, :], hTs.t[:, kc, :], W1.t[:, kc, hs_], kc == 0, kc == 7) for kc in range(8)], r=[hTs, W1], w=[bank[half]])
                    P.cp(qS.t[:, hs_], bank[half].t[0:4, :], r=[bank[half]], w=[qS])
                for s_ in range(4):
                    P.dma('sp', [(Kc.t[:, mt, :], I['cmk'][l, s_, mt * 128:(mt + 1) * 128, :]) for mt in range(2)], w=[Kc])
                    P.dma('sp', [(Vc.t[:, mt, :], I['cmv'][l, s_, mt * 128:(mt + 1) * 128, :]) for mt in range(2)], w=[Vc])
                    for half in range(2):
                        P.mm([(bank[2 + half].t[:, :], identf.t[0:4, s_:s_ + 1].to_broadcast([4, 128]), qS.t[0:4, half * 512:(half + 1) * 512], True, True)],
                             r=[identf, qS], w=[bank[2 + half]])
                    for mt in range(2):
                        for half in range(2):
                            hs_ = slice(half * 512, (half + 1) * 512)
                            P.tt(prodx.t[:, hs_], Kc.t[:, mt, hs_], bank[2 + half].t[:, :], ALU.mult, r=[Kc, bank[2 + half]], w=[prodx])
                        P.op('dve', lambda e: e.tensor_reduce(out=scx.t[:, mt, :], in_=prodx.t[:, :].rearrange("p (h d) -> p h d", h=4), op=ALU.add, axis=AX.X), r=[prodx], w=[scx])
                    P.act(scx.t[:].rearrange("p a b -> p (a b)"), scx.t[:].rearrange("p a b -> p (a b)"), AF.Exp, r=[scx], w=[scx], scale=1.0 / 16)
                    for half in range(2):
                        hs_ = slice(half * 512, (half + 1) * 512)
                        P.mm([(bank[4 + half].t[0:4, :], scx.t[:, mt, :], Vc.t[:, mt, hs_], mt == 0, mt == 1) for mt in range(2)], r=[scx, Vc], w=[bank[4 + half]])
                    P.mm([(bank[6].t[0:4, 0:1], scx.t[:, mt, :], onesf.t[:, 0:1], mt == 0, mt == 1) for mt in range(2)], r=[scx, onesf], w=[bank[6]])
                    P.op('dve', lambda e: e.reciprocal(out=obx.t[:, D:D + 1], in_=bank[6].t[0:4, 0:1]), r=[bank[6]], w=[obx])
                    for half in range(2):
                        hs_ = slice(half * 512, (half + 1) * 512)
                        P.stt(obx.t[:, hs_], bank[4 + half].t[0:4, :], obx.t[:, D:D + 1], dmx.t[:, hs_], ALU.mult, ALU.mult, r=[bank[4 + half], obx, dmx], w=[obx])
                    P.mm([(bank[7].t[:, s_ * 8 + c:s_ * 8 + c + 1], obx.t[0:4, c * 128:(c + 1) * 128], onesf.t[0:4, 0:1], True, True) for c in range(8)],
                         r=[obx, onesf], w=[bank[7]])
                P.cp(oTs.t[:, :, :], bank[7].t[:, 0:32].rearrange("p (s c) -> p c s", c=8), r=[bank[7]], w=[oTs])
                for half in range(2):
                    hs_ = slice(half * 512, (half + 1) * 512)
                    P.mm([(bank[half].t[0:4, :], oTs.t[:, kc, :], W2.t[:, kc, hs_], kc == 0, kc == 7) for kc in range(8)], r=[oTs, W2], w=[bank[half]])
                    P.tt(xsp.t[0:4, hs_], xsp.t[0:4, hs_], bank[half].t[0:4, :], ALU.add, r=[xsp, bank[half]], w=[xsp])
            if 'x2' in DBG and l == 0:
                dbg('x2', DBG['x2'].rearrange("(n p) c -> p n c", p=128), xres.t[:, :, :], [xres])
        if stage in ('all', 'PEER'):
            NTP = NT + 1 if WITH_S else NT
            xtile = lambda i: (xres.t[:, i, :] if i < NT else xsp.t[:, :])
            xbuf_ = lambda i: (xres if i < NT else xsp)
            P.reset('AL'); P.reset('AS')
            hTall = P.carve('AL', 'hTall', [128, 8, T + 128], BF16)
            Wpq = P.carve('AL', 'Wpq', [128, 8, 2048], BF16)
            skT = P.carve('AL', 'skT', [128, 16, 128], BF16)
            iotar = P.carve('AL', 'iotar', [128, 128])
            bd16 = P.carve('AL', 'bd16', [128, 8, 16], BF16)
            WcT = P.carve('AL', 'WcT', [128, 16, 128], BF16)
            ixT = P.carve('AL', 'ixT', [128, 2, 128])
            P.dma('sp', [(gbc.t[:], I['norm_ffn'][l:l + 1, :].to_broadcast([128, D]))], w=[gbc])
            for kc in range(8):
                P.dma('pool', [(Wpq.t[:, kc, :], I['peer_wq'][l, kc * 128:(kc + 1) * 128, :])], w=[Wpq])
            P.dma('sp', [(iotar.t[:, :], I['c_iota'][:, :])], w=[iotar])
            P.dma('pool', [(bd16.t[:].rearrange("p a b -> p (a b)"), I['c_bd16'][:, :])], w=[bd16])
            P.reset('AS')
            skf = P.carve('AS', 'skf', [128, 16, 128]); skb = P.carve('AS', 'skb', [128, 16, 128], BF16)
            P.dma('sp', [(skf.t[:, hc, :], I['peer_subkeys'][l, hc, :, :]) for hc in range(16)], w=[skf])
            P.cp(skb.t[:].rearrange("p a b -> p (a b)"), skf.t[:].rearrange("p a b -> p (a b)"), r=[skf], w=[skb], eng='act')
            for q4 in range(2):
                psT = bank[6].t[:].bitcast(BF16)
                P.tr([(psT[:, c * 128:(c + 1) * 128], skb.t[:, q4 * 8 + c, :]) for c in range(8)], identb.t, r=[skb, identb], w=[bank[6]])
                P.cp(skT.t[:, q4 * 8:(q4 + 1) * 8, :], psT.rearrange("p (k t) -> p k t", k=8), r=[bank[6]], w=[skT])
            for i in range(NTP):
                tokg = slice(i * 128, (i + 1) * 128)
                P.reset('AS')
                hbuf = P.carve('AS', 'hb0', [128, D], BF16)
                qTt = P.carve('AS', 'qTt', [128, 16, 128], BF16)
                sc = P.carve('AS', 'sc', [128, 16, 128]); scr = P.carve('AS', 'scr', [128, 16, 128])
                tops = P.carve('AS', 'tops', [128, 16, 16]); idx = P.carve('AS', 'idx', [128, 16, 16], U32)
                ixf = P.carve('AS', 'ixf', [128, 2, 128])
                cand = Buf('cand', sc.t[:].rearrange("p a b -> p (a b)").rearrange("p (h x) -> p h x", h=8)); sc = cand_alias(sc, cand)
                cscr = Buf('cscr', scr.t[:].rearrange("p a b -> p (a b)").rearrange("p (h x) -> p h x", h=8)); scr = cand_alias(scr, cscr)
                tsv = P.carve('AS', 'tsv', [128, 8, 16]); zz = P.carve('AS', 'zz', [128, 8, 4])
                wg = P.carve('AS', 'wg', [128, 8, 256])
                rmsnorm_tile(xtile(i), xbuf_(i), hbuf)
                psT = bank[6].t[:].bitcast(BF16)
                P.tr([(psT[:, kc * 128:(kc + 1) * 128], hbuf.t[:, kc * 128:(kc + 1) * 128]) for kc in range(8)], identb.t, r=[hbuf, identb], w=[bank[6]])
                P.cp(hTall.t[:, :, tokg], psT.rearrange("p (k t) -> p k t", k=8), r=[bank[6]], w=[hTall], eng='act')
                for hc in range(16):
                    bk = bank[hc % 2]
                    P.mm([(bk.t[:, 0:128], Wpq.t[:, kc, hc * 128:(hc + 1) * 128], hTall.t[:, kc, tokg], kc == 0, kc == 7) for kc in range(8)], r=[Wpq, hTall], w=[bk])
                    P.cp(qTt.t[:, hc, :], bk.t[:, 0:128], r=[bk], w=[qTt], eng=('act' if hc % 2 else 'dve'))
                for q4 in range(4):
                    bk = bank[2 + q4 % 2]
                    P.mm([(bk.t[:, c * 128:(c + 1) * 128], qTt.t[:, q4 * 4 + c, :], skT.t[:, q4 * 4 + c, :], True, True) for c in range(4)], r=[qTt, skT], w=[bk])
                    P.cp(sc.t[:, q4 * 4:(q4 + 1) * 4, :], bk.t[:, :].rearrange("p (c k) -> p c k", c=4), r=[bk], w=[sc], eng='act')
                for hc in range(16):
                    P.op('dve', lambda e: e.max(out=tops.t[:, hc, 0:8], in_=sc.t[:, hc, :]), r=[sc], w=[tops])
                    P.op('dve', lambda e: e.max_index(out=idx.t[:, hc, 0:8], in_max=tops.t[:, hc, 0:8], in_values=sc.t[:, hc, :]), r=[sc, tops], w=[idx])
                    P.op('dve', lambda e: e.match_replace(out=scr.t[:, hc, :], in_to_replace=tops.t[:, hc, 0:8], in_values=sc.t[:, hc, :], imm_value=-1e30),
                         r=[sc, tops], w=[scr])
                    P.op('dve', lambda e: e.max(out=tops.t[:, hc, 8:16], in_=scr.t[:, hc, :]), r=[scr], w=[tops])
                    P.op('dve', lambda e: e.max_index(out=idx.t[:, hc, 8:16], in_max=tops.t[:, hc, 8:16], in_values=scr.t[:, hc, :]), r=[scr, tops], w=[idx])
                tv = tops.t[:].rearrange("p (h c) i -> p h c i", c=2)
                iv = idx.t[:].rearrange("p (h c) i -> p h c i", c=2)
                for c in range(2):
                    P.cp(ixf.t[:, c, :].rearrange("p (h i) -> p h i", h=8), iv[:, :, c, :], r=[idx], w=[ixf])
                c4 = cand.t[:].rearrange("p h (i j) -> p h i j", i=16)
                for h in range(8):
                    P.tt(c4[:, h, :, :], tv[:, h, 0, :].unsqueeze(2).to_broadcast([128, 16, 16]), tv[:, h, 1, :].unsqueeze(1).to_broadcast([128, 16, 16]), ALU.add,
                         r=[tops], w=[cand])
                for h in range(8):
                    P.op('dve', lambda e: e.max(out=tsv.t[:, h, 0:8], in_=cand.t[:, h, :]), r=[cand], w=[tsv])
                    P.op('dve', lambda e: e.match_replace(out=cscr.t[:, h, :], in_to_replace=tsv.t[:, h, 0:8], in_values=cand.t[:, h, :], imm_value=-1e30),
                         r=[cand, tsv], w=[cscr])
                    P.op('dve', lambda e: e.max(out=tsv.t[:, h, 8:16], in_=cscr.t[:, h, :]), r=[cscr], w=[tsv])
                P.tt(cscr.t[:, :, 0:16], tsv.t[:, :, :], tsv.t[:, :, 0:1].to_broadcast([128, 8, 16]), ALU.subtract, r=[tsv], w=[cscr])
                P.act(cscr.t[:, :, 0:16], cscr.t[:, :, 0:16], AF.Exp, r=[cscr], w=[cscr])
                P.op('dve', lambda e: e.tensor_reduce(out=zz.t[:, :, 0], in_=cscr.t[:, :, 0:16], op=ALU.add, axis=AX.X), r=[cscr], w=[zz])
                P.op('dve', lambda e: e.reciprocal(out=zz.t[:, :, 1], in_=zz.t[:, :, 0]), r=[zz], w=[zz])
                P.tt(wg.t[:], cand.t[:], tsv.t[:, :, 0:1].to_broadcast([128, 8, 256]), ALU.subtract, r=[cand, tsv], w=[wg])
                P.act(wg.t[:].rearrange("p a b -> p (a b)"), wg.t[:].rearrange("p a b -> p (a b)"), AF.Exp, r=[wg], w=[wg])
                P.tt(cscr.t[:], cand.t[:], tsv.t[:, :, 15:16].to_broadcast([128, 8, 256]), ALU.is_ge, r=[cand, tsv], w=[cscr])
                P.tt(wg.t[:], wg.t[:], cscr.t[:], ALU.mult, r=[wg, cscr], w=[wg])
                P.tt(wg.t[:], wg.t[:], zz.t[:, :, 1:2].to_broadcast([128, 8, 256]), ALU.mult, r=[wg, zz], w=[wg])
                P.tr([(bank[4].t[:, c * 128:(c + 1) * 128], ixf.t[:, c, :]) for c in range(2)], identf.t, r=[ixf, identf], w=[bank[4]])
                P.cp(ixT.t[:].rearrange("p a b -> p (a b)"), bank[4].t[:, 0:256], r=[bank[4]], w=[ixT])
                w4 = wg.t[:].rearrange("p h (i j) -> p h i j", i=16)
                wcp = P.carve('AS', 'wcp', [128, 16, 128])
                for j in range(16):
                    P.cp(wcp.t[:, j, :].rearrange("p (h i) -> p h i", h=8), w4[:, :, :, j], r=[wg], w=[wcp], eng=('act' if j % 2 else 'dve'))
                for q4 in range(4):
                    bk = bank[q4 % 2]
                    P.tr([(bk.t[:, c * 128:(c + 1) * 128], wcp.t[:, q4 * 4 + c, :]) for c in range(4)], identf.t, r=[wcp, identf], w=[bk])
                    P.cp(WcT.t[:, q4 * 4:(q4 + 1) * 4, :], bk.t[:, :].rearrange("p (c t) -> p c t", c=4), r=[bk], w=[WcT], eng='act')
                P.reset('AS')
                Wtile = P.carve('AS', 'Wtile', [128, 128, 128], BF16)
                At = [P.carve('AS', 'At%d' % k, [128, 128], BF16) for k in range(2)]
                Bt = [P.carve('AS', 'Bt%d' % k, [128, 128], BF16) for k in range(2)]
                Wb_ = [P.carve('AS', 'Wbd%d' % k, [128, 8, 16], BF16) for k in range(2)]
                Yt = [P.carve('AS', 'Yt%d' % k, [128, 128], BF16) for k in range(2)]
                for t in range(128):
                    k = t % 2
                    P.ts(At[k].t[:, :], iotar.t[:, :], ixT.t[:, 0, t:t + 1], ALU.is_equal, r=[iotar, ixT], w=[At[k]])
                    P.ts(Bt[k].t[:, :], iotar.t[:, :], ixT.t[:, 1, t:t + 1], ALU.is_equal, r=[iotar, ixT], w=[Bt[k]])
                    P.tt(Wb_[k].t[:], WcT.t[:, :, t].unsqueeze(1).to_broadcast([128, 8, 16]), bd16.t[:], ALU.mult, r=[WcT, bd16], w=[Wb_[k]])
                    P.mm([(bank[2 + k].t[:, 0:128], Wb_[k].t[:].rearrange("p a b -> p (a b)"), At[k].t[:, :], True, True)], r=[Wb_[k], At[k]], w=[bank[2 + k]])
                    P.cp(Yt[k].t[:, :], bank[2 + k].t[:, 0:128], r=[bank[2 + k]], w=[Yt[k]], eng='act')
                    P.mm([(bank[4 + k].t[:, 0:128], Bt[k].t[:, :], Yt[k].t[:, :], True, True)], r=[Bt[k], Yt[k]], w=[bank[4 + k]])
                    P.cp(Wtile.t[:, :, t], bank[4 + k].t[:, 0:128], r=[bank[4 + k]], w=[Wtile], eng=('act' if t % 4 == 3 else 'dve'))
                P.dma('sp', [(Wd[i, :, :], Wtile.t[:].rearrange("p a b -> p (a b)"))], r=[Wtile], w=[Wdb], sembuf=Wdb)
            P.reset('AL'); P.reset('AS')
            hTall = P.carve('AL', 'hTall', [128, 8, T + 128], BF16)
            KB = 8
            UT = P.carve('AL', 'UT', [128, 8, KB * 128], BF16)
            Vb = P.carve('AL', 'Vb', [128, KB, D], BF16)
            Ub = [P.carve('AS', 'Ub%d' % k, [128, D], BF16) for k in range(2)]
            WTb = [P.carve('AS', 'WTb%d' % k, [128, KB, 128], BF16) for k in range(2)]
            gl = [P.carve('AS', 'gl%d' % k, [128, 128]) for k in range(2)]
            WGb = [P.carve('AS', 'WGb%d' % k, [128, 128], BF16) for k in range(2)]
            for kb in range(128 // KB):
                for kk_ in range(KB):
                    k1 = kb * KB + kk_
                    ub = Ub[kk_ % 2]
                    P.dma('pool', [(ub.t[:, :], I['peer_u'][l, k1 * 128:(k1 + 1) * 128, :])], w=[ub])
                    P.dma('pool', [(Vb.t[:, kk_, :], I['peer_v'][l, k1 * 128:(k1 + 1) * 128, :])], w=[Vb])
                    psT = bank[6].t[:].bitcast(BF16)
                    P.tr([(psT[:, dc * 128:(dc + 1) * 128], ub.t[:, dc * 128:(dc + 1) * 128]) for dc in range(8)], identb.t, r=[ub, identb], w=[bank[6]])
                    P.cp(UT.t[:, :, kk_ * 128:(kk_ + 1) * 128], psT.rearrange("p (k t) -> p k t", k=8), r=[bank[6]], w=[UT], eng=('act' if kk_ % 2 else 'dve'))
                for i in range(NTP):
                    tokg = slice(i * 128, (i + 1) * 128)
                    wtb = WTb[i % 2]
                    P.dma('sp', [(wtb.t[:].rearrange("p a b -> p (a b)"), Wd[i, :, kb * KB * 128:(kb + 1) * KB * 128])], r=[Wdb], w=[wtb])
                    for kk_ in range(KB):
                        k = kk_ % 2
                        P.mm([(bank[k].t[:, 0:128], UT.t[:, dc, kk_ * 128:(kk_ + 1) * 128], hTall.t[:, dc, tokg], dc == 0, dc == 7) for dc in range(8)],
                             r=[UT, hTall], w=[bank[k]])
                        P.act(gl[k].t[:, :], bank[k].t[:, 0:128], AF.Gelu_apprx_tanh, r=[bank[k]], w=[gl[k]])
                        P.tt(WGb[k].t[:, :], gl[k].t[:, :], wtb.t[:, kk_, :], ALU.mult, r=[gl[k], wtb], w=[WGb[k]])
                        P.mm([(bank[2 + half].t[:, :], WGb[k].t[:, :], Vb.t[:, kk_, half * 512:(half + 1) * 512], kk_ == 0, kk_ == KB - 1) for half in range(2)],
                             r=[WGb[k], Vb], w=[bank[2], bank[3]])
                    for half in range(2):
                        hs_ = slice(half * 512, (half + 1) * 512)
                        P.tt(xtile(i)[:, hs_], xtile(i)[:, hs_], bank[2 + half].t[:, :], ALU.add, r=[xbuf_(i), bank[2 + half]], w=[xbuf_(i)])
            if 'xs3' in DBG and l == 0:
                dbg('xs3', DBG['xs3'], xsp.t[0:4, :], [xsp])
            if 'x3' in DBG and l == 0:
                dbg('x3', DBG['x3'].rearrange("(n p) c -> p n c", p=128), xres.t[:, :, :], [xres])
        if stage != 'all':
            break
    if stage == 'all':
        P.reset('AS')
        ost = [P.carve('AS', 'ost%d' % i, [128, D]) for i in range(2)]
        P.dma('sp', [(gbc.t[:], I['final_norm'][0:1, :].to_broadcast([128, D]))], w=[gbc])
        for i in range(NT):
            ob = ost[i % 2]
            rmsnorm_tile(xres.t[:, i, :], xres, ob)
            P.dma('sp', [(O['y_p'][i * 128:(i + 1) * 128, :], ob.t[:, :])], r=[ob], sembuf=ob, is_out=True)
        if WITH_S:
            rmsnorm_tile(xsp.t[0:4, :], xsp, ost[0], npart=4)
            P.dma('sp', [(O['y_s'][:, :], ost[0].t[0:4, :])], r=[ost[0]], sembuf=ost[0], is_out=True)
    P.finish()
    return P, I, O, DBG

_ROPE_CACHE = {}
def _consts():
    c = {}
    c['c_ident'] = np.eye(128, dtype=np.float32)
    half = 32
    freq = (1.0 / (np.float32(10000.0) ** (np.arange(half, dtype=np.float32) / np.float32(half)))).astype(np.float32)
    def tab(pos):
        ang = pos.astype(np.float32)[:, None] * freq[None, :]
        return np.concatenate([np.cos(ang), np.sin(ang)], axis=1).astype(np.float32)
    c['c_rope_p'] = tab(np.arange(T))
    c['c_rope_s'] = tab(np.full((NS,), 16384))
    p = np.arange(128)
    c['c_causal'] = np.where(p[None, :] >= p[:, None], 0.0, NEG).astype(np.float32)
    c['c_tri'] = (p[:, None] <= p[None, :]).astype(np.float32)
    c['c_negmask'] = np.where(p[None, :] >= p[:, None], 0.0, NEG).astype(np.float32)
    cur = np.arange(8)[:, None]; n = np.arange(8)[None, :]
    past = np.where(n < cur, 0.0, -1e30).astype(np.float32).reshape(1, 64)
    pm = np.where(n < cur, 30000.0, 0.0).astype(np.float32).reshape(1, 64)
    c['c_past'] = np.repeat(past, 128, axis=0)
    c['c_pm'] = np.repeat(pm, 128, axis=0)
    c['c_iota'] = np.repeat(np.arange(128, dtype=np.float32)[None, :], 128, axis=0)
    pp = np.arange(128)
    c['c_bd16'] = (pp[:, None] // 16 == pp[None, :] // 16).astype(np.float32)
    c['c_pidx'] = pp[:, None].astype(np.float32)
    z = np.zeros((128, 127), np.float32); z[:, 63] = 1.0
    c['c_zsel'] = z
    c['c_dmask4'] = (np.arange(4)[:, None] == (np.arange(256)[None, :] // 64)).astype(np.float32)
    c['c_dmask4x'] = (np.arange(4)[:, None] == (np.arange(1024)[None, :] // 256)).astype(np.float32)
    return c


def _core_inputs(inp, c, consts, with_kv=True):
    m = {}
    A = np.ascontiguousarray
    s4 = slice(NS * c, NS * (c + 1))
    m['xp'] = A(inp['x_prompt'][c]); m['xs'] = A(inp['x_sample'][s4, 0]); m['memp'] = A(inp['mem_prompt'][c])
    m['pt'] = A(inp['page_table'][s4].astype(np.int32))
    m['st_lru_h'] = A(inp['state_lru_h'][:, s4]); m['st_lru_conv'] = A(inp['state_lru_conv'][:, s4])
    m['st_ssd'] = A(inp['state_ssd'][:, s4]); m['st_ssd_conv'] = A(inp['state_ssd_conv'][:, s4])
    m['st_rwkv'] = A(inp['state_rwkv'][:, s4]); m['st_rwkv_shift'] = A(inp['state_rwkv_shift'][:, s4])
    m['cmk'] = A(inp['cache_mem_k'][:, s4].reshape(DEPTH, NS, 256, D)); m['cmv'] = A(inp['cache_mem_v'][:, s4].reshape(DEPTH, NS, 256, D))
    if with_kv:
        m['ck'] = inp['cache_moba_k'].reshape(DEPTH * 5120 * 128, 256)
        m['cv'] = inp['cache_moba_v'].reshape(DEPTH * 5120 * 128, 256)
    for k in ['norm_mix', 'w_in', 'w_out', 'lru_conv_w', 'lru_conv_b', 'lru_wa', 'lru_ba', 'lru_wx', 'lru_bx', 'lru_lambda',
              'ssd_conv_w', 'ssd_conv_b', 'ssd_dt_bias', 'ssd_a_log', 'ssd_d', 'ssd_norm', 'rwkv_mu', 'rwkv_w0', 'rwkv_w_up',
              'rwkv_a0', 'rwkv_a_up', 'rwkv_g_up', 'rwkv_k_k', 'rwkv_k_a', 'rwkv_ln_w', 'rwkv_ln_b', 'norm_x', 'x_wq', 'x_wk',
              'x_wv', 'x_wo', 'norm_ffn', 'peer_wq', 'peer_u', 'peer_v']:
        m[k] = inp[k]
    m['rwkv_r_k'] = inp['rwkv_r_k'].reshape(DEPTH, 256)
    m['peer_subkeys'] = inp['peer_subkeys'].reshape(DEPTH, 16, 128, 128)
    m['final_norm'] = inp['final_norm'].reshape(1, D)
    m.update(consts)
    return {k: np.asarray(v) for k, v in m.items()}


_PROG = {}
def kernel(**inputs):
    inp = {k: np.asarray(v) for k, v in inputs.items()}
    if 'p' not in _PROG:
        _PROG['p'] = build({'stage': 'all', 'with_kv': True})
    P, I, O, DBG = _PROG['p']
    consts = _consts()
    in_maps = [_core_inputs(inp, c, consts, with_kv=True) for c in range(NCORES)]
    res = run_bass_kernel_spmd(P.nc, in_maps, core_ids=list(range(NCORES)))
    R = res.results
    def st(name, axis, shape_tail=None):
        a = np.stack([np.asarray(R[c][name]) for c in range(NCORES)], axis=axis)
        return a
    f32 = np.float32
    y_p = st('y_p', 0).astype(f32)
    y_s = np.concatenate([R[c]['y_s'] for c in range(NCORES)], 0).reshape(32, 1, D).astype(f32)
    k_p = st('k_p', 1).reshape(DEPTH, NCORES, T, 4, 64).astype(f32)
    v_p = st('v_p', 1).reshape(DEPTH, NCORES, T, 4, 64).astype(f32)
    lru_h_p = st('lru_h_p', 1).astype(f32)
    lru_conv_p = st('lru_conv_p', 1).astype(f32)
    ssd_p = st('ssd_p', 1).astype(f32)
    ssd_conv_p = st('ssd_conv_p', 1).astype(f32)
    rwkv_p = st('rwkv_p', 1).astype(f32)
    rwkv_shift_p = st('rwkv_shift_p', 1).astype(f32)
    mem_k_p = st('mem_k_p', 1).reshape(DEPTH, NCORES, 256, 4, 256).astype(f32)
    mem_v_p = st('mem_v_p', 1).reshape(DEPTH, NCORES, 256, 4, 256).astype(f32)
    def cat(name, tail):
        return np.concatenate([np.asarray(R[c][name]) for c in range(NCORES)], axis=1).reshape((DEPTH, 32) + tail).astype(f32)
    k_s = cat('k_s', (1, 4, 64)); v_s = cat('v_s', (1, 4, 64))
    lru_h_s = cat('lru_h_s', (256,)); lru_conv_s = cat('lru_conv_s', (3, 256))
    ssd_s = cat('ssd_s', (4, 64, 64)); ssd_conv_s = cat('ssd_conv_s', (3, 512))
    rwkv_s = cat('rwkv_s', (4, 64, 64)); rwkv_shift_s = cat('rwkv_shift_s', (896,))
    return (y_p, y_s, k_p, v_p, lru_h_p, lru_conv_p, ssd_p, ssd_conv_p, rwkv_p, rwkv_shift_p, mem_k_p, mem_v_p,
            k_s, v_s, lru_h_s, lru_conv_s, ssd_s, ssd_conv_s, rwkv_s, rwkv_shift_s)
```

```python
import numpy as np
import ml_dtypes
import concourse.bass as bass
import concourse.mybir as mybir
from concourse.bass_utils import run_bass_kernel_spmd

F32 = mybir.dt.float32
BF16 = mybir.dt.bfloat16
I32 = mybir.dt.int32
U32 = mybir.dt.uint32
AF = mybir.ActivationFunctionType
ALU = mybir.AluOpType
AX = mybir.AxisListType

NCORES = 8
D = 1024
T = 2048
NT = 16
TB = 512
NTB = 4
NS = 4
DEPTH = 2
INW = 2948
NEG = -30000.0


class Buf:
    def __init__(self, name, t=None):
        self.name = name
        self.t = t
        self.w = None
        self.rs = {}
        self.dsem = None
        self.dcnt = 0
        self.excl = False

    def __getitem__(self, k):
        return self.t[k]


class Prog:
    def __init__(self):
        self.nc = bass.Bass("TRN2", target_bir_lowering=False)
        nc = self.nc
        self.E = {'pe': nc.tensor, 'dve': nc.vector, 'act': nc.scalar, 'pool': nc.gpsimd, 'sp': nc.sync}
        self.sem = {e: nc.alloc_semaphore("sem_" + e) for e in self.E}
        self.cnt = {e: 0 for e in self.E}
        self.seen = {e: {} for e in self.E}
        self.out_events = []
        self.nsem = 5
        self.ninst = 0
        self.dbufs = []
        self.nops = 0
        self.stop_at = None
        self.oplog = None
        self.arenas = {}
        self.dsems = {}

    def sb(self, name, shape, dt=F32):
        return Buf(name, self.nc.alloc_sbuf_tensor(name, list(shape), dt))

    def arena(self, name, nbytes):
        a = self.nc.alloc_sbuf_tensor(name, [128, nbytes // 4], F32)
        self.arenas[name] = [a, 0, nbytes]

    def carve(self, an, name, shape, dt=F32):
        a = self.arenas[an]
        esz = 4 if dt in (F32, I32, U32) else 2
        n = int(np.prod(shape[1:]))
        nb = (n * esz + 3) // 4 * 4
        assert a[1] + nb <= a[2], "arena %s overflow carving %s (%d + %d > %d)" % (an, name, a[1], nb, a[2])
        ap = a[0][0:shape[0], a[1] // 4:(a[1] + nb) // 4]
        if esz == 2:
            ap = ap.bitcast(dt)[:, 0:n]
        elif dt != F32:
            ap = ap.bitcast(dt)
        if len(shape) == 3:
            ap = ap.rearrange("p (a b) -> p a b", a=shape[1])
        elif len(shape) == 4:
            ap = ap.rearrange("p (a b c) -> p a b c", a=shape[1], b=shape[2])
        a[1] += nb
        return Buf(name, ap)

    def reset(self, an):
        self.barrier()
        self.arenas[an][1] = 0

    def barrier(self):
        for e in self.E:
            for e2 in self.E:
                if e2 != e and e2 != 'sp' and self.cnt[e2] > 0:
                    self._wait(e, (self.sem[e2], self.cnt[e2], 'bar'))
            for b in self.dbufs:
                self._wait(e, (b.dsem, b.dcnt, 'dma'))

    def ps(self, name, shape, dt=F32):
        b = Buf(name, self.nc.alloc_psum_tensor(name, list(shape), dt))
        b.excl = True
        return b

    def din(self, name, shape, dt=F32):
        return self.nc.dram_tensor(name, list(shape), dt, kind="ExternalInput").ap()

    def dout(self, name, shape, dt=F32):
        return self.nc.dram_tensor(name, list(shape), dt, kind="ExternalOutput").ap()

    def _wait(self, eng, ev):
        if ev is None:
            return
        sem, val, src = ev
        if eng == 'pe' and src == 'pe':
            return
        k = id(sem)
        if self.seen[eng].get(k, 0) >= val:
            return
        self.E[eng].wait_ge(sem, val)
        self.seen[eng][k] = val
        self.ninst += 1

    def _deps(self, eng, r, w):
        for b in r:
            self._wait(eng, b.w)
            if b.excl:
                for ev in list(b.rs.values()):
                    self._wait(eng, ev)
        for b in w:
            self._wait(eng, b.w)
            for ev in list(b.rs.values()):
                self._wait(eng, ev)

    def _commit(self, ev, r, w):
        for b in r:
            b.rs[id(ev[0])] = ev
        for b in w:
            b.w = ev
            b.rs = {}

    def op(self, eng, fn, r=(), w=()):
        self.nops += 1
        if self.oplog is not None:
            import sys as _s
            f = _s._getframe(1)
            while f.f_code.co_name not in ('build',) and f.f_back is not None:
                f = f.f_back
            self.oplog.append((self.nops, eng, f.f_lineno))
        if self.stop_at is not None and self.nops > self.stop_at:
            return
        self._deps(eng, r, w)
        ins = fn(self.E[eng])
        if isinstance(ins, (list, tuple)):
            self.ninst += len(ins)
            ins = ins[-1]
        else:
            self.ninst += 1
        self.cnt[eng] += 1
        ins.then_inc(self.sem[eng], 1)
        ev = (self.sem[eng], self.cnt[eng], eng)
        self._commit(ev, r, w)

    def dma(self, q, pairs, r=(), w=(), sembuf=None, is_out=False, **kw):
        self.nops += 1
        if self.stop_at is not None and self.nops > self.stop_at:
            return
        self._deps(q, r, w)
        sb0 = sembuf if sembuf is not None else (w[0] if len(w) else r[0])
        if sb0.name not in self.dsems:
            self.dsems[sb0.name] = Buf("ds_" + sb0.name)
            self.dsems[sb0.name].dsem = self.nc.alloc_semaphore("ds_" + sb0.name)
            self.nsem += 1
            self.dbufs.append(self.dsems[sb0.name])
        sbuf = self.dsems[sb0.name]
        for (o, i) in pairs:
            self.E[q].dma_start(out=o, in_=i, **kw).then_inc(sbuf.dsem, 16)
            sbuf.dcnt += 16
            self.ninst += 1
        ev = (sbuf.dsem, sbuf.dcnt, 'dma')
        self._commit(ev, r, w)
        if is_out:
            self.out_events.append(ev)

    def dma_custom(self, q, fn, r=(), w=(), sembuf=None, is_out=False):
        self.nops += 1
        if self.stop_at is not None and self.nops > self.stop_at:
            return
        self._deps(q, r, w)
        sb0 = sembuf if sembuf is not None else (w[0] if len(w) else r[0])
        if sb0.name not in self.dsems:
            self.dsems[sb0.name] = Buf("ds_" + sb0.name)
            self.dsems[sb0.name].dsem = self.nc.alloc_semaphore("ds_" + sb0.name)
            self.nsem += 1
            self.dbufs.append(self.dsems[sb0.name])
        sbuf = self.dsems[sb0.name]
        fn(self.E[q]).then_inc(sbuf.dsem, 16)
        sbuf.dcnt += 16
        self.ninst += 1
        ev = (sbuf.dsem, sbuf.dcnt, 'dma')
        self._commit(ev, r, w)
        if is_out:
            self.out_events.append(ev)

    def finish(self):
        last = {}
        for ev in self.out_events:
            k = id(ev[0])
            if k not in last or last[k][1] < ev[1]:
                last[k] = ev
        for ev in last.values():
            self.E['sp'].wait_ge(ev[0], ev[1])
        for b in self.dbufs:
            self.E['sp'].wait_ge(b.dsem, b.dcnt)
        for e in self.E:
            if e != 'sp' and self.cnt[e] > 0:
                self.E['sp'].wait_ge(self.sem[e], self.cnt[e])

    def mm(self, groups, r=(), w=()):
        def fn(e):
            return [e.matmul(o, lhsT=l, rhs=rr, start=st, stop=sp) for (o, l, rr, st, sp) in groups]
        self.op('pe', fn, r, w)

    def tr(self, items, ident, r=(), w=()):
        def fn(e):
            return [e.transpose(o, i, ident[0:i.shape[0], 0:i.shape[0]]) for (o, i) in items]
        self.op('pe', fn, r, w)

    def act(self, out, in_, func, r=(), w=(), **kw):
        self.op('act', lambda e: e.activation(out=out, in_=in_, func=func, **kw), r, w)

    def tt(self, out, in0, in1, op, r=(), w=(), eng='dve'):
        self.op(eng, lambda e: e.tensor_tensor(out=out, in0=in0, in1=in1, op=op), r, w)

    def ts(self, out, in0, s1, op0, s2=None, op1=None, r=(), w=(), eng='dve', **kw):
        if op1 is None:
            self.op(eng, lambda e: e.tensor_scalar(out=out, in0=in0, scalar1=s1, scalar2=None, op0=op0, **kw), r, w)
        else:
            self.op(eng, lambda e: e.tensor_scalar(out=out, in0=in0, scalar1=s1, scalar2=s2, op0=op0, op1=op1, **kw), r, w)

    def stt(self, out, in0, scalar, in1, op0, op1, r=(), w=()):
        self.op('dve', lambda e: e.scalar_tensor_tensor(out=out, in0=in0, scalar=scalar, in1=in1, op0=op0, op1=op1), r, w)

    def cp(self, out, in_, r=(), w=(), eng='dve'):
        if eng == 'act':
            self.op('act', lambda e: e.copy(out=out, in_=in_), r, w)
        else:
            self.op(eng, lambda e: e.tensor_copy(out=out, in_=in_), r, w)

    def memset(self, ap, val, w=(), eng='dve'):
        self.op(eng, lambda e: e.memset(ap, val), (), w)


PM_ROWS = {}
def _pm_layout():
    names = ['lru_cw0', 'lru_cw1', 'lru_cw2', 'lru_cw3', 'lru_cb', 'lru_ba', 'lru_bx', 'lru_lam',
             'ssd_cw0a', 'ssd_cw1a', 'ssd_cw2a', 'ssd_cw3a', 'ssd_cba', 'ssd_cw0b', 'ssd_cw1b', 'ssd_cw2b', 'ssd_cw3b', 'ssd_cbb',
             'ssd_norm', 'rw_w0', 'rw_a0', 'rw_kk', 'rw_ka', 'rw_lnw', 'rw_lnb', 'rw_rk', 'rw_mu_r', 'rw_mu_k', 'rw_mu_v', 'rw_mu_x']
    for i, n in enumerate(names):
        PM_ROWS[n] = i
_pm_layout()
NPM = 32


def cand_alias(orig, view):
    class _Shared(Buf):
        pass
    view.__class__ = _AliasBuf
    view._o = orig
    return orig


class _AliasBuf(Buf):
    @property
    def w(self):
        return self._o.w
    @w.setter
    def w(self, v):
        if '_o' in self.__dict__:
            self._o.w = v
    @property
    def rs(self):
        return self._o.rs
    @rs.setter
    def rs(self, v):
        if '_o' in self.__dict__:
            self._o.rs = v


def build(dev=None):
    dev = dev or {}
    stage = dev.get('stage', 'all')
    P = Prog()
    P.stop_at = dev.get('stop_at')
    P.oplog = [] if dev.get('oplog') else None
    nc = P.nc
    I = {}
    def inp(name, shape, dt=F32):
        I[name] = P.din(name, shape, dt)
    inp('xp', [T, D]); inp('xs', [NS, D]); inp('memp', [256, D])
    inp('pt', [NS, 128], I32)
    inp('st_lru_h', [DEPTH, NS, 256]); inp('st_lru_conv', [DEPTH, NS, 3, 256])
    inp('st_ssd', [DEPTH, NS, 4, 64, 64]); inp('st_ssd_conv', [DEPTH, NS, 3, 512])
    inp('st_rwkv', [DEPTH, NS, 4, 64, 64]); inp('st_rwkv_shift', [DEPTH, NS, 896])
    inp('cmk', [DEPTH, NS, 256, D]); inp('cmv', [DEPTH, NS, 256, D])
    if dev.get('with_kv', True):
        inp('ck', [DEPTH * 5120 * 128, 256]); inp('cv', [DEPTH * 5120 * 128, 256])
    wshapes = {'norm_mix': [DEPTH, D], 'w_in': [DEPTH, D, INW], 'w_out': [DEPTH, D, D],
               'lru_conv_w': [DEPTH, 4, 256], 'lru_conv_b': [DEPTH, 256], 'lru_wa': [DEPTH, 4, 64, 64], 'lru_ba': [DEPTH, 256],
               'lru_wx': [DEPTH, 4, 64, 64], 'lru_bx': [DEPTH, 256], 'lru_lambda': [DEPTH, 256],
               'ssd_conv_w': [DEPTH, 4, 512], 'ssd_conv_b': [DEPTH, 512], 'ssd_dt_bias': [DEPTH, 4], 'ssd_a_log': [DEPTH, 4],
               'ssd_d': [DEPTH, 4], 'ssd_norm': [DEPTH, 256],
               'rwkv_mu': [DEPTH, 896], 'rwkv_w0': [DEPTH, 256], 'rwkv_w_up': [DEPTH, 32, 256], 'rwkv_a0': [DEPTH, 256],
               'rwkv_a_up': [DEPTH, 32, 256], 'rwkv_g_up': [DEPTH, 64, 256], 'rwkv_k_k': [DEPTH, 256], 'rwkv_k_a': [DEPTH, 256],
               'rwkv_r_k': [DEPTH, 256], 'rwkv_ln_w': [DEPTH, 256], 'rwkv_ln_b': [DEPTH, 256],
               'norm_x': [DEPTH, D], 'x_wq': [DEPTH, D, D], 'x_wk': [DEPTH, D, D], 'x_wv': [DEPTH, D, D], 'x_wo': [DEPTH, D, D],
               'norm_ffn': [DEPTH, D], 'peer_wq': [DEPTH, D, 2048], 'peer_subkeys': [DEPTH, 16, 128, 128],
               'peer_u': [DEPTH, 16384, D], 'peer_v': [DEPTH, 16384, D], 'final_norm': [1, D]}
    for k, s in wshapes.items():
        inp(k, s)
    inp('c_ident', [128, 128]); inp('c_rope_p', [T, 64]); inp('c_rope_s', [NS, 64])
    inp('c_causal', [128, 128]); inp('c_tri', [128, 128]); inp('c_negmask', [128, 128])
    inp('c_past', [128, 64]); inp('c_pm', [128, 64]); inp('c_iota', [128, 128]); inp('c_bd16', [128, 128])
    inp('c_pidx', [128, 1]); inp('c_zsel', [128, 127]); inp('c_dmask4', [4, 256]); inp('c_dmask4x', [4, D])
    O = {}
    def outp(name, shape):
        O[name] = P.dout(name, shape)
    outp('y_p', [T, D]); outp('y_s', [NS, D]); outp('k_p', [DEPTH, T, 256]); outp('v_p', [DEPTH, T, 256])
    outp('lru_h_p', [DEPTH, 256]); outp('lru_conv_p', [DEPTH, 3, 256]); outp('ssd_p', [DEPTH, 4, 64, 64])
    outp('ssd_conv_p', [DEPTH, 3, 512]); outp('rwkv_p', [DEPTH, 4, 64, 64]); outp('rwkv_shift_p', [DEPTH, 896])
    outp('mem_k_p', [DEPTH, 256, D]); outp('mem_v_p', [DEPTH, 256, D])
    outp('k_s', [DEPTH, NS, 256]); outp('v_s', [DEPTH, NS, 256]); outp('lru_h_s', [DEPTH, NS, 256])
    outp('lru_conv_s', [DEPTH, NS, 3, 256]); outp('ssd_s', [DEPTH, NS, 4, 64, 64]); outp('ssd_conv_s', [DEPTH, NS, 3, 512])
    outp('rwkv_s', [DEPTH, NS, 4, 64, 64]); outp('rwkv_shift_s', [DEPTH, NS, 896])
    dbg_specs = dev.get('dbg', {})
    DBG = {n: P.dout('dbg_' + n, s[0], s[1]) for n, s in dbg_specs.items()}

    Wd = nc.dram_tensor('Wd_scratch', [NT + 1, 128, 128 * 128], BF16, kind='Internal').ap()
    Wdb = Buf('Wdb')
    xres = P.sb('xres', [128, NT, D])
    bank = [P.ps('bank%d' % i, [128, 512]) for i in range(8)]
    identf = P.sb('identf', [128, 128]); identb = P.sb('identb', [128, 128], BF16)
    onesf = P.sb('onesf', [128, 128])
    causal = P.sb('causal', [128, 128], BF16)
    tri = P.sb('tri', [128, 128]); negmask = P.sb('negmask', [128, 128])
    cpast = P.sb('cpast', [128, 8, 8]); cpm = P.sb('cpm', [128, 8, 8])
    rope = P.sb('rope', [128, NT, 64])
    gbc = P.sb('gbc', [128, D])
    PM = P.sb('PM', [NPM, 256]); PT = P.sb('PT', [128, 2, NPM])
    small = P.sb('small', [128, 64])
    epsb = P.sb('epsb', [128, 4])
    BDones = P.sb('BDones', [128, 128]); hmask = P.sb('hmask', [128, 2])
    xsp = P.sb('xsp', [128, D]); hTs = P.sb('hTs', [128, 8, 4], BF16); ymTs = P.sb('ymTs', [128, 8, 4], BF16)
    ropes = P.sb('ropes', [4, 64]); PIf = P.sb('PIf', [128, 512])
    zsel = P.sb('zsel', [128, 127]); dmask4 = P.sb('dmask4', [4, 256]); pidx = P.sb('pidx', [128, 1])
    WITH_S = dev.get('with_s', dev.get('with_kv', True))
    WITH_KV = dev.get('with_kv', True)
    P.arena('AL', 78 * 1024)
    P.arena('AS', 44 * 1024)

    def dbg(name, dst_ap, src_ap, r):
        P.dma('sp', [(dst_ap, src_ap)], r=r, sembuf=Buf('dbg_' + name), is_out=True)

    P.dma('sp', [(identf.t[:], I['c_ident'][:, :])], w=[identf])
    P.cp(identb.t[:], identf.t[:], r=[identf], w=[identb])
    P.memset(onesf.t[:], 1.0, w=[onesf])
    P.dma('pool', [(causal.t[:], I['c_causal'][:, :])], w=[causal])
    P.dma('sp', [(tri.t[:], I['c_tri'][:, :]), (negmask.t[:], I['c_negmask'][:, :])], w=[tri, negmask], sembuf=tri)
    P.dma('sp', [(cpast.t[:].rearrange("p a b -> p (a b)"), I['c_past'][:, :]), (cpm.t[:].rearrange("p a b -> p (a b)"), I['c_pm'][:, :])],
          w=[cpast, cpm], sembuf=cpast)
    P.dma('sp', [(rope.t[:], I['c_rope_p'].rearrange("(n p) c -> p n c", p=128))], w=[rope])
    P.dma('sp', [(xres.t[:, i, :], I['xp'][i * 128:(i + 1) * 128, :]) for i in range(NT)], w=[xres])
    P.memset(xsp.t[:], 0.0, w=[xsp])
    P.memset(ymTs.t[:].rearrange("p a b -> p (a b)"), 0.0, w=[ymTs])
    P.dma('sp', [(xsp.t[4:128, :], I['xp'][4:128, :])], w=[xsp])
    P.dma('sp', [(xsp.t[0:4, :], I['xs'][:, :]), (ropes.t[:, :], I['c_rope_s'][:, :]), (zsel.t[:, :], I['c_zsel'][:, :]),
                 (dmask4.t[:, :], I['c_dmask4'][:, :]), (pidx.t[:, :], I['c_pidx'][:, :])], w=[xsp, ropes, zsel, dmask4, pidx], sembuf=xsp)
    PIu = P.carve('AS', 'PIu', [128, 512], U32)
    P.dma('sp', [(PIu.t[:, :].bitcast(I32), I['pt'].rearrange("s g -> (s g)").unsqueeze(0).to_broadcast([128, 512]))], w=[PIu])
    P.cp(PIf.t[:, :], PIu.t[:, :].bitcast(I32), r=[PIu], w=[PIf])
    P.ts(PIf.t[:, :], PIf.t[:, :], 128.0, ALU.mult, pidx.t[:, 0:1], ALU.add, r=[PIf, pidx], w=[PIf])
    P.memset(epsb.t[:, 0:1], 1e-6, w=[epsb])
    P.memset(epsb.t[:, 1:2], 64e-5, w=[epsb])
    P.memset(BDones.t[:], 0.0, w=[BDones]); P.memset(hmask.t[:], 0.0, w=[hmask])
    for h2 in range(2):
        pr = slice(h2 * 64, (h2 + 1) * 64)
        P.memset(BDones.t[pr, h2 * 64:(h2 + 1) * 64], 1.0, w=[BDones])
        P.memset(hmask.t[pr, h2:h2 + 1], 1.0, w=[hmask])

    def rmsnorm_tile(xt_ap, xbuf, hbuf, npart=128, width=D, gain=None, gbuf_=None, eps_col=0):
        gain = gbc.t[0:npart, 0:width] if gain is None else gain
        gbuf_ = gbc if gbuf_ is None else gbuf_
        P.act(hbuf.t[0:npart, 0:width], xt_ap, AF.Square, r=[xbuf], w=[hbuf, small], accum_out=small.t[0:npart, 0:1])
        P.act(small.t[0:npart, 1:2], small.t[0:npart, 0:1], AF.Sqrt, r=[small, epsb], w=[small], scale=1.0 / width, bias=epsb.t[0:npart, eps_col:eps_col + 1])
        P.op('dve', lambda e: e.reciprocal(out=small.t[0:npart, 2:3], in_=small.t[0:npart, 1:2]), r=[small], w=[small])
        P.stt(hbuf.t[0:npart, 0:width], xt_ap, small.t[0:npart, 2:3], gain, ALU.mult, ALU.mult, r=[xbuf, small, gbuf_], w=[hbuf])

    if stage == 'const':
        P.finish()
        return P, I, O, DBG

    for l in range(DEPTH):
        P.reset('AL'); P.reset('AS')
        hT = P.carve('AL', 'hT', [128, 8, TB], BF16)
        ymT = P.carve('AL', 'ymT', [128, 8, TB], BF16)
        WA = P.carve('AL', 'WA', [128, 8, D], BF16)
        KT = P.carve('AL', 'KT', [128, 2, T], BF16)
        Vaug = P.carve('AL', 'Vaug', [128, NT, 4, 65], BF16)
        kmT = P.carve('AL', 'kmT', [128, 2, 2, 8]); kmacc = P.carve('AL', 'kmacc', [128, 2])
        ubuf = [P.carve('AL', 'ubuf%d' % i, [128, 3 + TB]) for i in range(2)]
        hstate = P.carve('AL', 'hstate', [128, 2])
        BDa = [P.carve('AL', 'BDa%d' % i, [128, 128]) for i in range(2)]
        BDx = [P.carve('AL', 'BDx%d' % i, [128, 128]) for i in range(2)]
        lruc = P.carve('AL', 'lruc', [128, 2, 4])
        cbuf = [P.carve('AL', 'cbuf%d' % i, [128, 3 + TB]) for i in range(4)]
        hp4 = P.carve('AL', 'hp4', [128, 12])
        sT = P.carve('AL', 'sT', [128, 4, 64])
        nwbc = P.carve('AL', 'nwbc', [128, 256])
        cb7 = [P.carve('AL', 'cb7_%d' % i, [128, 129]) for i in range(7)]
        Wlow = P.carve('AL', 'Wlow', [128, 256])
        ST = P.carve('AL', 'ST', [128, 2, 64])
        rwc = P.carve('AL', 'rwc', [128, 8])

        def load_w(src, c0, c1):
            for kc in range(8):
                P.dma('pool', [(WA.t[:, kc, 0:c1 - c0], src[l, kc * 128:(kc + 1) * 128, c0:c1])], w=[WA])

        P.memset(Vaug.t[:].rearrange("p a b c -> p (a b c)"), 1.0, w=[Vaug])
        P.memset(ymT.t[:].rearrange("p a b -> p (a b)"), 0.0, w=[ymT])
        P.dma('sp', [(gbc.t[:], I['norm_mix'][l:l + 1, :].to_broadcast([128, D]))], w=[gbc])
        P.memset(PM.t[:], 0.0, w=[PM])
        rows = []
        def prow(name, src):
            rows.append((PM.t[PM_ROWS[name]:PM_ROWS[name] + 1, 0:src.shape[-1]], src))
        for j in range(4):
            prow('lru_cw%d' % j, I['lru_conv_w'][l, j:j + 1, :])
            prow('ssd_cw%da' % j, I['ssd_conv_w'][l, j:j + 1, 0:256])
            prow('ssd_cw%db' % j, I['ssd_conv_w'][l, j:j + 1, 256:512])
        prow('lru_cb', I['lru_conv_b'][l:l + 1, :]); prow('lru_ba', I['lru_ba'][l:l + 1, :]); prow('lru_bx', I['lru_bx'][l:l + 1, :])
        prow('lru_lam', I['lru_lambda'][l:l + 1, :])
        prow('ssd_cba', I['ssd_conv_b'][l:l + 1, 0:256]); prow('ssd_cbb', I['ssd_conv_b'][l:l + 1, 256:512])
        prow('ssd_norm', I['ssd_norm'][l:l + 1, :])
        prow('rw_w0', I['rwkv_w0'][l:l + 1, :]); prow('rw_a0', I['rwkv_a0'][l:l + 1, :]); prow('rw_kk', I['rwkv_k_k'][l:l + 1, :])
        prow('rw_ka', I['rwkv_k_a'][l:l + 1, :]); prow('rw_lnw', I['rwkv_ln_w'][l:l + 1, :]); prow('rw_lnb', I['rwkv_ln_b'][l:l + 1, :])
        prow('rw_rk', I['rwkv_r_k'][l:l + 1, :])
        prow('rw_mu_r', I['rwkv_mu'][l:l + 1, 0:256]); prow('rw_mu_k', I['rwkv_mu'][l:l + 1, 256:512]); prow('rw_mu_v', I['rwkv_mu'][l:l + 1, 512:768])
        prow('rw_mu_x', I['rwkv_mu'][l:l + 1, 768:896])
        P.dma('sp', rows, w=[PM])
        P.tr([(bank[7].t[:, c * NPM:(c + 1) * NPM], PM.t[0:NPM, c * 128:(c + 1) * 128]) for c in range(2)], identf.t, r=[PM, identf], w=[bank[7]])
        P.cp(PT.t[:].rearrange("p c r -> p (c r)"), bank[7].t[:, 0:2 * NPM], r=[bank[7]], w=[PT])
        def pcol(name, fc):
            return PT.t[:, fc, PM_ROWS[name]:PM_ROWS[name] + 1]
        for fc in range(2):
            P.memset(BDa[fc].t[:], 0.0, w=[BDa[fc]]); P.memset(BDx[fc].t[:], 0.0, w=[BDx[fc]])
            P.dma('sp', [(BDa[fc].t[b * 64:(b + 1) * 64, b * 64:(b + 1) * 64], I['lru_wa'][l, 2 * fc + b, :, :]) for b in range(2)], w=[BDa[fc]])
            P.dma('sp', [(BDx[fc].t[b * 64:(b + 1) * 64, b * 64:(b + 1) * 64], I['lru_wx'][l, 2 * fc + b, :, :]) for b in range(2)], w=[BDx[fc]])
            P.act(lruc.t[:, fc, 2:3], pcol('lru_lam', fc), AF.Exp, r=[PT], w=[lruc], scale=-1.0)
            P.act(lruc.t[:, fc, 3:4], lruc.t[:, fc, 2:3], AF.Ln, r=[lruc, onesf], w=[lruc], bias=onesf.t[:, 0:1])
            P.ts(lruc.t[:, fc, 0:1], lruc.t[:, fc, 3:4], -8.0, ALU.mult, r=[lruc], w=[lruc])
            P.ts(lruc.t[:, fc, 1:2], lruc.t[:, fc, 3:4], -16.0, ALU.mult, r=[lruc], w=[lruc])
            P.memset(ubuf[fc].t[:], 0.0, w=[ubuf[fc]])
        P.memset(hstate.t[:], 0.0, w=[hstate])
        P.memset(kmT.t[:].rearrange("p a b c -> p (a b c)"), 0.0, w=[kmT])
        P.dma('sp', [(hp4.t[:, 0:4], I['ssd_dt_bias'][l:l + 1, :].to_broadcast([128, 4])),
                     (hp4.t[:, 4:8], I['ssd_a_log'][l:l + 1, :].to_broadcast([128, 4])),
                     (hp4.t[:, 8:12], I['ssd_d'][l:l + 1, :].to_broadcast([128, 4])),
                     (nwbc.t[:, :], I['ssd_norm'][l:l + 1, :].to_broadcast([128, 256]))], w=[hp4, nwbc], sembuf=hp4)
        P.act(hp4.t[:, 4:8], hp4.t[:, 4:8], AF.Exp, r=[hp4], w=[hp4])
        P.ts(hp4.t[:, 4:8], hp4.t[:, 4:8], -1.0, ALU.mult, r=[hp4], w=[hp4])
        for c4 in range(4):
            P.memset(cbuf[c4].t[:], 0.0, w=[cbuf[c4]])
        P.memset(sT.t[:].rearrange("p a b -> p (a b)"), 0.0, w=[sT])
        P.dma('sp', [(Wlow.t[0:32, :], I['rwkv_w_up'][l, :, :]), (Wlow.t[32:64, :], I['rwkv_a_up'][l, :, :]), (Wlow.t[64:128, :], I['rwkv_g_up'][l, :, :])], w=[Wlow])
        for c7 in range(7):
            P.memset(cb7[c7].t[:], 0.0, w=[cb7[c7]])
        P.memset(ST.t[:].rearrange("p a b -> p (a b)"), 0.0, w=[ST])
        for fc in range(2):
            P.ts(rwc.t[:, fc:fc + 1], pcol('rw_ka', fc), -1.0, ALU.mult, 1.0, ALU.add, r=[PT], w=[rwc])


        def samp_norm(dst):
            hbs = P.carve('AS', 'hbs', [128, D], BF16)
            rmsnorm_tile(xsp.t[0:4, :], xsp, hbs, npart=4)
            psT_ = bank[6].t[:].bitcast(BF16)
            P.tr([(psT_[:, kc * 4:(kc + 1) * 4], hbs.t[0:4, kc * 128:(kc + 1) * 128]) for kc in range(8)], identb.t, r=[hbs, identb], w=[bank[6]])
            P.cp(dst.t[:].rearrange("p k s -> p (k s)"), psT_[:, 0:32], r=[bank[6]], w=[dst])

        def sproj(bk, o0, Wb, c0, n):
            P.mm([(bk.t[0:4, o0:o0 + n], hTs.t[:, kc, :], Wb.t[:, kc, c0:c0 + n], kc == 0, kc == 7) for kc in range(8)], r=[hTs, Wb], w=[bk])

        def put_y(ybuf, yap, chunk0, nch=2):
            P.tr([(bank[6].t[:, c * 4:(c + 1) * 4], yap[0:4, c * 128:(c + 1) * 128]) for c in range(nch)], identf.t, r=[ybuf, identf], w=[bank[6]])
            P.cp(ymTs.t[:, chunk0:chunk0 + nch, :], bank[6].t[:, 0:4 * nch].rearrange("p (c s) -> p c s", c=nch), r=[bank[6]], w=[ymTs])

        def brow(dst, o0, src2d, n):
            return (dst.t[0:4, o0:o0 + n], src2d.to_broadcast([4, n]))

        def samp_A():
            qks = P.carve('AS', 'qks', [4, 512]); vs1 = P.carve('AS', 'vs1', [4, 257]); st1 = P.carve('AS', 'st1', [4, 8, 32]); st2 = P.carve('AS', 'st2', [4, 8, 32])
            en = P.carve('AS', 'en', [4, 16]); enm = P.carve('AS', 'enm', [4, 4])
            qbc = P.carve('AS', 'qbc', [128, 256]); prod = P.carve('AS', 'prod', [128, 256])
            Kpg = [P.carve('AS', 'Kpg%d' % k, [128, 256]) for k in range(3)]
            Vpg = [P.carve('AS', 'Vpg%d' % k, [128, 257]) for k in range(2)]
            SC = P.carve('AS', 'SC', [128, 128, 4]); g64 = P.carve('AS', 'g64', [64, 260]); g4 = P.carve('AS', 'g4', [4, 64 + 8 + 64])
            rhb = P.carve('AS', 'rhb', [4, 64, 4]); ob = P.carve('AS', 'ob', [4, 260])
            PIl = P.carve('AS', 'PIl', [128, 512], U32); PIt = P.carve('AS', 'PIt', [128, 512])
            P.ts(PIt.t[:, :], PIf.t[:, :], float(l * 5120 * 128), ALU.add, r=[PIf], w=[PIt])
            P.cp(PIl.t[:, :], PIt.t[:, :], r=[PIt], w=[PIl])
            sproj(bank[0], 0, WA, 0, 512); sproj(bank[1], 0, WA, 512, 256)
            qk = bank[0].t[0:4, :].rearrange("p (c d) -> p c d", c=8)
            cosb = ropes.t[:, 0:32].unsqueeze(1).to_broadcast([4, 8, 32]); sinb = ropes.t[:, 32:64].unsqueeze(1).to_broadcast([4, 8, 32])
            qo = qks.t[:].rearrange("p (c d) -> p c d", c=8)
            P.tt(st1.t[:], qk[:, :, 0:32], cosb, ALU.mult, r=[bank[0], ropes], w=[st1]); P.tt(st2.t[:], qk[:, :, 32:64], sinb, ALU.mult, r=[bank[0], ropes], w=[st2])
            P.tt(qo[:, :, 0:32], st1.t[:], st2.t[:], ALU.subtract, r=[st1, st2], w=[qks])
            P.tt(st1.t[:], qk[:, :, 32:64], cosb, ALU.mult, r=[bank[0], ropes], w=[st1]); P.tt(st2.t[:], qk[:, :, 0:32], sinb, ALU.mult, r=[bank[0], ropes], w=[st2])
            P.tt(qo[:, :, 32:64], st1.t[:], st2.t[:], ALU.add, r=[st1, st2], w=[qks])
            P.memset(vs1.t[:, 256:257], 1.0, w=[vs1])
            P.cp(vs1.t[:, 0:256], bank[1].t[0:4, 0:256], r=[bank[1]], w=[vs1])
            P.dma('sp', [(O['k_s'][l, :, :], qks.t[:, 256:512]), (O['v_s'][l, :, :], vs1.t[:, 0:256])], r=[qks, vs1], sembuf=qks, is_out=True)
            P.tt(st1.t[:].rearrange("p a b -> p (a b)"), qks.t[:, 0:256], qks.t[:, 256:512], ALU.mult, r=[qks], w=[st1])
            P.op('dve', lambda e: e.tensor_reduce(out=en.t[:, 0:4], in_=st1.t[:].rearrange("p a b -> p (a b)").rearrange("p (h d) -> p h d", h=4), op=ALU.add, axis=AX.X), r=[st1], w=[en])
            P.act(en.t[:, 4:8], en.t[:, 0:4], AF.Exp, r=[en], w=[en], scale=0.125)
            for k in range(2):
                P.memset(Vpg[k].t[:, 256:257], 1.0, w=[Vpg[k]])
            for s_ in range(4):
                P.mm([(bank[2].t[:, 0:256], identf.t[0:4, s_:s_ + 1].to_broadcast([4, 128]), qks.t[0:4, 0:256], True, True)], r=[identf, qks], w=[bank[2]])
                P.cp(qbc.t[:, :], bank[2].t[:, 0:256], r=[bank[2]], w=[qbc])
                for pg in range(128):
                    kp_ = Kpg[pg % 3]
                    col = s_ * 128 + pg
                    P.dma_custom('pool', lambda e: e.indirect_dma_start(out=kp_.t[:, :], out_offset=None, in_=I['ck'][:, :],
                                                                        in_offset=bass.IndirectOffsetOnAxis(ap=PIl.t[:, col:col + 1], axis=0)), r=[PIl], w=[kp_])
                    n_ = pg // 2
                    P.mm([(bank[3].t[0:64, 0:256], zsel.t[:, 63 - n_:127 - n_], kp_.t[:, :], pg == 0, pg == 127)], r=[zsel, kp_], w=[bank[3]])
                    P.tt(prod.t[:, :], kp_.t[:, :], qbc.t[:, :], ALU.mult, r=[kp_, qbc], w=[prod])
                    P.op('dve', lambda e: e.tensor_reduce(out=SC.t[:, pg, :], in_=prod.t[:, :].rearrange("p (h d) -> p h d", h=4), op=ALU.add, axis=AX.X), r=[prod], w=[SC])
                P.tt(g64.t[:, 0:256], bank[3].t[0:64, 0:256], qbc.t[0:64, :], ALU.mult, r=[bank[3], qbc], w=[g64])
                P.op('dve', lambda e: e.tensor_reduce(out=g64.t[:, 256:260], in_=g64.t[:, 0:256].rearrange("p (h d) -> p h d", h=4), op=ALU.add, axis=AX.X), r=[g64], w=[g64])
                P.tr([(bank[2].t[0:4, 256:320], g64.t[0:64, 256:260])], identf.t, r=[g64, identf], w=[bank[2]])
                P.cp(g4.t[:, 0:64], bank[2].t[0:4, 256:320], r=[bank[2]], w=[g4])
                P.op('dve', lambda e: e.max(out=g4.t[:, 64:72], in_=g4.t[:, 0:64]), r=[g4], w=[g4])
                P.ts(g4.t[:, 72:136], g4.t[:, 0:64], g4.t[:, 66:67], ALU.is_ge, -1.0, ALU.add, r=[g4], w=[g4])
                P.ts(g4.t[:, 72:136], g4.t[:, 72:136], 30000.0, ALU.mult, r=[g4], w=[g4])
                P.tt(rhb.t[:], g4.t[:, 72:136].unsqueeze(2).to_broadcast([4, 64, 4]), identf.t[0:4, 0:4].unsqueeze(1).to_broadcast([4, 64, 4]), ALU.mult, r=[g4, identf], w=[rhb])
                P.mm([(bank[2].t[:, 0:256], onesf.t[0:4, 0:128], rhb.t[:].rearrange("p a b -> p (a b)"), True, True)], r=[onesf, rhb], w=[bank[2]])
                scv = SC.t[:].rearrange("p (n two) h -> p n two h", two=2)
                for two in range(2):
                    P.tt(scv[:, :, two, :], scv[:, :, two, :], bank[2].t[:, 0:256].rearrange("p (n h) -> p n h", h=4), ALU.add, r=[SC, bank[2]], w=[SC])
                P.act(SC.t[:].rearrange("p a b -> p (a b)"), SC.t[:].rearrange("p a b -> p (a b)"), AF.Exp, r=[SC], w=[SC], scale=0.125)
                for pg in range(128):
                    vp_ = Vpg[pg % 2]
                    col = s_ * 128 + pg
                    P.dma_custom('pool', lambda e: e.indirect_dma_start(out=vp_.t[:, 0:256], out_offset=None, in_=I['cv'][:, :],
                                                                        in_offset=bass.IndirectOffsetOnAxis(ap=PIl.t[:, col:col + 1], axis=0)), r=[PIl], w=[vp_])
                    P.mm([(bank[4].t[0:4, 0:257], SC.t[:, pg, :], vp_.t[:, 0:257], pg == 0, False)], r=[SC, vp_], w=[bank[4]])
                P.ts(enm.t[:, :], en.t[:, 4:8], identf.t[0:4, s_:s_ + 1], ALU.mult, r=[en, identf], w=[enm])
                P.mm([(bank[4].t[0:4, 0:257], enm.t[0:4, 0:4], vs1.t[0:4, 0:257], False, True)], r=[enm, vs1], w=[bank[4]])
                P.op('dve', lambda e: e.reciprocal(out=ob.t[:, 256:257], in_=bank[4].t[0:4, 256:257]), r=[bank[4]], w=[ob])
                P.stt(ob.t[:, 0:256], bank[4].t[0:4, 0:256], ob.t[:, 256:257], dmask4.t[:, :], ALU.mult, ALU.mult, r=[bank[4], ob, dmask4], w=[ob])
                P.mm([(bank[5].t[:, s_ * 2 + c:s_ * 2 + c + 1], ob.t[0:4, c * 128:(c + 1) * 128], onesf.t[0:4, 0:1], True, True) for c in range(2)],
                     r=[ob, onesf], w=[bank[5]])
            P.cp(ymTs.t[:, 0:2, :], bank[5].t[:, 0:8].rearrange("p (s c) -> p c s", c=2), r=[bank[5]], w=[ymTs])

        def samp_B():
            pb = P.carve('AS', 'pbB', [4, 2048]); buf = P.carve('AS', 'bufB', [4, 4, 256]); xc = P.carve('AS', 'xcB', [4, 256])
            xcT = P.carve('AS', 'xcTB', [128, 2, 4]); w1 = P.carve('AS', 'w1B', [4, 512]); w2 = P.carve('AS', 'w2B', [4, 512]); h0 = P.carve('AS', 'h0B', [4, 256])
            P.dma('sp', [brow(pb, 0, I['lru_conv_w'][l:l + 1, :, :].rearrange("o j f -> o (j f)"), 1024), brow(pb, 1024, I['lru_conv_b'][l:l + 1, :], 256),
                         brow(pb, 1280, I['lru_ba'][l:l + 1, :], 256), brow(pb, 1536, I['lru_bx'][l:l + 1, :], 256), brow(pb, 1792, I['lru_lambda'][l:l + 1, :], 256),
                         (buf.t[:, 0:3, :], I['st_lru_conv'][l, :, :, :]), (h0.t[:, :], I['st_lru_h'][l, :, :])], w=[pb, buf, h0], sembuf=pb)
            sproj(bank[0], 0, WA, 0, 512)
            P.cp(buf.t[:, 3, :], bank[0].t[0:4, 0:256], r=[bank[0]], w=[buf])
            P.tt(xc.t[:, :], buf.t[:, 0, :], pb.t[:, 0:256], ALU.mult, r=[buf, pb], w=[xc])
            for j in range(1, 4):
                P.tt(w1.t[:, 0:256], buf.t[:, j, :], pb.t[:, j * 256:(j + 1) * 256], ALU.mult, r=[buf, pb], w=[w1])
                P.tt(xc.t[:, :], xc.t[:, :], w1.t[:, 0:256], ALU.add, r=[xc, w1], w=[xc])
            P.tt(xc.t[:, :], xc.t[:, :], pb.t[:, 1024:1280], ALU.add, r=[xc, pb], w=[xc])
            P.tr([(bank[2].t[:, c * 4:(c + 1) * 4], xc.t[0:4, c * 128:(c + 1) * 128]) for c in range(2)], identf.t, r=[xc, identf], w=[bank[2]])
            P.cp(xcT.t[:].rearrange("p c s -> p (c s)"), bank[2].t[:, 0:8], r=[bank[2]], w=[xcT])
            P.mm([(bank[3].t[0:4, fc * 128:(fc + 1) * 128], xcT.t[:, fc, :], BDa[fc].t[:, :], True, True) for fc in range(2)] +
                 [(bank[3].t[0:4, 256 + fc * 128:256 + (fc + 1) * 128], xcT.t[:, fc, :], BDx[fc].t[:, :], True, True) for fc in range(2)],
                 r=[xcT] + BDa + BDx, w=[bank[3]])
            P.tt(w1.t[:, :], bank[3].t[0:4, 0:512], pb.t[:, 1280:1792], ALU.add, r=[bank[3], pb], w=[w1])
            P.act(w1.t[:, :], w1.t[:, :], AF.Sigmoid, r=[w1], w=[w1])
            P.act(w2.t[:, 0:256], pb.t[:, 1792:2048], AF.Exp, r=[pb], w=[w2], scale=-1.0)
            P.act(w2.t[:, 0:256], w2.t[:, 0:256], AF.Ln, r=[w2, onesf], w=[w2], bias=onesf.t[0:4, 0:1])
            P.tt(w2.t[:, 0:256], w2.t[:, 0:256], w1.t[:, 0:256], ALU.mult, r=[w2, w1], w=[w2])
            P.act(w2.t[:, 256:512], w2.t[:, 0:256], AF.Exp, r=[w2], w=[w2], scale=-16.0)
            P.act(w2.t[:, 0:256], w2.t[:, 0:256], AF.Exp, r=[w2], w=[w2], scale=-8.0)
            P.ts(w2.t[:, 256:512], w2.t[:, 256:512], 0.99999994, ALU.min, r=[w2], w=[w2])
            P.act(w2.t[:, 256:512], w2.t[:, 256:512], AF.Sqrt, r=[w2, onesf], w=[w2], scale=-1.0, bias=onesf.t[0:4, 0:1])
            P.tt(w1.t[:, 256:512], w1.t[:, 256:512], xc.t[:, :], ALU.mult, r=[w1, xc], w=[w1])
            P.tt(w2.t[:, 256:512], w2.t[:, 256:512], w1.t[:, 256:512], ALU.mult, r=[w2, w1], w=[w2])
            P.tt(h0.t[:, :], h0.t[:, :], w2.t[:, 0:256], ALU.mult, r=[h0, w2], w=[h0])
            P.tt(h0.t[:, :], h0.t[:, :], w2.t[:, 256:512], ALU.add, r=[h0, w2], w=[h0])
            P.act(w1.t[:, 0:256], bank[0].t[0:4, 256:512], AF.Gelu_apprx_tanh, r=[bank[0]], w=[w1])
            P.tt(w1.t[:, 0:256], w1.t[:, 0:256], h0.t[:, :], ALU.mult, r=[w1, h0], w=[w1])
            put_y(w1, w1.t, 2)
            P.dma('sp', [(O['lru_h_s'][l, :, :], h0.t[:, :]), (O['lru_conv_s'][l, :, :, :], buf.t[:, 1:4, :])], r=[h0, buf], sembuf=h0, is_out=True)

        def samp_C():
            pb = P.carve('AS', 'pbC', [4, 2836]); cbs = P.carve('AS', 'cbsC', [4, 4, 512]); xb = P.carve('AS', 'xbC', [4, 512]); w1 = P.carve('AS', 'w1C', [4, 512])
            dd = P.carve('AS', 'ddC', [4, 16]); xdt = P.carve('AS', 'xdtC', [4, 256]); yy = P.carve('AS', 'yyC', [4, 256]); ycb = P.carve('AS', 'ycbC', [4, 256])
            Sh = [P.carve('AS', 'ShC%d' % k, [4, 16, 64]) for k in range(1)]; Sw = P.carve('AS', 'SwC', [4, 16, 64])
            P.dma('sp', [brow(pb, 0, I['ssd_conv_w'][l:l + 1, :, :].rearrange("o j f -> o (j f)"), 2048), brow(pb, 2048, I['ssd_conv_b'][l:l + 1, :], 512),
                         brow(pb, 2560, I['ssd_dt_bias'][l:l + 1, :], 4), brow(pb, 2564, I['ssd_a_log'][l:l + 1, :], 4), brow(pb, 2568, I['ssd_d'][l:l + 1, :], 4),
                         brow(pb, 2580, I['ssd_norm'][l:l + 1, :], 256), (cbs.t[:, 0:3, :], I['st_ssd_conv'][l, :, :, :])], w=[pb, cbs], sembuf=pb)
            sproj(bank[0], 0, WA, 0, 256); sproj(bank[0], 256, WA, 768, 4); sproj(bank[1], 0, WA, 256, 512)
            P.cp(cbs.t[:, 3, :], bank[1].t[0:4, 0:512], r=[bank[1]], w=[cbs])
            P.tt(xb.t[:, :], cbs.t[:, 0, :], pb.t[:, 0:512], ALU.mult, r=[cbs, pb], w=[xb])
            for j in range(1, 4):
                P.tt(w1.t[:, :], cbs.t[:, j, :], pb.t[:, j * 512:(j + 1) * 512], ALU.mult, r=[cbs, pb], w=[w1])
                P.tt(xb.t[:, :], xb.t[:, :], w1.t[:, :], ALU.add, r=[xb, w1], w=[xb])
            P.tt(xb.t[:, :], xb.t[:, :], pb.t[:, 2048:2560], ALU.add, r=[xb, pb], w=[xb])
            P.act(xb.t[:, :], xb.t[:, :], AF.Silu, r=[xb], w=[xb])
            P.tt(dd.t[:, 0:4], bank[0].t[0:4, 256:260], pb.t[:, 2560:2564], ALU.add, r=[bank[0], pb], w=[dd])
            P.act(dd.t[:, 0:4], dd.t[:, 0:4], AF.Exp, r=[dd], w=[dd])
            P.act(dd.t[:, 0:4], dd.t[:, 0:4], AF.Ln, r=[dd, onesf], w=[dd], bias=onesf.t[0:4, 0:1])
            P.act(dd.t[:, 4:8], pb.t[:, 2564:2568], AF.Exp, r=[pb], w=[dd])
            P.tt(dd.t[:, 4:8], dd.t[:, 4:8], dd.t[:, 0:4], ALU.mult, r=[dd], w=[dd])
            P.act(dd.t[:, 4:8], dd.t[:, 4:8], AF.Exp, r=[dd], w=[dd], scale=-1.0)
            P.tt(xdt.t[:].rearrange("p (h d) -> p h d", h=4), xb.t[:, 0:256].rearrange("p (h d) -> p h d", h=4), dd.t[:, 0:4].unsqueeze(2).to_broadcast([4, 4, 64]),
                 ALU.mult, r=[xb, dd], w=[xdt])
            for h8 in range(16):
                h, ph = h8 // 4, h8 % 4
                g = h // 2
                sh = Sh[0]
                ps_ = slice(ph * 16, (ph + 1) * 16)
                hp_ = slice(h * 64 + ph * 16, h * 64 + (ph + 1) * 16)
                P.dma('sp', [(sh.t[:, :, :], I['st_ssd'][l, :, h, ps_, :])], w=[sh])
                Bg = xb.t[:, 256 + g * 64:256 + (g + 1) * 64]; Cg = xb.t[:, 384 + g * 64:384 + (g + 1) * 64]
                P.tt(Sw.t[:], xdt.t[:, hp_].unsqueeze(2).to_broadcast([4, 16, 64]), Bg.unsqueeze(1).to_broadcast([4, 16, 64]), ALU.mult, r=[xdt, xb], w=[Sw])
                P.stt(sh.t[:].rearrange("p a b -> p (a b)"), sh.t[:].rearrange("p a b -> p (a b)"), dd.t[:, 4 + h:5 + h], Sw.t[:].rearrange("p a b -> p (a b)"), ALU.mult, ALU.add,
                      r=[sh, dd, Sw], w=[sh])
                P.dma('sp', [(O['ssd_s'][l, :, h, ps_, :], sh.t[:, :, :])], r=[sh], sembuf=sh, is_out=True)
                P.tt(Sw.t[:], sh.t[:], Cg.unsqueeze(1).to_broadcast([4, 16, 64]), ALU.mult, r=[sh, xb], w=[Sw])
                P.op('dve', lambda e: e.tensor_reduce(out=yy.t[:, hp_], in_=Sw.t[:], op=ALU.add, axis=AX.X), r=[Sw], w=[yy])
            P.tt(w1.t[:, 0:256].rearrange("p (h d) -> p h d", h=4), xb.t[:, 0:256].rearrange("p (h d) -> p h d", h=4), pb.t[:, 2568:2572].unsqueeze(2).to_broadcast([4, 4, 64]),
                 ALU.mult, r=[xb, pb], w=[w1])
            P.tt(yy.t[:, :], yy.t[:, :], w1.t[:, 0:256], ALU.add, r=[yy, w1], w=[yy])
            P.act(w1.t[:, 0:256], bank[0].t[0:4, 0:256], AF.Silu, r=[bank[0]], w=[w1])
            P.tt(yy.t[:, :], yy.t[:, :], w1.t[:, 0:256], ALU.mult, r=[yy, w1], w=[yy])
            rmsnorm_tile(yy.t[:, :], yy, ycb, npart=4, width=256, gain=pb.t[:, 2580:2836], gbuf_=pb)
            put_y(ycb, ycb.t, 4)
            P.dma('sp', [(O['ssd_conv_s'][l, :, :, :], cbs.t[:, 1:4, :])], r=[cbs], sembuf=cbs, is_out=True)

        def samp_D():
            pb = P.carve('AS', 'pbD', [4, 2688]); cur = P.carve('AS', 'curD', [4, 896]); mm_ = P.carve('AS', 'mD', [4, 896]); prev = P.carve('AS', 'prevD', [4, 896])
            xTs = P.carve('AS', 'xTsD', [128, 4]); w1 = P.carve('AS', 'w1D', [4, 1024]); w2 = P.carve('AS', 'w2D', [4, 1024]); sm = P.carve('AS', 'smD', [4, 32])
            Sh = [P.carve('AS', 'ShD%d' % k, [4, 16, 64]) for k in range(1)]; Sw = P.carve('AS', 'SwD', [4, 16, 64]); yy = P.carve('AS', 'yyD', [4, 256]); sa = P.carve('AS', 'saD', [4, 16])
            names = ['rwkv_w0', 'rwkv_a0', 'rwkv_k_k', 'rwkv_k_a', 'rwkv_ln_w', 'rwkv_ln_b', 'rwkv_r_k']
            P.dma('sp', [brow(pb, 0, I['rwkv_mu'][l:l + 1, :], 896)] + [brow(pb, 896 + 256 * q, I[nm][l:l + 1, :], 256) for q, nm in enumerate(names)] +
                  [(prev.t[:, :], I['st_rwkv_shift'][l, :, :])], w=[pb, prev], sembuf=pb)
            PW0, PA0, PKK, PKA, PLW, PLB, PRK = [896 + 256 * q for q in range(7)]
            sproj(bank[0], 0, WA, 0, 512); sproj(bank[1], 0, WA, 512, 384)
            P.cp(cur.t[:, 0:512], bank[0].t[0:4, 0:512], r=[bank[0]], w=[cur]); P.cp(cur.t[:, 512:896], bank[1].t[0:4, 0:384], r=[bank[1]], w=[cur])
            P.dma('sp', [(O['rwkv_shift_s'][l, :, :], cur.t[:, :])], r=[cur], sembuf=cur, is_out=True)
            P.tt(mm_.t[:, :], prev.t[:, :], cur.t[:, :], ALU.subtract, r=[prev, cur], w=[mm_])
            P.tt(mm_.t[:, :], mm_.t[:, :], pb.t[:, 0:896], ALU.mult, r=[mm_, pb], w=[mm_])
            P.tt(mm_.t[:, :], mm_.t[:, :], cur.t[:, :], ALU.add, r=[mm_, cur], w=[mm_])
            R_, K_, V_ = mm_.t[:, 0:256], mm_.t[:, 256:512], mm_.t[:, 512:768]
            P.tr([(bank[2].t[:, 0:4], mm_.t[0:4, 768:896])], identf.t, r=[mm_, identf], w=[bank[2]])
            P.cp(xTs.t[:, :], bank[2].t[:, 0:4], r=[bank[2]], w=[xTs])
            xT3 = P.carve('AS', 'xT3D', [128, 3, 4])
            P.memset(xT3.t[:].rearrange("p a b -> p (a b)"), 0.0, w=[xT3])
            P.act(xT3.t[0:32, 0, :], xTs.t[0:32, :], AF.Tanh, r=[xTs], w=[xT3])
            P.cp(xT3.t[32:64, 1, :], xTs.t[32:64, :], r=[xTs], w=[xT3])
            P.act(xT3.t[64:128, 2, :], xTs.t[64:128, :], AF.Sigmoid, r=[xTs], w=[xT3])
            P.mm([(bank[3].t[0:4, 0:256], xT3.t[:, 0, :], Wlow.t[:, :], True, True)], r=[xT3, Wlow], w=[bank[3]])
            P.mm([(bank[3].t[0:4, 256:512], xT3.t[:, 1, :], Wlow.t[:, :], True, True)], r=[xT3, Wlow], w=[bank[3]])
            P.mm([(bank[4].t[0:4, 0:256], xT3.t[:, 2, :], Wlow.t[:, :], True, True)], r=[xT3, Wlow], w=[bank[4]])
            Dd, Aa, Gg, KKn = w1.t[:, 0:256], w1.t[:, 256:512], w1.t[:, 512:768], w1.t[:, 768:1024]
            KP, Bb, T1, T2 = w2.t[:, 0:256], w2.t[:, 256:512], w2.t[:, 512:768], w2.t[:, 768:1024]
            P.tt(w1.t[:, 0:512], bank[3].t[0:4, 0:512], pb.t[:, PW0:PW0 + 512], ALU.add, r=[bank[3], pb], w=[w1])
            P.act(w1.t[:, 0:512], w1.t[:, 0:512], AF.Sigmoid, r=[w1], w=[w1])
            P.act(Dd, Dd, AF.Exp, r=[w1], w=[w1], scale=-0.6065306597126334)
            P.cp(Gg, bank[4].t[0:4, 0:256], r=[bank[4]], w=[w1])
            P.tt(KKn, K_, pb.t[:, PKK:PKK + 256], ALU.mult, r=[mm_, pb], w=[w1])
            P.tt(T1, KKn, KKn, ALU.mult, r=[w1], w=[w2])
            P.op('dve', lambda e: e.tensor_reduce(out=sm.t[:, 0:4], in_=T1.rearrange("p (h d) -> p h d", h=4), op=ALU.add, axis=AX.X), r=[w2], w=[sm])
            P.act(sm.t[:, 0:4], sm.t[:, 0:4], AF.Sqrt, r=[sm], w=[sm]); P.ts(sm.t[:, 0:4], sm.t[:, 0:4], 1e-12, ALU.max, r=[sm], w=[sm])
            P.op('dve', lambda e: e.reciprocal(out=sm.t[:, 0:4], in_=sm.t[:, 0:4]), r=[sm], w=[sm])
            P.tt(KKn.rearrange("p (h d) -> p h d", h=4), KKn.rearrange("p (h d) -> p h d", h=4), sm.t[:, 0:4].unsqueeze(2).to_broadcast([4, 4, 64]), ALU.mult, r=[w1, sm], w=[w1])
            P.ts(T1, Aa, -1.0, ALU.add, r=[w1], w=[w2]); P.tt(T1, T1, pb.t[:, PKA:PKA + 256], ALU.mult, r=[w2, pb], w=[w2]); P.ts(T1, T1, 1.0, ALU.add, r=[w2], w=[w2])
            P.tt(KP, K_, T1, ALU.mult, r=[mm_, w2], w=[w2])
            P.tt(Bb, KKn, Aa, ALU.mult, r=[w1], w=[w2])
            for h8 in range(16):
                h, ph = h8 // 4, h8 % 4
                hs_ = slice(h * 64, (h + 1) * 64)
                vs_ = slice(h * 64 + ph * 16, h * 64 + (ph + 1) * 16)
                ps_ = slice(ph * 16, (ph + 1) * 16)
                sh = Sh[0]
                P.dma('sp', [(sh.t[:, :, :], I['st_rwkv'][l, :, h, ps_, :])], w=[sh])
                bk = lambda ap: ap.unsqueeze(1).to_broadcast([4, 16, 64])
                bv = lambda ap: ap.unsqueeze(2).to_broadcast([4, 16, 64])
                P.tt(Sw.t[:], sh.t[:], bk(KKn[:, hs_]), ALU.mult, r=[sh, w1], w=[Sw])
                P.op('dve', lambda e: e.tensor_reduce(out=sa.t[:, :], in_=Sw.t[:], op=ALU.add, axis=AX.X), r=[Sw], w=[sa])
                P.ts(sa.t[:, :], sa.t[:, :], -1.0, ALU.mult, r=[sa], w=[sa])
                P.tt(sh.t[:], sh.t[:], bk(Dd[:, hs_]), ALU.mult, r=[sh, w1], w=[sh])
                P.tt(Sw.t[:], bv(sa.t[:, :]), bk(Bb[:, hs_]), ALU.mult, r=[sa, w2], w=[Sw])
                P.tt(sh.t[:], sh.t[:], Sw.t[:], ALU.add, r=[sh, Sw], w=[sh])
                P.tt(Sw.t[:], bv(V_[:, vs_]), bk(KP[:, hs_]), ALU.mult, r=[mm_, w2], w=[Sw])
                P.tt(sh.t[:], sh.t[:], Sw.t[:], ALU.add, r=[sh, Sw], w=[sh])
                P.dma('sp', [(O['rwkv_s'][l, :, h, ps_, :], sh.t[:, :, :])], r=[sh], sembuf=sh, is_out=True)
                P.tt(Sw.t[:], sh.t[:], bk(R_[:, hs_]), ALU.mult, r=[sh, mm_], w=[Sw])
                P.op('dve', lambda e: e.tensor_reduce(out=yy.t[:, vs_], in_=Sw.t[:], op=ALU.add, axis=AX.X), r=[Sw], w=[yy])
            y3 = yy.t[:, :].rearrange("p (h d) -> p h d", h=4)
            P.op('dve', lambda e: e.tensor_reduce(out=sm.t[:, 4:8], in_=y3, op=ALU.add, axis=AX.X), r=[yy], w=[sm])
            P.ts(sm.t[:, 4:8], sm.t[:, 4:8], 1.0 / 64, ALU.mult, r=[sm], w=[sm])
            P.tt(y3, y3, sm.t[:, 4:8].unsqueeze(2).to_broadcast([4, 4, 64]), ALU.subtract, r=[yy, sm], w=[yy])
            P.tt(T1, yy.t[:, :], yy.t[:, :], ALU.mult, r=[yy], w=[w2])
            P.op('dve', lambda e: e.tensor_reduce(out=sm.t[:, 8:12], in_=T1.rearrange("p (h d) -> p h d", h=4), op=ALU.add, axis=AX.X), r=[w2], w=[sm])
            P.act(sm.t[:, 8:12], sm.t[:, 8:12], AF.Sqrt, r=[sm, epsb], w=[sm], scale=1.0 / 64, bias=epsb.t[0:4, 1:2])
            P.op('dve', lambda e: e.reciprocal(out=sm.t[:, 8:12], in_=sm.t[:, 8:12]), r=[sm], w=[sm])
            P.tt(y3, y3, sm.t[:, 8:12].unsqueeze(2).to_broadcast([4, 4, 64]), ALU.mult, r=[yy, sm], w=[yy])
            P.tt(yy.t[:, :], yy.t[:, :], pb.t[:, PLW:PLW + 256], ALU.mult, r=[yy, pb], w=[yy]); P.tt(yy.t[:, :], yy.t[:, :], pb.t[:, PLB:PLB + 256], ALU.add, r=[yy, pb], w=[yy])
            P.tt(T1, R_, KP, ALU.mult, r=[mm_, w2], w=[w2]); P.tt(T1, T1, pb.t[:, PRK:PRK + 256], ALU.mult, r=[w2, pb], w=[w2])
            P.op('dve', lambda e: e.tensor_reduce(out=sm.t[:, 12:16], in_=T1.rearrange("p (h d) -> p h d", h=4), op=ALU.add, axis=AX.X), r=[w2], w=[sm])
            P.tt(T2.rearrange("p (h d) -> p h d", h=4), V_.rearrange("p (h d) -> p h d", h=4), sm.t[:, 12:16].unsqueeze(2).to_broadcast([4, 4, 64]), ALU.mult, r=[mm_, sm], w=[w2])
            P.tt(yy.t[:, :], yy.t[:, :], T2, ALU.add, r=[yy, w2], w=[yy])
            P.tt(yy.t[:, :], yy.t[:, :], Gg, ALU.mult, r=[yy, w1], w=[yy])
            put_y(yy, yy.t, 6)

        def samp_W():
            for half in range(2):
                hs_ = slice(half * 512, (half + 1) * 512)
                P.mm([(bank[half].t[0:4, :], ymTs.t[:, kc, :], WA.t[:, kc, hs_], kc == 0, kc == 7) for kc in range(8)], r=[ymTs, WA], w=[bank[half]])
                P.tt(xsp.t[0:4, hs_], xsp.t[0:4, hs_], bank[half].t[0:4, :], ALU.add, r=[xsp, bank[half]], w=[xsp])
        if stage == 'init':
            P.finish()
            return P, I, O, DBG
        for tb in range(NTB):
            P.reset('AS')
            hb = [P.carve('AS', 'hb%d' % i, [128, D], BF16) for i in range(2)]
            if tb == 0 and WITH_S:
                samp_norm(hTs)
            for tt in range(4):
                i = tb * 4 + tt
                hbuf = hb[i % 2]
                rmsnorm_tile(xres.t[:, i, :], xres, hbuf)
                psT = bank[6].t[:].bitcast(BF16)
                P.tr([(psT[:, kc * 128:(kc + 1) * 128], hbuf.t[:, kc * 128:(kc + 1) * 128]) for kc in range(8)], identb.t,
                     r=[hbuf, identb], w=[bank[6]])
                P.cp(hT.t[:, :, tt * 128:(tt + 1) * 128], psT.rearrange("p (k t) -> p k t", k=8), r=[bank[6]], w=[hT], eng='act')
            if stage == 'norm':
                P.finish()
                return P, I, O, DBG
            P.reset('AS')
            QT32 = P.carve('AS', 'QT32', [128, 2, TB]); QTm = P.carve('AS', 'QTm', [128, 2, 2, TB], BF16)
            biasT = P.carve('AS', 'biasT', [128, TB], BF16)
            qkr = [P.carve('AS', 'qkr%d' % i, [128, 512]) for i in range(2)]
            vst = [P.carve('AS', 'vst%d' % i, [128, 256]) for i in range(2)]
            rtmp = [P.carve('AS', 'rtmp%d' % i, [128, 8, 32]) for i in range(2)]
            gbuf = P.carve('AS', 'gbuf', [128, 4, 8]); m8 = P.carve('AS', 'm8', [128, 4, 8]); selb = P.carve('AS', 'selb', [128, 4, 8])
            biasq = P.carve('AS', 'biasq', [128, 32])
            PTb = [P.carve('AS', 'PTb%d' % i, [128, 512], BF16) for i in range(2)]
            yatok = P.carve('AS', 'yatok', [128, 256], BF16)
            P.memset(QTm.t[:].rearrange("p a b c -> p (a b c)"), 0.0, w=[QTm])
            P.memset(biasT.t[:], 0.0, w=[biasT])
            load_w(I['w_in'], 0, 768)
            for tt in range(4):
                i = tb * 4 + tt
                par = i % 2
                tok = slice(tt * 128, (tt + 1) * 128)
                P.mm([(bank[0].t[:, 0:512], hT.t[:, kc, tok], WA.t[:, kc, 0:512], kc == 0, kc == 7) for kc in range(8)],
                     r=[hT, WA], w=[bank[0]])
                P.mm([(bank[1].t[:, 0:256], hT.t[:, kc, tok], WA.t[:, kc, 512:768], kc == 0, kc == 7) for kc in range(8)],
                     r=[hT, WA], w=[bank[1]])
                qk = bank[0].t[:].rearrange("p (c d) -> p c d", c=8)
                cosb = rope.t[:, i, 0:32].unsqueeze(1).to_broadcast([128, 8, 32])
                sinb = rope.t[:, i, 32:64].unsqueeze(1).to_broadcast([128, 8, 32])
                qo = qkr[par].t[:].rearrange("p (c d) -> p c d", c=8)
                t1, t2 = rtmp[0], rtmp[1]
                P.tt(t1.t[:], qk[:, :, 0:32], cosb, ALU.mult, r=[bank[0], rope], w=[t1])
                P.tt(t2.t[:], qk[:, :, 32:64], sinb, ALU.mult, r=[bank[0], rope], w=[t2])
                P.tt(qo[:, :, 0:32], t1.t[:], t2.t[:], ALU.subtract, r=[t1, t2], w=[qkr[par]])
                P.tt(t1.t[:], qk[:, :, 32:64], cosb, ALU.mult, r=[bank[0], rope], w=[t1])
                P.tt(t2.t[:], qk[:, :, 0:32], sinb, ALU.mult, r=[bank[0], rope], w=[t2])
                P.tt(qo[:, :, 32:64], t1.t[:], t2.t[:], ALU.add, r=[t1, t2], w=[qkr[par]])
                P.cp(vst[par].t[:], bank[1].t[:, 0:256], r=[bank[1]], w=[vst[par]], eng='act')
                P.cp(Vaug.t[:, i, :, 0:64], bank[1].t[:, 0:256].rearrange("p (h d) -> p h d", h=4), r=[bank[1]], w=[Vaug])
                P.dma('sp', [(O['k_p'][l, i * 128:(i + 1) * 128, :], qkr[par].t[:, 256:512])], r=[qkr[par]], sembuf=qkr[par], is_out=True)
                P.dma('sp', [(O['v_p'][l, i * 128:(i + 1) * 128, :], vst[par].t[:])], r=[vst[par]], sembuf=vst[par], is_out=True)
                if stage == 'rope':
                    P.finish()
                    return P, I, O, DBG
                qkb = PTb[0]
                P.cp(qkb.t[:, :], qkr[par].t[:, :], r=[qkr[par]], w=[qkb], eng='act')
                psb = bank[2].t[:].bitcast(BF16)
                P.tr([(psb[:, c * 128:(c + 1) * 128], qkb.t[:, c * 128:(c + 1) * 128]) for c in range(4)], identb.t, r=[qkb, identb], w=[bank[2]])
                for h2 in range(2):
                    pr = slice(h2 * 64, (h2 + 1) * 64)
                    P.cp(QTm.t[pr, :, h2, tok], psb[pr, 0:256].rearrange("p (c t) -> p c t", c=2), r=[bank[2]], w=[QTm])
                P.cp(KT.t[:, :, i * 128:(i + 1) * 128], psb[:, 256:512].rearrange("p (c t) -> p c t", c=2), r=[bank[2]], w=[KT], eng='act')
                P.tr([(bank[3].t[:, c * 128:(c + 1) * 128], qkr[par].t[:, c * 128:(c + 1) * 128]) for c in range(4)], identf.t,
                     r=[qkr[par], identf], w=[bank[3]])
                P.cp(QT32.t[:, :, tok], bank[3].t[:, 0:256].rearrange("p (c t) -> p c t", c=2), r=[bank[3]], w=[QT32], eng='act')
                n = i // 2
                if i % 2 == 0:
                    P.op('dve', lambda e: e.tensor_reduce(out=kmacc.t[:, :], in_=bank[3].t[:, 256:512].rearrange("p (c t) -> p c t", c=2),
                                                          op=ALU.add, axis=AX.X), r=[bank[3]], w=[kmacc])
                else:
                    P.op('dve', lambda e: e.tensor_reduce(out=small.t[:, 8:10], in_=bank[3].t[:, 256:512].rearrange("p (c t) -> p c t", c=2),
                                                          op=ALU.add, axis=AX.X), r=[bank[3]], w=[small])
                    for h2 in range(2):
                        pr = slice(h2 * 64, (h2 + 1) * 64)
                        P.tt(kmT.t[pr, :, h2, n], kmacc.t[pr, :], small.t[pr, 8:10], ALU.add, r=[kmacc, small], w=[kmT])
                if stage == 'qkT':
                    P.finish()
                    return P, I, O, DBG
                cur = i // 2
                P.mm([(bank[1].t[:, 256 + (2 * hp + h2) * 8:256 + (2 * hp + h2 + 1) * 8], QT32.t[:, hp, tok], kmT.t[:, hp, h2, :], True, True)
                      for hp in range(2) for h2 in range(2)], r=[QT32, kmT], w=[bank[1]])
                P.tt(gbuf.t[:], bank[1].t[:, 256:288].rearrange("p (h n) -> p h n", h=4), cpast.t[:, cur, :].unsqueeze(1).to_broadcast([128, 4, 8]),
                     ALU.add, r=[bank[1], cpast], w=[gbuf])
                for h in range(4):
                    P.op('dve', lambda e: e.max(out=m8.t[:, h, :], in_=gbuf.t[:, h, :]), r=[gbuf], w=[m8])
                P.tt(selb.t[:], gbuf.t[:], m8.t[:, :, 2:3].to_broadcast([128, 4, 8]), ALU.is_ge, r=[gbuf, m8], w=[selb])
                P.stt(biasq.t[:].rearrange("p (h n) -> p h n", h=4), selb.t[:], -1.0, cpm.t[:, cur, :].unsqueeze(1).to_broadcast([128, 4, 8]),
                      ALU.add, ALU.mult, r=[selb, cpm], w=[biasq])
                P.tr([(bank[1].t[0:32, 384:512], biasq.t[:, 0:32])], identf.t, r=[biasq, identf], w=[bank[1]])
                P.cp(biasT.t[0:32, tok], bank[1].t[0:32, 384:512], r=[bank[1]], w=[biasT])
            gcount = 0
            for tt in range(4):
                if stage in ('proj',):
                    break
                i = tb * 4 + tt
                cur = i // 2
                tok = slice(tt * 128, (tt + 1) * 128)
                for h in range(4):
                    keys = list(range(i + 1))
                    ngr = (len(keys) + 3) // 4
                    for g in range(ngr):
                        js = keys[g * 4:(g + 1) * 4]
                        ptb = PTb[gcount % 2]
                        gcount += 1
                        groups = []
                        for s_, j in enumerate(js):
                            o = bank[4].t[:, s_ * 128:(s_ + 1) * 128]
                            hasb = (j // 2) < cur
                            hasc = (j == i)
                            groups.append((o, KT.t[:, h // 2, j * 128:(j + 1) * 128], QTm.t[:, h // 2, h % 2, tok], True, not (hasb or hasc)))
                            if hasb:
                                c = h * 8 + j // 2
                                groups.append((o, identb.t[:, c:c + 1].to_broadcast([128, 128]), biasT.t[:, tok], False, True))
                            if hasc:
                                groups.append((o, identb.t[:, :], causal.t[:, :], False, True))
                        P.mm(groups, r=[KT, QTm, biasT, identb, causal], w=[bank[4]])
                        n = len(js) * 128
                        P.act(ptb.t[:, 0:n], bank[4].t[:, 0:n], AF.Exp, r=[bank[4]], w=[ptb], scale=0.125)
                        P.mm([(bank[5].t[:, 0:65], ptb.t[:, s_ * 128:(s_ + 1) * 128], Vaug.t[:, j, h, :], (g == 0 and s_ == 0), (j == i))
                              for s_, j in enumerate(js)], r=[ptb, Vaug], w=[bank[5]])
                    P.op('dve', lambda e: e.reciprocal(out=small.t[:, 4:5], in_=bank[5].t[:, 64:65]), r=[bank[5]], w=[small])
                    P.ts(yatok.t[:, h * 64:(h + 1) * 64], bank[5].t[:, 0:64], small.t[:, 4:5], ALU.mult, r=[bank[5], small], w=[yatok])
                psT = bank[6].t[:].bitcast(BF16)
                P.tr([(psT[:, c * 128:(c + 1) * 128], yatok.t[:, c * 128:(c + 1) * 128]) for c in range(2)], identb.t, r=[yatok, identb], w=[bank[6]])
                P.cp(ymT.t[:, 0:2, tok], psT[:, 0:256].rearrange("p (c t) -> p c t", c=2), r=[bank[6]], w=[ymT], eng='act')
            if tb == NTB - 1 and WITH_S and WITH_KV and stage not in ('proj',):
                samp_A()
            if stage not in ('proj', 'A'):
                P.reset('AS')
                S = [P.carve('AS', 'S%d' % i, [128, TB]) for i in range(8)]
                load_w(I['w_in'], 768, 1280)
            for fc in range(2):
                if stage in ('proj', 'A'):
                    break
                ub = ubuf[fc]
                if tb > 0:
                    P.cp(ub.t[:, 0:3], ub.t[:, TB:TB + 3], r=[ub], w=[ub])
                P.mm([(bank[0].t[:, :], WA.t[:, kc, fc * 128:(fc + 1) * 128], hT.t[:, kc, :], kc == 0, kc == 7) for kc in range(8)],
                     r=[WA, hT], w=[bank[0]])
                P.mm([(bank[1].t[:, :], WA.t[:, kc, 256 + fc * 128:256 + (fc + 1) * 128], hT.t[:, kc, :], kc == 0, kc == 7) for kc in range(8)],
                     r=[WA, hT], w=[bank[1]])
                P.cp(ub.t[:, 3:3 + TB], bank[0].t[:, :], r=[bank[0]], w=[ub], eng='act')
                xc, rr, ii, aa, a2, hs, gg = S[0], S[1], S[2], S[3], S[4], S[5], S[6]
                P.ts(xc.t[:], ub.t[:, 0:TB], pcol('lru_cw0', fc), ALU.mult, pcol('lru_cb', fc), ALU.add, r=[ub, PT], w=[xc])
                for j in range(1, 4):
                    P.stt(xc.t[:], ub.t[:, j:j + TB], pcol('lru_cw%d' % j, fc), xc.t[:], ALU.mult, ALU.add, r=[ub, PT, xc], w=[xc])
                P.mm([(bank[2].t[:, :], BDa[fc].t[:, :], xc.t[:, :], True, True)], r=[BDa[fc], xc], w=[bank[2]])
                P.mm([(bank[3].t[:, :], BDx[fc].t[:, :], xc.t[:, :], True, True)], r=[BDx[fc], xc], w=[bank[3]])
                P.act(rr.t[:], bank[2].t[:], AF.Sigmoid, r=[bank[2], PT], w=[rr], bias=pcol('lru_ba', fc))
                P.act(ii.t[:], bank[3].t[:], AF.Sigmoid, r=[bank[3], PT], w=[ii], bias=pcol('lru_bx', fc))
                P.act(aa.t[:], rr.t[:], AF.Exp, r=[rr, lruc], w=[aa], scale=lruc.t[:, fc, 0:1])
                P.act(a2.t[:], rr.t[:], AF.Exp, r=[rr, lruc], w=[a2], scale=lruc.t[:, fc, 1:2])
                P.ts(a2.t[:], a2.t[:], 0.99999994, ALU.min, r=[a2], w=[a2])
                P.act(a2.t[:], a2.t[:], AF.Sqrt, r=[a2, onesf], w=[a2], scale=-1.0, bias=onesf.t[:, 0:1])
                P.tt(ii.t[:], ii.t[:], xc.t[:], ALU.mult, r=[ii, xc], w=[ii])
                P.tt(a2.t[:], a2.t[:], ii.t[:], ALU.mult, r=[a2, ii], w=[a2])
                P.op('dve', lambda e: e.tensor_tensor_scan(out=hs.t[:], data0=aa.t[:], data1=a2.t[:], initial=hstate.t[:, fc:fc + 1],
                                                           op0=ALU.mult, op1=ALU.add), r=[aa, a2, hstate], w=[hs])
                P.cp(hstate.t[:, fc:fc + 1], hs.t[:, TB - 1:TB], r=[hs], w=[hstate])
                P.act(gg.t[:], bank[1].t[:], AF.Gelu_apprx_tanh, r=[bank[1]], w=[gg])
                P.tt(ymT.t[:, 2 + fc, :], hs.t[:], gg.t[:], ALU.mult, r=[hs, gg], w=[ymT])
                if tb == NTB - 1:
                    P.dma('sp', [(O['lru_conv_p'][l, :, fc * 128:(fc + 1) * 128].rearrange("j p -> p j"), ub.t[:, TB:TB + 3])],
                          r=[ub], sembuf=ub, is_out=True, allow_slow_non_contiguous=True)
            if tb == NTB - 1 and stage not in ('proj', 'A') and WITH_S:
                samp_B()
            if tb == NTB - 1 and stage not in ('proj', 'A'):
                P.dma('sp', [(O['lru_h_p'][l, :].rearrange("(c p) -> p c", p=128), hstate.t[:, :])], r=[hstate], sembuf=hstate, is_out=True,
                      allow_slow_non_contiguous=True)
            if stage not in ('proj', 'A', 'B'):
                P.reset('AS')
                S = [P.carve('AS', 'S%d' % i, [128, TB]) for i in range(6)]
                Q = [P.carve('AS', 'Q%d' % i, [128, 256]) for i in range(8)]
                Qb = [P.carve('AS', 'Qb%d' % i, [128, 256], BF16) for i in range(7)]
                load_w(I['w_in'], 1280, 2052)
                for c4 in range(4):
                    cbf = cbuf[c4]
                    if tb > 0:
                        P.cp(cbf.t[:, 0:3], cbf.t[:, TB:TB + 3], r=[cbf], w=[cbf])
                    P.mm([(bank[0].t[:, :], WA.t[:, kc, 256 + c4 * 128:256 + (c4 + 1) * 128], hT.t[:, kc, :], kc == 0, kc == 7) for kc in range(8)],
                         r=[WA, hT], w=[bank[0]])
                    P.cp(cbf.t[:, 3:3 + TB], bank[0].t[:, :], r=[bank[0]], w=[cbf], eng='act')
                    ab = 'a' if c4 < 2 else 'b'
                    fc = c4 % 2
                    xo = S[c4]
                    P.ts(xo.t[:], cbf.t[:, 0:TB], pcol('ssd_cw0' + ab, fc), ALU.mult, pcol('ssd_cb' + ab, fc), ALU.add, r=[cbf, PT], w=[xo])
                    for j in range(1, 4):
                        P.stt(xo.t[:], cbf.t[:, j:j + TB], pcol('ssd_cw%d%s' % (j, ab), fc), xo.t[:], ALU.mult, ALU.add, r=[cbf, PT, xo], w=[xo])
                    P.act(xo.t[:], xo.t[:], AF.Silu, r=[xo], w=[xo])
                    if tb == NTB - 1:
                        P.dma('sp', [(O['ssd_conv_p'][l, :, c4 * 128:(c4 + 1) * 128].rearrange("j p -> p j"), cbf.t[:, TB:TB + 3])],
                              r=[cbf], sembuf=cbf, is_out=True, allow_slow_non_contiguous=True)
                for g in range(2):
                    P.memset(S[4 + g].t[:], 0.0, w=[S[4 + g]])
                    pr = slice(g * 64, (g + 1) * 64)
                    P.cp(S[4 + g].t[pr, :], S[3].t[pr, :], r=[S[3]], w=[S[4 + g]])
                for tt in range(4):
                    tok = slice(tt * 128, (tt + 1) * 128)
                    xtok, dtt, xdt, yv, sz = Q[0], Q[1], Q[2], Q[5], Q[6]
                    Btb, xdtd, xdtb, sTb, MTb, CpTb, ycb = Qb[0], Qb[1], Qb[2], Qb[3], Qb[4], Qb[5], Qb[6]
                    P.mm([(bank[1].t[:, 0:256], hT.t[:, kc, tok], WA.t[:, kc, 0:256], kc == 0, kc == 7) for kc in range(8)], r=[hT, WA], w=[bank[1]])
                    P.mm([(bank[1].t[:, 256:260], hT.t[:, kc, tok], WA.t[:, kc, 768:772], kc == 0, kc == 7) for kc in range(8)], r=[hT, WA], w=[bank[1]])
                    P.tr([(bank[2].t[:, c * 128:(c + 1) * 128], S[c].t[:, tok]) for c in range(3)], identf.t, r=[S[0], S[1], S[2], identf], w=[bank[2]])
                    P.cp(xtok.t[:, :], bank[2].t[:, 0:256], r=[bank[2]], w=[xtok], eng='act')
                    P.cp(Btb.t[:, 0:128], bank[2].t[:, 256:384], r=[bank[2]], w=[Btb])
                    P.tt(dtt.t[:, 0:4], bank[1].t[:, 256:260], hp4.t[:, 0:4], ALU.add, r=[bank[1], hp4], w=[dtt])
                    P.act(dtt.t[:, 0:4], dtt.t[:, 0:4], AF.Exp, r=[dtt], w=[dtt])
                    P.act(dtt.t[:, 0:4], dtt.t[:, 0:4], AF.Ln, r=[dtt, onesf], w=[dtt], bias=onesf.t[:, 0:1])
                    P.tt(dtt.t[:, 4:8], dtt.t[:, 0:4], hp4.t[:, 4:8], ALU.mult, r=[dtt, hp4], w=[dtt])
                    P.tt(xdt.t[:].rearrange("p (h d) -> p h d", h=4), xtok.t[:].rearrange("p (h d) -> p h d", h=4),
                         dtt.t[:, 0:4].unsqueeze(2).to_broadcast([128, 4, 64]), ALU.mult, r=[xtok, dtt], w=[xdt])
                    P.mm([(bank[3].t[:, 0:4], tri.t[:, :], dtt.t[:, 4:8], True, True)], r=[tri, dtt], w=[bank[3]])
                    P.mm([(bank[3].t[:, 4:8], onesf.t[:, :], dtt.t[:, 4:8], True, True)], r=[onesf, dtt], w=[bank[3]])
                    P.cp(dtt.t[:, 8:16], bank[3].t[:, 0:8], r=[bank[3]], w=[dtt])
                    P.tt(dtt.t[:, 16:20], dtt.t[:, 12:16], dtt.t[:, 8:12], ALU.subtract, r=[dtt], w=[dtt])
                    P.act(dtt.t[:, 16:20], dtt.t[:, 16:20], AF.Exp, r=[dtt], w=[dtt])
                    P.act(dtt.t[:, 20:24], dtt.t[:, 12:16], AF.Exp, r=[dtt], w=[dtt])
                    P.mm([(bank[4].t[:, h * 128:(h + 1) * 128], dtt.t[:, 4 + h:5 + h].to_broadcast([128, 128]), tri.t[:, :], True, True) for h in range(4)],
                         r=[dtt, tri], w=[bank[4]])
                    P.mm([(bank[5].t[:, g * 128:(g + 1) * 128], S[2].t[:, tok], S[4 + g].t[:, tok], True, True) for g in range(2)],
                         r=[S[2], S[4], S[5]], w=[bank[5]])
                    P.tt(xdtd.t[:].rearrange("p (h d) -> p h d", h=4), xdt.t[:].rearrange("p (h d) -> p h d", h=4),
                         dtt.t[:, 16:20].unsqueeze(2).to_broadcast([128, 4, 64]), ALU.mult, r=[xdt, dtt], w=[xdtd])
                    P.cp(xdtb.t[:, :], xdt.t[:, :], r=[xdt], w=[xdtb], eng='act')
                    P.cp(sTb.t[:, :], sT.t[:].rearrange("p h d -> p (h d)"), r=[sT], w=[sTb])
                    for h in range(4):
                        g = h // 2
                        Dm = Q[3 + (h % 2)]
                        hs_ = slice((h % 2) * 128, (h % 2 + 1) * 128)
                        P.stt(Dm.t[:, 0:128], bank[4].t[:, h * 128:(h + 1) * 128], dtt.t[:, 8 + h:9 + h], negmask.t[:, :], ALU.subtract, ALU.add,
                              r=[bank[4], dtt, negmask], w=[Dm])
                        P.act(Dm.t[:, 0:128], Dm.t[:, 0:128], AF.Exp, r=[Dm], w=[Dm])
                        P.tt(MTb.t[:, hs_], Dm.t[:, 0:128], bank[5].t[:, g * 128:(g + 1) * 128], ALU.mult, r=[Dm, bank[5]], w=[MTb])
                        P.act(Dm.t[:, 128:256], bank[4].t[:, h * 128:(h + 1) * 128], AF.Exp, r=[bank[4]], w=[Dm])
                        P.tt(CpTb.t[:, hs_], S[4 + g].t[:, tok], Dm.t[:, 128:256], ALU.mult, r=[S[4 + g], Dm], w=[CpTb])
                        P.mm([(bank[6].t[:, h * 64:(h + 1) * 64], MTb.t[:, hs_], xdtb.t[:, h * 64:(h + 1) * 64], True, False),
                              (bank[6].t[:, h * 64:(h + 1) * 64], CpTb.t[:, hs_], sTb.t[:, h * 64:(h + 1) * 64], False, True)],
                             r=[MTb, CpTb, xdtb, sTb], w=[bank[6]])
                    P.mm([(bank[7].t[:, h * 64:(h + 1) * 64], Btb.t[:, 0:128], xdtd.t[:, h * 64:(h + 1) * 64], True, True) for h in range(4)],
                         r=[Btb, xdtd], w=[bank[7]])
                    for h in range(4):
                        pr = slice((h // 2) * 64, (h // 2 + 1) * 64)
                        P.stt(sT.t[pr, h, :], sT.t[pr, h, :], dtt.t[pr, 20 + h:21 + h], bank[7].t[pr, h * 64:(h + 1) * 64], ALU.mult, ALU.add,
                              r=[sT, dtt, bank[7]], w=[sT])
                    P.tt(yv.t[:].rearrange("p (h d) -> p h d", h=4), xtok.t[:].rearrange("p (h d) -> p h d", h=4),
                         hp4.t[:, 8:12].unsqueeze(2).to_broadcast([128, 4, 64]), ALU.mult, r=[xtok, hp4], w=[yv])
                    P.tt(yv.t[:, :], yv.t[:, :], bank[6].t[:, 0:256], ALU.add, r=[yv, bank[6]], w=[yv])
                    P.act(sz.t[:, :], bank[1].t[:, 0:256], AF.Silu, r=[bank[1]], w=[sz])
                    P.tt(yv.t[:, :], yv.t[:, :], sz.t[:, :], ALU.mult, r=[yv, sz], w=[yv])
                    rmsnorm_tile(yv.t[:, :], yv, ycb, width=256, gain=nwbc.t[:, :], gbuf_=nwbc)
                    psT = bank[3].t[:].bitcast(BF16)
                    P.tr([(psT[:, c * 128:(c + 1) * 128], ycb.t[:, c * 128:(c + 1) * 128]) for c in range(2)], identb.t, r=[ycb, identb], w=[bank[3]])
                    P.cp(ymT.t[:, 4:6, tok], psT[:, 0:256].rearrange("p (c t) -> p c t", c=2), r=[bank[3]], w=[ymT], eng='act')
                if tb == NTB - 1:
                    P.tr([(bank[2].t[:, c * 128:(c + 1) * 128], sT.t[:].rearrange("p h d -> p (h d)")[:, c * 128:(c + 1) * 128]) for c in range(2)],
                         identf.t, r=[sT, identf], w=[bank[2]])
                    P.cp(Q[7].t[:, :], bank[2].t[:, 0:256], r=[bank[2]], w=[Q[7]])
                    P.dma('sp', [(O['ssd_p'][l, h, :, :], Q[7].t[(h % 2) * 64:(h % 2 + 1) * 64, (h // 2) * 192:(h // 2) * 192 + 64]) for h in range(4)],
                          r=[Q[7]], sembuf=Q[7], is_out=True)
                    if WITH_S:
                        P.reset('AS')
                        samp_C()
            if stage not in ('proj', 'A', 'B', 'C'):
                P.reset('AS')
                load_w(I['w_in'], 2052, 2948)
                RW = [P.carve('AS', 'RW%d' % i, [128, 2, 128]) for i in range(16)]
                Wst = [P.carve('AS', 'Wst%d' % i, [128, 2, 64]) for i in range(2)]
                tmpst = [P.carve('AS', 'tmpst%d' % i, [128, 2, 64]) for i in range(2)]
                vtok = P.carve('AS', 'vtok', [128, 256], BF16)
                xT = P.carve('AS', 'xT', [128, 128])
                fl = lambda b: b.t[:].rearrange("p a b -> p (a b)")
                rT, kT_, vT, dl, dT, aT, gT, kk, kp, bT, nk0, nk1, rm0, rm1, t1, t2 = RW
                for sb in range(4):
                    tok = slice(sb * 128, (sb + 1) * 128)
                    for c7 in range(7):
                        cbx = cb7[c7]
                        P.mm([(bank[0].t[:, 0:128], WA.t[:, kc, c7 * 128:(c7 + 1) * 128], hT.t[:, kc, tok], kc == 0, kc == 7) for kc in range(8)],
                             r=[WA, hT], w=[bank[0]])
                        P.cp(cbx.t[:, 1:129], bank[0].t[:, 0:128], r=[bank[0]], w=[cbx], eng='act')
                        if c7 < 6:
                            dbuf = (rT, kT_, vT)[c7 // 2]
                            dst = dbuf.t[:, c7 % 2, :]
                            mu = pcol(('rw_mu_r', 'rw_mu_k', 'rw_mu_v')[c7 // 2], c7 % 2)
                        else:
                            dbuf = xT
                            dst = xT.t[:, :]
                            mu = pcol('rw_mu_x', 0)
                        P.tt(dl.t[:, 0, :], cbx.t[:, 0:128], cbx.t[:, 1:129], ALU.subtract, r=[cbx], w=[dl])
                        P.stt(dst, dl.t[:, 0, :], mu, cbx.t[:, 1:129], ALU.mult, ALU.add, r=[dl, PT, cbx], w=[dbuf])
                        P.cp(cbx.t[:, 0:1], cbx.t[:, 128:129], r=[cbx], w=[cbx])
                    P.act(xT.t[0:32, :], xT.t[0:32, :], AF.Tanh, r=[xT], w=[xT])
                    P.act(xT.t[64:128, :], xT.t[64:128, :], AF.Sigmoid, r=[xT], w=[xT])
                    for hp in range(2):
                        cols = slice(hp * 128, (hp + 1) * 128)
                        P.mm([(bank[1].t[:, 0:128], Wlow.t[0:32, cols], xT.t[0:32, :], True, True)], r=[Wlow, xT], w=[bank[1]])
                        P.act(dT.t[:, hp, :], bank[1].t[:, 0:128], AF.Sigmoid, r=[bank[1], PT], w=[dT], bias=pcol('rw_w0', hp))
                        P.act(dT.t[:, hp, :], dT.t[:, hp, :], AF.Exp, r=[dT], w=[dT], scale=-0.6065306597126334)
                        P.mm([(bank[1].t[:, 128:256], Wlow.t[32:64, cols], xT.t[32:64, :], True, True)], r=[Wlow, xT], w=[bank[1]])
                        P.act(aT.t[:, hp, :], bank[1].t[:, 128:256], AF.Sigmoid, r=[bank[1], PT], w=[aT], bias=pcol('rw_a0', hp))
                        P.mm([(bank[1].t[:, 256:384], Wlow.t[64:128, cols], xT.t[64:128, :], True, True)], r=[Wlow, xT], w=[bank[1]])
                        P.cp(gT.t[:, hp, :], bank[1].t[:, 256:384], r=[bank[1]], w=[gT], eng='act')
                        P.ts(kk.t[:, hp, :], kT_.t[:, hp, :], pcol('rw_kk', hp), ALU.mult, r=[kT_, PT], w=[kk])
                    P.tt(fl(t1), fl(kk), fl(kk), ALU.mult, r=[kk], w=[t1])
                    P.mm([(bank[2].t[:, 0:256], BDones.t[:, :], fl(t1), True, True)], r=[BDones, t1], w=[bank[2]])
                    P.act(fl(t2), bank[2].t[:, 0:256], AF.Sqrt, r=[bank[2]], w=[t2])
                    P.ts(fl(t2), fl(t2), 1e-12, ALU.max, r=[t2], w=[t2])
                    P.op('dve', lambda e: e.reciprocal(out=fl(t2), in_=fl(t2)), r=[t2], w=[t2])
                    P.tt(fl(kk), fl(kk), fl(t2), ALU.mult, r=[kk, t2], w=[kk])
                    P.ts(fl(nk0), fl(kk), hmask.t[:, 0:1], ALU.mult, -1.0, ALU.mult, r=[kk, hmask], w=[nk0])
                    P.ts(fl(nk1), fl(kk), hmask.t[:, 1:2], ALU.mult, -1.0, ALU.mult, r=[kk, hmask], w=[nk1])
                    for hp in range(2):
                        P.ts(t1.t[:, hp, :], aT.t[:, hp, :], pcol('rw_ka', hp), ALU.mult, rwc.t[:, hp:hp + 1], ALU.add, r=[aT, PT, rwc], w=[t1])
                    P.tt(fl(kp), fl(kT_), fl(t1), ALU.mult, r=[kT_, t1], w=[kp])
                    P.tt(fl(bT), fl(kk), fl(aT), ALU.mult, r=[kk, aT], w=[bT])
                    P.ts(fl(rm0), fl(rT), hmask.t[:, 0:1], ALU.mult, r=[rT, hmask], w=[rm0])
                    P.ts(fl(rm1), fl(rT), hmask.t[:, 1:2], ALU.mult, r=[rT, hmask], w=[rm1])
                    P.tr([(bank[2].t[:, 256 + hp * 128:256 + (hp + 1) * 128], vT.t[:, hp, :]) for hp in range(2)], identf.t, r=[vT, identf], w=[bank[2]])
                    P.cp(vtok.t[:, :], bank[2].t[:, 256:512], r=[bank[2]], w=[vtok])
                    for hp in range(2):
                        P.stt(t1.t[:, hp, :], rT.t[:, hp, :], pcol('rw_rk', hp), kp.t[:, hp, :], ALU.mult, ALU.mult, r=[rT, PT, kp], w=[t1])
                    P.mm([(bank[3].t[:, 0:256], BDones.t[:, :], fl(t1), True, True)], r=[BDones, t1], w=[bank[3]])
                    P.tt(fl(t2), bank[3].t[:, 0:256], fl(vT), ALU.mult, r=[bank[3], vT], w=[t2])
                    vtv = vtok.t[:, :].rearrange("p (hp h2 v) -> p hp h2 v", hp=2, h2=2)
                    nks = (nk0, nk1)
                    rms = (rm0, rm1)
                    Yb = bank[4]
                    vbank = (bank[5], bank[7])
                    def emit_vb(t):
                        vb_ = vbank[t % 2]
                        P.mm([(vb_.t[h2 * 64:(h2 + 1) * 64, 0:128].rearrange("p (a b) -> p a b", a=2), identb.t[:, t:t + 1].to_broadcast([128, 64]),
                               vtv[:, :, h2, :], True, True) for h2 in range(2)], r=[identb, vtok], w=[vb_])
                    def emit_y(t):
                        P.mm([(Yb.t[h2 * 64:(h2 + 1) * 64, hp * 128 + t:hp * 128 + t + 1], ST.t[:, hp, :], rms[h2].t[:, hp, t:t + 1], True, True)
                              for hp in range(2) for h2 in range(2)], r=[ST, rm0, rm1], w=[Yb])
                    emit_vb(0)
                    for t in range(128):
                        ws = Wst[t % 2]
                        tm = tmpst[t % 2]
                        vb_ = vbank[t % 2]
                        P.mm([(bank[6].t[h2 * 64:(h2 + 1) * 64, hp * 64:(hp + 1) * 64], nks[h2].t[:, hp, t:t + 1].to_broadcast([128, 64]), ST.t[:, hp, :], True, True)
                              for hp in range(2) for h2 in range(2)], r=[nk0, nk1, ST], w=[bank[6]])
                        if t > 0:
                            emit_y(t - 1)
                        if t + 1 < 128:
                            emit_vb(t + 1)
                        for hp in range(2):
                            P.act(ws.t[:, hp, :], vb_.t[:, hp * 64:(hp + 1) * 64], AF.Copy, r=[vb_, kp], w=[ws], scale=kp.t[:, hp, t:t + 1])
                        for hp in range(2):
                            P.stt(tm.t[:, hp, :], bank[6].t[:, hp * 64:(hp + 1) * 64], bT.t[:, hp, t:t + 1], ws.t[:, hp, :], ALU.mult, ALU.add,
                                  r=[bank[6], bT, ws], w=[tm])
                        for hp in range(2):
                            P.stt(ST.t[:, hp, :], ST.t[:, hp, :], dT.t[:, hp, t:t + 1], tm.t[:, hp, :], ALU.mult, ALU.add, r=[ST, dT, tm], w=[ST])
                    emit_y(127)
                    cen, sq = kk, kp
                    P.cp(fl(t1), Yb.t[:, 0:256], r=[Yb], w=[t1], eng='act')
                    P.mm([(bank[3].t[:, 0:256], BDones.t[:, :], fl(t1), True, True)], r=[BDones, t1], w=[bank[3]])
                    P.stt(fl(cen), bank[3].t[:, 0:256], -1.0 / 64, fl(t1), ALU.mult, ALU.add, r=[bank[3], t1], w=[cen])
                    P.tt(fl(sq), fl(cen), fl(cen), ALU.mult, r=[cen], w=[sq])
                    P.mm([(bank[3].t[:, 256:512], BDones.t[:, :], fl(sq), True, True)], r=[BDones, sq], w=[bank[3]])
                    P.act(fl(sq), bank[3].t[:, 256:512], AF.Sqrt, r=[bank[3], epsb], w=[sq], scale=1.0 / 64, bias=epsb.t[:, 1:2])
                    P.op('dve', lambda e: e.reciprocal(out=fl(sq), in_=fl(sq)), r=[sq], w=[sq])
                    P.tt(fl(cen), fl(cen), fl(sq), ALU.mult, r=[cen, sq], w=[cen])
                    for hp in range(2):
                        P.ts(cen.t[:, hp, :], cen.t[:, hp, :], pcol('rw_lnw', hp), ALU.mult, pcol('rw_lnb', hp), ALU.add, r=[cen, PT], w=[cen])
                    P.tt(fl(cen), fl(cen), fl(t2), ALU.add, r=[cen, t2], w=[cen])
                    P.tt(ymT.t[:, 6:8, tok], cen.t[:, :, :], gT.t[:, :, :], ALU.mult, r=[cen, gT], w=[ymT])
                if tb == NTB - 1:
                    P.dma('sp', [(O['rwkv_shift_p'][l, c7 * 128:(c7 + 1) * 128].rearrange("(p o) -> p o", o=1), cb7[c7].t[:, 128:129]) for c7 in range(7)],
                          r=cb7, sembuf=cb7[0], is_out=True, allow_slow_non_contiguous=True)
                    P.tr([(bank[2].t[:, 0:128], ST.t[:].rearrange("p a b -> p (a b)"))], identf.t, r=[ST, identf], w=[bank[2]])
                    P.cp(fl(t1)[:, 0:128], bank[2].t[:, 0:128], r=[bank[2]], w=[t1])
                    P.dma('sp', [(O['rwkv_p'][l, h, :, :], fl(t1)[(h // 2) * 64:(h // 2 + 1) * 64, (h % 2) * 64:(h % 2 + 1) * 64]) for h in range(4)],
                          r=[t1], sembuf=t1, is_out=True)
                    if WITH_S:
                        P.reset('AS')
                        samp_D()
            if stage in ('all', 'W', 'X', 'PEER'):
                P.reset('AS')
                load_w(I['w_out'], 0, 1024)
                for tt in range(4):
                    i = tb * 4 + tt
                    tok = slice(tt * 128, (tt + 1) * 128)
                    for half in range(2):
                        hs_ = slice(half * 512, (half + 1) * 512)
                        P.mm([(bank[half].t[:, :], ymT.t[:, kc, tok], WA.t[:, kc, hs_], kc == 0, kc == 7) for kc in range(8)], r=[ymT, WA], w=[bank[half]])
                        P.tt(xres.t[:, i, hs_], xres.t[:, i, hs_], bank[half].t[:, :], ALU.add, r=[xres, bank[half]], w=[xres])
                if tb == NTB - 1 and WITH_S:
                    samp_W()
            if 'ymTs' in DBG and l == 0 and tb == NTB - 1:
                dbg('ymTs', DBG['ymTs'], ymTs.t[:, :, :], [ymTs])
            if 'ymT' in DBG and l == 0:
                dbg('ymT', DBG['ymT'][:, :, tb * TB:(tb + 1) * TB], ymT.t[:, :, :], [ymT])
        if 'x1' in DBG and l == 0:
            dbg('x1', DBG['x1'].rearrange("(n p) c -> p n c", p=128), xres.t[:, :, :], [xres])
        if stage in ('all', 'X', 'PEER'):
            P.reset('AL'); P.reset('AS')
            hT = P.carve('AL', 'hT', [128, 8, TB], BF16)
            qT = P.carve('AL', 'qT', [128, 8, TB], BF16)
            oT = P.carve('AL', 'oT', [128, 8, TB], BF16)
            W1 = P.carve('AL', 'WA', [128, 8, D], BF16)
            W2 = P.carve('AL', 'WB', [128, 8, D], BF16)
            memT = P.carve('AL', 'memT', [128, 8, 256], BF16)
            KTm = P.carve('AL', 'KTm', [128, 8, 256], BF16)
            Vx = P.carve('AL', 'Vx', [128, 2, D], BF16)
            onesb = P.carve('AL', 'onesb', [128, 128], BF16)
            memf = P.carve('AS', 'memf', [128, 2, D])
            memb = P.carve('AS', 'memb', [128, 2, D], BF16)
            kst = [P.carve('AS', 'kst%d' % i, [128, D]) for i in range(2)]
            P.memset(onesb.t[:], 1.0, w=[onesb])
            def load2(Wb, src):
                for kc in range(8):
                    P.dma('pool', [(Wb.t[:, kc, :], src[l, kc * 128:(kc + 1) * 128, :])], w=[Wb])
            load2(W1, I['x_wk']); load2(W2, I['x_wv'])
            P.dma('sp', [(memf.t[:, mt, :], I['memp'][mt * 128:(mt + 1) * 128, :]) for mt in range(2)], w=[memf])
            P.cp(memb.t[:].rearrange("p a b -> p (a b)"), memf.t[:].rearrange("p a b -> p (a b)"), r=[memf], w=[memb], eng='act')
            for mt in range(2):
                psT = bank[6].t[:].bitcast(BF16)
                P.tr([(psT[:, kc * 128:(kc + 1) * 128], memb.t[:, mt, kc * 128:(kc + 1) * 128]) for kc in range(8)], identb.t, r=[memb, identb], w=[bank[6]])
                P.cp(memT.t[:, :, mt * 128:(mt + 1) * 128], psT.rearrange("p (k t) -> p k t", k=8), r=[bank[6]], w=[memT])
            for (Wb, oname, isv) in ((W1, 'mem_k_p', False), (W2, 'mem_v_p', True)):
                for mt in range(2):
                    stg = kst[mt]
                    for half in range(2):
                        hs_ = slice(half * 512, (half + 1) * 512)
                        P.mm([(bank[half].t[:, :], memT.t[:, kc, mt * 128:(mt + 1) * 128], Wb.t[:, kc, hs_], kc == 0, kc == 7) for kc in range(8)],
                             r=[memT, Wb], w=[bank[half]])
                        P.cp(stg.t[:, hs_], bank[half].t[:, :], r=[bank[half]], w=[stg], eng='act')
                        if isv:
                            P.cp(Vx.t[:, mt, hs_], bank[half].t[:, :], r=[bank[half]], w=[Vx])
                    P.dma('sp', [(O[oname][l, mt * 128:(mt + 1) * 128, :], stg.t[:, :])], r=[stg], sembuf=stg, is_out=True)
            for dc in range(8):
                P.mm([(bank[2].t[:, 0:256], W1.t[:, kc, dc * 128:(dc + 1) * 128], memT.t[:, kc, :], kc == 0, kc == 7) for kc in range(8)],
                     r=[W1, memT], w=[bank[2]])
                P.cp(KTm.t[:, dc, :], bank[2].t[:, 0:256], r=[bank[2]], w=[KTm])
            load2(W1, I['x_wq']); load2(W2, I['x_wo'])
            P.dma('sp', [(gbc.t[:], I['norm_x'][l:l + 1, :].to_broadcast([128, D]))], w=[gbc])
            for tb in range(NTB):
                P.reset('AS')
                hb = [P.carve('AS', 'hb%d' % i, [128, D], BF16) for i in range(2)]
                PTx = P.carve('AS', 'PTx', [128, 2, TB], BF16)
                rec = P.carve('AS', 'rec', [128, TB])
                for tt in range(4):
                    i = tb * 4 + tt
                    hbuf = hb[i % 2]
                    rmsnorm_tile(xres.t[:, i, :], xres, hbuf)
                    psT = bank[6].t[:].bitcast(BF16)
                    P.tr([(psT[:, kc * 128:(kc + 1) * 128], hbuf.t[:, kc * 128:(kc + 1) * 128]) for kc in range(8)], identb.t,
                         r=[hbuf, identb], w=[bank[6]])
                    P.cp(hT.t[:, :, tt * 128:(tt + 1) * 128], psT.rearrange("p (k t) -> p k t", k=8), r=[bank[6]], w=[hT], eng='act')
                for dc in range(8):
                    P.mm([(bank[0].t[:, :], W1.t[:, kc, dc * 128:(dc + 1) * 128], hT.t[:, kc, :], kc == 0, kc == 7) for kc in range(8)],
                         r=[W1, hT], w=[bank[0]])
                    P.cp(qT.t[:, dc, :], bank[0].t[:, :], r=[bank[0]], w=[qT], eng=('act' if dc % 2 else 'dve'))
                for h in range(4):
                    for mt in range(2):
                        P.mm([(bank[1 + mt].t[:, :], KTm.t[:, 2 * h + c, mt * 128:(mt + 1) * 128], qT.t[:, 2 * h + c, :], c == 0, c == 1) for c in range(2)],
                             r=[KTm, qT], w=[bank[1 + mt]])
                        P.act(PTx.t[:, mt, :], bank[1 + mt].t[:, :], AF.Exp, r=[bank[1 + mt]], w=[PTx], scale=1.0 / 16)
                    P.mm([(bank[3].t[:, :], onesb.t[:, :], PTx.t[:, mt, :], mt == 0, mt == 1) for mt in range(2)], r=[onesb, PTx], w=[bank[3]])
                    P.op('dve', lambda e: e.reciprocal(out=rec.t[:, :], in_=bank[3].t[:, :]), r=[bank[3]], w=[rec])
                    for c in range(2):
                        P.mm([(bank[4 + c].t[:, :], Vx.t[:, mt, h * 256 + c * 128:h * 256 + (c + 1) * 128], PTx.t[:, mt, :], mt == 0, mt == 1) for mt in range(2)],
                             r=[Vx, PTx], w=[bank[4 + c]])
                        P.tt(oT.t[:, 2 * h + c, :], bank[4 + c].t[:, :], rec.t[:, :], ALU.mult, r=[bank[4 + c], rec], w=[oT])
                for tt in range(4):
                    i = tb * 4 + tt
                    tok = slice(tt * 128, (tt + 1) * 128)
                    for half in range(2):
                        hs_ = slice(half * 512, (half + 1) * 512)
                        P.mm([(bank[half].t[:, :], oT.t[:, kc, tok], W2.t[:, kc, hs_], kc == 0, kc == 7) for kc in range(8)], r=[oT, W2], w=[bank[half]])
                        P.tt(xres.t[:, i, hs_], xres.t[:, i, hs_], bank[half].t[:, :], ALU.add, r=[xres, bank[half]], w=[xres])
            if WITH_S:
                P.reset('AS')
                samp_norm(hTs)
                qS = P.carve('AS', 'qS', [4, D]); oTs = P.carve('AS', 'oTs', [128, 8, 4], BF16)
                Kc = P.carve('AS', 'Kc', [128, 2, D]); Vc = P.carve('AS', 'Vc', [128, 2, D])
                prodx = P.carve('AS', 'prodx', [128, D]); scx = P.carve('AS', 'scx', [128, 2, 4]); obx = P.carve('AS', 'obx', [4, D + 4])
                dmx = P.carve('AS', 'dmx', [4, D])
                P.dma('sp', [(dmx.t[:, :], I['c_dmask4x'][:, :])], w=[dmx])
                for half in range(2):
                    hs_ = slice(half * 512, (half + 1) * 512)
                    P.mm([(bank[half].t[0:4, :], hTs.t[:, kc, :], W1.t[:, kc, hs_], kc == 0, kc == 7) for kc in range(8)], r=[hTs, W1], w=[bank[half]])
                    P.cp(qS.t[:, hs_], bank[half].t[0:4, :], r=[bank[half]], w=[qS])
                for s_ in range(4):
                    P.dma('sp', [(Kc.t[:, mt, :], I['cmk'][l, s_, mt * 128:(mt + 1) * 128, :]) for mt in range(2)], w=[Kc])
                    P.dma('sp', [(Vc.t[:, mt, :], I['cmv'][l, s_, mt * 128:(mt + 1) * 128, :]) for mt in range(2)], w=[Vc])
                    for half in range(2):
                        P.mm([(bank[2 + half].t[:, :], identf.t[0:4, s_:s_ + 1].to_broadcast([4, 128]), qS.t[0:4, half * 512:(half + 1) * 512], True, True)],
                             r=[identf, qS], w=[bank[2 + half]])
                    for mt in range(2):
                        for half in range(2):
                            hs_ = slice(half * 512, (half + 1) * 512)
                            P.tt(prodx.t[:, hs_], Kc.t[:, mt, hs_], bank[2 + half].t[:, :], ALU.mult, r=[Kc, bank[2 + half]], w=[prodx])
                        P.op('dve', lambda e: e.tensor_reduce(out=scx.t[:, mt, :], in_=prodx.t[:, :].rearrange("p (h d) -> p h d", h=4), op=ALU.add, axis=AX.X), r=[prodx], w=[scx])
                    P.act(scx.t[:].rearrange("p a b -> p (a b)"), scx.t[:].rearrange("p a b -> p (a b)"), AF.Exp, r=[scx], w=[scx], scale=1.0 / 16)
                    for half in range(2):
                        hs_ = slice(half * 512, (half + 1) * 512)
                        P.mm([(bank[4 + half].t[0:4, :], scx.t[:, mt, :], Vc.t[:, mt, hs_], mt == 0, mt == 1) for mt in range(2)], r=[scx, Vc], w=[bank[4 + half]])
                    P.mm([(bank[6].t[0:4, 0:1], scx.t[:, mt, :], onesf.t[:, 0:1], mt == 0, mt == 1) for mt in range(2)], r=[scx, onesf], w=[bank[6]])
                    P.op('dve', lambda e: e.reciprocal(out=obx.t[:, D:D + 1], in_=bank[6].t[0:4, 0:1]), r=[bank[6]], w=[obx])
                    for half in range(2):
                        hs_ = slice(half * 512, (half + 1) * 512)
                        P.stt(obx.t[:, hs_], bank[4 + half].t[0:4, :], obx.t[:, D:D + 1], dmx.t[:, hs_], ALU.mult, ALU.mult, r=[bank[4 + half], obx, dmx], w=[obx])
                    P.mm([(bank[7].t[:, s_ * 8 + c:s_ * 8 + c + 1], obx.t[0:4, c * 128:(c + 1) * 128], onesf.t[0:4, 0:1], True, True) for c in range(8)],
                         r=[obx, onesf], w=[bank[7]])
                P.cp(oTs.t[:, :, :], bank[7].t[:, 0:32].rearrange("p (s c) -> p c s", c=8), r=[bank[7]], w=[oTs])
                for half in range(2):
                    hs_ = slice(half * 512, (half + 1) * 512)
                    P.mm([(bank[half].t[0:4, :], oTs.t[:, kc, :], W2.t[:, kc, hs_], kc == 0, kc == 7) for kc in range(8)], r=[oTs, W2], w=[bank[half]])
                    P.tt(xsp.t[0:4, hs_], xsp.t[0:4, hs_], bank[half].t[0:4, :], ALU.add, r=[xsp, bank[half]], w=[xsp])
            if 'x2' in DBG and l == 0:
                dbg('x2', DBG['x2'].rearrange("(n p) c -> p n c", p=128), xres.t[:, :, :], [xres])
        if stage in ('all', 'PEER'):
            NTP = NT + 1 if WITH_S else NT
            xtile = lambda i: (xres.t[:, i, :] if i < NT else xsp.t[:, :])
            xbuf_ = lambda i: (xres if i < NT else xsp)
            P.reset('AL'); P.reset('AS')
            hTall = P.carve('AL', 'hTall', [128, 8, T + 128], BF16)
            Wpq = P.carve('AL', 'Wpq', [128, 8, 2048], BF16)
            skT = P.carve('AL', 'skT', [128, 16, 128], BF16)
            iotar = P.carve('AL', 'iotar', [128, 128])
            bd16 = P.carve('AL', 'bd16', [128, 8, 16], BF16)
            WcT = P.carve('AL', 'WcT', [128, 16, 128], BF16)
            ixT = P.carve('AL', 'ixT', [128, 2, 128])
            P.dma('sp', [(gbc.t[:], I['norm_ffn'][l:l + 1, :].to_broadcast([128, D]))], w=[gbc])
            for kc in range(8):
                P.dma('pool', [(Wpq.t[:, kc, :], I['peer_wq'][l, kc * 128:(kc + 1) * 128, :])], w=[Wpq])
            P.dma('sp', [(iotar.t[:, :], I['c_iota'][:, :])], w=[iotar])
            P.dma('pool', [(bd16.t[:].rearrange("p a b -> p (a b)"), I['c_bd16'][:, :])], w=[bd16])
            P.reset('AS')
            skf = P.carve('AS', 'skf', [128, 16, 128]); skb = P.carve('AS', 'skb', [128, 16, 128], BF16)
            P.dma('sp', [(skf.t[:, hc, :], I['peer_subkeys'][l, hc, :, :]) for hc in range(16)], w=[skf])
            P.cp(skb.t[:].rearrange("p a b -> p (a b)"), skf.t[:].rearrange("p a b -> p (a b)"), r=[skf], w=[skb], eng='act')
            for q4 in range(2):
                psT = bank[6].t[:].bitcast(BF16)
                P.tr([(psT[:, c * 128:(c + 1) * 128], skb.t[:, q4 * 8 + c, :]) for c in range(8)], identb.t, r=[skb, identb], w=[bank[6]])
                P.cp(skT.t[:, q4 * 8:(q4 + 1) * 8, :], psT.rearrange("p (k t) -> p k t", k=8), r=[bank[6]], w=[skT])
            for i in range(NTP):
                tokg = slice(i * 128, (i + 1) * 128)
                P.reset('AS')
                hbuf = P.carve('AS', 'hb0', [128, D], BF16)
                qTt = P.carve('AS', 'qTt', [128, 16, 128], BF16)
                sc = P.carve('AS', 'sc', [128, 16, 128]); scr = P.carve('AS', 'scr', [128, 16, 128])
                tops = P.carve('AS', 'tops', [128, 16, 16]); idx = P.carve('AS', 'idx', [128, 16, 16], U32)
                ixf = P.carve('AS', 'ixf', [128, 2, 128])
                cand = Buf('cand', sc.t[:].rearrange("p a b -> p (a b)").rearrange("p (h x) -> p h x", h=8)); sc = cand_alias(sc, cand)
                cscr = Buf('cscr', scr.t[:].rearrange("p a b -> p (a b)").rearrange("p (h x) -> p h x", h=8)); scr = cand_alias(scr, cscr)
                tsv = P.carve('AS', 'tsv', [128, 8, 16]); zz = P.carve('AS', 'zz', [128, 8, 4])
                wg = P.carve('AS', 'wg', [128, 8, 256])
                rmsnorm_tile(xtile(i), xbuf_(i), hbuf)
                psT = bank[6].t[:].bitcast(BF16)
                P.tr([(psT[:, kc * 128:(kc + 1) * 128], hbuf.t[:, kc * 128:(kc + 1) * 128]) for kc in range(8)], identb.t, r=[hbuf, identb], w=[bank[6]])
                P.cp(hTall.t[:, :, tokg], psT.rearrange("p (k t) -> p k t", k=8), r=[bank[6]], w=[hTall], eng='act')
                for hc in range(16):
                    bk = bank[hc % 2]
                    P.mm([(bk.t[:, 0:128], Wpq.t[:, kc, hc * 128:(hc + 1) * 128], hTall.t[:, kc, tokg], kc == 0, kc == 7) for kc in range(8)], r=[Wpq, hTall], w=[bk])
                    P.cp(qTt.t[:, hc, :], bk.t[:, 0:128], r=[bk], w=[qTt], eng=('act' if hc % 2 else 'dve'))
                for q4 in range(4):
                    bk = bank[2 + q4 % 2]
                    P.mm([(bk.t[:, c * 128:(c + 1) * 128], qTt.t[:, q4 * 4 + c, :], skT.t[:, q4 * 4 + c, :], True, True) for c in range(4)], r=[qTt, skT], w=[bk])
                    P.cp(sc.t[:, q4 * 4:(q4 + 1) * 4, :], bk.t[:, :].rearrange("p (c k) -> p c k", c=4), r=[bk], w=[sc], eng='act')
                for hc in range(16):
                    P.op('dve', lambda e: e.max(out=tops.t[:, hc, 0:8], in_=sc.t[:, hc, :]), r=[sc], w=[tops])
                    P.op('dve', lambda e: e.max_index(out=idx.t[:, hc, 0:8], in_max=tops.t[:, hc, 0:8], in_values=sc.t[:, hc, :]), r=[sc, tops], w=[idx])
                    P.op('dve', lambda e: e.match_replace(out=scr.t[:, hc, :], in_to_replace=tops.t[:, hc, 0:8], in_values=sc.t[:, hc, :], imm_value=-1e30),
                         r=[sc, tops], w=[scr])
                    P.op('dve', lambda e: e.max(out=tops.t[:, hc, 8:16], in_=scr.t[:, hc, :]), r=[scr], w=[tops])
                    P.op('dve', lambda e: e.max_index(out=idx.t[:, hc, 8:16], in_max=tops.t[:, hc, 8:16], in_values=scr.t[:, hc, :]), r=[scr, tops], w=[idx])
                tv = tops.t[:].rearrange("p (h c) i -> p h c i", c=2)
                iv = idx.t[:].rearrange("p (h c) i -> p h c i", c=2)
                for c in range(2):
                    P.cp(ixf.t[:, c, :].rearrange("p (h i) -> p h i", h=8), iv[:, :, c, :], r=[idx], w=[ixf])
                c4 = cand.t[:].rearrange("p h (i j) -> p h i j", i=16)
                for h in range(8):
                    P.tt(c4[:, h, :, :], tv[:, h, 0, :].unsqueeze(2).to_broadcast([128, 16, 16]), tv[:, h, 1, :].unsqueeze(1).to_broadcast([128, 16, 16]), ALU.add,
                         r=[tops], w=[cand])
                for h in range(8):
                    P.op('dve', lambda e: e.max(out=tsv.t[:, h, 0:8], in_=cand.t[:, h, :]), r=[cand], w=[tsv])
                    P.op('dve', lambda e: e.match_replace(out=cscr.t[:, h, :], in_to_replace=tsv.t[:, h, 0:8], in_values=cand.t[:, h, :], imm_value=-1e30),
                         r=[cand, tsv], w=[cscr])
                    P.op('dve', lambda e: e.max(out=tsv.t[:, h, 8:16], in_=cscr.t[:, h, :]), r=[cscr], w=[tsv])
                P.tt(cscr.t[:, :, 0:16], tsv.t[:, :, :], tsv.t[:, :, 0:1].to_broadcast([128, 8, 16]), ALU.subtract, r=[tsv], w=[cscr])
                P.act(cscr.t[:, :, 0:16], cscr.t[:, :, 0:16], AF.Exp, r=[cscr], w=[cscr])
                P.op('dve', lambda e: e.tensor_reduce(out=zz.t[:, :, 0], in_=cscr.t[:, :, 0:16], op=ALU.add, axis=AX.X), r=[cscr], w=[zz])
                P.op('dve', lambda e: e.reciprocal(out=zz.t[:, :, 1], in_=zz.t[:, :, 0]), r=[zz], w=[zz])
                P.tt(wg.t[:], cand.t[:], tsv.t[:, :, 0:1].to_broadcast([128, 8, 256]), ALU.subtract, r=[cand, tsv], w=[wg])
                P.act(wg.t[:].rearrange("p a b -> p (a b)"), wg.t[:].rearrange("p a b -> p (a b)"), AF.Exp, r=[wg], w=[wg])
                P.tt(cscr.t[:], cand.t[:], tsv.t[:, :, 15:16].to_broadcast([128, 8, 256]), ALU.is_ge, r=[cand, tsv], w=[cscr])
                P.tt(wg.t[:], wg.t[:], cscr.t[:], ALU.mult, r=[wg, cscr], w=[wg])
                P.tt(wg.t[:], wg.t[:], zz.t[:, :, 1:2].to_broadcast([128, 8, 256]), ALU.mult, r=[wg, zz], w=[wg])
                P.tr([(bank[4].t[:, c * 128:(c + 1) * 128], ixf.t[:, c, :]) for c in range(2)], identf.t, r=[ixf, identf], w=[bank[4]])
                P.cp(ixT.t[:].rearrange("p a b -> p (a b)"), bank[4].t[:, 0:256], r=[bank[4]], w=[ixT])
                w4 = wg.t[:].rearrange("p h (i j) -> p h i j", i=16)
                wcp = P.carve('AS', 'wcp', [128, 16, 128])
                for j in range(16):
                    P.cp(wcp.t[:, j, :].rearrange("p (h i) -> p h i", h=8), w4[:, :, :, j], r=[wg], w=[wcp], eng=('act' if j % 2 else 'dve'))
                for q4 in range(4):
                    bk = bank[q4 % 2]
                    P.tr([(bk.t[:, c * 128:(c + 1) * 128], wcp.t[:, q4 * 4 + c, :]) for c in range(4)], identf.t, r=[wcp, identf], w=[bk])
                    P.cp(WcT.t[:, q4 * 4:(q4 + 1) * 4, :], bk.t[:, :].rearrange("p (c t) -> p c t", c=4), r=[bk], w=[WcT], eng='act')
                P.reset('AS')
                Wtile = P.carve('AS', 'Wtile', [128, 128, 128], BF16)
                At = [P.carve('AS', 'At%d' % k, [128, 128], BF16) for k in range(2)]
                Bt = [P.carve('AS', 'Bt%d' % k, [128, 128], BF16) for k in range(2)]
                Wb_ = [P.carve('AS', 'Wbd%d' % k, [128, 8, 16], BF16) for k in range(2)]
                Yt = [P.carve('AS', 'Yt%d' % k, [128, 128], BF16) for k in range(2)]
                for t in range(128):
                    k = t % 2
                    P.ts(At[k].t[:, :], iotar.t[:, :], ixT.t[:, 0, t:t + 1], ALU.is_equal, r=[iotar, ixT], w=[At[k]])
                    P.ts(Bt[k].t[:, :], iotar.t[:, :], ixT.t[:, 1, t:t + 1], ALU.is_equal, r=[iotar, ixT], w=[Bt[k]])
                    P.tt(Wb_[k].t[:], WcT.t[:, :, t].unsqueeze(1).to_broadcast([128, 8, 16]), bd16.t[:], ALU.mult, r=[WcT, bd16], w=[Wb_[k]])
                    P.mm([(bank[2 + k].t[:, 0:128], Wb_[k].t[:].rearrange("p a b -> p (a b)"), At[k].t[:, :], True, True)], r=[Wb_[k], At[k]], w=[bank[2 + k]])
                    P.cp(Yt[k].t[:, :], bank[2 + k].t[:, 0:128], r=[bank[2 + k]], w=[Yt[k]], eng='act')
                    P.mm([(bank[4 + k].t[:, 0:128], Bt[k].t[:, :], Yt[k].t[:, :], True, True)], r=[Bt[k], Yt[k]], w=[bank[4 + k]])
                    P.cp(Wtile.t[:, :, t], bank[4 + k].t[:, 0:128], r=[bank[4 + k]], w=[Wtile], eng=('act' if t % 4 == 3 else 'dve'))
                P.dma('sp', [(Wd[i, :, :], Wtile.t[:].rearrange("p a b -> p (a b)"))], r=[Wtile], w=[Wdb], sembuf=Wdb)
            P.reset('AL'); P.reset('AS')
            hTall = P.carve('AL', 'hTall', [128, 8, T + 128], BF16)
            KB = 8
            UT = P.carve('AL', 'UT', [128, 8, KB * 128], BF16)
            Vb = P.carve('AL', 'Vb', [128, KB, D], BF16)
            Ub = [P.carve('AS', 'Ub%d' % k, [128, D], BF16) for k in range(2)]
            WTb = [P.carve('AS', 'WTb%d' % k, [128, KB, 128], BF16) for k in range(2)]
            gl = [P.carve('AS', 'gl%d' % k, [128, 128]) for k in range(2)]
            WGb = [P.carve('AS', 'WGb%d' % k, [128, 128], BF16) for k in range(2)]
            for kb in range(128 // KB):
                for kk_ in range(KB):
                    k1 = kb * KB + kk_
                    ub = Ub[kk_ % 2]
                    P.dma('pool', [(ub.t[:, :], I['peer_u'][l, k1 * 128:(k1 + 1) * 128, :])], w=[ub])
                    P.dma('pool', [(Vb.t[:, kk_, :], I['peer_v'][l, k1 * 128:(k1 + 1) * 128, :])], w=[Vb])
                    psT = bank[6].t[:].bitcast(BF16)
                    P.tr([(psT[:, dc * 128:(dc + 1) * 128], ub.t[:, dc * 128:(dc + 1) * 128]) for dc in range(8)], identb.t, r=[ub, identb], w=[bank[6]])
                    P.cp(UT.t[:, :, kk_ * 128:(kk_ + 1) * 128], psT.rearrange("p (k t) -> p k t", k=8), r=[bank[6]], w=[UT], eng=('act' if kk_ % 2 else 'dve'))
                for i in range(NTP):
                    tokg = slice(i * 128, (i + 1) * 128)
                    wtb = WTb[i % 2]
                    P.dma('sp', [(wtb.t[:].rearrange("p a b -> p (a b)"), Wd[i, :, kb * KB * 128:(kb + 1) * KB * 128])], r=[Wdb], w=[wtb])
                    for kk_ in range(KB):
                        k = kk_ % 2
                        P.mm([(bank[k].t[:, 0:128], UT.t[:, dc, kk_ * 128:(kk_ + 1) * 128], hTall.t[:, dc, tokg], dc == 0, dc == 7) for dc in range(8)],
                             r=[UT, hTall], w=[bank[k]])
                        P.act(gl[k].t[:, :], bank[k].t[:, 0:128], AF.Gelu_apprx_tanh, r=[bank[k]], w=[gl[k]])
                        P.tt(WGb[k].t[:, :], gl[k].t[:, :], wtb.t[:, kk_, :], ALU.mult, r=[gl[k], wtb], w=[WGb[k]])
                        P.mm([(bank[2 + half].t[:, :], WGb[k].t[:, :], Vb.t[:, kk_, half * 512:(half + 1) * 512], kk_ == 0, kk_ == KB - 1) for half in range(2)],
                             r=[WGb[k], Vb], w=[bank[2], bank[3]])
                    for half in range(2):
                        hs_ = slice(half * 512, (half + 1) * 512)
                        P.tt(xtile(i)[:, hs_], xtile(i)[:, hs_], bank[2 + half].t[:, :], ALU.add, r=[xbuf_(i), bank[2 + half]], w=[xbuf_(i)])
            if 'xs3' in DBG and l == 0:
                dbg('xs3', DBG['xs3'], xsp.t[0:4, :], [xsp])
            if 'x3' in DBG and l == 0:
                dbg('x3', DBG['x3'].rearrange("(n p) c -> p n c", p=128), xres.t[:, :, :], [xres])
        if stage != 'all':
            break
    if stage == 'all':
        P.reset('AS')
        ost = [P.carve('AS', 'ost%d' % i, [128, D]) for i in range(2)]
        P.dma('sp', [(gbc.t[:], I['final_norm'][0:1, :].to_broadcast([128, D]))], w=[gbc])
        for i in range(NT):
            ob = ost[i % 2]
            rmsnorm_tile(xres.t[:, i, :], xres, ob)
            P.dma('sp', [(O['y_p'][i * 128:(i + 1) * 128, :], ob.t[:, :])], r=[ob], sembuf=ob, is_out=True)
        if WITH_S:
            rmsnorm_tile(xsp.t[0:4, :], xsp, ost[0], npart=4)
            P.dma('sp', [(O['y_s'][:, :], ost[0].t[0:4, :])], r=[ost[0]], sembuf=ost[0], is_out=True)
    P.finish()
    return P, I, O, DBG

_ROPE_CACHE = {}
def _consts():
    c = {}
    c['c_ident'] = np.eye(128, dtype=np.float32)
    half = 32
    freq = (1.0 / (np.float32(10000.0) ** (np.arange(half, dtype=np.float32) / np.float32(half)))).astype(np.float32)
    def tab(pos):
        ang = pos.astype(np.float32)[:, None] * freq[None, :]
        return np.concatenate([np.cos(ang), np.sin(ang)], axis=1).astype(np.float32)
    c['c_rope_p'] = tab(np.arange(T))
    c['c_rope_s'] = tab(np.full((NS,), 16384))
    p = np.arange(128)
    c['c_causal'] = np.where(p[None, :] >= p[:, None], 0.0, NEG).astype(np.float32)
    c['c_tri'] = (p[:, None] <= p[None, :]).astype(np.float32)
    c['c_negmask'] = np.where(p[None, :] >= p[:, None], 0.0, NEG).astype(np.float32)
    cur = np.arange(8)[:, None]; n = np.arange(8)[None, :]
    past = np.where(n < cur, 0.0, -1e30).astype(np.float32).reshape(1, 64)
    pm = np.where(n < cur, 30000.0, 0.0).astype(np.float32).reshape(1, 64)
    c['c_past'] = np.repeat(past, 128, axis=0)
    c['c_pm'] = np.repeat(pm, 128, axis=0)
    c['c_iota'] = np.repeat(np.arange(128, dtype=np.float32)[None, :], 128, axis=0)
    pp = np.arange(128)
    c['c_bd16'] = (pp[:, None] // 16 == pp[None, :] // 16).astype(np.float32)
    c['c_pidx'] = pp[:, None].astype(np.float32)
    z = np.zeros((128, 127), np.float32); z[:, 63] = 1.0
    c['c_zsel'] = z
    c['c_dmask4'] = (np.arange(4)[:, None] == (np.arange(256)[None, :] // 64)).astype(np.float32)
    c['c_dmask4x'] = (np.arange(4)[:, None] == (np.arange(1024)[None, :] // 256)).astype(np.float32)
    return c


def _core_inputs(inp, c, consts, with_kv=True):
    m = {}
    A = np.ascontiguousarray
    s4 = slice(NS * c, NS * (c + 1))
    m['xp'] = A(inp['x_prompt'][c]); m['xs'] = A(inp['x_sample'][s4, 0]); m['memp'] = A(inp['mem_prompt'][c])
    m['pt'] = A(inp['page_table'][s4].astype(np.int32))
    m['st_lru_h'] = A(inp['state_lru_h'][:, s4]); m['st_lru_conv'] = A(inp['state_lru_conv'][:, s4])
    m['st_ssd'] = A(inp['state_ssd'][:, s4]); m['st_ssd_conv'] = A(inp['state_ssd_conv'][:, s4])
    m['st_rwkv'] = A(inp['state_rwkv'][:, s4]); m['st_rwkv_shift'] = A(inp['state_rwkv_shift'][:, s4])
    m['cmk'] = A(inp['cache_mem_k'][:, s4].reshape(DEPTH, NS, 256, D)); m['cmv'] = A(inp['cache_mem_v'][:, s4].reshape(DEPTH, NS, 256, D))
    if with_kv:
        m['ck'] = inp['cache_moba_k'].reshape(DEPTH * 5120 * 128, 256)
        m['cv'] = inp['cache_moba_v'].reshape(DEPTH * 5120 * 128, 256)
    for k in ['norm_mix', 'w_in', 'w_out', 'lru_conv_w', 'lru_conv_b', 'lru_wa', 'lru_ba', 'lru_wx', 'lru_bx', 'lru_lambda',
              'ssd_conv_w', 'ssd_conv_b', 'ssd_dt_bias', 'ssd_a_log', 'ssd_d', 'ssd_norm', 'rwkv_mu', 'rwkv_w0', 'rwkv_w_up',
              'rwkv_a0', 'rwkv_a_up', 'rwkv_g_up', 'rwkv_k_k', 'rwkv_k_a', 'rwkv_ln_w', 'rwkv_ln_b', 'norm_x', 'x_wq', 'x_wk',
              'x_wv', 'x_wo', 'norm_ffn', 'peer_wq', 'peer_u', 'peer_v']:
        m[k] = inp[k]
    m['rwkv_r_k'] = inp['rwkv_r_k'].reshape(DEPTH, 256)
    m['peer_subkeys'] = inp['peer_subkeys'].reshape(DEPTH, 16, 128, 128)
    m['final_norm'] = inp['final_norm'].reshape(1, D)
    m.update(consts)
    return {k: np.asarray(v) for k, v in m.items()}


_PROG = {}
def kernel(**inputs):
    inp = {k: np.asarray(v) for k, v in inputs.items()}
    if 'p' not in _PROG:
        _PROG['p'] = build({'stage': 'all', 'with_kv': True})
    P, I, O, DBG = _PROG['p']
    consts = _consts()
    in_maps = [_core_inputs(inp, c, consts, with_kv=True) for c in range(NCORES)]
    res = run_bass_kernel_spmd(P.nc, in_maps, core_ids=list(range(NCORES)))
    R = res.results
    def st(name, axis, shape_tail=None):
        a = np.stack([np.asarray(R[c][name]) for c in range(NCORES)], axis=axis)
        return a
    f32 = np.float32
    y_p = st('y_p', 0).astype(f32)
    y_s = np.concatenate([R[c]['y_s'] for c in range(NCORES)], 0).reshape(32, 1, D).astype(f32)
    k_p = st('k_p', 1).reshape(DEPTH, NCORES, T, 4, 64).astype(f32)
    v_p = st('v_p', 1).reshape(DEPTH, NCORES, T, 4, 64).astype(f32)
    lru_h_p = st('lru_h_p', 1).astype(f32)
    lru_conv_p = st('lru_conv_p', 1).astype(f32)
    ssd_p = st('ssd_p', 1).astype(f32)
    ssd_conv_p = st('ssd_conv_p', 1).astype(f32)
    rwkv_p = st('rwkv_p', 1).astype(f32)
    rwkv_shift_p = st('rwkv_shift_p', 1).astype(f32)
    mem_k_p = st('mem_k_p', 1).reshape(DEPTH, NCORES, 256, 4, 256).astype(f32)
    mem_v_p = st('mem_v_p', 1).reshape(DEPTH, NCORES, 256, 4, 256).astype(f32)
    def cat(name, tail):
        return np.concatenate([np.asarray(R[c][name]) for c in range(NCORES)], axis=1).reshape((DEPTH, 32) + tail).astype(f32)
    k_s = cat('k_s', (1, 4, 64)); v_s = cat('v_s', (1, 4, 64))
    lru_h_s = cat('lru_h_s', (256,)); lru_conv_s = cat('lru_conv_s', (3, 256))
    ssd_s = cat('ssd_s', (4, 64, 64)); ssd_conv_s = cat('ssd_conv_s', (3, 512))
    rwkv_s = cat('rwkv_s', (4, 64, 64)); rwkv_shift_s = cat('rwkv_shift_s', (896,))
    return (y_p, y_s, k_p, v_p, lru_h_p, lru_conv_p, ssd_p, ssd_conv_p, rwkv_p, rwkv_shift_p, mem_k_p, mem_v_p,
            k_s, v_s, lru_h_s, lru_conv_s, ssd_s, ssd_conv_s, rwkv_s, rwkv_shift_s)
```

```python
import numpy as np
import ml_dtypes
import concourse.bass as bass
import concourse.mybir as mybir
from concourse.bass_utils import run_bass_kernel_spmd

F32 = mybir.dt.float32
BF16 = mybir.dt.bfloat16
I32 = mybir.dt.int32
U32 = mybir.dt.uint32
AF = mybir.ActivationFunctionType
ALU = mybir.AluOpType
AX = mybir.AxisListType

NCORES = 8
D = 1024
T = 2048
NT = 16
TB = 512
NTB = 4
NS = 4
DEPTH = 2
INW = 2948
NEG = -30000.0


class Buf:
    def __init__(self, name, t=None):
        self.name = name
        self.t = t
        self.w = None
        self.rs = {}
        self.dsem = None
        self.dcnt = 0
        self.excl = False

    def __getitem__(self, k):
        return self.t[k]


class Prog:
    def __init__(self):
        self.nc = bass.Bass("TRN2", target_bir_lowering=False)
        nc = self.nc
        self.E = {'pe': nc.tensor, 'dve': nc.vector, 'act': nc.scalar, 'pool': nc.gpsimd, 'sp': nc.sync}
        self.sem = {e: nc.alloc_semaphore("sem_" + e) for e in self.E}
        self.cnt = {e: 0 for e in self.E}
        self.seen = {e: {} for e in self.E}
        self.out_events = []
        self.nsem = 5
        self.ninst = 0
        self.dbufs = []
        self.nops = 0
        self.stop_at = None
        self.oplog = None
        self.arenas = {}
        self.dsems = {}

    def sb(self, name, shape, dt=F32):
        return Buf(name, self.nc.alloc_sbuf_tensor(name, list(shape), dt))

    def arena(self, name, nbytes):
        a = self.nc.alloc_sbuf_tensor(name, [128, nbytes // 4], F32)
        self.arenas[name] = [a, 0, nbytes]

    def carve(self, an, name, shape, dt=F32):
        a = self.arenas[an]
        esz = 4 if dt in (F32, I32, U32) else 2
        n = int(np.prod(shape[1:]))
        nb = (n * esz + 3) // 4 * 4
        assert a[1] + nb <= a[2], "arena %s overflow carving %s (%d + %d > %d)" % (an, name, a[1], nb, a[2])
        ap = a[0][0:shape[0], a[1] // 4:(a[1] + nb) // 4]
        if esz == 2:
            ap = ap.bitcast(dt)[:, 0:n]
        elif dt != F32:
            ap = ap.bitcast(dt)
        if len(shape) == 3:
            ap = ap.rearrange("p (a b) -> p a b", a=shape[1])
        elif len(shape) == 4:
            ap = ap.rearrange("p (a b c) -> p a b c", a=shape[1], b=shape[2])
        a[1] += nb
        return Buf(name, ap)

    def reset(self, an):
        self.barrier()
        self.arenas[an][1] = 0

    def barrier(self):
        for e in self.E:
            for e2 in self.E:
                if e2 != e and e2 != 'sp' and self.cnt[e2] > 0:
                    self._wait(e, (self.sem[e2], self.cnt[e2], 'bar'))
            for b in self.dbufs:
                self._wait(e, (b.dsem, b.dcnt, 'dma'))

    def ps(self, name, shape, dt=F32):
        b = Buf(name, self.nc.alloc_psum_tensor(name, list(shape), dt))
        b.excl = True
        return b

    def din(self, name, shape, dt=F32):
        return self.nc.dram_tensor(name, list(shape), dt, kind="ExternalInput").ap()

    def dout(self, name, shape, dt=F32):
        return self.nc.dram_tensor(name, list(shape), dt, kind="ExternalOutput").ap()

    def _wait(self, eng, ev):
        if ev is None:
            return
        sem, val, src = ev
        if eng == 'pe' and src == 'pe':
            return
        k = id(sem)
        if self.seen[eng].get(k, 0) >= val:
            return
        self.E[eng].wait_ge(sem, val)
        self.seen[eng][k] = val
        self.ninst += 1

    def _deps(self, eng, r, w):
        for b in r:
            self._wait(eng, b.w)
            if b.excl:
                for ev in list(b.rs.values()):
                    self._wait(eng, ev)
        for b in w:
            self._wait(eng, b.w)
            for ev in list(b.rs.values()):
                self._wait(eng, ev)

    def _commit(self, ev, r, w):
        for b in r:
            b.rs[id(ev[0])] = ev
        for b in w:
            b.w = ev
            b.rs = {}

    def op(self, eng, fn, r=(), w=()):
        self.nops += 1
        if self.oplog is not None:
            import sys as _s
            f = _s._getframe(1)
            while f.f_code.co_name not in ('build',) and f.f_back is not None:
                f = f.f_back
            self.oplog.append((self.nops, eng, f.f_lineno))
        if self.stop_at is not None and self.nops > self.stop_at:
            return
        self._deps(eng, r, w)
        ins = fn(self.E[eng])
        if isinstance(ins, (list, tuple)):
            self.ninst += len(ins)
            ins = ins[-1]
        else:
            self.ninst += 1
        self.cnt[eng] += 1
        ins.then_inc(self.sem[eng], 1)
        ev = (self.sem[eng], self.cnt[eng], eng)
        self._commit(ev, r, w)

    def dma(self, q, pairs, r=(), w=(), sembuf=None, is_out=False, **kw):
        self.nops += 1
        if self.stop_at is not None and self.nops > self.stop_at:
            return
        self._deps(q, r, w)
        sb0 = sembuf if sembuf is not None else (w[0] if len(w) else r[0])
        if sb0.name not in self.dsems:
            self.dsems[sb0.name] = Buf("ds_" + sb0.name)
            self.dsems[sb0.name].dsem = self.nc.alloc_semaphore("ds_" + sb0.name)
            self.nsem += 1
            self.dbufs.append(self.dsems[sb0.name])
        sbuf = self.dsems[sb0.name]
        for (o, i) in pairs:
            self.E[q].dma_start(out=o, in_=i, **kw).then_inc(sbuf.dsem, 16)
            sbuf.dcnt += 16
            self.ninst += 1
        ev = (sbuf.dsem, sbuf.dcnt, 'dma')
        self._commit(ev, r, w)
        if is_out:
            self.out_events.append(ev)

    def dma_custom(self, q, fn, r=(), w=(), sembuf=None, is_out=False):
        self.nops += 1
        if self.stop_at is not None and self.nops > self.stop_at:
            return
        self._deps(q, r, w)
        sb0 = sembuf if sembuf is not None else (w[0] if len(w) else r[0])
        if sb0.name not in self.dsems:
            self.dsems[sb0.name] = Buf("ds_" + sb0.name)
            self.dsems[sb0.name].dsem = self.nc.alloc_semaphore("ds_" + sb0.name)
            self.nsem += 1
            self.dbufs.append(self.dsems[sb0.name])
        sbuf = self.dsems[sb0.name]
        fn(self.E[q]).then_inc(sbuf.dsem, 16)
        sbuf.dcnt += 16
        self.ninst += 1
        ev = (sbuf.dsem, sbuf.dcnt, 'dma')
        self._commit(ev, r, w)
        if is_out:
            self.out_events.append(ev)

    def finish(self):
        last = {}
        for ev in self.out_events:
            k = id(ev[0])
            if k not in last or last[k][1] < ev[1]:
                last[k] = ev
        for ev in last.values():
            self.E['sp'].wait_ge(ev[0], ev[1])
        for b in self.dbufs:
            self.E['sp'].wait_ge(b.dsem, b.dcnt)
        for e in self.E:
            if e != 'sp' and self.cnt[e] > 0:
                self.E['sp'].wait_ge(self.sem[e], self.cnt[e])

    def mm(self, groups, r=(), w=()):
        def fn(e):
            return [e.matmul(o, lhsT=l, rhs=rr, start=st, stop=sp) for (o, l, rr, st, sp) in groups]
        self.op('pe', fn, r, w)

    def tr(self, items, ident, r=(), w=()):
        def fn(e):
            return [e.transpose(o, i, ident[0:i.shape[0], 0:i.shape[0]]) for (o, i) in items]
        self.op('pe', fn, r, w)

    def act(self, out, in_, func, r=(), w=(), **kw):
        self.op('act', lambda e: e.activation(out=out, in_=in_, func=func, **kw), r, w)

    def tt(self, out, in0, in1, op, r=(), w=(), eng='dve'):
        self.op(eng, lambda e: e.tensor_tensor(out=out, in0=in0, in1=in1, op=op), r, w)

    def ts(self, out, in0, s1, op0, s2=None, op1=None, r=(), w=(), eng='dve', **kw):
        if op1 is None:
            self.op(eng, lambda e: e.tensor_scalar(out=out, in0=in0, scalar1=s1, scalar2=None, op0=op0, **kw), r, w)
        else:
            self.op(eng, lambda e: e.tensor_scalar(out=out, in0=in0, scalar1=s1, scalar2=s2, op0=op0, op1=op1, **kw), r, w)

    def stt(self, out, in0, scalar, in1, op0, op1, r=(), w=()):
        self.op('dve', lambda e: e.scalar_tensor_tensor(out=out, in0=in0, scalar=scalar, in1=in1, op0=op0, op1=op1), r, w)

    def cp(self, out, in_, r=(), w=(), eng='dve'):
        if eng == 'act':
            self.op('act', lambda e: e.copy(out=out, in_=in_), r, w)
        else:
            self.op(eng, lambda e: e.tensor_copy(out=out, in_=in_), r, w)

    def memset(self, ap, val, w=(), eng='dve'):
        self.op(eng, lambda e: e.memset(ap, val), (), w)


PM_ROWS = {}
def _pm_layout():
    names = ['lru_cw0', 'lru_cw1', 'lru_cw2', 'lru_cw3', 'lru_cb', 'lru_ba', 'lru_bx', 'lru_lam',
             'ssd_cw0a', 'ssd_cw1a', 'ssd_cw2a', 'ssd_cw3a', 'ssd_cba', 'ssd_cw0b', 'ssd_cw1b', 'ssd_cw2b', 'ssd_cw3b', 'ssd_cbb',
             'ssd_norm', 'rw_w0', 'rw_a0', 'rw_kk', 'rw_ka', 'rw_lnw', 'rw_lnb', 'rw_rk', 'rw_mu_r', 'rw_mu_k', 'rw_mu_v', 'rw_mu_x']
    for i, n in enumerate(names):
        PM_ROWS[n] = i
_pm_layout()
NPM = 32


def cand_alias(orig, view):
    class _Shared(Buf):
        pass
    view.__class__ = _AliasBuf
    view._o = orig
    return orig


class _AliasBuf(Buf):
    @property
    def w(self):
        return self._o.w
    @w.setter
    def w(self, v):
        if '_o' in self.__dict__:
            self._o.w = v
    @property
    def rs(self):
        return self._o.rs
    @rs.setter
    def rs(self, v):
        if '_o' in self.__dict__:
            self._o.rs = v


def build(dev=None):
    dev = dev or {}
    stage = dev.get('stage', 'all')
    P = Prog()
    P.stop_at = dev.get('stop_at')
    P.oplog = [] if dev.get('oplog') else None
    nc = P.nc
    I = {}
    def inp(name, shape, dt=F32):
        I[name] = P.din(name, shape, dt)
    inp('xp', [T, D]); inp('xs', [NS, D]); inp('memp', [256, D])
    inp('pt', [NS, 128], I32)
    inp('st_lru_h', [DEPTH, NS, 256]); inp('st_lru_conv', [DEPTH, NS, 3, 256])
    inp('st_ssd', [DEPTH, NS, 4, 64, 64]); inp('st_ssd_conv', [DEPTH, NS, 3, 512])
    inp('st_rwkv', [DEPTH, NS, 4, 64, 64]); inp('st_rwkv_shift', [DEPTH, NS, 896])
    inp('cmk', [DEPTH, NS, 256, D]); inp('cmv', [DEPTH, NS, 256, D])
    if dev.get('with_kv', True):
        inp('ck', [DEPTH * 5120 * 128, 256]); inp('cv', [DEPTH * 5120 * 128, 256])
    wshapes = {'norm_mix': [DEPTH, D], 'w_in': [DEPTH, D, INW], 'w_out': [DEPTH, D, D],
               'lru_conv_w': [DEPTH, 4, 256], 'lru_conv_b': [DEPTH, 256], 'lru_wa': [DEPTH, 4, 64, 64], 'lru_ba': [DEPTH, 256],
               'lru_wx': [DEPTH, 4, 64, 64], 'lru_bx': [DEPTH, 256], 'lru_lambda': [DEPTH, 256],
               'ssd_conv_w': [DEPTH, 4, 512], 'ssd_conv_b': [DEPTH, 512], 'ssd_dt_bias': [DEPTH, 4], 'ssd_a_log': [DEPTH, 4],
               'ssd_d': [DEPTH, 4], 'ssd_norm': [DEPTH, 256],
               'rwkv_mu': [DEPTH, 896], 'rwkv_w0': [DEPTH, 256], 'rwkv_w_up': [DEPTH, 32, 256], 'rwkv_a0': [DEPTH, 256],
               'rwkv_a_up': [DEPTH, 32, 256], 'rwkv_g_up': [DEPTH, 64, 256], 'rwkv_k_k': [DEPTH, 256], 'rwkv_k_a': [DEPTH, 256],
               'rwkv_r_k': [DEPTH, 256], 'rwkv_ln_w': [DEPTH, 256], 'rwkv_ln_b': [DEPTH, 256],
               'norm_x': [DEPTH, D], 'x_wq': [DEPTH, D, D], 'x_wk': [DEPTH, D, D], 'x_wv': [DEPTH, D, D], 'x_wo': [DEPTH, D, D],
               'norm_ffn': [DEPTH, D], 'peer_wq': [DEPTH, D, 2048], 'peer_subkeys': [DEPTH, 16, 128, 128],
               'peer_u': [DEPTH, 16384, D], 'peer_v': [DEPTH, 16384, D], 'final_norm': [1, D]}
    for k, s in wshapes.items():
        inp(k, s)
    inp('c_ident', [128, 128]); inp('c_rope_p', [T, 64]); inp('c_rope_s', [NS, 64])
    inp('c_causal', [128, 128]); inp('c_tri', [128, 128]); inp('c_negmask', [128, 128])
    inp('c_past', [128, 64]); inp('c_pm', [128, 64]); inp('c_iota', [128, 128]); inp('c_bd16', [128, 128])
    inp('c_pidx', [128, 1]); inp('c_zsel', [128, 127]); inp('c_dmask4', [4, 256]); inp('c_dmask4x', [4, D])
    O = {}
    def outp(name, shape):
        O[name] = P.dout(name, shape)
    outp('y_p', [T, D]); outp('y_s', [NS, D]); outp('k_p', [DEPTH, T, 256]); outp('v_p', [DEPTH, T, 256])
    outp('lru_h_p', [DEPTH, 256]); outp('lru_conv_p', [DEPTH, 3, 256]); outp('ssd_p', [DEPTH, 4, 64, 64])
    outp('ssd_conv_p', [DEPTH, 3, 512]); outp('rwkv_p', [DEPTH, 4, 64, 64]); outp('rwkv_shift_p', [DEPTH, 896])
    outp('mem_k_p', [DEPTH, 256, D]); outp('mem_v_p', [DEPTH, 256, D])
    outp('k_s', [DEPTH, NS, 256]); outp('v_s', [DEPTH, NS, 256]); outp('lru_h_s', [DEPTH, NS, 256])
    outp('lru_conv_s', [DEPTH, NS, 3, 256]); outp('ssd_s', [DEPTH, NS, 4, 64, 64]); outp('ssd_conv_s', [DEPTH, NS, 3, 512])
    outp('rwkv_s', [DEPTH, NS, 4, 64, 64]); outp('rwkv_shift_s', [DEPTH, NS, 896])
    dbg_specs = dev.get('dbg', {})
    DBG = {n: P.dout('dbg_' + n, s[0], s[1]) for n, s in dbg_specs.items()}

    Wd = nc.dram_tensor('Wd_scratch', [NT + 1, 128, 128 * 128], BF16, kind='Internal').ap()
    Wdb = Buf('Wdb')
    xres = P.sb('xres', [128, NT, D])
    bank = [P.ps('bank%d' % i, [128, 512]) for i in range(8)]
    identf = P.sb('identf', [128, 128]); identb = P.sb('identb', [128, 128], BF16)
    onesf = P.sb('onesf', [128, 128])
    causal = P.sb('causal', [128, 128], BF16)
    tri = P.sb('tri', [128, 128]); negmask = P.sb('negmask', [128, 128])
    cpast = P.sb('cpast', [128, 8, 8]); cpm = P.sb('cpm', [128, 8, 8])
    rope = P.sb('rope', [128, NT, 64])
    gbc = P.sb('gbc', [128, D])
    PM = P.sb('PM', [NPM, 256]); PT = P.sb('PT', [128, 2, NPM])
    small = P.sb('small', [128, 64])
    epsb = P.sb('epsb', [128, 4])
    BDones = P.sb('BDones', [128, 128]); hmask = P.sb('hmask', [128, 2])
    xsp = P.sb('xsp', [128, D]); hTs = P.sb('hTs', [128, 8, 4], BF16); ymTs = P.sb('ymTs', [128, 8, 4], BF16)
    ropes = P.sb('ropes', [4, 64]); PIf = P.sb('PIf', [128, 512])
    zsel = P.sb('zsel', [128, 127]); dmask4 = P.sb('dmask4', [4, 256]); pidx = P.sb('pidx', [128, 1])
    WITH_S = dev.get('with_s', dev.get('with_kv', True))
    WITH_KV = dev.get('with_kv', True)
    P.arena('AL', 78 * 1024)
    P.arena('AS', 44 * 1024)

    def dbg(name, dst_ap, src_ap, r):
        P.dma('sp', [(dst_ap, src_ap)], r=r, sembuf=Buf('dbg_' + name), is_out=True)

    P.dma('sp', [(identf.t[:], I['c_ident'][:, :])], w=[identf])
    P.cp(identb.t[:], identf.t[:], r=[identf], w=[identb])
    P.memset(onesf.t[:], 1.0, w=[onesf])
    P.dma('pool', [(causal.t[:], I['c_causal'][:, :])], w=[causal])
    P.dma('sp', [(tri.t[:], I['c_tri'][:, :]), (negmask.t[:], I['c_negmask'][:, :])], w=[tri, negmask], sembuf=tri)
    P.dma('sp', [(cpast.t[:].rearrange("p a b -> p (a b)"), I['c_past'][:, :]), (cpm.t[:].rearrange("p a b -> p (a b)"), I['c_pm'][:, :])],
          w=[cpast, cpm], sembuf=cpast)
    P.dma('sp', [(rope.t[:], I['c_rope_p'].rearrange("(n p) c -> p n c", p=128))], w=[rope])
    P.dma('sp', [(xres.t[:, i, :], I['xp'][i * 128:(i + 1) * 128, :]) for i in range(NT)], w=[xres])
    P.memset(xsp.t[:], 0.0, w=[xsp])
    P.memset(ymTs.t[:].rearrange("p a b -> p (a b)"), 0.0, w=[ymTs])
    P.dma('sp', [(xsp.t[4:128, :], I['xp'][4:128, :])], w=[xsp])
    P.dma('sp', [(xsp.t[0:4, :], I['xs'][:, :]), (ropes.t[:, :], I['c_rope_s'][:, :]), (zsel.t[:, :], I['c_zsel'][:, :]),
                 (dmask4.t[:, :], I['c_dmask4'][:, :]), (pidx.t[:, :], I['c_pidx'][:, :])], w=[xsp, ropes, zsel, dmask4, pidx], sembuf=xsp)
    PIu = P.carve('AS', 'PIu', [128, 512], U32)
    P.dma('sp', [(PIu.t[:, :].bitcast(I32), I['pt'].rearrange("s g -> (s g)").unsqueeze(0).to_broadcast([128, 512]))], w=[PIu])
    P.cp(PIf.t[:, :], PIu.t[:, :].bitcast(I32), r=[PIu], w=[PIf])
    P.ts(PIf.t[:, :], PIf.t[:, :], 128.0, ALU.mult, pidx.t[:, 0:1], ALU.add, r=[PIf, pidx], w=[PIf])
    P.memset(epsb.t[:, 0:1], 1e-6, w=[epsb])
    P.memset(epsb.t[:, 1:2], 64e-5, w=[epsb])
    P.memset(BDones.t[:], 0.0, w=[BDones]); P.memset(hmask.t[:], 0.0, w=[hmask])
    for h2 in range(2):
        pr = slice(h2 * 64, (h2 + 1) * 64)
        P.memset(BDones.t[pr, h2 * 64:(h2 + 1) * 64], 1.0, w=[BDones])
        P.memset(hmask.t[pr, h2:h2 + 1], 1.0, w=[hmask])

    def rmsnorm_tile(xt_ap, xbuf, hbuf, npart=128, width=D, gain=None, gbuf_=None, eps_col=0):
        gain = gbc.t[0:npart, 0:width] if gain is None else gain
        gbuf_ = gbc if gbuf_ is None else gbuf_
        P.act(hbuf.t[0:npart, 0:width], xt_ap, AF.Square, r=[xbuf], w=[hbuf, small], accum_out=small.t[0:npart, 0:1])
        P.act(small.t[0:npart, 1:2], small.t[0:npart, 0:1], AF.Sqrt, r=[small, epsb], w=[small], scale=1.0 / width, bias=epsb.t[0:npart, eps_col:eps_col + 1])
        P.op('dve', lambda e: e.reciprocal(out=small.t[0:npart, 2:3], in_=small.t[0:npart, 1:2]), r=[small], w=[small])
        P.stt(hbuf.t[0:npart, 0:width], xt_ap, small.t[0:npart, 2:3], gain, ALU.mult, ALU.mult, r=[xbuf, small, gbuf_], w=[hbuf])

    if stage == 'const':
        P.finish()
        return P, I, O, DBG

    for l in range(DEPTH):
        P.reset('AL'); P.reset('AS')
        hT = P.carve('AL', 'hT', [128, 8, TB], BF16)
        ymT = P.carve('AL', 'ymT', [128, 8, TB], BF16)
        WA = P.carve('AL', 'WA', [128, 8, D], BF16)
        KT = P.carve('AL', 'KT', [128, 2, T], BF16)
        Vaug = P.carve('AL', 'Vaug', [128, NT, 4, 65], BF16)
        kmT = P.carve('AL', 'kmT', [128, 2, 2, 8]); kmacc = P.carve('AL', 'kmacc', [128, 2])
        ubuf = [P.carve('AL', 'ubuf%d' % i, [128, 3 + TB]) for i in range(2)]
        hstate = P.carve('AL', 'hstate', [128, 2])
        BDa = [P.carve('AL', 'BDa%d' % i, [128, 128]) for i in range(2)]
        BDx = [P.carve('AL', 'BDx%d' % i, [128, 128]) for i in range(2)]
        lruc = P.carve('AL', 'lruc', [128, 2, 4])
        cbuf = [P.carve('AL', 'cbuf%d' % i, [128, 3 + TB]) for i in range(4)]
        hp4 = P.carve('AL', 'hp4', [128, 12])
        sT = P.carve('AL', 'sT', [128, 4, 64])
        nwbc = P.carve('AL', 'nwbc', [128, 256])
        cb7 = [P.carve('AL', 'cb7_%d' % i, [128, 129]) for i in range(7)]
        Wlow = P.carve('AL', 'Wlow', [128, 256])
        ST = P.carve('AL', 'ST', [128, 2, 64])
        rwc = P.carve('AL', 'rwc', [128, 8])

        def load_w(src, c0, c1):
            for kc in range(8):
                P.dma('pool', [(WA.t[:, kc, 0:c1 - c0], src[l, kc * 128:(kc + 1) * 128, c0:c1])], w=[WA])

        P.memset(Vaug.t[:].rearrange("p a b c -> p (a b c)"), 1.0, w=[Vaug])
        P.memset(ymT.t[:].rearrange("p a b -> p (a b)"), 0.0, w=[ymT])
        P.dma('sp', [(gbc.t[:], I['norm_mix'][l:l + 1, :].to_broadcast([128, D]))], w=[gbc])
        P.memset(PM.t[:], 0.0, w=[PM])
        rows = []
        def prow(name, src):
            rows.append((PM.t[PM_ROWS[name]:PM_ROWS[name] + 1, 0:src.shape[-1]], src))
        for j in range(4):
            prow('lru_cw%d' % j, I['lru_conv_w'][l, j:j + 1, :])
            prow('ssd_cw%da' % j, I['ssd_conv_w'][l, j:j + 1, 0:256])
            prow('ssd_cw%db' % j, I['ssd_conv_w'][l, j:j + 1, 256:512])
        prow('lru_cb', I['lru_conv_b'][l:l + 1, :]); prow('lru_ba', I['lru_ba'][l:l + 1, :]); prow('lru_bx', I['lru_bx'][l:l + 1, :])
        prow('lru_lam', I['lru_lambda'][l:l + 1, :])
        prow('ssd_cba', I['ssd_conv_b'][l:l + 1, 0:256]); prow('ssd_cbb', I['ssd_conv_b'][l:l + 1, 256:512])
        prow('ssd_norm', I['ssd_norm'][l:l + 1, :])
        prow('rw_w0', I['rwkv_w0'][l:l + 1, :]); prow('rw_a0', I['rwkv_a0'][l:l + 1, :]); prow('rw_kk', I['rwkv_k_k'][l:l + 1, :])
        prow('rw_ka', I['rwkv_k_a'][l:l + 1, :]); prow('rw_lnw', I['rwkv_ln_w'][l:l + 1, :]); prow('rw_lnb', I['rwkv_ln_b'][l:l + 1, :])
        prow('rw_rk', I['rwkv_r_k'][l:l + 1, :])
        prow('rw_mu_r', I['rwkv_mu'][l:l + 1, 0:256]); prow('rw_mu_k', I['rwkv_mu'][l:l + 1, 256:512]); prow('rw_mu_v', I['rwkv_mu'][l:l + 1, 512:768])
        prow('rw_mu_x', I['rwkv_mu'][l:l + 1, 768:896])
        P.dma('sp', rows, w=[PM])
        P.tr([(bank[7].t[:, c * NPM:(c + 1) * NPM], PM.t[0:NPM, c * 128:(c + 1) * 128]) for c in range(2)], identf.t, r=[PM, identf], w=[bank[7]])
        P.cp(PT.t[:].rearrange("p c r -> p (c r)"), bank[7].t[:, 0:2 * NPM], r=[bank[7]], w=[PT])
        def pcol(name, fc):
            return PT.t[:, fc, PM_ROWS[name]:PM_ROWS[name] + 1]
        for fc in range(2):
            P.memset(BDa[fc].t[:], 0.0, w=[BDa[fc]]); P.memset(BDx[fc].t[:], 0.0, w=[BDx[fc]])
            P.dma('sp', [(BDa[fc].t[b * 64:(b + 1) * 64, b * 64:(b + 1) * 64], I['lru_wa'][l, 2 * fc + b, :, :]) for b in range(2)], w=[BDa[fc]])
            P.dma('sp', [(BDx[fc].t[b * 64:(b + 1) * 64, b * 64:(b + 1) * 64], I['lru_wx'][l, 2 * fc + b, :, :]) for b in range(2)], w=[BDx[fc]])
            P.act(lruc.t[:, fc, 2:3], pcol('lru_lam', fc), AF.Exp, r=[PT], w=[lruc], scale=-1.0)
            P.act(lruc.t[:, fc, 3:4], lruc.t[:, fc, 2:3], AF.Ln, r=[lruc, onesf], w=[lruc], bias=onesf.t[:, 0:1])
            P.ts(lruc.t[:, fc, 0:1], lruc.t[:, fc, 3:4], -8.0, ALU.mult, r=[lruc], w=[lruc])
            P.ts(lruc.t[:, fc, 1:2], lruc.t[:, fc, 3:4], -16.0, ALU.mult, r=[lruc], w=[lruc])
            P.memset(ubuf[fc].t[:], 0.0, w=[ubuf[fc]])
        P.memset(hstate.t[:], 0.0, w=[hstate])
        P.memset(kmT.t[:].rearrange("p a b c -> p (a b c)"), 0.0, w=[kmT])
        P.dma('sp', [(hp4.t[:, 0:4], I['ssd_dt_bias'][l:l + 1, :].to_broadcast([128, 4])),
                     (hp4.t[:, 4:8], I['ssd_a_log'][l:l + 1, :].to_broadcast([128, 4])),
                     (hp4.t[:, 8:12], I['ssd_d'][l:l + 1, :].to_broadcast([128, 4])),
                     (nwbc.t[:, :], I['ssd_norm'][l:l + 1, :].to_broadcast([128, 256]))], w=[hp4, nwbc], sembuf=hp4)
        P.act(hp4.t[:, 4:8], hp4.t[:, 4:8], AF.Exp, r=[hp4], w=[hp4])
        P.ts(hp4.t[:, 4:8], hp4.t[:, 4:8], -1.0, ALU.mult, r=[hp4], w=[hp4])
        for c4 in range(4):
            P.memset(cbuf[c4].t[:], 0.0, w=[cbuf[c4]])
        P.memset(sT.t[:].rearrange("p a b -> p (a b)"), 0.0, w=[sT])
        P.dma('sp', [(Wlow.t[0:32, :], I['rwkv_w_up'][l, :, :]), (Wlow.t[32:64, :], I['rwkv_a_up'][l, :, :]), (Wlow.t[64:128, :], I['rwkv_g_up'][l, :, :])], w=[Wlow])
        for c7 in range(7):
            P.memset(cb7[c7].t[:], 0.0, w=[cb7[c7]])
        P.memset(ST.t[:].rearrange("p a b -> p (a b)"), 0.0, w=[ST])
        for fc in range(2):
            P.ts(rwc.t[:, fc:fc + 1], pcol('rw_ka', fc), -1.0, ALU.mult, 1.0, ALU.add, r=[PT], w=[rwc])


        def samp_norm(dst):
            hbs = P.carve('AS', 'hbs', [128, D], BF16)
            rmsnorm_tile(xsp.t[0:4, :], xsp, hbs, npart=4)
            psT_ = bank[6].t[:].bitcast(BF16)
            P.tr([(psT_[:, kc * 4:(kc + 1) * 4], hbs.t[0:4, kc * 128:(kc + 1) * 128]) for kc in range(8)], identb.t, r=[hbs, identb], w=[bank[6]])
            P.cp(dst.t[:].rearrange("p k s -> p (k s)"), psT_[:, 0:32], r=[bank[6]], w=[dst])

        def sproj(bk, o0, Wb, c0, n):
            P.mm([(bk.t[0:4, o0:o0 + n], hTs.t[:, kc, :], Wb.t[:, kc, c0:c0 + n], kc == 0, kc == 7) for kc in range(8)], r=[hTs, Wb], w=[bk])

        def put_y(ybuf, yap, chunk0, nch=2):
            P.tr([(bank[6].t[:, c * 4:(c + 1) * 4], yap[0:4, c * 128:(c + 1) * 128]) for c in range(nch)], identf.t, r=[ybuf, identf], w=[bank[6]])
            P.cp(ymTs.t[:, chunk0:chunk0 + nch, :], bank[6].t[:, 0:4 * nch].rearrange("p (c s) -> p c s", c=nch), r=[bank[6]], w=[ymTs])

        def brow(dst, o0, src2d, n):
            return (dst.t[0:4, o0:o0 + n], src2d.to_broadcast([4, n]))

        def samp_A():
            qks = P.carve('AS', 'qks', [4, 512]); vs1 = P.carve('AS', 'vs1', [4, 257]); st1 = P.carve('AS', 'st1', [4, 8, 32]); st2 = P.carve('AS', 'st2', [4, 8, 32])
            en = P.carve('AS', 'en', [4, 16]); enm = P.carve('AS', 'enm', [4, 4])
            qbc = P.carve('AS', 'qbc', [128, 256]); prod = P.carve('AS', 'prod', [128, 256])
            Kpg = [P.carve('AS', 'Kpg%d' % k, [128, 256]) for k in range(3)]
            Vpg = [P.carve('AS', 'Vpg%d' % k, [128, 257]) for k in range(2)]
            SC = P.carve('AS', 'SC', [128, 128, 4]); g64 = P.carve('AS', 'g64', [64, 260]); g4 = P.carve('AS', 'g4', [4, 64 + 8 + 64])
            rhb = P.carve('AS', 'rhb', [4, 64, 4]); ob = P.carve('AS', 'ob', [4, 260])
            PIl = P.carve('AS', 'PIl', [128, 512], U32); PIt = P.carve('AS', 'PIt', [128, 512])
            P.ts(PIt.t[:, :], PIf.t[:, :], float(l * 5120 * 128), ALU.add, r=[PIf], w=[PIt])
            P.cp(PIl.t[:, :], PIt.t[:, :], r=[PIt], w=[PIl])
            sproj(bank[0], 0, WA, 0, 512); sproj(bank[1], 0, WA, 512, 256)
            qk = bank[0].t[0:4, :].rearrange("p (c d) -> p c d", c=8)
            cosb = ropes.t[:, 0:32].unsqueeze(1).to_broadcast([4, 8, 32]); sinb = ropes.t[:, 32:64].unsqueeze(1).to_broadcast([4, 8, 32])
            qo = qks.t[:].rearrange("p (c d) -> p c d", c=8)
            P.tt(st1.t[:], qk[:, :, 0:32], cosb, ALU.mult, r=[bank[0], ropes], w=[st1]); P.tt(st2.t[:], qk[:, :, 32:64], sinb, ALU.mult, r=[bank[0], ropes], w=[st2])
            P.tt(qo[:, :, 0:32], st1.t[:], st2.t[:], ALU.subtract, r=[st1, st2], w=[qks])
            P.tt(st1.t[:], qk[:, :, 32:64], cosb, ALU.mult, r=[bank[0], ropes], w=[st1]); P.tt(st2.t[:], qk[:, :, 0:32], sinb, ALU.mult, r=[bank[0], ropes], w=[st2])
            P.tt(qo[:, :, 32:64], st1.t[:], st2.t[:], ALU.add, r=[st1, st2], w=[qks])
            P.memset(vs1.t[:, 256:257], 1.0, w=[vs1])
            P.cp(vs1.t[:, 0:256], bank[1].t[0:4, 0:256], r=[bank[1]], w=[vs1])
            P.dma('sp', [(O['k_s'][l, :, :], qks.t[:, 256:512]), (O['v_s'][l, :, :], vs1.t[:, 0:256])], r=[qks, vs1], sembuf=qks, is_out=True)
            P.tt(st1.t[:].rearrange("p a b -> p (a b)"), qks.t[:, 0:256], qks.t[:, 256:512], ALU.mult, r=[qks], w=[st1])
            P.op('dve', lambda e: e.tensor_reduce(out=en.t[:, 0:4], in_=st1.t[:].rearrange("p a b -> p (a b)").rearrange("p (h d) -> p h d", h=4), op=ALU.add, axis=AX.X), r=[st1], w=[en])
            P.act(en.t[:, 4:8], en.t[:, 0:4], AF.Exp, r=[en], w=[en], scale=0.125)
            for k in range(2):
                P.memset(Vpg[k].t[:, 256:257], 1.0, w=[Vpg[k]])
            for s_ in range(4):
                P.mm([(bank[2].t[:, 0:256], identf.t[0:4, s_:s_ + 1].to_broadcast([4, 128]), qks.t[0:4, 0:256], True, True)], r=[identf, qks], w=[bank[2]])
                P.cp(qbc.t[:, :], bank[2].t[:, 0:256], r=[bank[2]], w=[qbc])
                for pg in range(128):
                    kp_ = Kpg[pg % 3]
                    col = s_ * 128 + pg
                    P.dma_custom('pool', lambda e: e.indirect_dma_start(out=kp_.t[:, :], out_offset=None, in_=I['ck'][:, :],
                                                                        in_offset=bass.IndirectOffsetOnAxis(ap=PIl.t[:, col:col + 1], axis=0)), r=[PIl], w=[kp_])
                    n_ = pg // 2
                    P.mm([(bank[3].t[0:64, 0:256], zsel.t[:, 63 - n_:127 - n_], kp_.t[:, :], pg == 0, pg == 127)], r=[zsel, kp_], w=[bank[3]])
                    P.tt(prod.t[:, :], kp_.t[:, :], qbc.t[:, :], ALU.mult, r=[kp_, qbc], w=[prod])
                    P.op('dve', lambda e: e.tensor_reduce(out=SC.t[:, pg, :], in_=prod.t[:, :].rearrange("p (h d) -> p h d", h=4), op=ALU.add, axis=AX.X), r=[prod], w=[SC])
                P.tt(g64.t[:, 0:256], bank[3].t[0:64, 0:256], qbc.t[0:64, :], ALU.mult, r=[bank[3], qbc], w=[g64])
                P.op('dve', lambda e: e.tensor_reduce(out=g64.t[:, 256:260], in_=g64.t[:, 0:256].rearrange("p (h d) -> p h d", h=4), op=ALU.add, axis=AX.X), r=[g64], w=[g64])
                P.tr([(bank[2].t[0:4, 256:320], g64.t[0:64, 256:260])], identf.t, r=[g64, identf], w=[bank[2]])
                P.cp(g4.t[:, 0:64], bank[2].t[0:4, 256:320], r=[bank[2]], w=[g4])
                P.op('dve', lambda e: e.max(out=g4.t[:, 64:72], in_=g4.t[:, 0:64]), r=[g4], w=[g4])
                P.ts(g4.t[:, 72:136], g4.t[:, 0:64], g4.t[:, 66:67], ALU.is_ge, -1.0, ALU.add, r=[g4], w=[g4])
                P.ts(g4.t[:, 72:136], g4.t[:, 72:136], 30000.0, ALU.mult, r=[g4], w=[g4])
                P.tt(rhb.t[:], g4.t[:, 72:136].unsqueeze(2).to_broadcast([4, 64, 4]), identf.t[0:4, 0:4].unsqueeze(1).to_broadcast([4, 64, 4]), ALU.mult, r=[g4, identf], w=[rhb])
                P.mm([(bank[2].t[:, 0:256], onesf.t[0:4, 0:128], rhb.t[:].rearrange("p a b -> p (a b)"), True, True)], r=[onesf, rhb], w=[bank[2]])
                scv = SC.t[:].rearrange("p (n two) h -> p n two h", two=2)
                for two in range(2):
                    P.tt(scv[:, :, two, :], scv[:, :, two, :], bank[2].t[:, 0:256].rearrange("p (n h) -> p n h", h=4), ALU.add, r=[SC, bank[2]], w=[SC])
                P.act(SC.t[:].rearrange("p a b -> p (a b)"), SC.t[:].rearrange("p a b -> p (a b)"), AF.Exp, r=[SC], w=[SC], scale=0.125)
                for pg in range(128):
                    vp_ = Vpg[pg % 2]
                    col = s_ * 128 + pg
                    P.dma_custom('pool', lambda e: e.indirect_dma_start(out=vp_.t[:, 0:256], out_offset=None, in_=I['cv'][:, :],
                                                                        in_offset=bass.IndirectOffsetOnAxis(ap=PIl.t[:, col:col + 1], axis=0)), r=[PIl], w=[vp_])
                    P.mm([(bank[4].t[0:4, 0:257], SC.t[:, pg, :], vp_.t[:, 0:257], pg == 0, False)], r=[SC, vp_], w=[bank[4]])
                P.ts(enm.t[:, :], en.t[:, 4:8], identf.t[0:4, s_:s_ + 1], ALU.mult, r=[en, identf], w=[enm])
                P.mm([(bank[4].t[0:4, 0:257], enm.t[0:4, 0:4], vs1.t[0:4, 0:257], False, True)], r=[enm, vs1], w=[bank[4]])
                P.op('dve', lambda e: e.reciprocal(out=ob.t[:, 256:257], in_=bank[4].t[0:4, 256:257]), r=[bank[4]], w=[ob])
                P.stt(ob.t[:, 0:256], bank[4].t[0:4, 0:256], ob.t[:, 256:257], dmask4.t[:, :], ALU.mult, ALU.mult, r=[bank[4], ob, dmask4], w=[ob])
                P.mm([(bank[5].t[:, s_ * 2 + c:s_ * 2 + c + 1], ob.t[0:4, c * 128:(c + 1) * 128], onesf.t[0:4, 0:1], True, True) for c in range(2)],
                     r=[ob, onesf], w=[bank[5]])
            P.cp(ymTs.t[:, 0:2, :], bank[5].t[:, 0:8].rearrange("p (s c) -> p c s", c=2), r=[bank[5]], w=[ymTs])

        def samp_B():
            pb = P.carve('AS', 'pbB', [4, 2048]); buf = P.carve('AS', 'bufB', [4, 4, 256]); xc = P.carve('AS', 'xcB', [4, 256])
            xcT = P.carve('AS', 'xcTB', [128, 2, 4]); w1 = P.carve('AS', 'w1B', [4, 512]); w2 = P.carve('AS', 'w2B', [4, 512]); h0 = P.carve('AS', 'h0B', [4, 256])
            P.dma('sp', [brow(pb, 0, I['lru_conv_w'][l:l + 1, :, :].rearrange("o j f -> o (j f)"), 1024), brow(pb, 1024, I['lru_conv_b'][l:l + 1, :], 256),
                         brow(pb, 1280, I['lru_ba'][l:l + 1, :], 256), brow(pb, 1536, I['lru_bx'][l:l + 1, :], 256), brow(pb, 1792, I['lru_lambda'][l:l + 1, :], 256),
                         (buf.t[:, 0:3, :], I['st_lru_conv'][l, :, :, :]), (h0.t[:, :], I['st_lru_h'][l, :, :])], w=[pb, buf, h0], sembuf=pb)
            sproj(bank[0], 0, WA, 0, 512)
            P.cp(buf.t[:, 3, :], bank[0].t[0:4, 0:256], r=[bank[0]], w=[buf])
            P.tt(xc.t[:, :], buf.t[:, 0, :], pb.t[:, 0:256], ALU.mult, r=[buf, pb], w=[xc])
            for j in range(1, 4):
                P.tt(w1.t[:, 0:256], buf.t[:, j, :], pb.t[:, j * 256:(j + 1) * 256], ALU.mult, r=[buf, pb], w=[w1])
                P.tt(xc.t[:, :], xc.t[:, :], w1.t[:, 0:256], ALU.add, r=[xc, w1], w=[xc])
            P.tt(xc.t[:, :], xc.t[:, :], pb.t[:, 1024:1280], ALU.add, r=[xc, pb], w=[xc])
            P.tr([(bank[2].t[:, c * 4:(c + 1) * 4], xc.t[0:4, c * 128:(c + 1) * 128]) for c in range(2)], identf.t, r=[xc, identf], w=[bank[2]])
            P.cp(xcT.t[:].rearrange("p c s -> p (c s)"), bank[2].t[:, 0:8], r=[bank[2]], w=[xcT])
            P.mm([(bank[3].t[0:4, fc * 128:(fc + 1) * 128], xcT.t[:, fc, :], BDa[fc].t[:, :], True, True) for fc in range(2)] +
                 [(bank[3].t[0:4, 256 + fc * 128:256 + (fc + 1) * 128], xcT.t[:, fc, :], BDx[fc].t[:, :], True, True) for fc in range(2)],
                 r=[xcT] + BDa + BDx, w=[bank[3]])
            P.tt(w1.t[:, :], bank[3].t[0:4, 0:512], pb.t[:, 1280:1792], ALU.add, r=[bank[3], pb], w=[w1])
            P.act(w1.t[:, :], w1.t[:, :], AF.Sigmoid, r=[w1], w=[w1])
            P.act(w2.t[:, 0:256], pb.t[:, 1792:2048], AF.Exp, r=[pb], w=[w2], scale=-1.0)
            P.act(w2.t[:, 0:256], w2.t[:, 0:256], AF.Ln, r=[w2, onesf], w=[w2], bias=onesf.t[0:4, 0:1])
            P.tt(w2.t[:, 0:256], w2.t[:, 0:256], w1.t[:, 0:256], ALU.mult, r=[w2, w1], w=[w2])
            P.act(w2.t[:, 256:512], w2.t[:, 0:256], AF.Exp, r=[w2], w=[w2], scale=-16.0)
            P.act(w2.t[:, 0:256], w2.t[:, 0:256], AF.Exp, r=[w2], w=[w2], scale=-8.0)
            P.ts(w2.t[:, 256:512], w2.t[:, 256:512], 0.99999994, ALU.min, r=[w2], w=[w2])
            P.act(w2.t[:, 256:512], w2.t[:, 256:512], AF.Sqrt, r=[w2, onesf], w=[w2], scale=-1.0, bias=onesf.t[0:4, 0:1])
            P.tt(w1.t[:, 256:512], w1.t[:, 256:512], xc.t[:, :], ALU.mult, r=[w1, xc], w=[w1])
            P.tt(w2.t[:, 256:512], w2.t[:, 256:512], w1.t[:, 256:512], ALU.mult, r=[w2, w1], w=[w2])
            P.tt(h0.t[:, :], h0.t[:, :], w2.t[:, 0:256], ALU.mult, r=[h0, w2], w=[h0])
            P.tt(h0.t[:, :], h0.t[:, :], w2.t[:, 256:512], ALU.add, r=[h0, w2], w=[h0])
            P.act(w1.t[:, 0:256], bank[0].t[0:4, 256:512], AF.Gelu_apprx_tanh, r=[bank[0]], w=[w1])
            P.tt(w1.t[:, 0:256], w1.t[:, 0:256], h0.t[:, :], ALU.mult, r=[w1, h0], w=[w1])
            put_y(w1, w1.t, 2)
            P.dma('sp', [(O['lru_h_s'][l, :, :], h0.t[:, :]), (O['lru_conv_s'][l, :, :, :], buf.t[:, 1:4, :])], r=[h0, buf], sembuf=h0, is_out=True)

        def samp_C():
            pb = P.carve('AS', 'pbC', [4, 2836]); cbs = P.carve('AS', 'cbsC', [4, 4, 512]); xb = P.carve('AS', 'xbC', [4, 512]); w1 = P.carve('AS', 'w1C', [4, 512])
            dd = P.carve('AS', 'ddC', [4, 16]); xdt = P.carve('AS', 'xdtC', [4, 256]); yy = P.carve('AS', 'yyC', [4, 256]); ycb = P.carve('AS', 'ycbC', [4, 256])
            Sh = [P.carve('AS', 'ShC%d' % k, [4, 16, 64]) for k in range(1)]; Sw = P.carve('AS', 'SwC', [4, 16, 64])
            P.dma('sp', [brow(pb, 0, I['ssd_conv_w'][l:l + 1, :, :].rearrange("o j f -> o (j f)"), 2048), brow(pb, 2048, I['ssd_conv_b'][l:l + 1, :], 512),
                         brow(pb, 2560, I['ssd_dt_bias'][l:l + 1, :], 4), brow(pb, 2564, I['ssd_a_log'][l:l + 1, :], 4), brow(pb, 2568, I['ssd_d'][l:l + 1, :], 4),
                         brow(pb, 2580, I['ssd_norm'][l:l + 1, :], 256), (cbs.t[:, 0:3, :], I['st_ssd_conv'][l, :, :, :])], w=[pb, cbs], sembuf=pb)
            sproj(bank[0], 0, WA, 0, 256); sproj(bank[0], 256, WA, 768, 4); sproj(bank[1], 0, WA, 256, 512)
            P.cp(cbs.t[:, 3, :], bank[1].t[0:4, 0:512], r=[bank[1]], w=[cbs])
            P.tt(xb.t[:, :], cbs.t[:, 0, :], pb.t[:, 0:512], ALU.mult, r=[cbs, pb], w=[xb])
            for j in range(1, 4):
                P.tt(w1.t[:, :], cbs.t[:, j, :], pb.t[:, j * 512:(j + 1) * 512], ALU.mult, r=[cbs, pb], w=[w1])
                P.tt(xb.t[:, :], xb.t[:, :], w1.t[:, :], ALU.add, r=[xb, w1], w=[xb])
            P.tt(xb.t[:, :], xb.t[:, :], pb.t[:, 2048:2560], ALU.add, r=[xb, pb], w=[xb])
            P.act(xb.t[:, :], xb.t[:, :], AF.Silu, r=[xb], w=[xb])
            P.tt(dd.t[:, 0:4], bank[0].t[0:4, 256:260], pb.t[:, 2560:2564], ALU.add, r=[bank[0], pb], w=[dd])
            P.act(dd.t[:, 0:4], dd.t[:, 0:4], AF.Exp, r=[dd], w=[dd])
            P.act(dd.t[:, 0:4], dd.t[:, 0:4], AF.Ln, r=[dd, onesf], w=[dd], bias=onesf.t[0:4, 0:1])
            P.act(dd.t[:, 4:8], pb.t[:, 2564:2568], AF.Exp, r=[pb], w=[dd])
            P.tt(dd.t[:, 4:8], dd.t[:, 4:8], dd.t[:, 0:4], ALU.mult, r=[dd], w=[dd])
            P.act(dd.t[:, 4:8], dd.t[:, 4:8], AF.Exp, r=[dd], w=[dd], scale=-1.0)
            P.tt(xdt.t[:].rearrange("p (h d) -> p h d", h=4), xb.t[:, 0:256].rearrange("p (h d) -> p h d", h=4), dd.t[:, 0:4].unsqueeze(2).to_broadcast([4, 4, 64]),
                 ALU.mult, r=[xb, dd], w=[xdt])
            for h8 in range(16):
                h, ph = h8 // 4, h8 % 4
                g = h // 2
                sh = Sh[0]
                ps_ = slice(ph * 16, (ph + 1) * 16)
                hp_ = slice(h * 64 + ph * 16, h * 64 + (ph + 1) * 16)
                P.dma('sp', [(sh.t[:, :, :], I['st_ssd'][l, :, h, ps_, :])], w=[sh])
                Bg = xb.t[:, 256 + g * 64:256 + (g + 1) * 64]; Cg = xb.t[:, 384 + g * 64:384 + (g + 1) * 64]
                P.tt(Sw.t[:], xdt.t[:, hp_].unsqueeze(2).to_broadcast([4, 16, 64]), Bg.unsqueeze(1).to_broadcast([4, 16, 64]), ALU.mult, r=[xdt, xb], w=[Sw])
                P.stt(sh.t[:].rearrange("p a b -> p (a b)"), sh.t[:].rearrange("p a b -> p (a b)"), dd.t[:, 4 + h:5 + h], Sw.t[:].rearrange("p a b -> p (a b)"), ALU.mult, ALU.add,
                      r=[sh, dd, Sw], w=[sh])
                P.dma('sp', [(O['ssd_s'][l, :, h, ps_, :], sh.t[:, :, :])], r=[sh], sembuf=sh, is_out=True)
                P.tt(Sw.t[:], sh.t[:], Cg.unsqueeze(1).to_broadcast([4, 16, 64]), ALU.mult, r=[sh, xb], w=[Sw])
                P.op('dve', lambda e: e.tensor_reduce(out=yy.t[:, hp_], in_=Sw.t[:], op=ALU.add, axis=AX.X), r=[Sw], w=[yy])
            P.tt(w1.t[:, 0:256].rearrange("p (h d) -> p h d", h=4), xb.t[:, 0:256].rearrange("p (h d) -> p h d", h=4), pb.t[:, 2568:2572].unsqueeze(2).to_broadcast([4, 4, 64]),
                 ALU.mult, r=[xb, pb], w=[w1])
            P.tt(yy.t[:, :], yy.t[:, :], w1.t[:, 0:256], ALU.add, r=[yy, w1], w=[yy])
            P.act(w1.t[:, 0:256], bank[0].t[0:4, 0:256], AF.Silu, r=[bank[0]], w=[w1])
            P.tt(yy.t[:, :], yy.t[:, :], w1.t[:, 0:256], ALU.mult, r=[yy, w1], w=[yy])
            rmsnorm_tile(yy.t[:, :], yy, ycb, npart=4, width=256, gain=pb.t[:, 2580:2836], gbuf_=pb)
            put_y(ycb, ycb.t, 4)
            P.dma('sp', [(O['ssd_conv_s'][l, :, :, :], cbs.t[:, 1:4, :])], r=[cbs], sembuf=cbs, is_out=True)

        def samp_D():
            pb = P.carve('AS', 'pbD', [4, 2688]); cur = P.carve('AS', 'curD', [4, 896]); mm_ = P.carve('AS', 'mD', [4, 896]); prev = P.carve('AS', 'prevD', [4, 896])
            xTs = P.carve('AS', 'xTsD', [128, 4]); w1 = P.carve('AS', 'w1D', [4, 1024]); w2 = P.carve('AS', 'w2D', [4, 1024]); sm = P.carve('AS', 'smD', [4, 32])
            Sh = [P.carve('AS', 'ShD%d' % k, [4, 16, 64]) for k in range(1)]; Sw = P.carve('AS', 'SwD', [4, 16, 64]); yy = P.carve('AS', 'yyD', [4, 256]); sa = P.carve('AS', 'saD', [4, 16])
            names = ['rwkv_w0', 'rwkv_a0', 'rwkv_k_k', 'rwkv_k_a', 'rwkv_ln_w', 'rwkv_ln_b', 'rwkv_r_k']
            P.dma('sp', [brow(pb, 0, I['rwkv_mu'][l:l + 1, :], 896)] + [brow(pb, 896 + 256 * q, I[nm][l:l + 1, :], 256) for q, nm in enumerate(names)] +
                  [(prev.t[:, :], I['st_rwkv_shift'][l, :, :])], w=[pb, prev], sembuf=pb)
            PW0, PA0, PKK, PKA, PLW, PLB, PRK = [896 + 256 * q for q in range(7)]
            sproj(bank[0], 0, WA, 0, 512); sproj(bank[1], 0, WA, 512, 384)
            P.cp(cur.t[:, 0:512], bank[0].t[0:4, 0:512], r=[bank[0]], w=[cur]); P.cp(cur.t[:, 512:896], bank[1].t[0:4, 0:384], r=[bank[1]], w=[cur])
            P.dma('sp', [(O['rwkv_shift_s'][l, :, :], cur.t[:, :])], r=[cur], sembuf=cur, is_out=True)
            P.tt(mm_.t[:, :], prev.t[:, :], cur.t[:, :], ALU.subtract, r=[prev, cur], w=[mm_])
            P.tt(mm_.t[:, :], mm_.t[:, :], pb.t[:, 0:896], ALU.mult, r=[mm_, pb], w=[mm_])
            P.tt(mm_.t[:, :], mm_.t[:, :], cur.t[:, :], ALU.add, r=[mm_, cur], w=[mm_])
            R_, K_, V_ = mm_.t[:, 0:256], mm_.t[:, 256:512], mm_.t[:, 512:768]
            P.tr([(bank[2].t[:, 0:4], mm_.t[0:4, 768:896])], identf.t, r=[mm_, identf], w=[bank[2]])
            P.cp(xTs.t[:, :], bank[2].t[:, 0:4], r=[bank[2]], w=[xTs])
            xT3 = P.carve('AS', 'xT3D', [128, 3, 4])
            P.memset(xT3.t[:].rearrange("p a b -> p (a b)"), 0.0, w=[xT3])
            P.act(xT3.t[0:32, 0, :], xTs.t[0:32, :], AF.Tanh, r=[xTs], w=[xT3])
            P.cp(xT3.t[32:64, 1, :], xTs.t[32:64, :], r=[xTs], w=[xT3])
            P.act(xT3.t[64:128, 2, :], xTs.t[64:128, :], AF.Sigmoid, r=[xTs], w=[xT3])
            P.mm([(bank[3].t[0:4, 0:256], xT3.t[:, 0, :], Wlow.t[:, :], True, True)], r=[xT3, Wlow], w=[bank[3]])
            P.mm([(bank[3].t[0:4, 256:512], xT3.t[:, 1, :], Wlow.t[:, :], True, True)], r=[xT3, Wlow], w=[bank[3]])
            P.mm([(bank[4].t[0:4, 0:256], xT3.t[:, 2, :], Wlow.t[:, :], True, True)], r=[xT3, Wlow], w=[bank[4]])
            Dd, Aa, Gg, KKn = w1.t[:, 0:256], w1.t[:, 256:512], w1.t[:, 512:768], w1.t[:, 768:1024]
            KP, Bb, T1, T2 = w2.t[:, 0:256], w2.t[:, 256:512], w2.t[:, 512:768], w2.t[:, 768:1024]
            P.tt(w1.t[:, 0:512], bank[3].t[0:4, 0:512], pb.t[:, PW0:PW0 + 512], ALU.add, r=[bank[3], pb], w=[w1])
            P.act(w1.t[:, 0:512], w1.t[:, 0:512], AF.Sigmoid, r=[w1], w=[w1])
            P.act(Dd, Dd, AF.Exp, r=[w1], w=[w1], scale=-0.6065306597126334)
            P.cp(Gg, bank[4].t[0:4, 0:256], r=[bank[4]], w=[w1])
            P.tt(KKn, K_, pb.t[:, PKK:PKK + 256], ALU.mult, r=[mm_, pb], w=[w1])
            P.tt(T1, KKn, KKn, ALU.mult, r=[w1], w=[w2])
            P.op('dve', lambda e: e.tensor_reduce(out=sm.t[:, 0:4], in_=T1.rearrange("p (h d) -> p h d", h=4), op=ALU.add, axis=AX.X), r=[w2], w=[sm])
            P.act(sm.t[:, 0:4], sm.t[:, 0:4], AF.Sqrt, r=[sm], w=[sm]); P.ts(sm.t[:, 0:4], sm.t[:, 0:4], 1e-12, ALU.max, r=[sm], w=[sm])
            P.op('dve', lambda e: e.reciprocal(out=sm.t[:, 0:4], in_=sm.t[:, 0:4]), r=[sm], w=[sm])
            P.tt(KKn.rearrange("p (h d) -> p h d", h=4), KKn.rearrange("p (h d) -> p h d", h=4), sm.t[:, 0:4].unsqueeze(2).to_broadcast([4, 4, 64]), ALU.mult, r=[w1, sm], w=[w1])
            P.ts(T1, Aa, -1.0, ALU.add, r=[w1], w=[w2]); P.tt(T1, T1, pb.t[:, PKA:PKA + 256], ALU.mult, r=[w2, pb], w=[w2]); P.ts(T1, T1, 1.0, ALU.add, r=[w2], w=[w2])
            P.tt(KP, K_, T1, ALU.mult, r=[mm_, w2], w=[w2])
            P.tt(Bb, KKn, Aa, ALU.mult, r=[w1], w=[w2])
            for h8 in range(16):
                h, ph = h8 // 4, h8 % 4
                hs_ = slice(h * 64, (h + 1) * 64)
                vs_ = slice(h * 64 + ph * 16, h * 64 + (ph + 1) * 16)
                ps_ = slice(ph * 16, (ph + 1) * 16)
                sh = Sh[0]
                P.dma('sp', [(sh.t[:, :, :], I['st_rwkv'][l, :, h, ps_, :])], w=[sh])
                bk = lambda ap: ap.unsqueeze(1).to_broadcast([4, 16, 64])
                bv = lambda ap: ap.unsqueeze(2).to_broadcast([4, 16, 64])
                P.tt(Sw.t[:], sh.t[:], bk(KKn[:, hs_]), ALU.mult, r=[sh, w1], w=[Sw])
                P.op('dve', lambda e: e.tensor_reduce(out=sa.t[:, :], in_=Sw.t[:], op=ALU.add, axis=AX.X), r=[Sw], w=[sa])
                P.ts(sa.t[:, :], sa.t[:, :], -1.0, ALU.mult, r=[sa], w=[sa])
                P.tt(sh.t[:], sh.t[:], bk(Dd[:, hs_]), ALU.mult, r=[sh, w1], w=[sh])
                P.tt(Sw.t[:], bv(sa.t[:, :]), bk(Bb[:, hs_]), ALU.mult, r=[sa, w2], w=[Sw])
                P.tt(sh.t[:], sh.t[:], Sw.t[:], ALU.add, r=[sh, Sw], w=[sh])
                P.tt(Sw.t[:], bv(V_[:, vs_]), bk(KP[:, hs_]), ALU.mult, r=[mm_, w2], w=[Sw])
                P.tt(sh.t[:], sh.t[:], Sw.t[:], ALU.add, r=[sh, Sw], w=[sh])
                P.dma('sp', [(O['rwkv_s'][l, :, h, ps_, :], sh.t[:, :, :])], r=[sh], sembuf=sh, is_out=True)
                P.tt(Sw.t[:], sh.t[:], bk(R_[:, hs_]), ALU.mult, r=[sh, mm_], w=[Sw])
                P.op('dve', lambda e: e.tensor_reduce(out=yy.t[:, vs_], in_=Sw.t[:], op=ALU.add, axis=AX.X), r=[Sw], w=[yy])
            y3 = yy.t[:, :].rearrange("p (h d) -> p h d", h=4)
            P.op('dve', lambda e: e.tensor_reduce(out=sm.t[:, 4:8], in_=y3, op=ALU.add, axis=AX.X), r=[yy], w=[sm])
            P.ts(sm.t[:, 4:8], sm.t[:, 4:8], 1.0 / 64, ALU.mult, r=[sm], w=[sm])
            P.tt(y3, y3, sm.t[:, 4:8].unsqueeze(2).to_broadcast([4, 4, 64]), ALU.subtract, r=[yy, sm], w=[yy])
            P.tt(T1, yy.t[:, :], yy.t[:, :], ALU.mult, r=[yy], w=[w2])
            P.op('dve', lambda e: e.tensor_reduce(out=sm.t[:, 8:12], in_=T1.rearrange("p (h d) -> p h d", h=4), op=ALU.add, axis=AX.X), r=[w2], w=[sm])
            P.act(sm.t[:, 8:12], sm.t[:, 8:12], AF.Sqrt, r=[sm, epsb], w=[sm], scale=1.0 / 64, bias=epsb.t[0:4, 1:2])
            P.op('dve', lambda e: e.reciprocal(out=sm.t[:, 8:12], in_=sm.t[:, 8:12]), r=[sm], w=[sm])
            P.tt(y3, y3, sm.t[:, 8:12].unsqueeze(2).to_broadcast([4, 4, 64]), ALU.mult, r=[yy, sm], w=[yy])
            P.tt(yy.t[:, :], yy.t[:, :], pb.t[:, PLW:PLW + 256], ALU.mult, r=[yy, pb], w=[yy]); P.tt(yy.t[:, :], yy.t[:, :], pb.t[:, PLB:PLB + 256], ALU.add, r=[yy, pb], w=[yy])
            P.tt(T1, R_, KP, ALU.mult, r=[mm_, w2], w=[w2]); P.tt(T1, T1, pb.t[:, PRK:PRK + 256], ALU.mult, r=[w2, pb], w=[w2])
            P.op('dve', lambda e: e.tensor_reduce(out=sm.t[:, 12:16], in_=T1.rearrange("p (h d) -> p h d", h=4), op=ALU.add, axis=AX.X), r=[w2], w=[sm])
            P.tt(T2.rearrange("p (h d) -> p h d", h=4), V_.rearrange("p (h d) -> p h d", h=4), sm.t[:, 12:16].unsqueeze(2).to_broadcast([4, 4, 64]), ALU.mult, r=[mm_, sm], w=[w2])
            P.tt(yy.t[:, :], yy.t[:, :], T2, ALU.add, r=[yy, w2], w=[yy])
            P.tt(yy.t[:, :], yy.t[:, :], Gg, ALU.mult, r=[yy, w1], w=[yy])
            put_y(yy, yy.t, 6)

        def samp_W():
            for half in range(2):
                hs_ = slice(half * 512, (half + 1) * 512)
                P.mm([(bank[half].t[0:4, :], ymTs.t[:, kc, :], WA.t[:, kc, hs_], kc == 0, kc == 7) for kc in range(8)], r=[ymTs, WA], w=[bank[half]])
                P.tt(xsp.t[0:4, hs_], xsp.t[0:4, hs_], bank[half].t[0:4, :], ALU.add, r=[xsp, bank[half]], w=[xsp])
        if stage == 'init':
            P.finish()
            return P, I, O, DBG
        for tb in range(NTB):
            P.reset('AS')
            hb = [P.carve('AS', 'hb%d' % i, [128, D], BF16) for i in range(2)]
            if tb == 0 and WITH_S:
                samp_norm(hTs)
            for tt in range(4):
                i = tb * 4 + tt
                hbuf = hb[i % 2]
                rmsnorm_tile(xres.t[:, i, :], xres, hbuf)
                psT = bank[6].t[:].bitcast(BF16)
                P.tr([(psT[:, kc * 128:(kc + 1) * 128], hbuf.t[:, kc * 128:(kc + 1) * 128]) for kc in range(8)], identb.t,
                     r=[hbuf, identb], w=[bank[6]])
                P.cp(hT.t[:, :, tt * 128:(tt + 1) * 128], psT.rearrange("p (k t) -> p k t", k=8), r=[bank[6]], w=[hT], eng='act')
            if stage == 'norm':
                P.finish()
                return P, I, O, DBG
            P.reset('AS')
            QT32 = P.carve('AS', 'QT32', [128, 2, TB]); QTm = P.carve('AS', 'QTm', [128, 2, 2, TB], BF16)
            biasT = P.carve('AS', 'biasT', [128, TB], BF16)
            qkr = [P.carve('AS', 'qkr%d' % i, [128, 512]) for i in range(2)]
            vst = [P.carve('AS', 'vst%d' % i, [128, 256]) for i in range(2)]
            rtmp = [P.carve('AS', 'rtmp%d' % i, [128, 8, 32]) for i in range(2)]
            gbuf = P.carve('AS', 'gbuf', [128, 4, 8]); m8 = P.carve('AS', 'm8', [128, 4, 8]); selb = P.carve('AS', 'selb', [128, 4, 8])
            biasq = P.carve('AS', 'biasq', [128, 32])
            PTb = [P.carve('AS', 'PTb%d' % i, [128, 512], BF16) for i in range(2)]
            yatok = P.carve('AS', 'yatok', [128, 256], BF16)
            P.memset(QTm.t[:].rearrange("p a b c -> p (a b c)"), 0.0, w=[QTm])
            P.memset(biasT.t[:], 0.0, w=[biasT])
            load_w(I['w_in'], 0, 768)
            for tt in range(4):
                i = tb * 4 + tt
                par = i % 2
                tok = slice(tt * 128, (tt + 1) * 128)
                P.mm([(bank[0].t[:, 0:512], hT.t[:, kc, tok], WA.t[:, kc, 0:512], kc == 0, kc == 7) for kc in range(8)],
                     r=[hT, WA], w=[bank[0]])
                P.mm([(bank[1].t[:, 0:256], hT.t[:, kc, tok], WA.t[:, kc, 512:768], kc == 0, kc == 7) for kc in range(8)],
                     r=[hT, WA], w=[bank[1]])
                qk = bank[0].t[:].rearrange("p (c d) -> p c d", c=8)
                cosb = rope.t[:, i, 0:32].unsqueeze(1).to_broadcast([128, 8, 32])
                sinb = rope.t[:, i, 32:64].unsqueeze(1).to_broadcast([128, 8, 32])
                qo = qkr[par].t[:].rearrange("p (c d) -> p c d", c=8)
                t1, t2 = rtmp[0], rtmp[1]
                P.tt(t1.t[:], qk[:, :, 0:32], cosb, ALU.mult, r=[bank[0], rope], w=[t1])
                P.tt(t2.t[:], qk[:, :, 32:64], sinb, ALU.mult, r=[bank[0], rope], w=[t2])
                P.tt(qo[:, :, 0:32], t1.t[:], t2.t[:], ALU.subtract, r=[t1, t2], w=[qkr[par]])
                P.tt(t1.t[:], qk[:, :, 32:64], cosb, ALU.mult, r=[bank[0], rope], w=[t1])
                P.tt(t2.t[:], qk[:, :, 0:32], sinb, ALU.mult, r=[bank[0], rope], w=[t2])
                P.tt(qo[:, :, 32:64], t1.t[:], t2.t[:], ALU.add, r=[t1, t2], w=[qkr[par]])
                P.cp(vst[par].t[:], bank[1].t[:, 0:256], r=[bank[1]], w=[vst[par]], eng='act')
                P.cp(Vaug.t[:, i, :, 0:64], bank[1].t[:, 0:256].rearrange("p (h d) -> p h d", h=4), r=[bank[1]], w=[Vaug])
                P.dma('sp', [(O['k_p'][l, i * 128:(i + 1) * 128, :], qkr[par].t[:, 256:512])], r=[qkr[par]], sembuf=qkr[par], is_out=True)
                P.dma('sp', [(O['v_p'][l, i * 128:(i + 1) * 128, :], vst[par].t[:])], r=[vst[par]], sembuf=vst[par], is_out=True)
                if stage == 'rope':
                    P.finish()
                    return P, I, O, DBG
                qkb = PTb[0]
                P.cp(qkb.t[:, :], qkr[par].t[:, :], r=[qkr[par]], w=[qkb], eng='act')
                psb = bank[2].t[:].bitcast(BF16)
                P.tr([(psb[:, c * 128:(c + 1) * 128], qkb.t[:, c * 128:(c + 1) * 128]) for c in range(4)], identb.t, r=[qkb, identb], w=[bank[2]])
                for h2 in range(2):
                    pr = slice(h2 * 64, (h2 + 1) * 64)
                    P.cp(QTm.t[pr, :, h2, tok], psb[pr, 0:256].rearrange("p (c t) -> p c t", c=2), r=[bank[2]], w=[QTm])
                P.cp(KT.t[:, :, i * 128:(i + 1) * 128], psb[:, 256:512].rearrange("p (c t) -> p c t", c=2), r=[bank[2]], w=[KT], eng='act')
                P.tr([(bank[3].t[:, c * 128:(c + 1) * 128], qkr[par].t[:, c * 128:(c + 1) * 128]) for c in range(4)], identf.t,
                     r=[qkr[par], identf], w=[bank[3]])
                P.cp(QT32.t[:, :, tok], bank[3].t[:, 0:256].rearrange("p (c t) -> p c t", c=2), r=[bank[3]], w=[QT32], eng='act')
                n = i // 2
                if i % 2 == 0:
                    P.op('dve', lambda e: e.tensor_reduce(out=kmacc.t[:, :], in_=bank[3].t[:, 256:512].rearrange("p (c t) -> p c t", c=2),
                                                          op=ALU.add, axis=AX.X), r=[bank[3]], w=[kmacc])
                else:
                    P.op('dve', lambda e: e.tensor_reduce(out=small.t[:, 8:10], in_=bank[3].t[:, 256:512].rearrange("p (c t) -> p c t", c=2),
                                                          op=ALU.add, axis=AX.X), r=[bank[3]], w=[small])
                    for h2 in range(2):
                        pr = slice(h2 * 64, (h2 + 1) * 64)
                        P.tt(kmT.t[pr, :, h2, n], kmacc.t[pr, :], small.t[pr, 8:10], ALU.add, r=[kmacc, small], w=[kmT])
                if stage == 'qkT':
                    P.finish()
                    return P, I, O, DBG
                cur = i // 2
                P.mm([(bank[1].t[:, 256 + (2 * hp + h2) * 8:256 + (2 * hp + h2 + 1) * 8], QT32.t[:, hp, tok], kmT.t[:, hp, h2, :], True, True)
                      for hp in range(2) for h2 in range(2)], r=[QT32, kmT], w=[bank[1]])
                P.tt(gbuf.t[:], bank[1].t[:, 256:288].rearrange("p (h n) -> p h n", h=4), cpast.t[:, cur, :].unsqueeze(1).to_broadcast([128, 4, 8]),
                     ALU.add, r=[bank[1], cpast], w=[gbuf])
                for h in range(4):
                    P.op('dve', lambda e: e.max(out=m8.t[:, h, :], in_=gbuf.t[:, h, :]), r=[gbuf], w=[m8])
                P.tt(selb.t[:], gbuf.t[:], m8.t[:, :, 2:3].to_broadcast([128, 4, 8]), ALU.is_ge, r=[gbuf, m8], w=[selb])
                P.stt(biasq.t[:].rearrange("p (h n) -> p h n", h=4), selb.t[:], -1.0, cpm.t[:, cur, :].unsqueeze(1).to_broadcast([128, 4, 8]),
                      ALU.add, ALU.mult, r=[selb, cpm], w=[biasq])
                P.tr([(bank[1].t[0:32, 384:512], biasq.t[:, 0:32])], identf.t, r=[biasq, identf], w=[bank[1]])
                P.cp(biasT.t[0:32, tok], bank[1].t[0:32, 384:512], r=[bank[1]], w=[biasT])
            gcount = 0
            for tt in range(4):
                if stage in ('proj',):
                    break
                i = tb * 4 + tt
                cur = i // 2
                tok = slice(tt * 128, (tt + 1) * 128)
                for h in range(4):
                    keys = list(range(i + 1))
                    ngr = (len(keys) + 3) // 4
                    for g in range(ngr):
                        js = keys[g * 4:(g + 1) * 4]
                        ptb = PTb[gcount % 2]
                        gcount += 1
                        groups = []
                        for s_, j in enumerate(js):
                            o = bank[4].t[:, s_ * 128:(s_ + 1) * 128]
                            hasb = (j // 2) < cur
                            hasc = (j == i)
                            groups.append((o, KT.t[:, h // 2, j * 128:(j + 1) * 128], QTm.t[:, h // 2, h % 2, tok], True, not (hasb or hasc)))
                            if hasb:
                                c = h * 8 + j // 2
                                groups.append((o, identb.t[:, c:c + 1].to_broadcast([128, 128]), biasT.t[:, tok], False, True))
                            if hasc:
                                groups.append((o, identb.t[:, :], causal.t[:, :], False, True))
                        P.mm(groups, r=[KT, QTm, biasT, identb, causal], w=[bank[4]])
                        n = len(js) * 128
                        P.act(ptb.t[:, 0:n], bank[4].t[:, 0:n], AF.Exp, r=[bank[4]], w=[ptb], scale=0.125)
                        P.mm([(bank[5].t[:, 0:65], ptb.t[:, s_ * 128:(s_ + 1) * 128], Vaug.t[:, j, h, :], (g == 0 and s_ == 0), (j == i))
                              for s_, j in enumerate(js)], r=[ptb, Vaug], w=[bank[5]])
                    P.op('dve', lambda e: e.reciprocal(out=small.t[:, 4:5], in_=bank[5].t[:, 64:65]), r=[bank[5]], w=[small])
                    P.ts(yatok.t[:, h * 64:(h + 1) * 64], bank[5].t[:, 0:64], small.t[:, 4:5], ALU.mult, r=[bank[5], small], w=[yatok])
                psT = bank[6].t[:].bitcast(BF16)
                P.tr([(psT[:, c * 128:(c + 1) * 128], yatok.t[:, c * 128:(c + 1) * 128]) for c in range(2)], identb.t, r=[yatok, identb], w=[bank[6]])
                P.cp(ymT.t[:, 0:2, tok], psT[:, 0:256].rearrange("p (c t) -> p c t", c=2), r=[bank[6]], w=[ymT], eng='act')
            if tb == NTB - 1 and WITH_S and WITH_KV and stage not in ('proj',):
                samp_A()
            if stage not in ('proj', 'A'):
                P.reset('AS')
                S = [P.carve('AS', 'S%d' % i, [128, TB]) for i in range(8)]
                load_w(I['w_in'], 768, 1280)
            for fc in range(2):
                if stage in ('proj', 'A'):
                    break
                ub = ubuf[fc]
                if tb > 0:
                    P.cp(ub.t[:, 0:3], ub.t[:, TB:TB + 3], r=[ub], w=[ub])
                P.mm([(bank[0].t[:, :], WA.t[:, kc, fc * 128:(fc + 1) * 128], hT.t[:, kc, :], kc == 0, kc == 7) for kc in range(8)],
                     r=[WA, hT], w=[bank[0]])
                P.mm([(bank[1].t[:, :], WA.t[:, kc, 256 + fc * 128:256 + (fc + 1) * 128], hT.t[:, kc, :], kc == 0, kc == 7) for kc in range(8)],
                     r=[WA, hT], w=[bank[1]])
                P.cp(ub.t[:, 3:3 + TB], bank[0].t[:, :], r=[bank[0]], w=[ub], eng='act')
                xc, rr, ii, aa, a2, hs, gg = S[0], S[1], S[2], S[3], S[4], S[5], S[6]
                P.ts(xc.t[:], ub.t[:, 0:TB], pcol('lru_cw0', fc), ALU.mult, pcol('lru_cb', fc), ALU.add, r=[ub, PT], w=[xc])
                for j in range(1, 4):
                    P.stt(xc.t[:], ub.t[:, j:j + TB], pcol('lru_cw%d' % j, fc), xc.t[:], ALU.mult, ALU.add, r=[ub, PT, xc], w=[xc])
                P.mm([(bank[2].t[:, :], BDa[fc].t[:, :], xc.t[:, :], True, True)], r=[BDa[fc], xc], w=[bank[2]])
                P.mm([(bank[3].t[:, :], BDx[fc].t[:, :], xc.t[:, :], True, True)], r=[BDx[fc], xc], w=[bank[3]])
                P.act(rr.t[:], bank[2].t[:], AF.Sigmoid, r=[bank[2], PT], w=[rr], bias=pcol('lru_ba', fc))
                P.act(ii.t[:], bank[3].t[:], AF.Sigmoid, r=[bank[3], PT], w=[ii], bias=pcol('lru_bx', fc))
                P.act(aa.t[:], rr.t[:], AF.Exp, r=[rr, lruc], w=[aa], scale=lruc.t[:, fc, 0:1])
                P.act(a2.t[:], rr.t[:], AF.Exp, r=[rr, lruc], w=[a2], scale=lruc.t[:, fc, 1:2])
                P.ts(a2.t[:], a2.t[:], 0.99999994, ALU.min, r=[a2], w=[a2])
                P.act(a2.t[:], a2.t[:], AF.Sqrt, r=[a2, onesf], w=[a2], scale=-1.0, bias=onesf.t[:, 0:1])
                P.tt(ii.t[:], ii.t[:], xc.t[:], ALU.mult, r=[ii, xc], w=[ii])
                P.tt(a2.t[:], a2.t[:], ii.t[:], ALU.mult, r=[a2, ii], w=[a2])
                P.op('dve', lambda e: e.tensor_tensor_scan(out=hs.t[:], data0=aa.t[:], data1=a2.t[:], initial=hstate.t[:, fc:fc + 1],
                                                           op0=ALU.mult, op1=ALU.add), r=[aa, a2, hstate], w=[hs])
                P.cp(hstate.t[:, fc:fc + 1], hs.t[:, TB - 1:TB], r=[hs], w=[hstate])
                P.act(gg.t[:], bank[1].t[:], AF.Gelu_apprx_tanh, r=[bank[1]], w=[gg])
                P.tt(ymT.t[:, 2 + fc, :], hs.t[:], gg.t[:], ALU.mult, r=[hs, gg], w=[ymT])
                if tb == NTB - 1:
                    P.dma('sp', [(O['lru_conv_p'][l, :, fc * 128:(fc + 1) * 128].rearrange("j p -> p j"), ub.t[:, TB:TB + 3])],
                          r=[ub], sembuf=ub, is_out=True, allow_slow_non_contiguous=True)
            if tb == NTB - 1 and stage not in ('proj', 'A') and WITH_S:
                samp_B()
            if tb == NTB - 1 and stage not in ('proj', 'A'):
                P.dma('sp', [(O['lru_h_p'][l, :].rearrange("(c p) -> p c", p=128), hstate.t[:, :])], r=[hstate], sembuf=hstate, is_out=True,
                      allow_slow_non_contiguous=True)
            if stage not in ('proj', 'A', 'B'):
                P.reset('AS')
                S = [P.carve('AS', 'S%d' % i, [128, TB]) for i in range(6)]
                Q = [P.carve('AS', 'Q%d' % i, [128, 256]) for i in range(8)]
                Qb = [P.carve('AS', 'Qb%d' % i, [128, 256], BF16) for i in range(7)]
                load_w(I['w_in'], 1280, 2052)
                for c4 in range(4):
                    cbf = cbuf[c4]
                    if tb > 0:
                        P.cp(cbf.t[:, 0:3], cbf.t[:, TB:TB + 3], r=[cbf], w=[cbf])
                    P.mm([(bank[0].t[:, :], WA.t[:, kc, 256 + c4 * 128:256 + (c4 + 1) * 128], hT.t[:, kc, :], kc == 0, kc == 7) for kc in range(8)],
                         r=[WA, hT], w=[bank[0]])
                    P.cp(cbf.t[:, 3:3 + TB], bank[0].t[:, :], r=[bank[0]], w=[cbf], eng='act')
                    ab = 'a' if c4 < 2 else 'b'
                    fc = c4 % 2
                    xo = S[c4]
                    P.ts(xo.t[:], cbf.t[:, 0:TB], pcol('ssd_cw0' + ab, fc), ALU.mult, pcol('ssd_cb' + ab, fc), ALU.add, r=[cbf, PT], w=[xo])
                    for j in range(1, 4):
                        P.stt(xo.t[:], cbf.t[:, j:j + TB], pcol('ssd_cw%d%s' % (j, ab), fc), xo.t[:], ALU.mult, ALU.add, r=[cbf, PT, xo], w=[xo])
                    P.act(xo.t[:], xo.t[:], AF.Silu, r=[xo], w=[xo])
                    if tb == NTB - 1:
                        P.dma('sp', [(O['ssd_conv_p'][l, :, c4 * 128:(c4 + 1) * 128].rearrange("j p -> p j"), cbf.t[:, TB:TB + 3])],
                              r=[cbf], sembuf=cbf, is_out=True, allow_slow_non_contiguous=True)
                for g in range(2):
                    P.memset(S[4 + g].t[:], 0.0, w=[S[4 + g]])
                    pr = slice(g * 64, (g + 1) * 64)
                    P.cp(S[4 + g].t[pr, :], S[3].t[pr, :], r=[S[3]], w=[S[4 + g]])
                for tt in range(4):
                    tok = slice(tt * 128, (tt + 1) * 128)
                    xtok, dtt, xdt, yv, sz = Q[0], Q[1], Q[2], Q[5], Q[6]
                    Btb, xdtd, xdtb, sTb, MTb, CpTb, ycb = Qb[0], Qb[1], Qb[2], Qb[3], Qb[4], Qb[5], Qb[6]
                    P.mm([(bank[1].t[:, 0:256], hT.t[:, kc, tok], WA.t[:, kc, 0:256], kc == 0, kc == 7) for kc in range(8)], r=[hT, WA], w=[bank[1]])
                    P.mm([(bank[1].t[:, 256:260], hT.t[:, kc, tok], WA.t[:, kc, 768:772], kc == 0, kc == 7) for kc in range(8)], r=[hT, WA], w=[bank[1]])
                    P.tr([(bank[2].t[:, c * 128:(c + 1) * 128], S[c].t[:, tok]) for c in range(3)], identf.t, r=[S[0], S[1], S[2], identf], w=[bank[2]])
                    P.cp(xtok.t[:, :], bank[2].t[:, 0:256], r=[bank[2]], w=[xtok], eng='act')
                    P.cp(Btb.t[:, 0:128], bank[2].t[:, 256:384], r=[bank[2]], w=[Btb])
                    P.tt(dtt.t[:, 0:4], bank[1].t[:, 256:260], hp4.t[:, 0:4], ALU.add, r=[bank[1], hp4], w=[dtt])
                    P.act(dtt.t[:, 0:4], dtt.t[:, 0:4], AF.Exp, r=[dtt], w=[dtt])
                    P.act(dtt.t[:, 0:4], dtt.t[:, 0:4], AF.Ln, r=[dtt, onesf], w=[dtt], bias=onesf.t[:, 0:1])
                    P.tt(dtt.t[:, 4:8], dtt.t[:, 0:4], hp4.t[:, 4:8], ALU.mult, r=[dtt, hp4], w=[dtt])
                    P.tt(xdt.t[:].rearrange("p (h d) -> p h d", h=4), xtok.t[:].rearrange("p (h d) -> p h d", h=4),
                         dtt.t[:, 0:4].unsqueeze(2).to_broadcast([128, 4, 64]), ALU.mult, r=[xtok, dtt], w=[xdt])
                    P.mm([(bank[3].t[:, 0:4], tri.t[:, :], dtt.t[:, 4:8], True, True)], r=[tri, dtt], w=[bank[3]])
                    P.mm([(bank[3].t[:, 4:8], onesf.t[:, :], dtt.t[:, 4:8], True, True)], r=[onesf, dtt], w=[bank[3]])
                    P.cp(dtt.t[:, 8:16], bank[3].t[:, 0:8], r=[bank[3]], w=[dtt])
                    P.tt(dtt.t[:, 16:20], dtt.t[:, 12:16], dtt.t[:, 8:12], ALU.subtract, r=[dtt], w=[dtt])
                    P.act(dtt.t[:, 16:20], dtt.t[:, 16:20], AF.Exp, r=[dtt], w=[dtt])
                    P.act(dtt.t[:, 20:24], dtt.t[:, 12:16], AF.Exp, r=[dtt], w=[dtt])
                    P.mm([(bank[4].t[:, h * 128:(h + 1) * 128], dtt.t[:, 4 + h:5 + h].to_broadcast([128, 128]), tri.t[:, :], True, True) for h in range(4)],
                         r=[dtt, tri], w=[bank[4]])
                    P.mm([(bank[5].t[:, g * 128:(g + 1) * 128], S[2].t[:, tok], S[4 + g].t[:, tok], True, True) for g in range(2)],
                         r=[S[2], S[4], S[5]], w=[bank[5]])
                    P.tt(xdtd.t[:].rearrange("p (h d) -> p h d", h=4), xdt.t[:].rearrange("p (h d) -> p h d", h=4),
                         dtt.t[:, 16:20].unsqueeze(2).to_broadcast([128, 4, 64]), ALU.mult, r=[xdt, dtt], w=[xdtd])
                    P.cp(xdtb.t[:, :], xdt.t[:, :], r=[xdt], w=[xdtb], eng='act')
                    P.cp(sTb.t[:, :], sT.t[:].rearrange("p h d -> p (h d)"), r=[sT], w=[sTb])
                    for h in range(4):
                        g = h // 2
                        Dm = Q[3 + (h % 2)]
                        hs_ = slice((h % 2) * 128, (h % 2 + 1) * 128)
                        P.stt(Dm.t[:, 0:128], bank[4].t[:, h * 128:(h + 1) * 128], dtt.t[:, 8 + h:9 + h], negmask.t[:, :], ALU.subtract, ALU.add,
                              r=[bank[4], dtt, negmask], w=[Dm])
                        P.act(Dm.t[:, 0:128], Dm.t[:, 0:128], AF.Exp, r=[Dm], w=[Dm])
                        P.tt(MTb.t[:, hs_], Dm.t[:, 0:128], bank[5].t[:, g * 128:(g + 1) * 128], ALU.mult, r=[Dm, bank[5]], w=[MTb])
                        P.act(Dm.t[:, 128:256], bank[4].t[:, h * 128:(h + 1) * 128], AF.Exp, r=[bank[4]], w=[Dm])
                        P.tt(CpTb.t[:, hs_], S[4 + g].t[:, tok], Dm.t[:, 128:256], ALU.mult, r=[S[4 + g], Dm], w=[CpTb])
                        P.mm([(bank[6].t[:, h * 64:(h + 1) * 64], MTb.t[:, hs_], xdtb.t[:, h * 64:(h + 1) * 64], True, False),
                              (bank[6].t[:, h * 64:(h + 1) * 64], CpTb.t[:, hs_], sTb.t[:, h * 64:(h + 1) * 64], False, True)],
                             r=[MTb, CpTb, xdtb, sTb], w=[bank[6]])
                    P.mm([(bank[7].t[:, h * 64:(h + 1) * 64], Btb.t[:, 0:128], xdtd.t[:, h * 64:(h + 1) * 64], True, True) for h in range(4)],
                         r=[Btb, xdtd], w=[bank[7]])
                    for h in range(4):
                        pr = slice((h // 2) * 64, (h // 2 + 1) * 64)
                        P.stt(sT.t[pr, h, :], sT.t[pr, h, :], dtt.t[pr, 20 + h:21 + h], bank[7].t[pr, h * 64:(h + 1) * 64], ALU.mult, ALU.add,
                              r=[sT, dtt, bank[7]], w=[sT])
                    P.tt(yv.t[:].rearrange("p (h d) -> p h d", h=4), xtok.t[:].rearrange("p (h d) -> p h d", h=4),
                         hp4.t[:, 8:12].unsqueeze(2).to_broadcast([128, 4, 64]), ALU.mult, r=[xtok, hp4], w=[yv])
                    P.tt(yv.t[:, :], yv.t[:, :], bank[6].t[:, 0:256], ALU.add, r=[yv, bank[6]], w=[yv])
                    P.act(sz.t[:, :], bank[1].t[:, 0:256], AF.Silu, r=[bank[1]], w=[sz])
                    P.tt(yv.t[:, :], yv.t[:, :], sz.t[:, :], ALU.mult, r=[yv, sz], w=[yv])
                    rmsnorm_tile(yv.t[:, :], yv, ycb, width=256, gain=nwbc.t[:, :], gbuf_=nwbc)
                    psT = bank[3].t[:].bitcast(BF16)
                    P.tr([(psT[:, c * 128:(c + 1) * 128], ycb.t[:, c * 128:(c + 1) * 128]) for c in range(2)], identb.t, r=[ycb, identb], w=[bank[3]])
                    P.cp(ymT.t[:, 4:6, tok], psT[:, 0:256].rearrange("p (c t) -> p c t", c=2), r=[bank[3]], w=[ymT], eng='act')
                if tb == NTB - 1:
                    P.tr([(bank[2].t[:, c * 128:(c + 1) * 128], sT.t[:].rearrange("p h d -> p (h d)")[:, c * 128:(c + 1) * 128]) for c in range(2)],
                         identf.t, r=[sT, identf], w=[bank[2]])
                    P.cp(Q[7].t[:, :], bank[2].t[:, 0:256], r=[bank[2]], w=[Q[7]])
                    P.dma('sp', [(O['ssd_p'][l, h, :, :], Q[7].t[(h % 2) * 64:(h % 2 + 1) * 64, (h // 2) * 192:(h // 2) * 192 + 64]) for h in range(4)],
                          r=[Q[7]], sembuf=Q[7], is_out=True)
                    if WITH_S:
                        P.reset('AS')
                        samp_C()
            if stage not in ('proj', 'A', 'B', 'C'):
                P.reset('AS')
                load_w(I['w_in'], 2052, 2948)
                RW = [P.carve('AS', 'RW%d' % i, [128, 2, 128]) for i in range(16)]
                Wst = [P.carve('AS', 'Wst%d' % i, [128, 2, 64]) for i in range(2)]
                tmpst = [P.carve('AS', 'tmpst%d' % i, [128, 2, 64]) for i in range(2)]
                vtok = P.carve('AS', 'vtok', [128, 256], BF16)
                xT = P.carve('AS', 'xT', [128, 128])
                fl = lambda b: b.t[:].rearrange("p a b -> p (a b)")
                rT, kT_, vT, dl, dT, aT, gT, kk, kp, bT, nk0, nk1, rm0, rm1, t1, t2 = RW
                for sb in range(4):
                    tok = slice(sb * 128, (sb + 1) * 128)
                    for c7 in range(7):
                        cbx = cb7[c7]
                        P.mm([(bank[0].t[:, 0:128], WA.t[:, kc, c7 * 128:(c7 + 1) * 128], hT.t[:, kc, tok], kc == 0, kc == 7) for kc in range(8)],
                             r=[WA, hT], w=[bank[0]])
                        P.cp(cbx.t[:, 1:129], bank[0].t[:, 0:128], r=[bank[0]], w=[cbx], eng='act')
                        if c7 < 6:
                            dbuf = (rT, kT_, vT)[c7 // 2]
                            dst = dbuf.t[:, c7 % 2, :]
                            mu = pcol(('rw_mu_r', 'rw_mu_k', 'rw_mu_v')[c7 // 2], c7 % 2)
                        else:
                            dbuf = xT
                            dst = xT.t[:, :]
                            mu = pcol('rw_mu_x', 0)
                        P.tt(dl.t[:, 0, :], cbx.t[:, 0:128], cbx.t[:, 1:129], ALU.subtract, r=[cbx], w=[dl])
                        P.stt(dst, dl.t[:, 0, :], mu, cbx.t[:, 1:129], ALU.mult, ALU.add, r=[dl, PT, cbx], w=[dbuf])
                        P.cp(cbx.t[:, 0:1], cbx.t[:, 128:129], r=[cbx], w=[cbx])
                    P.act(xT.t[0:32, :], xT.t[0:32, :], AF.Tanh, r=[xT], w=[xT])
                    P.act(xT.t[64:128, :], xT.t[64:128, :], AF.Sigmoid, r=[xT], w=[xT])
                    for hp in range(2):
                        cols = slice(hp * 128, (hp + 1) * 128)
                        P.mm([(bank[1].t[:, 0:128], Wlow.t[0:32, cols], xT.t[0:32, :], True, True)], r=[Wlow, xT], w=[bank[1]])
                        P.act(dT.t[:, hp, :], bank[1].t[:, 0:128], AF.Sigmoid, r=[bank[1], PT], w=[dT], bias=pcol('rw_w0', hp))
                        P.act(dT.t[:, hp, :], dT.t[:, hp, :], AF.Exp, r=[dT], w=[dT], scale=-0.6065306597126334)
                        P.mm([(bank[1].t[:, 128:256], Wlow.t[32:64, cols], xT.t[32:64, :], True, True)], r=[Wlow, xT], w=[bank[1]])
                        P.act(aT.t[:, hp, :], bank[1].t[:, 128:256], AF.Sigmoid, r=[bank[1], PT], w=[aT], bias=pcol('rw_a0', hp))
                        P.mm([(bank[1].t[:, 256:384], Wlow.t[64:128, cols], xT.t[64:128, :], True, True)], r=[Wlow, xT], w=[bank[1]])
                        P.cp(gT.t[:, hp, :], bank[1].t[:, 256:384], r=[bank[1]], w=[gT], eng='act')
                        P.ts(kk.t[:, hp, :], kT_.t[:, hp, :], pcol('rw_kk', hp), ALU.mult, r=[kT_, PT], w=[kk])
                    P.tt(fl(t1), fl(kk), fl(kk), ALU.mult, r=[kk], w=[t1])
                    P.mm([(bank[2].t[:, 0:256], BDones.t[:, :], fl(t1), True, True)], r=[BDones, t1], w=[bank[2]])
                    P.act(fl(t2), bank[2].t[:, 0:256], AF.Sqrt, r=[bank[2]], w=[t2])
                    P.ts(fl(t2), fl(t2), 1e-12, ALU.max, r=[t2], w=[t2])
                    P.op('dve', lambda e: e.reciprocal(out=fl(t2), in_=fl(t2)), r=[t2], w=[t2])
                    P.tt(fl(kk), fl(kk), fl(t2), ALU.mult, r=[kk, t2], w=[kk])
                    P.ts(fl(nk0), fl(kk), hmask.t[:, 0:1], ALU.mult, -1.0, ALU.mult, r=[kk, hmask], w=[nk0])
                    P.ts(fl(nk1), fl(kk), hmask.t[:, 1:2], ALU.mult, -1.0, ALU.mult, r=[kk, hmask], w=[nk1])
                    for hp in range(2):
                        P.ts(t1.t[:, hp, :], aT.t[:, hp, :], pcol('rw_ka', hp), ALU.mult, rwc.t[:, hp:hp + 1], ALU.add, r=[aT, PT, rwc], w=[t1])
                    P.tt(fl(kp), fl(kT_), fl(t1), ALU.mult, r=[kT_, t1], w=[kp])
                    P.tt(fl(bT), fl(kk), fl(aT), ALU.mult, r=[kk, aT], w=[bT])
                    P.ts(fl(rm0), fl(rT), hmask.t[:, 0:1], ALU.mult, r=[rT, hmask], w=[rm0])
                    P.ts(fl(rm1), fl(rT), hmask.t[:, 1:2], ALU.mult, r=[rT, hmask], w=[rm1])
                    P.tr([(bank[2].t[:, 256 + hp * 128:256 + (hp + 1) * 128], vT.t[:, hp, :]) for hp in range(2)], identf.t, r=[vT, identf], w=[bank[2]])
                    P.cp(vtok.t[:, :], bank[2].t[:, 256:512], r=[bank[2]], w=[vtok])
                    for hp in range(2):
                        P.stt(t1.t[:, hp, :], rT.t[:, hp, :], pcol('rw_rk', hp), kp.t[:, hp, :], ALU.mult, ALU.mult, r=[rT, PT, kp], w=[t1])
                    P.mm([(bank[3].t[:, 0:256], BDones.t[:, :], fl(t1), True, True)], r=[BDones, t1], w=[bank[3]])
                    P.tt(fl(t2), bank[3].t[:, 0:256], fl(vT), ALU.mult, r=[bank[3], vT], w=[t2])
                    vtv = vtok.t[:, :].rearrange("p (hp h2 v) -> p hp h2 v", hp=2, h2=2)
                    nks = (nk0, nk1)
                    rms = (rm0, rm1)
                    Yb = bank[4]
                    vbank = (bank[5], bank[7])
                    def emit_vb(t):
                        vb_ = vbank[t % 2]
                        P.mm([(vb_.t[h2 * 64:(h2 + 1) * 64, 0:128].rearrange("p (a b) -> p a b", a=2), identb.t[:, t:t + 1].to_broadcast([128, 64]),
                               vtv[:, :, h2, :], True, True) for h2 in range(2)], r=[identb, vtok], w=[vb_])
                    def emit_y(t):
                        P.mm([(Yb.t[h2 * 64:(h2 + 1) * 64, hp * 128 + t:hp * 128 + t + 1], ST.t[:, hp, :], rms[h2].t[:, hp, t:t + 1], True, True)
                              for hp in range(2) for h2 in range(2)], r=[ST, rm0, rm1], w=[Yb])
                    emit_vb(0)
                    for t in range(128):
                        ws = Wst[t % 2]
                        tm = tmpst[t % 2]
                        vb_ = vbank[t % 2]
                        P.mm([(bank[6].t[h2 * 64:(h2 + 1) * 64, hp * 64:(hp + 1) * 64], nks[h2].t[:, hp, t:t + 1].to_broadcast([128, 64]), ST.t[:, hp, :], True, True)
                              for hp in range(2) for h2 in range(2)], r=[nk0, nk1, ST], w=[bank[6]])
                        if t > 0:
                            emit_y(t - 1)
                        if t + 1 < 128:
                            emit_vb(t + 1)
                        for hp in range(2):
                            P.act(ws.t[:, hp, :], vb_.t[:, hp * 64:(hp + 1) * 64], AF.Copy, r=[vb_, kp], w=[ws], scale=kp.t[:, hp, t:t + 1])
                        for hp in range(2):
                            P.stt(tm.t[:, hp, :], bank[6].t[:, hp * 64:(hp + 1) * 64], bT.t[:, hp, t:t + 1], ws.t[:, hp, :], ALU.mult, ALU.add,
                                  r=[bank[6], bT, ws], w=[tm])
                        for hp in range(2):
                            P.stt(ST.t[:, hp, :], ST.t[:, hp, :], dT.t[:, hp, t:t + 1], tm.t[:, hp, :], ALU.mult, ALU.add, r=[ST, dT, tm], w=[ST])
                    emit_y(127)
                    cen, sq = kk, kp
                    P.cp(fl(t1), Yb.t[:, 0:256], r=[Yb], w=[t1], eng='act')
                    P.mm([(bank[3].t[:, 0:256], BDones.t[:, :], fl(t1), True, True)], r=[BDones, t1], w=[bank[3]])
                    P.stt(fl(cen), bank[3].t[:, 0:256], -1.0 / 64, fl(t1), ALU.mult, ALU.add, r=[bank[3], t1], w=[cen])
                    P.tt(fl(sq), fl(cen), fl(cen), ALU.mult, r=[cen], w=[sq])
                    P.mm([(bank[3].t[:, 256:512], BDones.t[:, :], fl(sq), True, True)], r=[BDones, sq], w=[bank[3]])
                    P.act(fl(sq), bank[3].t[:, 256:512], AF.Sqrt, r=[bank[3], epsb], w=[sq], scale=1.0 / 64, bias=epsb.t[:, 1:2])
                    P.op('dve', lambda e: e.reciprocal(out=fl(sq), in_=fl(sq)), r=[sq], w=[sq])
                    P.tt(fl(cen), fl(cen), fl(sq), ALU.mult, r=[cen, sq], w=[cen])
                    for hp in range(2):
                        P.ts(cen.t[:, hp, :], cen.t[:, hp, :], pcol('rw_lnw', hp), ALU.mult, pcol('rw_lnb', hp), ALU.add, r=[cen, PT], w=[cen])
                    P.tt(fl(cen), fl(cen), fl(t2), ALU.add, r=[cen, t2], w=[cen])
                    P.tt(ymT.t[:, 6:8, tok], cen.t[:, :, :], gT.t[:, :, :], ALU.mult, r=[cen, gT], w=[ymT])
                if tb == NTB - 1:
                    P.dma('sp', [(O['rwkv_shift_p'][l, c7 * 128:(c7 + 1) * 128].rearrange("(p o) -> p o", o=1), cb7[c7].t[:, 128:129]) for c7 in range(7)],
                          r=cb7, sembuf=cb7[0], is_out=True, allow_slow_non_contiguous=True)
                    P.tr([(bank[2].t[:, 0:128], ST.t[:].rearrange("p a b -> p (a b)"))], identf.t, r=[ST, identf], w=[bank[2]])
                    P.cp(fl(t1)[:, 0:128], bank[2].t[:, 0:128], r=[bank[2]], w=[t1])
                    P.dma('sp', [(O['rwkv_p'][l, h, :, :], fl(t1)[(h // 2) * 64:(h // 2 + 1) * 64, (h % 2) * 64:(h % 2 + 1) * 64]) for h in range(4)],
                          r=[t1], sembuf=t1, is_out=True)
                    if WITH_S:
                        P.reset('AS')
                        samp_D()
            if stage in ('all', 'W', 'X', 'PEER'):
                P.reset('AS')
                load_w(I['w_out'], 0, 1024)
                for tt in range(4):
                    i = tb * 4 + tt
                    tok = slice(tt * 128, (tt + 1) * 128)
                    for half in range(2):
                        hs_ = slice(half * 512, (half + 1) * 512)
                        P.mm([(bank[half].t[:, :], ymT.t[:, kc, tok], WA.t[:, kc, hs_], kc == 0, kc == 7) for kc in range(8)], r=[ymT, WA], w=[bank[half]])
                        P.tt(xres.t[:, i, hs_], xres.t[:, i, hs_], bank[half].t[:, :], ALU.add, r=[xres, bank[half]], w=[xres])
                if tb == NTB - 1 and WITH_S:
                    samp_W()
            if 'ymTs' in DBG and l == 0 and tb == NTB - 1:
                dbg('ymTs', DBG['ymTs'], ymTs.t[:, :, :], [ymTs])
            if 'ymT' in DBG and l == 0:
                dbg('ymT', DBG['ymT'][:, :, tb * TB:(tb + 1) * TB], ymT.t[:, :, :], [ymT])
        if 'x1' in DBG and l == 0:
            dbg('x1', DBG['x1'].rearrange("(n p) c -> p n c", p=128), xres.t[:, :, :], [xres])
        if stage in ('all', 'X', 'PEER'):
            P.reset('AL'); P.reset('AS')
            hT = P.carve('AL', 'hT', [128, 8, TB], BF16)
            qT = P.carve('AL', 'qT', [128, 8, TB], BF16)
            oT = P.carve('AL', 'oT', [128, 8, TB], BF16)
            W1 = P.carve('AL', 'WA', [128, 8, D], BF16)
            W2 = P.carve('AL', 'WB', [128, 8, D], BF16)
            memT = P.carve('AL', 'memT', [128, 8, 256], BF16)
            KTm = P.carve('AL', 'KTm', [128, 8, 256], BF16)
            Vx = P.carve('AL', 'Vx', [128, 2, D], BF16)
            onesb = P.carve('AL', 'onesb', [128, 128], BF16)
            memf = P.carve('AS', 'memf', [128, 2, D])
            memb = P.carve('AS', 'memb', [128, 2, D], BF16)
            kst = [P.carve('AS', 'kst%d' % i, [128, D]) for i in range(2)]
            P.memset(onesb.t[:], 1.0, w=[onesb])
            def load2(Wb, src):
                for kc in range(8):
                    P.dma('pool', [(Wb.t[:, kc, :], src[l, kc * 128:(kc + 1) * 128, :])], w=[Wb])
            load2(W1, I['x_wk']); load2(W2, I['x_wv'])
            P.dma('sp', [(memf.t[:, mt, :], I['memp'][mt * 128:(mt + 1) * 128, :]) for mt in range(2)], w=[memf])
            P.cp(memb.t[:].rearrange("p a b -> p (a b)"), memf.t[:].rearrange("p a b -> p (a b)"), r=[memf], w=[memb], eng='act')
            for mt in range(2):
                psT = bank[6].t[:].bitcast(BF16)
                P.tr([(psT[:, kc * 128:(kc + 1) * 128], memb.t[:, mt, kc * 128:(kc + 1) * 128]) for kc in range(8)], identb.t, r=[memb, identb], w=[bank[6]])
                P.cp(memT.t[:, :, mt * 128:(mt + 1) * 128], psT.rearrange("p (k t) -> p k t", k=8), r=[bank[6]], w=[memT])
            for (Wb, oname, isv) in ((W1, 'mem_k_p', False), (W2, 'mem_v_p', True)):
                for mt in range(2):
                    stg = kst[mt]
                    for half in range(2):
                        hs_ = slice(half * 512, (half + 1) * 512)
                        P.mm([(bank[half].t[:, :], memT.t[:, kc, mt * 128:(mt + 1) * 128], Wb.t[:, kc, hs_], kc == 0, kc == 7) for kc in range(8)],
                             r=[memT, Wb], w=[bank[half]])
                        P.cp(stg.t[:, hs_], bank[half].t[:, :], r=[bank[half]], w=[stg], eng='act')
                        if isv:
                            P.cp(Vx.t[:, mt, hs_], bank[half].t[:, :], r=[bank[half]], w=[Vx])
                    P.dma('sp', [(O[oname][l, mt * 128:(mt + 1) * 128, :], stg.t[:, :])], r=[stg], sembuf=stg, is_out=True)
            for dc in range(8):
                P.mm([(bank[2].t[:, 0:256], W1.t[:, kc, dc * 128:(dc + 1) * 128], memT.t[:, kc, :], kc == 0, kc == 7) for kc in range(8)],
                     r=[W1, memT], w=[bank[2]])
                P.cp(KTm.t[:, dc, :], bank[2].t[:, 0:256], r=[bank[2]], w=[KTm])
            load2(W1, I['x_wq']); load2(W2, I['x_wo'])
            P.dma('sp', [(gbc.t[:], I['norm_x'][l:l + 1, :].to_broadcast([128, D]))], w=[gbc])
            for tb in range(NTB):
                P.reset('AS')
                hb = [P.carve('AS', 'hb%d' % i, [128, D], BF16) for i in range(2)]
                PTx = P.carve('AS', 'PTx', [128, 2, TB], BF16)
                rec = P.carve('AS', 'rec', [128, TB])
                for tt in range(4):
                    i = tb * 4 + tt
                    hbuf = hb[i % 2]
                    rmsnorm_tile(xres.t[:, i, :], xres, hbuf)
                    psT = bank[6].t[:].bitcast(BF16)
                    P.tr([(psT[:, kc * 128:(kc + 1) * 128], hbuf.t[:, kc * 128:(kc + 1) * 128]) for kc in range(8)], identb.t,
                         r=[hbuf, identb], w=[bank[6]])
                    P.cp(hT.t[:, :, tt * 128:(tt + 1) * 128], psT.rearrange("p (k t) -> p k t", k=8), r=[bank[6]], w=[hT], eng='act')
                for dc in range(8):
                    P.mm([(bank[0].t[:, :], W1.t[:, kc, dc * 128:(dc + 1) * 128], hT.t[:, kc, :], kc == 0, kc == 7) for kc in range(8)],
                         r=[W1, hT], w=[bank[0]])
                    P.cp(qT.t[:, dc, :], bank[0].t[:, :], r=[bank[0]], w=[qT], eng=('act' if dc % 2 else 'dve'))
                for h in range(4):
                    for mt in range(2):
                        P.mm([(bank[1 + mt].t[:, :], KTm.t[:, 2 * h + c, mt * 128:(mt + 1) * 128], qT.t[:, 2 * h + c, :], c == 0, c == 1) for c in range(2)],
                             r=[KTm, qT], w=[bank[1 + mt]])
                        P.act(PTx.t[:, mt, :], bank[1 + mt].t[:, :], AF.Exp, r=[bank[1 + mt]], w=[PTx], scale=1.0 / 16)
                    P.mm([(bank[3].t[:, :], onesb.t[:, :], PTx.t[:, mt, :], mt == 0, mt == 1) for mt in range(2)], r=[onesb, PTx], w=[bank[3]])
                    P.op('dve', lambda e: e.reciprocal(out=rec.t[:, :], in_=bank[3].t[:, :]), r=[bank[3]], w=[rec])
                    for c in range(2):
                        P.mm([(bank[4 + c].t[:, :], Vx.t[:, mt, h * 256 + c * 128:h * 256 + (c + 1) * 128], PTx.t[:, mt, :], mt == 0, mt == 1) for mt in range(2)],
                             r=[Vx, PTx], w=[bank[4 + c]])
                        P.tt(oT.t[:, 2 * h + c, :], bank[4 + c].t[:, :], rec.t[:, :], ALU.mult, r=[bank[4 + c], rec], w=[oT])
                for tt in range(4):
                    i = tb * 4 + tt
                    tok = slice(tt * 128, (tt + 1) * 128)
                    for half in range(2):
                        hs_ = slice(half * 512, (half + 1) * 512)
                        P.mm([(bank[half].t[:, :], oT.t[:, kc, tok], W2.t[:, kc, hs_], kc == 0, kc == 7) for kc in range(8)], r=[oT, W2], w=[bank[half]])
                        P.tt(xres.t[:, i, hs_], xres.t[:, i, hs_], bank[half].t[:, :], ALU.add, r=[xres, bank[half]], w=[xres])
            if WITH_S:
                P.reset('AS')
                samp_norm(hTs)
                qS = P.carve('AS', 'qS', [4, D]); oTs = P.carve('AS', 'oTs', [128, 8, 4], BF16)
                Kc = P.carve('AS', 'Kc', [128, 2, D]); Vc = P.carve('AS', 'Vc', [128, 2, D])
                prodx = P.carve('AS', 'prodx', [128, D]); scx = P.carve('AS', 'scx', [128, 2, 4]); obx = P.carve('AS', 'obx', [4, D + 4])
                dmx = P.carve('AS', 'dmx', [4, D])
                P.dma('sp', [(dmx.t[:, :], I['c_dmask4x'][:, :])], w=[dmx])
                for half in range(2):
                    hs_ = slice(half * 512, (half + 1) * 512)
                    P.mm([(bank[half].t[0:4, :], hTs.t[:, kc, :], W1.t[:, kc, hs_], kc == 0, kc == 7) for kc in range(8)], r=[hTs, W1], w=[bank[half]])
                    P.cp(qS.t[:, hs_], bank[half].t[0:4, :], r=[bank[half]], w=[qS])
                for s_ in range(4):
                    P.dma('sp', [(Kc.t[:, mt, :], I['cmk'][l, s_, mt * 128:(mt + 1) * 128, :]) for mt in range(2)], w=[Kc])
                    P.dma('sp', [(Vc.t[:, mt, :], I['cmv'][l, s_, mt * 128:(mt + 1) * 128, :]) for mt in range(2)], w=[Vc])
                    for half in range(2):
                        P.mm([(bank[2 + half].t[:, :], identf.t[0:4, s_:s_ + 1].to_broadcast([4, 128]), qS.t[0:4, half * 512:(half + 1) * 512], True, True)],
                             r=[identf, qS], w=[bank[2 + half]])
                    for mt in range(2):
                        for half in range(2):
                            hs_ = slice(half * 512, (half + 1) * 512)
                            P.tt(prodx.t[:, hs_], Kc.t[:, mt, hs_], bank[2 + half].t[:, :], ALU.mult, r=[Kc, bank[2 + half]], w=[prodx])
                        P.op('dve', lambda e: e.tensor_reduce(out=scx.t[:, mt, :], in_=prodx.t[:, :].rearrange("p (h d) -> p h d", h=4), op=ALU.add, axis=AX.X), r=[prodx], w=[scx])
                    P.act(scx.t[:].rearrange("p a b -> p (a b)"), scx.t[:].rearrange("p a b -> p (a b)"), AF.Exp, r=[scx], w=[scx], scale=1.0 / 16)
                    for half in range(2):
                        hs_ = slice(half * 512, (half + 1) * 512)
                        P.mm([(bank[4 + half].t[0:4, :], scx.t[:, mt, :], Vc.t[:, mt, hs_], mt == 0, mt == 1) for mt in range(2)], r=[scx, Vc], w=[bank[4 + half]])
                    P.mm([(bank[6].t[0:4, 0:1], scx.t[:, mt, :], onesf.t[:, 0:1], mt == 0, mt == 1) for mt in range(2)], r=[scx, onesf], w=[bank[6]])
                    P.op('dve', lambda e: e.reciprocal(out=obx.t[:, D:D + 1], in_=bank[6].t[0:4, 0:1]), r=[bank[6]], w=[obx])
                    for half in range(2):
                        hs_ = slice(half * 512, (half + 1) * 512)
                        P.stt(obx.t[:, hs_], bank[4 + half].t[0:4, :], obx.t[:, D:D + 1], dmx.t[:, hs_], ALU.mult, ALU.mult, r=[bank[4 + half], obx, dmx], w=[obx])
                    P.mm([(bank[7].t[:, s_ * 8 + c:s_ * 8 + c + 1], obx.t[0:4, c * 128:(c + 1) * 128], onesf.t[0:4, 0:1], True, True) for c in range(8)],
                         r=[obx, onesf], w=[bank[7]])
                P.cp(oTs.t[:, :, :], bank[7].t[:, 0:32].rearrange("p (s c) -> p c s", c=8), r=[bank[7]], w=[oTs])
                for half in range(2):
                    hs_ = slice(half * 512, (half + 1) * 512)
                    P.mm([(bank[half].t[0:4, :], oTs.t[:, kc, :], W2.t[:, kc, hs_], kc == 0, kc == 7) for kc in range(8)], r=[oTs, W2], w=[bank[half]])
                    P.tt(xsp.t[0:4, hs_], xsp.t[0:4, hs_], bank[half].t[0:4, :], ALU.add, r=[xsp, bank[half]], w=[xsp])
            if 'x2' in DBG and l == 0:
                dbg('x2', DBG['x2'].rearrange("(n p) c -> p n c", p=128), xres.t[:, :, :], [xres])
        if stage in ('all', 'PEER'):
            NTP = NT + 1 if WITH_S else NT
            xtile = lambda i: (xres.t[:, i, :] if i < NT else xsp.t[:, :])
            xbuf_ = lambda i: (xres if i < NT else xsp)
            P.reset('AL'); P.reset('AS')
            hTall = P.carve('AL', 'hTall', [128, 8, T + 128], BF16)
            Wpq = P.carve('AL', 'Wpq', [128, 8, 2048], BF16)
            skT = P.carve('AL', 'skT', [128, 16, 128], BF16)
            iotar = P.carve('AL', 'iotar', [128, 128])
            bd16 = P.carve('AL', 'bd16', [128, 8, 16], BF16)
            WcT = P.carve('AL', 'WcT', [128, 16, 128], BF16)
            ixT = P.carve('AL', 'ixT', [128, 2, 128])
            P.dma('sp', [(gbc.t[:], I['norm_ffn'][l:l + 1, :].to_broadcast([128, D]))], w=[gbc])
            for kc in range(8):
                P.dma('pool', [(Wpq.t[:, kc, :], I['peer_wq'][l, kc * 128:(kc + 1) * 128, :])], w=[Wpq])
            P.dma('sp', [(iotar.t[:, :], I['c_iota'][:, :])], w=[iotar])
            P.dma('pool', [(bd16.t[:].rearrange("p a b -> p (a b)"), I['c_bd16'][:, :])], w=[bd16])
            P.reset('AS')
            skf = P.carve('AS', 'skf', [128, 16, 128]); skb = P.carve('AS', 'skb', [128, 16, 128], BF16)
            P.dma('sp', [(skf.t[:, hc, :], I['peer_subkeys'][l, hc, :, :]) for hc in range(16)], w=[skf])
            P.cp(skb.t[:].rearrange("p a b -> p (a b)"), skf.t[:].rearrange("p a b -> p (a b)"), r=[skf], w=[skb], eng='act')
            for q4 in range(2):
                psT = bank[6].t[:].bitcast(BF16)
                P.tr([(psT[:, c * 128:(c + 1) * 128], skb.t[:, q4 * 8 + c, :]) for c in range(8)], identb.t, r=[skb, identb], w=[bank[6]])
                P.cp(skT.t[:, q4 * 8:(q4 + 1) * 8, :], psT.rearrange("p (k t) -> p k t", k=8), r=[bank[6]], w=[skT])
            for i in range(NTP):
                tokg = slice(i * 128, (i + 1) * 128)
                P.reset('AS')
                hbuf = P.carve('AS', 'hb0', [128, D], BF16)
                qTt = P.carve('AS', 'qTt', [128, 16, 128], BF16)
                sc = P.carve('AS', 'sc', [128, 16, 128]); scr = P.carve('AS', 'scr', [128, 16, 128])
                tops = P.carve('AS', 'tops', [128, 16, 16]); idx = P.carve('AS', 'idx', [128, 16, 16], U32)
                ixf = P.carve('AS', 'ixf', [128, 2, 128])
                cand = Buf('cand', sc.t[:].rearrange("p a b -> p (a b)").rearrange("p (h x) -> p h x", h=8)); sc = cand_alias(sc, cand)
                cscr = Buf('cscr', scr.t[:].rearrange("p a b -> p (a b)").rearrange("p (h x) -> p h x", h=8)); scr = cand_alias(scr, cscr)
                tsv = P.carve('AS', 'tsv', [128, 8, 16]); zz = P.carve('AS', 'zz', [128, 8, 4])
                wg = P.carve('AS', 'wg', [128, 8, 256])
                rmsnorm_tile(xtile(i), xbuf_(i), hbuf)
                psT = bank[6].t[:].bitcast(BF16)
                P.tr([(psT[:, kc * 128:(kc + 1) * 128], hbuf.t[:, kc * 128:(kc + 1) * 128]) for kc in range(8)], identb.t, r=[hbuf, identb], w=[bank[6]])
                P.cp(hTall.t[:, :, tokg], psT.rearrange("p (k t) -> p k t", k=8), r=[bank[6]], w=[hTall], eng='act')
                for hc in range(16):
                    bk = bank[hc % 2]
                    P.mm([(bk.t[:, 0:128], Wpq.t[:, kc, hc * 128:(hc + 1) * 128], hTall.t[:, kc, tokg], kc == 0, kc == 7) for kc in range(8)], r=[Wpq, hTall], w=[bk])
                    P.cp(qTt.t[:, hc, :], bk.t[:, 0:128], r=[bk], w=[qTt], eng=('act' if hc % 2 else 'dve'))
                for q4 in range(4):
                    bk = bank[2 + q4 % 2]
                    P.mm([(bk.t[:, c * 128:(c + 1) * 128], qTt.t[:, q4 * 4 + c, :], skT.t[:, q4 * 4 + c, :], True, True) for c in range(4)], r=[qTt, skT], w=[bk])
                    P.cp(sc.t[:, q4 * 4:(q4 + 1) * 4, :], bk.t[:, :].rearrange("p (c k) -> p c k", c=4), r=[bk], w=[sc], eng='act')
                for hc in range(16):
                    P.op('dve', lambda e: e.max(out=tops.t[:, hc, 0:8], in_=sc.t[:, hc, :]), r=[sc], w=[tops])
                    P.op('dve', lambda e: e.max_index(out=idx.t[:, hc, 0:8], in_max=tops.t[:, hc, 0:8], in_values=sc.t[:, hc, :]), r=[sc, tops], w=[idx])
                    P.op('dve', lambda e: e.match_replace(out=scr.t[:, hc, :], in_to_replace=tops.t[:, hc, 0:8], in_values=sc.t[:, hc, :], imm_value=-1e30),
                         r=[sc, tops], w=[scr])
                    P.op('dve', lambda e: e.max(out=tops.t[:, hc, 8:16], in_=scr.t[:, hc, :]), r=[scr], w=[tops])
                    P.op('dve', lambda e: e.max_index(out=idx.t[:, hc, 8:16], in_max=tops.t[:, hc, 8:16], in_values=scr.t[:, hc, :]), r=[scr, tops], w=[idx])
                tv = tops.t[:].rearrange("p (h c) i -> p h c i", c=2)
                iv = idx.t[:].rearrange("p (h c) i -> p h c i", c=2)
                for c in range(2):
                    P.cp(ixf.t[:, c, :].rearrange("p (h i) -> p h i", h=8), iv[:, :, c, :], r=[idx], w=[ixf])
                c4 = cand.t[:].rearrange("p h (i j) -> p h i j", i=16)
                for h in range(8):
                    P.tt(c4[:, h, :, :], tv[:, h, 0, :].unsqueeze(2).to_broadcast([128, 16, 16]), tv[:, h, 1, :].unsqueeze(1).to_broadcast([128, 16, 16]), ALU.add,
                         r=[tops], w=[cand])
                for h in range(8):
                    P.op('dve', lambda e: e.max(out=tsv.t[:, h, 0:8], in_=cand.t[:, h, :]), r=[cand], w=[tsv])
                    P.op('dve', lambda e: e.match_replace(out=cscr.t[:, h, :], in_to_replace=tsv.t[:, h, 0:8], in_values=cand.t[:, h, :], imm_value=-1e30),
                         r=[cand, tsv], w=[cscr])
                    P.op('dve', lambda e: e.max(out=tsv.t[:, h, 8:16], in_=cscr.t[:, h, :]), r=[cscr], w=[tsv])
                P.tt(cscr.t[:, :, 0:16], tsv.t[:, :, :], tsv.t[:, :, 0:1].to_broadcast([128, 8, 16]), ALU.subtract, r=[tsv], w=[cscr])
                P.act(cscr.t[:, :, 0:16], cscr.t[:, :, 0:16], AF.Exp, r=[cscr], w=[cscr])
                P.op('dve', lambda e: e.tensor_reduce(out=zz.t[:, :, 0], in_=cscr.t[:, :, 0:16], op=ALU.add, axis=AX.X), r=[cscr], w=[zz])
                P.op('dve', lambda e: e.reciprocal(out=zz.t[:, :, 1], in_=zz.t[:, :, 0]), r=[zz], w=[zz])
                P.tt(wg.t[:], cand.t[:], tsv.t[:, :, 0:1].to_broadcast([128, 8, 256]), ALU.subtract, r=[cand, tsv], w=[wg])
                P.act(wg.t[:].rearrange("p a b -> p (a b)"), wg.t[:].rearrange("p a b -> p (a b)"), AF.Exp, r=[wg], w=[wg])
                P.tt(cscr.t[:], cand.t[:], tsv.t[:, :, 15:16].to_broadcast([128, 8, 256]), ALU.is_ge, r=[cand, tsv], w=[cscr])
                P.tt(wg.t[:], wg.t[:], cscr.t[:], ALU.mult, r=[wg, cscr], w=[wg])
                P.tt(wg.t[:], wg.t[:], zz.t[:, :, 1:2].to_broadcast([128, 8, 256]), ALU.mult, r=[wg, zz], w=[wg])
                P.tr([(bank[4].t[:, c * 128:(c + 1) * 128], ixf.t[:, c, :]) for c in range(2)], identf.t, r=[ixf, identf], w=[bank[4]])
                P.cp(ixT.t[:].rearrange("p a b -> p (a b)"), bank[4].t[:, 0:256], r=[bank[4]], w=[ixT])
                w4 = wg.t[:].rearrange("p h (i j) -> p h i j", i=16)
                wcp = P.carve('AS', 'wcp', [128, 16, 128])
                for j in range(16):
                    P.cp(wcp.t[:, j, :].rearrange("p (h i) -> p h i", h=8), w4[:, :, :, j], r=[wg], w=[wcp], eng=('act' if j % 2 else 'dve'))
                for q4 in range(4):
                    bk = bank[q4 % 2]
                    P.tr([(bk.t[:, c * 128:(c + 1) * 128], wcp.t[:, q4 * 4 + c, :]) for c in range(4)], identf.t, r=[wcp, identf], w=[bk])
                    P.cp(WcT.t[:, q4 * 4:(q4 + 1) * 4, :], bk.t[:, :].rearrange("p (c t) -> p c t", c=4), r=[bk], w=[WcT], eng='act')
                P.reset('AS')
                Wtile = P.carve('AS', 'Wtile', [128, 128, 128], BF16)
                At = [P.carve('AS', 'At%d' % k, [128, 128], BF16) for k in range(2)]
                Bt = [P.carve('AS', 'Bt%d' % k, [128, 128], BF16) for k in range(2)]
                Wb_ = [P.carve('AS', 'Wbd%d' % k, [128, 8, 16], BF16) for k in range(2)]
                Yt = [P.carve('AS', 'Yt%d' % k, [128, 128], BF16) for k in range(2)]
                for t in range(128):
                    k = t % 2
                    P.ts(At[k].t[:, :], iotar.t[:, :], ixT.t[:, 0, t:t + 1], ALU.is_equal, r=[iotar, ixT], w=[At[k]])
                    P.ts(Bt[k].t[:, :], iotar.t[:, :], ixT.t[:, 1, t:t + 1], ALU.is_equal, r=[iotar, ixT], w=[Bt[k]], eng='pool')
                    P.tt(Wb_[k].t[:], WcT.t[:, :, t].unsqueeze(1).to_broadcast([128, 8, 16]), bd16.t[:], ALU.mult, r=[WcT, bd16], w=[Wb_[k]])
                    P.mm([(bank[2 + k].t[:, 0:128], Wb_[k].t[:].rearrange("p a b -> p (a b)"), At[k].t[:, :], True, True)], r=[Wb_[k], At[k]], w=[bank[2 + k]])
                    P.cp(Yt[k].t[:, :], bank[2 + k].t[:, 0:128], r=[bank[2 + k]], w=[Yt[k]], eng='act')
                    P.mm([(bank[4 + k].t[:, 0:128], Bt[k].t[:, :], Yt[k].t[:, :], True, True)], r=[Bt[k], Yt[k]], w=[bank[4 + k]])
                    P.cp(Wtile.t[:, :, t], bank[4 + k].t[:, 0:128], r=[bank[4 + k]], w=[Wtile], eng=('act' if t % 2 else 'dve'))
                P.dma('sp', [(Wd[i, :, :], Wtile.t[:].rearrange("p a b -> p (a b)"))], r=[Wtile], w=[Wdb], sembuf=Wdb)
            P.reset('AL'); P.reset('AS')
            hTall = P.carve('AL', 'hTall', [128, 8, T + 128], BF16)
            KB = 8
            UT = P.carve('AL', 'UT', [128, 8, KB * 128], BF16)
            Vb = P.carve('AL', 'Vb', [128, KB, D], BF16)
            Ub = [P.carve('AS', 'Ub%d' % k, [128, D], BF16) for k in range(2)]
            WTb = [P.carve('AS', 'WTb%d' % k, [128, KB, 128], BF16) for k in range(2)]
            gl = [P.carve('AS', 'gl%d' % k, [128, 128]) for k in range(2)]
            WGb = [P.carve('AS', 'WGb%d' % k, [128, 128], BF16) for k in range(2)]
            for kb in range(128 // KB):
                for kk_ in range(KB):
                    k1 = kb * KB + kk_
                    ub = Ub[kk_ % 2]
                    P.dma('pool', [(ub.t[:, :], I['peer_u'][l, k1 * 128:(k1 + 1) * 128, :])], w=[ub])
                    P.dma('pool', [(Vb.t[:, kk_, :], I['peer_v'][l, k1 * 128:(k1 + 1) * 128, :])], w=[Vb])
                    psT = bank[6].t[:].bitcast(BF16)
                    P.tr([(psT[:, dc * 128:(dc + 1) * 128], ub.t[:, dc * 128:(dc + 1) * 128]) for dc in range(8)], identb.t, r=[ub, identb], w=[bank[6]])
                    P.cp(UT.t[:, :, kk_ * 128:(kk_ + 1) * 128], psT.rearrange("p (k t) -> p k t", k=8), r=[bank[6]], w=[UT], eng=('act' if kk_ % 2 else 'dve'))
                for i in range(NTP):
                    tokg = slice(i * 128, (i + 1) * 128)
                    wtb = WTb[i % 2]
                    P.dma('sp', [(wtb.t[:].rearrange("p a b -> p (a b)"), Wd[i, :, kb * KB * 128:(kb + 1) * KB * 128])], r=[Wdb], w=[wtb])
                    for kk_ in range(KB):
                        k = kk_ % 2
                        P.mm([(bank[k].t[:, 0:128], UT.t[:, dc, kk_ * 128:(kk_ + 1) * 128], hTall.t[:, dc, tokg], dc == 0, dc == 7) for dc in range(8)],
                             r=[UT, hTall], w=[bank[k]])
                        P.act(gl[k].t[:, :], bank[k].t[:, 0:128], AF.Gelu_apprx_tanh, r=[bank[k]], w=[gl[k]])
                        P.tt(WGb[k].t[:, :], gl[k].t[:, :], wtb.t[:, kk_, :], ALU.mult, r=[gl[k], wtb], w=[WGb[k]])
                        P.mm([(bank[2 + half].t[:, :], WGb[k].t[:, :], Vb.t[:, kk_, half * 512:(half + 1) * 512], kk_ == 0, kk_ == KB - 1) for half in range(2)],
                             r=[WGb[k], Vb], w=[bank[2], bank[3]])
                    for half in range(2):
                        hs_ = slice(half * 512, (half + 1) * 512)
                        P.tt(xtile(i)[:, hs_], xtile(i)[:, hs_], bank[2 + half].t[:, :], ALU.add, r=[xbuf_(i), bank[2 + half]], w=[xbuf_(i)])
            if 'xs3' in DBG and l == 0:
                dbg('xs3', DBG['xs3'], xsp.t[0:4, :], [xsp])
            if 'x3' in DBG and l == 0:
                dbg('x3', DBG['x3'].rearrange("(n p) c -> p n c", p=128), xres.t[:, :, :], [xres])
        if stage != 'all':
            break
    if stage == 'all':
        P.reset('AS')
        ost = [P.carve('AS', 'ost%d' % i, [128, D]) for i in range(2)]
        P.dma('sp', [(gbc.t[:], I['final_norm'][0:1, :].to_broadcast([128, D]))], w=[gbc])
        for i in range(NT):
            ob = ost[i % 2]
            rmsnorm_tile(xres.t[:, i, :], xres, ob)
            P.dma('sp', [(O['y_p'][i * 128:(i + 1) * 128, :], ob.t[:, :])], r=[ob], sembuf=ob, is_out=True)
        if WITH_S:
            rmsnorm_tile(xsp.t[0:4, :], xsp, ost[0], npart=4)
            P.dma('sp', [(O['y_s'][:, :], ost[0].t[0:4, :])], r=[ost[0]], sembuf=ost[0], is_out=True)
    P.finish()
    return P, I, O, DBG

_ROPE_CACHE = {}
def _consts():
    c = {}
    c['c_ident'] = np.eye(128, dtype=np.float32)
    half = 32
    freq = (1.0 / (np.float32(10000.0) ** (np.arange(half, dtype=np.float32) / np.float32(half)))).astype(np.float32)
    def tab(pos):
        ang = pos.astype(np.float32)[:, None] * freq[None, :]
        return np.concatenate([np.cos(ang), np.sin(ang)], axis=1).astype(np.float32)
    c['c_rope_p'] = tab(np.arange(T))
    c['c_rope_s'] = tab(np.full((NS,), 16384))
    p = np.arange(128)
    c['c_causal'] = np.where(p[None, :] >= p[:, None], 0.0, NEG).astype(np.float32)
    c['c_tri'] = (p[:, None] <= p[None, :]).astype(np.float32)
    c['c_negmask'] = np.where(p[None, :] >= p[:, None], 0.0, NEG).astype(np.float32)
    cur = np.arange(8)[:, None]; n = np.arange(8)[None, :]
    past = np.where(n < cur, 0.0, -1e30).astype(np.float32).reshape(1, 64)
    pm = np.where(n < cur, 30000.0, 0.0).astype(np.float32).reshape(1, 64)
    c['c_past'] = np.repeat(past, 128, axis=0)
    c['c_pm'] = np.repeat(pm, 128, axis=0)
    c['c_iota'] = np.repeat(np.arange(128, dtype=np.float32)[None, :], 128, axis=0)
    pp = np.arange(128)
    c['c_bd16'] = (pp[:, None] // 16 == pp[None, :] // 16).astype(np.float32)
    c['c_pidx'] = pp[:, None].astype(np.float32)
    z = np.zeros((128, 127), np.float32); z[:, 63] = 1.0
    c['c_zsel'] = z
    c['c_dmask4'] = (np.arange(4)[:, None] == (np.arange(256)[None, :] // 64)).astype(np.float32)
    c['c_dmask4x'] = (np.arange(4)[:, None] == (np.arange(1024)[None, :] // 256)).astype(np.float32)
    return c


def _core_inputs(inp, c, consts, with_kv=True):
    m = {}
    A = np.ascontiguousarray
    s4 = slice(NS * c, NS * (c + 1))
    m['xp'] = A(inp['x_prompt'][c]); m['xs'] = A(inp['x_sample'][s4, 0]); m['memp'] = A(inp['mem_prompt'][c])
    m['pt'] = A(inp['page_table'][s4].astype(np.int32))
    m['st_lru_h'] = A(inp['state_lru_h'][:, s4]); m['st_lru_conv'] = A(inp['state_lru_conv'][:, s4])
    m['st_ssd'] = A(inp['state_ssd'][:, s4]); m['st_ssd_conv'] = A(inp['state_ssd_conv'][:, s4])
    m['st_rwkv'] = A(inp['state_rwkv'][:, s4]); m['st_rwkv_shift'] = A(inp['state_rwkv_shift'][:, s4])
    m['cmk'] = A(inp['cache_mem_k'][:, s4].reshape(DEPTH, NS, 256, D)); m['cmv'] = A(inp['cache_mem_v'][:, s4].reshape(DEPTH, NS, 256, D))
    if with_kv:
        m['ck'] = inp['cache_moba_k'].reshape(DEPTH * 5120 * 128, 256)
        m['cv'] = inp['cache_moba_v'].reshape(DEPTH * 5120 * 128, 256)
    for k in ['norm_mix', 'w_in', 'w_out', 'lru_conv_w', 'lru_conv_b', 'lru_wa', 'lru_ba', 'lru_wx', 'lru_bx', 'lru_lambda',
              'ssd_conv_w', 'ssd_conv_b', 'ssd_dt_bias', 'ssd_a_log', 'ssd_d', 'ssd_norm', 'rwkv_mu', 'rwkv_w0', 'rwkv_w_up',
              'rwkv_a0', 'rwkv_a_up', 'rwkv_g_up', 'rwkv_k_k', 'rwkv_k_a', 'rwkv_ln_w', 'rwkv_ln_b', 'norm_x', 'x_wq', 'x_wk',
              'x_wv', 'x_wo', 'norm_ffn', 'peer_wq', 'peer_u', 'peer_v']:
        m[k] = inp[k]
    m['rwkv_r_k'] = inp['rwkv_r_k'].reshape(DEPTH, 256)
    m['peer_subkeys'] = inp['peer_subkeys'].reshape(DEPTH, 16, 128, 128)
    m['final_norm'] = inp['final_norm'].reshape(1, D)
    m.update(consts)
    return {k: np.asarray(v) for k, v in m.items()}


_PROG = {}
def kernel(**inputs):
    inp = {k: np.asarray(v) for k, v in inputs.items()}
    if 'p' not in _PROG:
        _PROG['p'] = build({'stage': 'all', 'with_kv': True})
    P, I, O, DBG = _PROG['p']
    consts = _consts()
    in_maps = [_core_inputs(inp, c, consts, with_kv=True) for c in range(NCORES)]
    res = run_bass_kernel_spmd(P.nc, in_maps, core_ids=list(range(NCORES)))
    R = res.results
    def st(name, axis, shape_tail=None):
        a = np.stack([np.asarray(R[c][name]) for c in range(NCORES)], axis=axis)
        return a
    f32 = np.float32
    y_p = st('y_p', 0).astype(f32)
    y_s = np.concatenate([R[c]['y_s'] for c in range(NCORES)], 0).reshape(32, 1, D).astype(f32)
    k_p = st('k_p', 1).reshape(DEPTH, NCORES, T, 4, 64).astype(f32)
    v_p = st('v_p', 1).reshape(DEPTH, NCORES, T, 4, 64).astype(f32)
    lru_h_p = st('lru_h_p', 1).astype(f32)
    lru_conv_p = st('lru_conv_p', 1).astype(f32)
    ssd_p = st('ssd_p', 1).astype(f32)
    ssd_conv_p = st('ssd_conv_p', 1).astype(f32)
    rwkv_p = st('rwkv_p', 1).astype(f32)
    rwkv_shift_p = st('rwkv_shift_p', 1).astype(f32)
    mem_k_p = st('mem_k_p', 1).reshape(DEPTH, NCORES, 256, 4, 256).astype(f32)
    mem_v_p = st('mem_v_p', 1).reshape(DEPTH, NCORES, 256, 4, 256).astype(f32)
    def cat(name, tail):
        return np.concatenate([np.asarray(R[c][name]) for c in range(NCORES)], axis=1).reshape((DEPTH, 32) + tail).astype(f32)
    k_s = cat('k_s', (1, 4, 64)); v_s = cat('v_s', (1, 4, 64))
    lru_h_s = cat('lru_h_s', (256,)); lru_conv_s = cat('lru_conv_s', (3, 256))
    ssd_s = cat('ssd_s', (4, 64, 64)); ssd_conv_s = cat('ssd_conv_s', (3, 512))
    rwkv_s = cat('rwkv_s', (4, 64, 64)); rwkv_shift_s = cat('rwkv_shift_s', (896,))
    return (y_p, y_s, k_p, v_p, lru_h_p, lru_conv_p, ssd_p, ssd_conv_p, rwkv_p, rwkv_shift_p, mem_k_p, mem_v_p,
            k_s, v_s, lru_h_s, lru_conv_s, ssd_s, ssd_conv_s, rwkv_s, rwkv_shift_s)
```
